# Optimizing a Trainium2 kernel written in Bass

```python
import jax, jax.numpy as jnp
from jax import lax
import numpy as np

D_MODEL = 1024
BATCH = 8
SEQ = 8192
DEPTH = 2

GRID_W = 64
CTX_LEN = 256
N_MOD = 9
D_FF = 2816
NORM_EPS = 1e-6

MLA_HEADS = 8
MLA_NOPE = 64
MLA_ROPE = 32
MLA_V = 64
MLA_Q_RANK = 384
MLA_KV_RANK = 128
MLA_IN = MLA_Q_RANK + MLA_KV_RANK + MLA_ROPE
QK_DIM = MLA_NOPE + MLA_ROPE
ATTN_SCALE = QK_DIM ** -0.5
ROPE_THETA = 10000.0
Q_BLOCK = 128

CONV_CH = 512
CONV_K = 31

IN_WIDTH = MLA_IN + 2 * CONV_CH
MIX_WIDTH = MLA_HEADS * MLA_V + CONV_CH

RWKV_HEAD = 64
RWKV_HEADS = D_MODEL // RWKV_HEAD
DECAY_LORA = 64
AAA_LORA = 64
GATE_LORA = 128
GN_EPS = RWKV_HEAD * 1e-5

N_EVEN = (DEPTH + 1) // 2
N_ODD = DEPTH // 2

kernel_name = 'hybrid_mla_conformer_rwkv7_dit'


def rms_norm(x, gain=None):
    xf = x.astype(jnp.float32)
    y = xf * lax.rsqrt(jnp.mean(xf * xf, axis=-1, keepdims=True) + NORM_EPS)
    if gain is not None:
        y = y * gain.astype(jnp.float32)
    return y.astype(x.dtype)


def layer_norm(x, gain, bias):
    xf = x.astype(jnp.float32)
    xc = xf - jnp.mean(xf, axis=-1, keepdims=True)
    y = xc * lax.rsqrt(jnp.mean(xc * xc, axis=-1, keepdims=True) + NORM_EPS)
    return (y * gain.astype(jnp.float32) + bias.astype(jnp.float32)).astype(x.dtype)


def modulate(x, shift, scale):
    return rms_norm(x) * (1 + scale) + shift


def ada_mods(cond, w, b):
    m = jax.nn.silu(cond) @ w + b
    return jnp.split(m[:, None, :], N_MOD, axis=-1)


def half_ffn(x, shift, scale, gate, w1, w3, w2):
    h = modulate(x, shift, scale)
    return x + 0.5 * gate * ((jax.nn.silu(h @ w1) * (h @ w3)) @ w2)


def axial_rope_tables(row, col, dtype):
    n_freq = MLA_ROPE // 4
    inv = ROPE_THETA ** (-jnp.arange(n_freq, dtype=jnp.float32) / n_freq)
    ang_r = row.astype(jnp.float32)[:, None] * inv
    ang_c = col.astype(jnp.float32)[:, None] * inv
    return tuple(t.astype(dtype) for t in (jnp.cos(ang_r), jnp.sin(ang_r), jnp.cos(ang_c), jnp.sin(ang_c)))


def rotate_half(x, cos, sin):
    x1, x2 = jnp.split(x, 2, axis=-1)
    return jnp.concatenate([x1 * cos - x2 * sin, x2 * cos + x1 * sin], axis=-1)


def apply_axial_rope(x, tables):
    cos_r, sin_r, cos_c, sin_c = tables
    nope, rot = x[..., :MLA_NOPE], x[..., MLA_NOPE:]
    rot_r, rot_c = jnp.split(rot, 2, axis=-1)
    return jnp.concatenate([nope, rotate_half(rot_r, cos_r, sin_r), rotate_half(rot_c, cos_c, sin_c)], axis=-1)


def mla_q(z, q_norm, w_qb, qk_norm_q):
    B, L, _ = z.shape
    q = (rms_norm(z[..., :MLA_Q_RANK], q_norm) @ w_qb).reshape(B, L, MLA_HEADS, QK_DIM)
    return rms_norm(q, qk_norm_q).transpose(0, 2, 1, 3)


def mla_kv(z, kv_norm, w_kvb, qk_norm_k):
    B, L, _ = z.shape
    c_kv = z[..., MLA_Q_RANK:MLA_Q_RANK + MLA_KV_RANK]
    k_rope = z[..., MLA_Q_RANK + MLA_KV_RANK:MLA_IN]
    kv = (rms_norm(c_kv, kv_norm) @ w_kvb).reshape(B, L, MLA_HEADS, MLA_NOPE + MLA_V)
    k_nope, v = kv[..., :MLA_NOPE], kv[..., MLA_NOPE:]
    k = jnp.concatenate([k_nope, jnp.broadcast_to(k_rope[:, :, None, :], (B, L, MLA_HEADS, MLA_ROPE))], axis=-1)
    return rms_norm(k, qk_norm_k).transpose(0, 2, 1, 3), v.transpose(0, 2, 1, 3)


def attend(q, k, v):
    s = jnp.einsum('bhqe,bhke->bhqk', q, k, preferred_element_type=jnp.float32) * ATTN_SCALE
    p = jax.nn.softmax(s, axis=-1).astype(v.dtype)
    return jnp.einsum('bhqk,bhkd->bhqd', p, v)


def blocked_attend(q, k, v):
    B, H, L, E = q.shape
    qb = q.reshape(B, H, L // Q_BLOCK, Q_BLOCK, E).transpose(2, 0, 1, 3, 4)
    out = lax.map(lambda qi: attend(qi, k, v), qb)
    return out.transpose(1, 2, 0, 3, 4).reshape(B, H, L, v.shape[-1])


def merge_heads(a):
    B, H, L, E = a.shape
    return a.transpose(0, 2, 1, 3).reshape(B, L, H * E)


def conformer_conv(u, dw_w, dw_b, n_g, n_b):
    a, g = jnp.split(u, 2, axis=-1)
    h = a * jax.nn.sigmoid(g)
    h = lax.conv_general_dilated(h, dw_w, window_strides=(1,), padding=[(CONV_K // 2, CONV_K // 2)],
                                 dimension_numbers=('NWC', 'WIO', 'NWC'), feature_group_count=CONV_CH) + dw_b
    return jax.nn.silu(layer_norm(h, n_g, n_b))


def mla_conv_mixer(h_lat, h_ctx, rope, p, need_ctx):
    w_in, q_norm, w_qb, kv_norm, w_kvb, qn_q, qn_k, dw_w, dw_b, cn_g, cn_b, w_out = p
    z_lat = h_lat @ w_in
    z_ctx = h_ctx @ (w_in if need_ctx else w_in[:, :MLA_IN])
    q_l = apply_axial_rope(mla_q(z_lat, q_norm, w_qb, qn_q), rope)
    k_l, v_l = mla_kv(z_lat, kv_norm, w_kvb, qn_k)
    k_l = apply_axial_rope(k_l, rope)
    k_c, v_c = mla_kv(z_ctx, kv_norm, w_kvb, qn_k)
    att_l = blocked_attend(q_l, jnp.concatenate([k_c, k_l], axis=2), jnp.concatenate([v_c, v_l], axis=2))
    conv_l = conformer_conv(z_lat[..., MLA_IN:], dw_w, dw_b, cn_g, cn_b)
    out_l = jnp.concatenate([merge_heads(att_l), conv_l], axis=-1) @ w_out
    if not need_ctx:
        return out_l, None
    att_c = attend(mla_q(z_ctx, q_norm, w_qb, qn_q), k_c, v_c)
    conv_c = conformer_conv(z_ctx[..., MLA_IN:], dw_w, dw_b, cn_g, cn_b)
    out_c = jnp.concatenate([merge_heads(att_c), conv_c], axis=-1) @ w_out
    return out_l, out_c


def token_shift(x):
    xp = jnp.pad(x, ((0, 0), (1, 1), (0, 0)))
    return 0.5 * (xp[:, :-2] + xp[:, 2:]) - x


def rwkv_streams(h, x_mix, w_r, w_k, w_v, w0, w1, w2, a0, a1, a2, k_k, k_a):
    B, L, _ = h.shape
    heads = lambda t: t.reshape(B, L, RWKV_HEADS, RWKV_HEAD).astype(jnp.float32)
    xx = token_shift(h)
    xr, xw, xk, xv, xa, xg = (h + xx * x_mix[j] for j in range(6))
    r = heads(xr @ w_r)
    k = xk @ w_k
    v = heads(xv @ w_v)
    kk = heads(k * k_k)
    kk = kk / jnp.maximum(jnp.sqrt(jnp.sum(kk * kk, axis=-1, keepdims=True)), 1e-12)
    per_dir = []
    for d in range(2):
        w_pre = (w0[d] + jnp.tanh(xw @ w1[d]) @ w2[d]).astype(jnp.float32)
        w_log = -jax.nn.softplus(-w_pre) - 0.5
        decay = heads(jnp.exp(-jnp.exp(w_log)))
        a = jax.nn.sigmoid(a0[d] + (xa @ a1[d]) @ a2[d])
        k_d = heads(k * (1 + (a - 1) * k_a))
        per_dir.append((decay, k_d, -kk, kk * heads(a)))
    return r, v, xg, per_dir


def wkv_scan(state0, r, decay, k, a_vec, b_vec, v, reverse):
    def step(S, inp):
        r_t, w_t, k_t, a_t, b_t, v_t = inp
        sa = jnp.einsum('bhvk,bhk->bhv', S, a_t)
        S = S * w_t[:, :, None, :] + sa[..., None] * b_t[:, :, None, :] + v_t[..., None] * k_t[:, :, None, :]
        return S, jnp.einsum('bhvk,bhk->bhv', S, r_t)
    xs = tuple(jnp.moveaxis(t, 1, 0) for t in (r, decay, k, a_vec, b_vec, v))
    S, ys = lax.scan(step, state0, xs, reverse=reverse)
    return S, jnp.moveaxis(ys, 0, 1)


def rwkv_readout(y, r, v, k_mean, xg, r_k, ln_g, ln_b, g1, g2, w_o, dtype):
    B, L = y.shape[:2]
    yc = y - jnp.mean(y, axis=-1, keepdims=True)
    yn = (yc * lax.rsqrt(jnp.mean(yc * yc, axis=-1, keepdims=True) + GN_EPS)).reshape(B, L, D_MODEL)
    yn = yn * ln_g.astype(jnp.float32) + ln_b.astype(jnp.float32)
    bonus = (jnp.sum(r * k_mean * r_k.astype(jnp.float32), axis=-1, keepdims=True) * v).reshape(B, L, D_MODEL)
    g = jax.nn.sigmoid(xg @ g1) @ g2
    return ((yn + bonus).astype(dtype) * g) @ w_o


def rwkv_mixer(h_lat, h_ctx, p, need_ctx):
    (x_mix, w_r, w_k, w_v, w0, w1, w2, a0, a1, a2, g1, g2, k_k, k_a, r_k, ln_g, ln_b, w_o) = p
    sp = (x_mix, w_r, w_k, w_v, w0, w1, w2, a0, a1, a2, k_k, k_a)
    r_l, v_l, xg_l, dirs_l = rwkv_streams(h_lat, *sp)
    r_c, v_c, xg_c, dirs_c = rwkv_streams(h_ctx, *sp)
    s0 = jnp.zeros((h_lat.shape[0], RWKV_HEADS, RWKV_HEAD, RWKV_HEAD), jnp.float32)
    y_l, y_c = [], []
    for d, rev in enumerate((False, True)):
        s_ctx, yc = wkv_scan(s0, r_c, *dirs_c[d], v_c, rev)
        _, yl = wkv_scan(s_ctx, r_l, *dirs_l[d], v_l, rev)
        y_l.append(yl)
        y_c.append(yc)
    ro = (r_k, ln_g, ln_b, g1, g2, w_o)
    out_l = rwkv_readout(y_l[0] + y_l[1], r_l, v_l, 0.5 * (dirs_l[0][1] + dirs_l[1][1]), xg_l, *ro, h_lat.dtype)
    if not need_ctx:
        return out_l, None
    out_c = rwkv_readout(y_c[0] + y_c[1], r_c, v_c, 0.5 * (dirs_c[0][1] + dirs_c[1][1]), xg_c, *ro, h_ctx.dtype)
    return out_l, out_c


def setup_inputs(seed: int = 0) -> dict:
    key = jax.random.key(seed)
    ks = iter(jax.random.split(key, 48))
    NE, NO, D, H, N = N_EVEN, N_ODD, D_MODEL, RWKV_HEADS, RWKV_HEAD

    def nrm(shape, scale):
        return scale * jax.random.normal(next(ks), shape, jnp.float32)

    def unif(shape, lo, hi):
        return jax.random.uniform(next(ks), shape, jnp.float32, lo, hi)

    def gain(shape):
        return 1.0 + nrm(shape, 0.02)

    return {
        'x': nrm((BATCH, SEQ, D), 1.0),
        'c': nrm((BATCH, D), 1.0),
        'ctx': nrm((BATCH, CTX_LEN, D), 1.0),
        'c_ctx': nrm((D,), 1.0),
        'ada_w': nrm((DEPTH, D, N_MOD * D), 0.5 * D ** -0.5),
        'ada_b': nrm((DEPTH, N_MOD * D), 0.02),
        'ffn_w1': nrm((DEPTH, 2, D, D_FF), D ** -0.5),
        'ffn_w3': nrm((DEPTH, 2, D, D_FF), D ** -0.5),
        'ffn_w2': nrm((DEPTH, 2, D_FF, D), D_FF ** -0.5),
        'mla_w_in': nrm((NE, D, IN_WIDTH), D ** -0.5),
        'mla_q_norm': gain((NE, MLA_Q_RANK)),
        'mla_w_qb': nrm((NE, MLA_Q_RANK, MLA_HEADS * QK_DIM), MLA_Q_RANK ** -0.5),
        'mla_kv_norm': gain((NE, MLA_KV_RANK)),
        'mla_w_kvb': nrm((NE, MLA_KV_RANK, MLA_HEADS * (MLA_NOPE + MLA_V)), MLA_KV_RANK ** -0.5),
        'mla_qk_norm_q': gain((NE, QK_DIM)),
        'mla_qk_norm_k': gain((NE, QK_DIM)),
        'conv_dw_w': nrm((NE, CONV_K, 1, CONV_CH), CONV_K ** -0.5),
        'conv_dw_b': nrm((NE, CONV_CH), 0.02),
        'conv_norm_g': gain((NE, CONV_CH)),
        'conv_norm_b': nrm((NE, CONV_CH), 0.02),
        'mix_w_out': nrm((NE, MIX_WIDTH, D), MIX_WIDTH ** -0.5),
        'rwkv_x_mix': unif((NO, 6, D), 0.0, 1.0),
        'rwkv_w_r': nrm((NO, D, D), D ** -0.5),
        'rwkv_w_k': nrm((NO, D, D), D ** -0.5),
        'rwkv_w_v': nrm((NO, D, D), D ** -0.5),
        'rwkv_w0': unif((NO, 2, D), -6.0, -1.0),
        'rwkv_w1': nrm((NO, 2, D, DECAY_LORA), D ** -0.5),
        'rwkv_w2': nrm((NO, 2, DECAY_LORA, D), 0.5 * DECAY_LORA ** -0.5),
        'rwkv_a0': nrm((NO, 2, D), 0.1),
        'rwkv_a1': nrm((NO, 2, D, AAA_LORA), D ** -0.5),
        'rwkv_a2': nrm((NO, 2, AAA_LORA, D), 0.5 * AAA_LORA ** -0.5),
        'rwkv_g1': nrm((NO, D, GATE_LORA), D ** -0.5),
        'rwkv_g2': nrm((NO, GATE_LORA, D), GATE_LORA ** -0.5),
        'rwkv_k_k': 0.85 + nrm((NO, D), 0.02),
        'rwkv_k_a': gain((NO, D)),
        'rwkv_r_k': nrm((NO, H, N), 0.1),
        'rwkv_ln_g': gain((NO, D)),
        'rwkv_ln_b': nrm((NO, D), 0.02),
        'rwkv_w_o': nrm((NO, D, D), D ** -0.5),
    }


def reference(x, c, ctx, c_ctx, ada_w, ada_b, ffn_w1, ffn_w3, ffn_w2,
              mla_w_in, mla_q_norm, mla_w_qb, mla_kv_norm, mla_w_kvb, mla_qk_norm_q, mla_qk_norm_k,
              conv_dw_w, conv_dw_b, conv_norm_g, conv_norm_b, mix_w_out,
              rwkv_x_mix, rwkv_w_r, rwkv_w_k, rwkv_w_v, rwkv_w0, rwkv_w1, rwkv_w2,
              rwkv_a0, rwkv_a1, rwkv_a2, rwkv_g1, rwkv_g2, rwkv_k_k, rwkv_k_a, rwkv_r_k,
              rwkv_ln_g, rwkv_ln_b, rwkv_w_o):
    rows = x.shape[1] // GRID_W
    row = jnp.repeat(jnp.arange(rows), GRID_W)
    col = jnp.tile(jnp.arange(GRID_W), rows)
    rope = axial_rope_tables(row, col, x.dtype)
    for i in range(DEPTH):
        last = i == DEPTH - 1
        j = i // 2
        m_l = ada_mods(c, ada_w[i], ada_b[i])
        m_c = ada_mods(c_ctx[None], ada_w[i], ada_b[i])
        f1 = (ffn_w1[i, 0], ffn_w3[i, 0], ffn_w2[i, 0])
        f2 = (ffn_w1[i, 1], ffn_w3[i, 1], ffn_w2[i, 1])
        x = half_ffn(x, *m_l[0:3], *f1)
        ctx = half_ffn(ctx, *m_c[0:3], *f1)
        h_l = modulate(x, m_l[3], m_l[4])
        h_c = modulate(ctx, m_c[3], m_c[4])
        if i % 2 == 0:
            p = (mla_w_in[j], mla_q_norm[j], mla_w_qb[j], mla_kv_norm[j], mla_w_kvb[j], mla_qk_norm_q[j],
                 mla_qk_norm_k[j], conv_dw_w[j], conv_dw_b[j], conv_norm_g[j], conv_norm_b[j], mix_w_out[j])
            out_l, out_c = mla_conv_mixer(h_l, h_c, rope, p, not last)
        else:
            p = (rwkv_x_mix[j], rwkv_w_r[j], rwkv_w_k[j], rwkv_w_v[j], rwkv_w0[j], rwkv_w1[j], rwkv_w2[j],
                 rwkv_a0[j], rwkv_a1[j], rwkv_a2[j], rwkv_g1[j], rwkv_g2[j], rwkv_k_k[j], rwkv_k_a[j],
                 rwkv_r_k[j], rwkv_ln_g[j], rwkv_ln_b[j], rwkv_w_o[j])
            out_l, out_c = rwkv_mixer(h_l, h_c, p, not last)
        x = x + m_l[5] * out_l
        x = half_ffn(x, *m_l[6:9], *f2)
        if not last:
            ctx = ctx + m_c[5] * out_c
            ctx = half_ffn(ctx, *m_c[6:9], *f2)
    return x
```

```python
import os
import numpy as np
import concourse.bass as bass
import concourse.mybir as mybir
from concourse.bass_utils import run_bass_kernel_spmd

F32 = mybir.dt.float32
BF16 = mybir.dt.bfloat16
AF = mybir.ActivationFunctionType
ALU = mybir.AluOpType
AX = mybir.AxisListType

D = 1024
DFF = 2816
NFF = DFF // 128
EPS = 1e-6
CTX = 256


class Tk:
    __slots__ = ("w", "r")

    def __init__(self):
        self.w = None
        self.r = {}


class Buf:
    def __init__(self, t, k=None, excl=False):
        self.t = t
        self.k = k if k is not None else Tk()
        self.excl = excl

    def __getitem__(self, key):
        return self.t[key]


class GBuf:
    def __init__(self, buf, ngroups):
        self.t = buf.t
        self.g = [Buf(buf.t) for _ in range(ngroups)]

    def __getitem__(self, key):
        return self.t[key]


class Prog:
    def __init__(self, nc):
        self.nc = nc
        self.engs = {}
        for name, obj in [("pe", nc.tensor), ("act", nc.scalar), ("dve", nc.vector),
                          ("pool", nc.gpsimd), ("sp", nc.sync)]:
            self.engs[name] = dict(name=name, obj=obj, sem=nc.alloc_semaphore("s_" + name), cnt=0,
                                   waited={}, dsems=None)
        for name, n in [("sp", 12), ("pool", 8), ("act", 4)]:
            e = self.engs[name]
            e["dsems"] = [nc.alloc_semaphore(f"d_{name}{i}") for i in range(n)]
            e["dvals"] = [0] * n
            e["rr"] = 0
        self.nops = 0

    def _wait(self, e, tok):
        sem, val = tok
        key = sem.num
        if e["waited"].get(key, 0) >= val:
            return
        if e["name"] == "pe" and sem is e["sem"]:
            return
        e["obj"].wait_ge(sem, val)
        e["waited"][key] = val
        self.nops += 1

    def _deps(self, e, r, w):
        for b in r:
            k = b.k
            if k.w is not None:
                self._wait(e, k.w)
            if b.excl:
                for tok in k.r.values():
                    self._wait(e, tok)
        for b in w:
            k = b.k
            if k.w is not None:
                self._wait(e, k.w)
            for tok in k.r.values():
                self._wait(e, tok)

    def _update(self, tok, r, w):
        sem, val = tok
        for b in r:
            b.k.r[sem.num] = tok
        for b in w:
            b.k.w = tok
            b.k.r = {}

    def op(self, eng, fn, r=(), w=()):
        e = self.engs[eng]
        self._deps(e, r, w)
        ins = fn(e["obj"])
        e["cnt"] += 1
        ins.then_inc(e["sem"], 1)
        tok = (e["sem"], e["cnt"])
        self._update(tok, r, w)
        self.nops += 1
        return tok

    def dma(self, eng, out, in_, r=(), w=(), **kw):
        e = self.engs[eng]
        self._deps(e, r, w)
        i = e["rr"]
        e["rr"] = (i + 1) % len(e["dsems"])
        sem = e["dsems"][i]
        if e["dvals"][i] > 0:
            self._wait(e, (sem, e["dvals"][i]))
        ins = e["obj"].dma_start(out=out, in_=in_, **kw)
        e["dvals"][i] += 16
        ins.then_inc(sem, 16)
        tok = (sem, e["dvals"][i])
        self._update(tok, r, w)
        self.nops += 1
        return tok

    def barrier(self):
        toks = []
        for e in self.engs.values():
            if e["cnt"] > 0:
                toks.append((e["sem"], e["cnt"]))
            if e["dsems"]:
                for s, v in zip(e["dsems"], e["dvals"]):
                    if v > 0:
                        toks.append((s, v))
        for e in self.engs.values():
            for t in toks:
                if t[0] is e["sem"]:
                    continue
                self._wait(e, t)

    def finish(self):
        self.barrier()


class Ctx:
    def __init__(self, nc):
        self.nc = nc
        self.P = Prog(nc)
        self.uid = 0
        self.stack = []

    def sb(self, shape, dt, name=None):
        self.uid += 1
        cm = self.nc.sbuf_tensor(f"{name or 'sb'}_{self.uid}", list(shape), dt)
        t = cm.__enter__()
        self.stack[-1].append(cm)
        return Buf(t)

    def ps(self, shape, dt, name=None):
        self.uid += 1
        cm = self.nc.psum_tensor(f"{name or 'ps'}_{self.uid}", list(shape), dt)
        t = cm.__enter__()
        self.stack[-1].append(cm)
        return Buf(t, excl=True)

    def push(self):
        self.stack.append([])

    def pop(self):
        self.P.barrier()
        for cm in reversed(self.stack.pop()):
            cm.__exit__(None, None, None)

    def dram(self, name, shape, dt):
        return self.nc.dram_tensor(name, list(shape), dt, kind="Internal").ap()


def stage_ada(C, cond_d, ada_w_d, ada_b_d, mods_d):
    P = C.P
    C.push()
    cond = C.sb([128, 8, 2], F32)
    scond = C.sb([128, 8, 2], F32)
    wb = [C.sb([128, 8, 512], F32) for _ in range(3)]
    bb = [C.sb([2, 512], F32) for _ in range(3)]
    mrow = [C.sb([2, 9 * D], F32) for _ in range(2)]
    pss = [C.ps([2, 512], F32) for _ in range(2)]
    P.dma("sp", cond[:], cond_d, w=[cond])
    P.op("act", lambda e: e.activation(out=scond[:], in_=cond[:], func=AF.Silu), r=[cond], w=[scond])
    it = 0
    for l in range(2):
        wv = ada_w_d[l].rearrange("(k p) n -> p k n", p=128)
        for n in range(18):
            w = wb[it % 3]
            b = bb[it % 3]
            ps = pss[it % 2]
            P.dma("sp", w[:], wv[:, :, n * 512:(n + 1) * 512], w=[w])
            P.dma("sp", b[:], ada_b_d[l, n * 512:(n + 1) * 512].partition_broadcast(2), w=[b])
            for kc in range(8):
                P.op("pe", lambda e, kc=kc, w=w, ps=ps: e.matmul(ps[:], lhsT=scond[:, kc, :], rhs=w[:, kc, :],
                                                              start=(kc == 0), stop=(kc == 7)),
                     r=[scond, w], w=[ps])
            P.op("dve", lambda e, ps=ps, b=b, l=l, n=n: e.tensor_tensor(out=mrow[l][:, n * 512:(n + 1) * 512],
                                                                      in0=ps[:], in1=b[:], op=ALU.add),
                 r=[ps, b], w=[mrow[l]])
            it += 1
        P.dma("sp", mods_d[l], mrow[l][:], r=[mrow[l]])
    C.pop()


def load_bc(C, dst, src_row, eng="sp"):
    C.P.dma(eng, dst[:], src_row.partition_broadcast(128), w=[dst])


def make_hT_sub(C, xs, sub, sc1, sh, hT, tmp, hb, psT, ident, stat, epsb):
    P = C.P
    junk, ssq, rt, rstd = stat
    P.op("act", lambda e: e.activation(out=junk[:], in_=xs[:], func=AF.Square, accum_out=ssq[:]),
         r=[xs], w=[junk, ssq])
    P.op("act", lambda e: e.activation(out=rt[:], in_=ssq[:], func=AF.Sqrt, bias=epsb[:], scale=1.0 / D),
         r=[ssq, epsb], w=[rt])
    P.op("dve", lambda e: e.reciprocal(out=rstd[:], in_=rt[:]), r=[rt], w=[rstd])
    P.op("dve", lambda e: e.scalar_tensor_tensor(out=tmp[:], in0=xs[:], scalar=rstd[:, 0:1],
                                                 in1=sc1[:], op0=ALU.mult, op1=ALU.mult),
         r=[xs, rstd, sc1], w=[tmp])
    P.op("pool", lambda e: e.tensor_tensor(out=hb[:], in0=tmp[:], in1=sh[:], op=ALU.add),
         r=[tmp, sh], w=[hb])
    for kc in range(8):
        P.op("pe", lambda e, kc=kc: e.transpose(out=psT[:, kc * 128:(kc + 1) * 128],
                                                in_=hb[:, kc * 128:(kc + 1) * 128], identity=ident[:]),
             r=[hb, ident], w=[psT])
    P.op("act", lambda e: e.copy(out=hT[:, :, sub * 128:(sub + 1) * 128],
                                 in_=psT[:].rearrange("p (k t) -> p k t", k=8)),
         r=[psT], w=[hT])


def stage_ffn(C, segs, w1_d, w3_d, w2_d, mods_d, l, mbase, consts):
    P = C.P
    C.push()
    W1 = C.sb([128, 8, DFF], BF16, "W1")
    W3 = C.sb([128, 8, DFF], BF16, "W3")
    W2 = C.sb([128, NFF, D], BF16, "W2")
    P.dma("pool", W1[:], w1_d.rearrange("(k p) n -> p k n", p=128), w=[W1])
    P.dma("pool", W3[:], w3_d.rearrange("(k p) n -> p k n", p=128), w=[W3])
    P.dma("pool", W2[:], w2_d.rearrange("(k p) n -> p k n", p=128), w=[W2])
    ident, epsb = consts
    sc1 = C.sb([128, D], F32)
    sh = C.sb([128, D], F32)
    gt = C.sb([128, D], F32)
    NXB = 3
    xbs = [C.sb([128, D], F32, "xs") for _ in range(NXB)]
    hT = C.sb([128, 8, 512], BF16, "hT")
    gT = C.sb([128, NFF, 512], BF16, "gT")
    tmp = C.sb([128, D], F32)
    hb = C.sb([128, D], BF16)
    junk = C.sb([128, D], BF16)
    ssq = C.sb([128, 1], F32)
    rt = C.sb([128, 1], F32)
    rstd = C.sb([128, 1], F32)
    sil = [C.sb([128, 512], BF16, "sil") for _ in range(2)]
    psT = C.ps([128, D], BF16, "psT")
    ps1 = [C.ps([128, 512], F32, "ps1") for _ in range(2)]
    ps3 = [C.ps([128, 512], F32, "ps3") for _ in range(2)]
    pso = [C.ps([128, 512], F32, "pso") for _ in range(2)]
    stat = (junk, ssq, rt, rstd)
    cur_which = None
    tiles = []
    for (xd, ntok, which) in segs:
        t0 = 0
        while t0 < ntok:
            n = min(512, ntok - t0)
            tiles.append((xd, t0, n, which))
            t0 += n
    loads = []
    for (xd, t0, n, which) in tiles:
        for ph in range(2):
            for sub in range(n // 128):
                loads.append((xd, t0 + sub * 128))
    lstate = dict(i=0)

    def issue_load():
        i = lstate["i"]
        if i >= len(loads):
            return
        xd, r0 = loads[i]
        xb = xbs[i % NXB]
        P.dma("sp", xb[:], xd[r0:r0 + 128, :], w=[xb])
        lstate["i"] = i + 1

    cons = dict(i=0)

    def next_x():
        i = cons["i"]
        cons["i"] = i + 1
        while lstate["i"] < min(i + NXB - 1, len(loads)):
            issue_load()
        if lstate["i"] <= i:
            issue_load()
        return xbs[i % NXB]

    it = 0
    oi = 0
    for (xd, t0, n, which) in tiles:
        nsub = n // 128
        if which != cur_which:
            cur_which = which
            load_bc(C, sh, mods_d[l, which, (mbase + 0) * D:(mbase + 1) * D], "act")
            load_bc(C, sc1, mods_d[l, which, (mbase + 1) * D:(mbase + 2) * D], "act")
            load_bc(C, gt, mods_d[l, which, (mbase + 2) * D:(mbase + 3) * D], "act")
            P.op("dve", lambda e: e.tensor_scalar_add(out=sc1[:], in0=sc1[:], scalar1=1.0), r=[sc1], w=[sc1])
            P.op("dve", lambda e: e.tensor_scalar_mul(out=gt[:], in0=gt[:], scalar1=0.5), r=[gt], w=[gt])
        for sub in range(nsub):
            xs = next_x()
            make_hT_sub(C, xs, sub, sc1, sh, hT, tmp, hb, psT, ident, stat, epsb)
        for f in range(NFF):
            p1 = ps1[it % 2]
            p3 = ps3[it % 2]
            sl = sil[it % 2]
            it += 1
            for kc in range(8):
                P.op("pe", lambda e, kc=kc, f=f, p1=p1: e.matmul(p1[:, 0:n], lhsT=W1[:, kc, f * 128:(f + 1) * 128],
                                                                rhs=hT[:, kc, 0:n], start=(kc == 0), stop=(kc == 7)),
                     r=[W1, hT], w=[p1])
            for kc in range(8):
                P.op("pe", lambda e, kc=kc, f=f, p3=p3: e.matmul(p3[:, 0:n], lhsT=W3[:, kc, f * 128:(f + 1) * 128],
                                                                rhs=hT[:, kc, 0:n], start=(kc == 0), stop=(kc == 7)),
                     r=[W3, hT], w=[p3])
            P.op("act", lambda e, p1=p1, sl=sl: e.activation(out=sl[:, 0:n], in_=p1[:, 0:n], func=AF.Silu),
                 r=[p1], w=[sl])
            P.op("dve", lambda e, p3=p3, sl=sl, f=f: e.tensor_tensor(out=gT[:, f, 0:n], in0=sl[:, 0:n], in1=p3[:, 0:n],
                                                                    op=ALU.mult), r=[sl, p3], w=[gT])
        for sub in range(nsub):
            xs = next_x()
            for dh in range(2):
                po = pso[oi % 2]
                oi += 1
                for f in range(NFF):
                    P.op("pe", lambda e, f=f, sub=sub, dh=dh, po=po: e.matmul(
                        po[:], lhsT=gT[:, f, sub * 128:(sub + 1) * 128], rhs=W2[:, f, dh * 512:(dh + 1) * 512],
                        start=(f == 0), stop=(f == NFF - 1)), r=[gT, W2], w=[po])
                P.op("dve", lambda e, po=po, dh=dh: e.tensor_tensor(out=tmp[:, dh * 512:(dh + 1) * 512], in0=po[:],
                                                                   in1=gt[:, dh * 512:(dh + 1) * 512],
                                                                   op=ALU.mult), r=[po, gt], w=[tmp])
                P.op("pool", lambda e, dh=dh, xs=xs: e.tensor_tensor(
                    out=xs[:, dh * 512:(dh + 1) * 512], in0=tmp[:, dh * 512:(dh + 1) * 512],
                    in1=xs[:, dh * 512:(dh + 1) * 512], op=ALU.add), r=[tmp, xs], w=[xs])
            P.dma("act", xd[t0 + sub * 128:t0 + (sub + 1) * 128, :], xs[:], r=[xs])
    C.pop()


ATT_SCALE = 96 ** -0.5
NH = 8


def rms_bc(C, ss_ps, n_feat, rows, n, rt, rstd, epsb):
    P = C.P
    P.op("act", lambda e: e.activation(out=rt[0:rows, 0:n], in_=ss_ps[0:rows, 0:n], func=AF.Ln,
                                       bias=epsb[0:rows, :], scale=1.0 / n_feat), r=[ss_ps, epsb], w=[rt])
    P.op("act", lambda e: e.activation(out=rstd[0:rows, 0:n], in_=rt[0:rows, 0:n], func=AF.Exp, scale=-0.5),
         r=[rt], w=[rstd])


def stage_mla_proj(C, segs, S, wd, mods_d, l, consts, scr):
    P = C.P
    C.push()
    ident, epsb = consts
    w_in_d, w_qb_d, w_kvb_d, mlac_d, cos_d, sin_d, prot_d, ones_d = wd
    qT_d, qcT_d, kT_d, v_d, glu_l_d, glu_c_d = scr
    Win = C.sb([128, 8, 1568], BF16, "Win")
    Wqb = C.sb([128, 3, 768], BF16, "Wqb")
    Wkvb = C.sb([128, 1024], BF16, "Wkvb")
    P.dma("pool", Win[:], w_in_d.rearrange("(k p) n -> p k n", p=128), w=[Win])
    P.dma("pool", Wqb[:], w_qb_d.rearrange("(k p) n -> p k n", p=128), w=[Wqb])
    P.dma("pool", Wkvb[:], w_kvb_d, w=[Wkvb])
    mlac = C.sb([128, 144], F32, "mlac")
    P.dma("sp", mlac[:], mlac_d, w=[mlac])
    onesf = C.sb([128, 128], F32, "onesf")
    onesb = C.sb([128, 128], BF16, "onesb")
    prot = C.sb([32, 32], F32, "prot")
    P.dma("sp", onesf[:], ones_d, w=[onesf])
    P.dma("sp", prot[:], prot_d, w=[prot])
    P.op("dve", lambda e: e.tensor_copy(out=onesb[:], in_=onesf[:]), r=[onesf], w=[onesb])
    zero = C.sb([128, 4, 16], F32, "zero")
    P.op("dve", lambda e: e.memset(zero[:], 0.0), w=[zero])
    if "stop1" in os.environ.get("KDBG", ""):
        C.pop()
        return
    KDBG = os.environ.get("KDBG", "")
    for (gd, ntok) in ((glu_l_d, S), (glu_c_d, CTX)):
        if "nopad" in KDBG:
            break
        gv = gd.rearrange("(c p) t -> p c t", p=128)
        P.dma("sp", gv[:, :, 0:15], zero[:, :, 0:15], r=[zero])
        P.dma("sp", gv[:, :, 15 + ntok:30 + ntok], zero[:, :, 0:15], r=[zero])
    sc1 = C.sb([128, D], F32)
    sh = C.sb([128, D], F32)
    NXB = 3
    xbs = [C.sb([128, D], F32, "xs") for _ in range(NXB)]
    hT = C.sb([128, 8, 512], BF16, "hT")
    tmp = C.sb([128, D], F32)
    hb = C.sb([128, D], BF16)
    junk = C.sb([128, D], BF16)
    ssq = C.sb([128, 1], F32)
    rt1 = C.sb([128, 1], F32)
    rstd1 = C.sb([128, 1], F32)
    stat = (junk, ssq, rt1, rstd1)
    zq = C.sb([128, 3, 512], F32, "zq")
    sqb = [C.sb([128, 512], BF16, "sqb") for _ in range(2)]
    cqn = C.sb([128, 3, 512], BF16, "cqn")
    zkv = C.sb([128, 512], F32, "zkv")
    ckvn = C.sb([128, 512], BF16, "ckvn")
    kr_raw = C.sb([32, 512], F32, "kr_raw")
    sq_kr = C.sb([32, 512], BF16, "sq_kr")
    rts = [C.sb([128, 512], F32, "rt") for _ in range(3)]
    rstds = [C.sb([128, 512], F32, "rstd") for _ in range(3)]
    rr = dict(i=0)

    def nrs():
        rr["i"] += 1
        return rts[rr["i"] % 3], rstds[rr["i"] % 3]
    sig = [C.sb([128, 512], F32, "sig") for _ in range(2)]
    glu = [C.sb([128, 512], F32, "glu") for _ in range(2)]
    cos = C.sb([32, 512], F32, "cos")
    sin = C.sb([32, 512], F32, "sin")
    hn_o = [C.sb([64, 512], BF16, "hn_o") for _ in range(2)]
    hr = [C.sb([32, 512], F32, "hr") for _ in range(2)]
    t1 = [C.sb([32, 512], F32, "t1") for _ in range(2)]
    t2 = [C.sb([32, 512], F32, "t2") for _ in range(2)]
    hr_o = [C.sb([32, 512], BF16, "hr_o") for _ in range(2)]
    sqn = [C.sb([64, 512], BF16, "sqn") for _ in range(2)]
    sqr = [C.sb([32, 512], BF16, "sqr") for _ in range(2)]
    vt = [C.sb([128, 512], BF16, "vt") for _ in range(2)]
    psT = C.ps([128, D], BF16, "psT")
    psA = [C.ps([128, 512], F32, "psA") for _ in range(int(os.environ.get("NPSA", "3")))]
    psB = [C.ps([128, 512], F32, "psB") for _ in range(2)]
    psR = [C.ps([32, 512], F32, "psR") for _ in range(2)]
    cnt = dict(a=0, b=0, r=0, g=0, h=0, v=0, s=0)

    def nxt(lst, key):
        i = cnt[key]
        cnt[key] = i + 1
        return lst[i % len(lst)]

    cur_which = None
    for (xd, ntok, which, koff, is_lat) in segs:
        for t0 in range(0, ntok, 512):
            n = min(512, ntok - t0)
            nsub = n // 128
            if which != cur_which:
                cur_which = which
                load_bc(C, sh, mods_d[l, which, 3 * D:4 * D], "act")
                load_bc(C, sc1, mods_d[l, which, 4 * D:5 * D], "act")
                P.op("dve", lambda e: e.tensor_scalar_add(out=sc1[:], in0=sc1[:], scalar1=1.0), r=[sc1], w=[sc1])
            if is_lat:
                P.dma("sp", cos[:, 0:n], cos_d[:, t0:t0 + n], w=[cos])
                P.dma("sp", sin[:, 0:n], sin_d[:, t0:t0 + n], w=[sin])
            for sub in range(nsub):
                xs = xbs[cnt["s"] % NXB]
                cnt["s"] += 1
                P.dma("sp", xs[:], xd[t0 + sub * 128:t0 + (sub + 1) * 128, :], w=[xs])
                make_hT_sub(C, xs, sub, sc1, sh, hT, tmp, hb, psT, ident, stat, epsb)

            def proj(ps, col0, ncol):
                for kc in range(8):
                    P.op("pe", lambda e, kc=kc: e.matmul(ps[0:ncol, 0:n], lhsT=Win[:, kc, col0:col0 + ncol],
                                                         rhs=hT[:, kc, 0:n], start=(kc == 0), stop=(kc == 7)),
                         r=[Win, hT], w=[ps])

            if "stop2" in KDBG:
                continue
            pss = nxt(psB, "b")
            for c in range(3):
                ps = nxt(psA, "a")
                proj(ps, c * 128, 128)
                sq = nxt(sqb, "g")
                if "nosq" not in KDBG:
                    P.op("act", lambda e, ps=ps, sq=sq: e.activation(out=sq[:, 0:n], in_=ps[:, 0:n], func=AF.Square),
                         r=[ps], w=[sq])
                if "nocp" not in KDBG:
                    P.op("dve", lambda e, ps=ps, c=c: e.tensor_copy(out=zq[:, c, 0:n], in_=ps[:, 0:n]), r=[ps], w=[zq])
                if "cq1" in KDBG:
                    continue
                P.op("pe", lambda e, sq=sq, c=c: e.matmul(pss[:, 0:n], lhsT=onesb[:], rhs=sq[:, 0:n],
                                                         start=(c == 0), stop=(c == 2)), r=[onesb, sq], w=[pss])
            if "cq1" in KDBG or "cq2" in KDBG:
                continue
            rt, rstd = nrs()
            rms_bc(C, pss, 384, 128, n, rt, rstd, epsb)
            if "cq3" in KDBG:
                continue
            for c in range(3):
                P.op("dve", lambda e, c=c: e.scalar_tensor_tensor(out=cqn[:, c, 0:n], in0=zq[:, c, 0:n],
                                                                  scalar=mlac[:, c:c + 1], in1=rstd[:, 0:n],
                                                                  op0=ALU.mult, op1=ALU.mult),
                     r=[zq, mlac, rstd], w=[cqn])
            if "stop3" in KDBG:
                continue
            ps = nxt(psA, "a")
            proj(ps, 384, 128)
            sq = nxt(sqb, "g")
            P.op("act", lambda e, ps=ps, sq=sq: e.activation(out=sq[:, 0:n], in_=ps[:, 0:n], func=AF.Square),
                 r=[ps], w=[sq])
            P.op("dve", lambda e, ps=ps: e.tensor_copy(out=zkv[:, 0:n], in_=ps[:, 0:n]), r=[ps], w=[zkv])
            pss = nxt(psB, "b")
            P.op("pe", lambda e, sq=sq: e.matmul(pss[:, 0:n], lhsT=onesb[:], rhs=sq[:, 0:n], start=True, stop=True),
                 r=[onesb, sq], w=[pss])
            rt, rstd = nrs()
            rms_bc(C, pss, 128, 128, n, rt, rstd, epsb)
            P.op("dve", lambda e: e.scalar_tensor_tensor(out=ckvn[:, 0:n], in0=zkv[:, 0:n], scalar=mlac[:, 3:4],
                                                         in1=rstd[:, 0:n], op0=ALU.mult, op1=ALU.mult),
                 r=[zkv, mlac, rstd], w=[ckvn])
            if "stop4" in KDBG:
                continue
            ps = nxt(psA, "a")
            proj(ps, 512, 32)
            P.op("act", lambda e, ps=ps: e.activation(out=sq_kr[:, 0:n], in_=ps[0:32, 0:n], func=AF.Square),
                 r=[ps], w=[sq_kr])
            P.op("dve", lambda e, ps=ps: e.tensor_copy(out=kr_raw[:, 0:n], in_=ps[0:32, 0:n]), r=[ps], w=[kr_raw])
            gd = glu_l_d if is_lat else glu_c_d
            for c in range(4):
                if "noglu" in KDBG:
                    break
                pa = nxt(psA, "a")
                proj(pa, 544 + c * 128, 128)
                pg = nxt(psA, "a")
                proj(pg, 1056 + c * 128, 128)
                sg = nxt(sig, "h")
                gl = glu[cnt["h"] % 2]
                P.op("act", lambda e, pg=pg, sg=sg: e.activation(out=sg[:, 0:n], in_=pg[:, 0:n], func=AF.Sigmoid),
                     r=[pg], w=[sg])
                P.op("dve", lambda e, pa=pa, sg=sg, gl=gl: e.tensor_tensor(out=gl[:, 0:n], in0=sg[:, 0:n],
                                                                          in1=pa[:, 0:n], op=ALU.mult),
                     r=[sg, pa], w=[gl])
                P.dma("act", gd[c * 128:(c + 1) * 128, 15 + t0:15 + t0 + n], gl[:, 0:n], r=[gl])

            def head_side(ps_n, ps_r_or_raw, raw_is_sbuf, sq_r_shared, gcol_n, gcol_r, dst, dcol0, rope):
                sn = nxt(sqn, "v")
                P.op("act", lambda e: e.activation(out=sn[:, 0:n], in_=ps_n[0:64, 0:n], func=AF.Square),
                     r=[ps_n], w=[sn])
                if sq_r_shared is None:
                    sr = sqr[cnt["v"] % 2]
                    P.op("act", lambda e: e.activation(out=sr[:, 0:n], in_=ps_r_or_raw[0:32, 0:n], func=AF.Square),
                         r=[ps_r_or_raw], w=[sr])
                else:
                    sr = sq_r_shared
                pss = nxt(psB, "b")
                P.op("pe", lambda e: e.matmul(pss[0:64, 0:n], lhsT=onesb[0:64, 0:64], rhs=sn[:, 0:n],
                                              start=True, stop=False), r=[onesb, sn], w=[pss])
                P.op("pe", lambda e: e.matmul(pss[0:64, 0:n], lhsT=onesb[0:32, 0:64], rhs=sr[:, 0:n],
                                              start=False, stop=True), r=[onesb, sr], w=[pss])
                rt, rstd = nrs()
                rms_bc(C, pss, 96, 64, n, rt, rstd, epsb)
                ho = nxt(hn_o, "r")
                P.op("dve", lambda e: e.scalar_tensor_tensor(out=ho[:, 0:n], in0=ps_n[0:64, 0:n],
                                                             scalar=mlac[0:64, gcol_n:gcol_n + 1],
                                                             in1=rstd[0:64, 0:n], op0=ALU.mult, op1=ALU.mult),
                     r=[ps_n, mlac, rstd], w=[ho])
                P.dma("act", dst[0:64, dcol0:dcol0 + n], ho[:, 0:n], r=[ho])
                i = cnt["r"]
                h_r = hr[i % 2]
                ro = hr_o[i % 2]
                if rope:
                    P.op("dve", lambda e: e.scalar_tensor_tensor(out=h_r[:, 0:n], in0=ps_r_or_raw[0:32, 0:n],
                                                                 scalar=mlac[0:32, gcol_r:gcol_r + 1],
                                                                 in1=rstd[0:32, 0:n], op0=ALU.mult, op1=ALU.mult),
                         r=[ps_r_or_raw, mlac, rstd], w=[h_r])
                    pr = nxt(psR, "s")
                    P.op("pe", lambda e: e.matmul(pr[:, 0:n], lhsT=prot[:], rhs=h_r[:, 0:n], start=True, stop=True),
                         r=[prot, h_r], w=[pr])
                    a1 = t1[i % 2]
                    a2 = t2[i % 2]
                    P.op("pool", lambda e: e.tensor_tensor(out=a1[:, 0:n], in0=h_r[:, 0:n], in1=cos[:, 0:n],
                                                           op=ALU.mult), r=[h_r, cos], w=[a1])
                    P.op("dve", lambda e: e.tensor_tensor(out=a2[:, 0:n], in0=pr[:, 0:n], in1=sin[:, 0:n],
                                                          op=ALU.mult), r=[pr, sin], w=[a2])
                    P.op("pool", lambda e: e.tensor_tensor(out=ro[:, 0:n], in0=a1[:, 0:n], in1=a2[:, 0:n],
                                                           op=ALU.add), r=[a1, a2], w=[ro])
                else:
                    P.op("dve", lambda e: e.scalar_tensor_tensor(out=ro[:, 0:n], in0=ps_r_or_raw[0:32, 0:n],
                                                                 scalar=mlac[0:32, gcol_r:gcol_r + 1],
                                                                 in1=rstd[0:32, 0:n], op0=ALU.mult, op1=ALU.mult),
                         r=[ps_r_or_raw, mlac, rstd], w=[ro])
                P.dma("act", dst[64:96, dcol0:dcol0 + n], ro[:, 0:n], r=[ro])

            for h in range(NH):
                if "noheads" in KDBG:
                    break
                pqn = nxt(psA, "a")
                for c in range(3):
                    P.op("pe", lambda e, c=c: e.matmul(pqn[0:64, 0:n], lhsT=Wqb[:, c, h * 96:h * 96 + 64],
                                                       rhs=cqn[:, c, 0:n], start=(c == 0), stop=(c == 2)),
                         r=[Wqb, cqn], w=[pqn])
                pqr = nxt(psA, "a")
                for c in range(3):
                    P.op("pe", lambda e, c=c: e.matmul(pqr[0:32, 0:n], lhsT=Wqb[:, c, h * 96 + 64:h * 96 + 96],
                                                       rhs=cqn[:, c, 0:n], start=(c == 0), stop=(c == 2)),
                         r=[Wqb, cqn], w=[pqr])
                if is_lat:
                    head_side(pqn, pqr, False, None, 4, 5, qT_d[h], t0, True)
                else:
                    head_side(pqn, pqr, False, None, 4, 5, qcT_d[h], t0, False)
                pkn = nxt(psA, "a")
                P.op("pe", lambda e: e.matmul(pkn[0:64, 0:n], lhsT=Wkvb[:, h * 128:h * 128 + 64], rhs=ckvn[:, 0:n],
                                              start=True, stop=True), r=[Wkvb, ckvn], w=[pkn])
                head_side(pkn, kr_raw, True, sq_kr, 6, 7, kT_d[h], koff + t0, is_lat)
            for sub in range(nsub):
                if "nov" in KDBG:
                    break
                pv = nxt(psA, "a")
                P.op("pe", lambda e, sub=sub: e.matmul(
                    pv[:].rearrange("p (h e) -> p h e", e=64), lhsT=ckvn[:, sub * 128:(sub + 1) * 128],
                    rhs=Wkvb[:].rearrange("p (h e) -> p h e", e=128)[:, :, 64:128], start=True, stop=True),
                     r=[ckvn, Wkvb], w=[pv])
                vb = nxt(vt, "v")
                P.op("act", lambda e: e.copy(out=vb[:], in_=pv[:]), r=[pv], w=[vb])
                r0 = koff + t0 + sub * 128
                P.dma("act", v_d[r0:r0 + 128, :], vb[:], r=[vb])
    C.pop()


def stage_conv(C, segs, mlac_d, ones_d, scr, consts):
    P = C.P
    C.push()
    ident, epsb = consts
    mixT_d = scr
    mlac = C.sb([128, 144], F32, "mlac")
    onesf = C.sb([128, 128], F32, "onesf")
    P.dma("sp", mlac[:], mlac_d, w=[mlac])
    P.dma("sp", onesf[:], ones_d, w=[onesf])
    G = [C.sb([128, 4, 542], F32, "G") for _ in range(2)]
    acc = [C.sb([128, 512], F32, "acc") for _ in range(4)]
    sq = [C.sb([128, 512], F32, "sq") for _ in range(2)]
    mean = C.sb([128, 512], F32, "mean")
    m2 = C.sb([128, 512], F32, "m2")
    var = C.sb([128, 512], F32, "var")
    rt = C.sb([128, 512], F32, "rt")
    rstd = C.sb([128, 512], F32, "rstd")
    tt = [C.sb([128, 512], F32, "tt") for _ in range(2)]
    ob = [C.sb([128, 512], BF16, "ob") for _ in range(2)]
    ps1 = C.ps([128, 512], F32, "ps1")
    ps2 = C.ps([128, 512], F32, "ps2")
    W0 = 8
    it = 0
    for (gd, ntok, koff) in segs:
        gv = gd.rearrange("(c p) t -> p c t", p=128)
        for t0 in range(0, ntok, 512):
            n = min(512, ntok - t0)
            g = G[it % 2]
            it += 1
            P.dma("sp", g[:, :, 0:n + 30], gv[:, :, t0:t0 + n + 30], w=[g])
            for c in range(4):
                eng = "dve"
                a = acc[c]
                P.op(eng, lambda e, c=c, a=a: e.tensor_scalar(out=a[:, 0:n], in0=g[:, c, 0:n],
                                                             scalar1=mlac[:, W0 + c * 31:W0 + c * 31 + 1],
                                                             scalar2=mlac[:, 132 + c:133 + c],
                                                             op0=ALU.mult, op1=ALU.add), r=[g, mlac], w=[a])
                for j in range(1, 31):
                    P.op(eng, lambda e, c=c, a=a, j=j: e.scalar_tensor_tensor(
                        out=a[:, 0:n], in0=g[:, c, j:j + n], scalar=mlac[:, W0 + c * 31 + j:W0 + c * 31 + j + 1],
                        in1=a[:, 0:n], op0=ALU.mult, op1=ALU.add), r=[g, mlac, a], w=[a])
            for c in range(4):
                a = acc[c]
                s_ = sq[c % 2]
                P.op("act", lambda e, a=a, s_=s_: e.activation(out=s_[:, 0:n], in_=a[:, 0:n], func=AF.Square),
                     r=[a], w=[s_])
                P.op("pe", lambda e, a=a, c=c: e.matmul(ps1[:, 0:n], lhsT=onesf[:], rhs=a[:, 0:n],
                                                       start=(c == 0), stop=(c == 3)), r=[onesf, a], w=[ps1])
                P.op("pe", lambda e, s_=s_, c=c: e.matmul(ps2[:, 0:n], lhsT=onesf[:], rhs=s_[:, 0:n],
                                                         start=(c == 0), stop=(c == 3)), r=[onesf, s_], w=[ps2])
            P.op("act", lambda e: e.activation(out=mean[:, 0:n], in_=ps1[:, 0:n], func=AF.Copy, scale=1.0 / 512),
                 r=[ps1], w=[mean])
            P.op("dve", lambda e: e.tensor_tensor(out=m2[:, 0:n], in0=mean[:, 0:n], in1=mean[:, 0:n], op=ALU.mult),
                 r=[mean], w=[m2])
            P.op("dve", lambda e: e.scalar_tensor_tensor(out=var[:, 0:n], in0=ps2[:, 0:n], scalar=1.0 / 512,
                                                         in1=m2[:, 0:n], op0=ALU.mult, op1=ALU.subtract),
                 r=[ps2, m2], w=[var])
            P.op("act", lambda e: e.activation(out=rt[:, 0:n], in_=var[:, 0:n], func=AF.Ln, bias=epsb[:],
                                               scale=1.0), r=[var, epsb], w=[rt])
            P.op("act", lambda e: e.activation(out=rstd[:, 0:n], in_=rt[:, 0:n], func=AF.Exp, scale=-0.5),
                 r=[rt], w=[rstd])
            for c in range(4):
                a = acc[c]
                t_ = tt[c % 2]
                o_ = ob[c % 2]
                P.op("dve", lambda e, a=a, t_=t_: e.tensor_tensor(out=t_[:, 0:n], in0=a[:, 0:n], in1=mean[:, 0:n],
                                                                 op=ALU.subtract), r=[a, mean], w=[t_])
                P.op("pool", lambda e, t_=t_: e.tensor_tensor(out=t_[:, 0:n], in0=t_[:, 0:n], in1=rstd[:, 0:n],
                                                             op=ALU.mult), r=[t_, rstd], w=[t_])
                P.op("act", lambda e, t_=t_, o_=o_, c=c: e.activation(out=o_[:, 0:n], in_=t_[:, 0:n], func=AF.Silu,
                                                                     bias=mlac[:, 140 + c:141 + c],
                                                                     scale=mlac[:, 136 + c:137 + c]),
                     r=[t_, mlac], w=[o_])
                P.dma("act", mixT_d[512 + c * 128:512 + (c + 1) * 128, koff + t0:koff + t0 + n], o_[:, 0:n], r=[o_])
    C.pop()


def stage_attn(C, S, scr, ones_d, do_ctx=True):
    P = C.P
    C.push()
    qT_d, qcT_d, kT_d, v_d, mixT_d = scr
    NK = CTX + S
    NKC = NK // 128
    onesf = C.sb([128, 128], F32, "onesf")
    P.dma("sp", onesf[:], ones_d, w=[onesf])
    kTs = [C.sb([96, NK], BF16, "kT") for _ in range(2)]
    Vs = [C.sb([128, NKC, 65], BF16, "V") for _ in range(2)]
    for V in Vs:
        P.op("dve", lambda e, V=V: e.memset(V[:, :, 64:65], 1.0), w=[V])
    qs = [C.sb([96, 512], BF16, "q") for _ in range(2)]
    pTs = [C.sb([128, 512], BF16, "pT") for _ in range(3)]
    oT = [C.sb([65, 512], F32, "oT") for _ in range(2)]
    rden = [C.sb([65, 512], F32, "rden") for _ in range(2)]
    att = [C.sb([64, 512], BF16, "att") for _ in range(2)]
    psS = [C.ps([128, 512], F32, "psS") for _ in range(3)]
    psO = [C.ps([65, 512], F32, "psO") for _ in range(2)]
    psB = [C.ps([64, 512], F32, "psB") for _ in range(2)]
    vv = v_d.rearrange("(kc p) (h e) -> p kc h e", p=128, e=64)
    si = 0
    qi = 0
    for h in range(NH):
        kT = kTs[h % 2]
        V = Vs[h % 2]
        P.dma("sp", kT[:], kT_d[h], w=[kT])
        P.dma("pool", V[:, :, 0:64], vv[:, :, h, :], w=[V])
        qtiles = [(qT_d, t0, 512, 0, NKC, CTX + t0) for t0 in range(0, S, 512)]
        if do_ctx:
            qtiles.append((qcT_d, 0, CTX, 0, CTX // 128, 0))
        for (qd, t0, n, kc0, kc1, ocol) in qtiles:
            q = qs[qi % 2]
            po = psO[qi % 2]
            o_ = oT[qi % 2]
            rd = rden[qi % 2]
            pb = psB[qi % 2]
            at = att[qi % 2]
            qi += 1
            P.dma("sp", q[:, 0:n], qd[h, :, t0:t0 + n], w=[q])
            prev = None
            for kc in range(kc0, kc1):
                ps = psS[si % 3]
                pT = pTs[si % 3]
                si += 1
                P.op("pe", lambda e: e.matmul(ps[:, 0:n], lhsT=kT[:, kc * 128:(kc + 1) * 128],
                                              rhs=q[:, 0:n], start=True, stop=True), r=[kT, q], w=[ps])
                P.op("act", lambda e: e.activation(out=pT[:, 0:n], in_=ps[:, 0:n], func=AF.Exp,
                                                   scale=ATT_SCALE), r=[ps], w=[pT])
                if prev is not None:
                    pk, ppT = prev
                    P.op("pe", lambda e: e.matmul(po[:, 0:n], lhsT=V[:, pk, :], rhs=ppT[:, 0:n],
                                                  start=(pk == kc0), stop=False), r=[V, ppT], w=[po])
                prev = (kc, pT)
            pk, ppT = prev
            P.op("pe", lambda e: e.matmul(po[:, 0:n], lhsT=V[:, pk, :], rhs=ppT[:, 0:n],
                                          start=(pk == kc0), stop=True), r=[V, ppT], w=[po])
            P.op("dve", lambda e: e.tensor_copy(out=o_[:, 0:n], in_=po[:, 0:n]), r=[po], w=[o_])
            P.op("act", lambda e: e.activation(out=rd[64:65, 0:n], in_=o_[64:65, 0:n], func=AF.Ln), r=[o_], w=[rd])
            P.op("act", lambda e: e.activation(out=rd[64:65, 0:n], in_=rd[64:65, 0:n], func=AF.Exp, scale=-1.0),
                 r=[rd], w=[rd])
            P.op("pe", lambda e: e.matmul(pb[:, 0:n], lhsT=onesf[64:65, 0:64], rhs=rd[64:65, 0:n],
                                          start=True, stop=True), r=[onesf, rd], w=[pb])
            P.op("dve", lambda e: e.tensor_tensor(out=at[:, 0:n], in0=o_[0:64, 0:n], in1=pb[:, 0:n], op=ALU.mult),
                 r=[o_, pb], w=[at])
            P.dma("pool", mixT_d[h * 64:(h + 1) * 64, ocol:ocol + n], at[:, 0:n], r=[at])
    C.pop()


def stage_outproj(C, segs, w_out_d, mixT_d, mods_d, l, row0=0):
    P = C.P
    C.push()
    Wout = C.sb([128, 8, D], BF16, "Wout")
    P.dma("pool", Wout[:], w_out_d.rearrange("(k p) n -> p k n", p=128), w=[Wout])
    gt = C.sb([128, D], F32)
    mixs = [C.sb([128, 8, 512], BF16, "mix") for _ in range(2)]
    xbs = [C.sb([128, D], F32, "xs") for _ in range(3)]
    tmp = C.sb([128, D], F32)
    pso = [C.ps([128, 512], F32, "pso") for _ in range(2)]
    mv = mixT_d.rearrange("(c p) t -> p c t", p=128)
    cur_which = None
    it = 0
    xi = 0
    oi = 0
    for (xd, ntok, which, koff) in segs:
        for t0 in range(0, ntok, 512):
            n = min(512, ntok - t0)
            if which != cur_which:
                cur_which = which
                load_bc(C, gt, mods_d[l, which, 5 * D:6 * D], "act")
            mx = mixs[it % 2]
            it += 1
            P.dma("sp", mx[:, :, 0:n], mv[:, :, koff + t0:koff + t0 + n], w=[mx])
            for sub in range(n // 128):
                xs = xbs[xi % 3]
                xi += 1
                r0 = t0 + sub * 128
                P.dma("sp", xs[:], xd[r0:r0 + 128, :], w=[xs])
                for dh in range(2):
                    po = pso[oi % 2]
                    oi += 1
                    for c in range(8):
                        P.op("pe", lambda e, c=c, po=po: e.matmul(po[:], lhsT=mx[:, c, sub * 128:(sub + 1) * 128],
                                                                 rhs=Wout[:, c, dh * 512:(dh + 1) * 512],
                                                                 start=(c == 0), stop=(c == 7)), r=[mx, Wout], w=[po])
                    P.op("dve", lambda e, po=po: e.tensor_tensor(out=tmp[:, dh * 512:(dh + 1) * 512], in0=po[:],
                                                                in1=gt[:, dh * 512:(dh + 1) * 512], op=ALU.mult),
                         r=[po, gt], w=[tmp])
                    P.op("pool", lambda e, xs=xs: e.tensor_tensor(out=xs[:, dh * 512:(dh + 1) * 512],
                                                                 in0=tmp[:, dh * 512:(dh + 1) * 512],
                                                                 in1=xs[:, dh * 512:(dh + 1) * 512], op=ALU.add),
                         r=[tmp, xs], w=[xs])
                P.dma("act", xd[r0:r0 + 128, :], xs[:], r=[xs])
    C.pop()


def make_consts(C, ident_d):
    P = C.P
    ident = C.sb([128, 128], BF16, "ident")
    identf = C.sb([128, 128], F32, "identf")
    epsb = C.sb([128, 1], F32, "epsb")
    P.dma("sp", identf[:], ident_d, w=[identf])
    P.op("dve", lambda e: e.tensor_copy(out=ident[:], in_=identf[:]), r=[identf], w=[ident])
    P.op("dve", lambda e: e.memset(epsb[:], EPS), w=[epsb])
    return ident, epsb


NHR = 16
GN_EPS = 64 * 1e-5
DEC_C = -float(np.exp(-0.5))


def bc3(ap2, n):
    return ap2.unsqueeze(2).to_broadcast([ap2.shape[0], ap2.shape[1], n])


def stage_rwkv_h(C, segs, mods_d, l, consts, hT_d):
    P = C.P
    C.push()
    ident, epsb = consts
    sc1 = C.sb([128, D], F32)
    sh = C.sb([128, D], F32)
    xbs = [C.sb([128, D], F32, "xs") for _ in range(3)]
    hTs = [C.sb([128, 8, 512], BF16, "hT") for _ in range(2)]
    tmp = C.sb([128, D], F32)
    hb = C.sb([128, D], BF16)
    junk = C.sb([128, D], BF16)
    ssq = C.sb([128, 1], F32)
    rt1 = C.sb([128, 1], F32)
    rstd1 = C.sb([128, 1], F32)
    stat = (junk, ssq, rt1, rstd1)
    zero = C.sb([128, 8, 1], BF16, "zero")
    P.op("dve", lambda e: e.memset(zero[:], 0.0), w=[zero])
    psT = C.ps([128, D], BF16, "psT")
    hv = hT_d.rearrange("(c p) t -> p c t", p=128)
    xi = 0
    ti = 0
    cur_which = None
    with C.nc.allow_non_contiguous_dma(reason="tiny zero pad columns"):
        for (xd, ntok, which, col0) in segs:
            P.dma("sp", hv[:, :, col0 - 1:col0], zero[:], r=[zero])
            P.dma("sp", hv[:, :, col0 + ntok:col0 + ntok + 1], zero[:], r=[zero])
    for (xd, ntok, which, col0) in segs:
        for t0 in range(0, ntok, 512):
            n = min(512, ntok - t0)
            if which != cur_which:
                cur_which = which
                load_bc(C, sh, mods_d[l, which, 3 * D:4 * D], "act")
                load_bc(C, sc1, mods_d[l, which, 4 * D:5 * D], "act")
                P.op("dve", lambda e: e.tensor_scalar_add(out=sc1[:], in0=sc1[:], scalar1=1.0), r=[sc1], w=[sc1])
            hT = hTs[ti % 2]
            ti += 1
            for sub in range(n // 128):
                xs = xbs[xi % 3]
                xi += 1
                P.dma("sp", xs[:], xd[t0 + sub * 128:t0 + (sub + 1) * 128, :], w=[xs])
                make_hT_sub(C, xs, sub, sc1, sh, hT, tmp, hb, psT, ident, stat, epsb)
            P.dma("act", hv[:, :, col0 + t0:col0 + t0 + n], hT[:, :, 0:n], r=[hT])
    C.pop()


def stage_rwkv_proj(C, segs, wd, hT_d, scr):
    P = C.P
    C.push()
    (w_r_d, w_k_d, w_v_d, w1_d, w2_d, a1_d, a2_d, g1_d, w0_d, rwv_d, rwc_d) = wd
    (rT_d, nkkT_d, bT_d, kdT_d, sig_d, v_d, sbon_d, sigG_d) = scr
    Wr = C.sb([128, 8, D], BF16, "Wr")
    Wk = C.sb([128, 8, D], BF16, "Wk")
    Wv = C.sb([128, 8, D], BF16, "Wv")
    for W, wdram in ((Wr, w_r_d), (Wk, w_k_d), (Wv, w_v_d)):
        P.dma("pool", W[:], wdram.rearrange("(k p) n -> p k n", p=128), w=[W])
    W1 = [C.sb([128, 8, 64], BF16, "W1") for _ in range(2)]
    A1 = [C.sb([128, 8, 64], BF16, "A1") for _ in range(2)]
    W2 = [C.sb([64, D], BF16, "W2") for _ in range(2)]
    A2 = [C.sb([64, D], BF16, "A2") for _ in range(2)]
    G1 = C.sb([128, 8, 128], BF16, "G1")
    w0bc = [C.sb([128, D], F32, "w0bc") for _ in range(2)]
    for d in range(2):
        P.dma("pool", W1[d][:], w1_d[d].rearrange("(k p) n -> p k n", p=128), w=[W1[d]])
        P.dma("pool", A1[d][:], a1_d[d].rearrange("(k p) n -> p k n", p=128), w=[A1[d]])
        P.dma("pool", W2[d][:], w2_d[d], w=[W2[d]])
        P.dma("pool", A2[d][:], a2_d[d], w=[A2[d]])
        load_bc(C, w0bc[d], w0_d[d], "sp")
    P.dma("pool", G1[:], g1_d.rearrange("(k p) n -> p k n", p=128), w=[G1])
    rwv = C.sb([128, 88], F32, "rwv")
    P.dma("sp", rwv[:], rwv_d, w=[rwv])
    XM, KK, KA, RK, A0 = 0, 48, 56, 64, 72
    omk = C.sb([128, 8], F32, "omk")
    rkh = C.sb([128, 8], F32, "rkh")
    P.op("dve", lambda e: e.tensor_scalar(out=omk[:], in0=rwv[:, KA:KA + 8], scalar1=-1.0, scalar2=1.0,
                                          op0=ALU.mult, op1=ALU.add), r=[rwv], w=[omk])
    P.op("dve", lambda e: e.tensor_scalar_mul(out=rkh[:], in0=rwv[:, RK:RK + 8], scalar1=0.5), r=[rwv], w=[rkh])
    blk = C.sb([128, 128], BF16, "blk")
    hsel = C.sb([128, 2], BF16, "hsel")
    blkf = C.sb([128, 130], F32, "blkf")
    P.dma("sp", blkf[:], rwc_d[:, 0:130], w=[blkf])
    P.op("dve", lambda e: e.tensor_copy(out=blk[:], in_=blkf[:, 0:128]), r=[blkf], w=[blk])
    P.op("dve", lambda e: e.tensor_copy(out=hsel[:], in_=blkf[:, 128:130]), r=[blkf], w=[hsel])
    hTh = [C.sb([128, 8, 514], BF16, "hTh") for _ in range(2)]
    tt = GBuf(C.sb([128, 8, 512], F32, "tt"), 8)
    xx = C.sb([128, 8, 512], F32, "xx")
    xj = [C.sb([128, 8, 512], BF16, "xj") for _ in range(2)]
    kT = GBuf(C.sb([128, 8, 512], F32, "kT"), 8)
    kkT = GBuf(C.sb([128, 8, 512], BF16, "kkT"), 8)
    rTs = GBuf(C.sb([128, 8, 512], BF16, "rTs"), 8)
    kdsum = tt
    ob = [C.sb([128, 512], BF16, "ob") for _ in range(3)]
    of = [C.sb([128, 512], F32, "of") for _ in range(3)]
    hid = [C.sb([128, 512], BF16, "hid") for _ in range(2)]
    sbo = [C.sb([128, 16], F32, "sbo") for _ in range(2)]
    psA = [C.ps([128, 512], F32, "psA") for _ in range(5)]
    psS = [C.ps([128, 512], F32, "psS") for _ in range(2)]
    cnt = dict(a=0, o=0, f=0, h=0, x=0, t=0, s=0)

    def nxt(lst, key):
        i = cnt[key]
        cnt[key] = i + 1
        return lst[i % len(lst)]

    hv = hT_d.rearrange("(c p) t -> p c t", p=128)
    for (ntok, col0, koff) in segs:
        for t0 in range(0, ntok, 512):
            n = min(512, ntok - t0)
            nsub = n // 128
            hh = nxt(hTh, "t")
            P.dma("sp", hh[:, :, 0:n + 2], hv[:, :, col0 + t0 - 1:col0 + t0 + n + 1], w=[hh])
            hc = hh[:, :, 1:n + 1]
            P.op("dve", lambda e: e.tensor_tensor(out=tt[:, :, 0:n], in0=hh[:, :, 0:n], in1=hh[:, :, 2:n + 2],
                                                  op=ALU.add), r=[hh], w=tt.g)
            P.op("dve", lambda e: e.scalar_tensor_tensor(out=xx[:, :, 0:n], in0=tt[:, :, 0:n], scalar=0.5, in1=hc,
                                                         op0=ALU.mult, op1=ALU.subtract), r=tt.g + [hh], w=[xx])

            def mix(j):
                x_ = nxt(xj, "x")
                P.op("dve", lambda e: e.tensor_tensor(out=tt[:, :, 0:n], in0=xx[:, :, 0:n],
                                                      in1=bc3(rwv[:, XM + j * 8:XM + j * 8 + 8], n), op=ALU.mult),
                     r=[xx, rwv], w=tt.g)
                P.op("pool", lambda e: e.tensor_tensor(out=x_[:, :, 0:n], in0=tt[:, :, 0:n], in1=hc, op=ALU.add),
                     r=tt.g + [hh], w=[x_])
                return x_

            def projT(W, x_, p, ncol=128, col0_=None):
                ps = nxt(psA, "a")
                c0 = p * 128 if col0_ is None else col0_
                for kc in range(8):
                    P.op("pe", lambda e, kc=kc: e.matmul(ps[0:ncol, 0:n], lhsT=W[:, kc, c0:c0 + ncol],
                                                         rhs=x_[:, kc, 0:n], start=(kc == 0), stop=(kc == 7)),
                         r=[W, x_], w=[ps])
                return ps

            tok0 = koff + t0
            x_ = mix(0)
            for p in range(8):
                ps = projT(Wr, x_, p)
                P.op("act", lambda e: e.copy(out=rTs[:, p, 0:n], in_=ps[:, 0:n]), r=[ps], w=[rTs.g[p]])
            P.dma("act", rT_d.rearrange("(c p) t -> p c t", p=128)[:, :, tok0:tok0 + n], rTs[:, :, 0:n], r=rTs.g)
            x_ = mix(2)
            for p in range(8):
                ps = projT(Wk, x_, p)
                P.op("act", lambda e: e.copy(out=kT[:, p, 0:n], in_=ps[:, 0:n]), r=[ps], w=[kT.g[p]])
                kr = nxt(of, "f")
                P.op("dve", lambda e: e.tensor_scalar_mul(out=kr[:, 0:n], in0=kT[:, p, 0:n],
                                                          scalar1=rwv[:, KK + p:KK + p + 1]), r=[kT.g[p], rwv], w=[kr])
                sq = nxt(ob, "o")
                P.op("act", lambda e: e.activation(out=sq[:, 0:n], in_=kr[:, 0:n], func=AF.Square), r=[kr], w=[sq])
                pss = nxt(psA, "a")
                P.op("pe", lambda e: e.matmul(pss[:, 0:n], lhsT=blk[:], rhs=sq[:, 0:n], start=True, stop=True),
                     r=[blk, sq], w=[pss])
                nr = nxt(of, "f")
                P.op("dve", lambda e: e.tensor_scalar_max(out=nr[:, 0:n], in0=pss[:, 0:n], scalar1=1e-24),
                     r=[pss], w=[nr])
                P.op("act", lambda e: e.activation(out=nr[:, 0:n], in_=nr[:, 0:n], func=AF.Ln), r=[nr], w=[nr])
                P.op("act", lambda e: e.activation(out=nr[:, 0:n], in_=nr[:, 0:n], func=AF.Exp, scale=-0.5),
                     r=[nr], w=[nr])
                P.op("dve", lambda e: e.tensor_tensor(out=kkT[:, p, 0:n], in0=kr[:, 0:n], in1=nr[:, 0:n], op=ALU.mult),
                     r=[kr, nr], w=[kkT.g[p]])
                nk = nxt(ob, "o")
                P.op("pool", lambda e: e.tensor_scalar_mul(out=nk[:, 0:n], in0=kkT[:, p, 0:n], scalar1=-1.0),
                     r=[kkT.g[p]], w=[nk])
                P.dma("act", nkkT_d[p * 128:(p + 1) * 128, tok0:tok0 + n], nk[:, 0:n], r=[nk])
            x_ = mix(3)
            for sub in range(nsub):
                for dh in range(2):
                    ps = nxt(psA, "a")
                    for kc in range(8):
                        P.op("pe", lambda e, kc=kc: e.matmul(ps[:], lhsT=x_[:, kc, sub * 128:(sub + 1) * 128],
                                                             rhs=Wv[:, kc, dh * 512:(dh + 1) * 512],
                                                             start=(kc == 0), stop=(kc == 7)), r=[x_, Wv], w=[ps])
                    vb = nxt(ob, "o")
                    P.op("act", lambda e: e.copy(out=vb[:], in_=ps[:]), r=[ps], w=[vb])
                    P.dma("act", v_d[tok0 + sub * 128:tok0 + (sub + 1) * 128, dh * 512:(dh + 1) * 512], vb[:], r=[vb])
            x_ = mix(1)
            for d in range(2):
                ps = projT(W1[d], x_, 0, 64, 0)
                hd = nxt(hid, "h")
                P.op("act", lambda e: e.activation(out=hd[0:64, 0:n], in_=ps[0:64, 0:n], func=AF.Tanh), r=[ps], w=[hd])
                for sub in range(nsub):
                    for dh in range(2):
                        ps2 = nxt(psA, "a")
                        P.op("pe", lambda e: e.matmul(ps2[:], lhsT=hd[0:64, sub * 128:(sub + 1) * 128],
                                                      rhs=W2[d][:, dh * 512:(dh + 1) * 512], start=True, stop=True),
                             r=[hd, W2[d]], w=[ps2])
                        o1 = nxt(of, "f")
                        P.op("dve", lambda e: e.tensor_tensor(out=o1[:], in0=ps2[:],
                                                              in1=w0bc[d][:, dh * 512:(dh + 1) * 512], op=ALU.add),
                             r=[ps2, w0bc[d]], w=[o1])
                        P.op("act", lambda e: e.activation(out=o1[:], in_=o1[:], func=AF.Sigmoid), r=[o1], w=[o1])
                        P.dma("act", sig_d[d, tok0 + sub * 128:tok0 + (sub + 1) * 128, dh * 512:(dh + 1) * 512],
                              o1[:], r=[o1])
            x_ = mix(5)
            ps = projT(G1, x_, 0, 128, 0)
            sg = nxt(ob, "o")
            P.op("act", lambda e: e.activation(out=sg[:, 0:n], in_=ps[:, 0:n], func=AF.Sigmoid), r=[ps], w=[sg])
            P.dma("act", sigG_d[:, tok0:tok0 + n], sg[:, 0:n], r=[sg])
            x_ = mix(4)
            for d in range(2):
                ps = projT(A1[d], x_, 0, 64, 0)
                hd = nxt(hid, "h")
                P.op("act", lambda e: e.copy(out=hd[0:64, 0:n], in_=ps[0:64, 0:n]), r=[ps], w=[hd])
                for p in range(8):
                    ps2 = nxt(psA, "a")
                    P.op("pe", lambda e: e.matmul(ps2[:, 0:n], lhsT=A2[d][:, p * 128:(p + 1) * 128], rhs=hd[0:64, 0:n],
                                                  start=True, stop=True), r=[A2[d], hd], w=[ps2])
                    av = nxt(of, "f")
                    P.op("act", lambda e: e.activation(out=av[:, 0:n], in_=ps2[:, 0:n], func=AF.Sigmoid,
                                                       bias=rwv[:, A0 + d * 8 + p:A0 + d * 8 + p + 1]),
                         r=[ps2, rwv], w=[av])
                    bb = nxt(ob, "o")
                    P.op("dve", lambda e: e.tensor_tensor(out=bb[:, 0:n], in0=kkT[:, p, 0:n], in1=av[:, 0:n],
                                                          op=ALU.mult), r=[kkT.g[p], av], w=[bb])
                    P.dma("act", bT_d[d, p * 128:(p + 1) * 128, tok0:tok0 + n], bb[:, 0:n], r=[bb])
                    P.op("dve", lambda e: e.tensor_scalar(out=av[:, 0:n], in0=av[:, 0:n],
                                                          scalar1=rwv[:, KA + p:KA + p + 1], scalar2=omk[:, p:p + 1],
                                                          op0=ALU.mult, op1=ALU.add), r=[av, rwv, omk], w=[av])
                    kd = nxt(ob, "o")
                    P.op("dve", lambda e: e.tensor_tensor(out=kd[:, 0:n], in0=kT[:, p, 0:n], in1=av[:, 0:n],
                                                          op=ALU.mult), r=[kT.g[p], av], w=[kd])
                    P.dma("act", kdT_d[d, p * 128:(p + 1) * 128, tok0:tok0 + n], kd[:, 0:n], r=[kd])
                    if d == 0:
                        P.op("pool", lambda e: e.tensor_tensor(out=kdsum[:, p, 0:n], in0=kT[:, p, 0:n], in1=av[:, 0:n],
                                                               op=ALU.mult), r=[kT.g[p], av], w=[kdsum.g[p]])
                    else:
                        P.op("dve", lambda e: e.tensor_tensor(out=av[:, 0:n], in0=kT[:, p, 0:n], in1=av[:, 0:n],
                                                              op=ALU.mult), r=[kT.g[p], av], w=[av])
                        P.op("dve", lambda e: e.tensor_tensor(out=kdsum[:, p, 0:n], in0=kdsum[:, p, 0:n],
                                                              in1=av[:, 0:n], op=ALU.add), r=[kdsum.g[p], av], w=[kdsum.g[p]])
            for p in range(8):
                P.op("dve", lambda e: e.scalar_tensor_tensor(out=kdsum[:, p, 0:n], in0=kdsum[:, p, 0:n],
                                                             scalar=rkh[:, p:p + 1], in1=rTs[:, p, 0:n],
                                                             op0=ALU.mult, op1=ALU.mult),
                     r=[kdsum.g[p], rkh, rTs.g[p]], w=[kdsum.g[p]])
            prod = nxt(xj, "x")
            P.op("act", lambda e: e.copy(out=prod[:, :, 0:n], in_=kdsum[:, :, 0:n]), r=kdsum.g, w=[prod])
            for sub in range(nsub):
                pb = nxt(psS, "s")
                for p in range(8):
                    P.op("pe", lambda e: e.matmul(pb[:, 2 * p:2 * p + 2], lhsT=prod[:, p, sub * 128:(sub + 1) * 128],
                                                  rhs=hsel[:], start=True, stop=True), r=[prod, hsel], w=[pb])
                so = sbo[sub % 2]
                P.op("dve", lambda e: e.tensor_copy(out=so[:], in_=pb[:, 0:16]), r=[pb], w=[so])
                P.dma("act", sbon_d[tok0 + sub * 128:tok0 + (sub + 1) * 128, :], so[:], r=[so])
    C.pop()


RWC_TRI = 130
RWC_LV = 256 + 1536
RWC_DIR = 256 + 1536 + 1024 + 12 * 512
RWC_IREP = 130 + 2 * RWC_DIR
RWC_N = RWC_IREP + 512


def stage_rwkv_scan(C, NK, scr, rwc_d, consts):
    P = C.P
    C.push()
    ident, epsb = consts
    (rT_d, nkkT_d, bT_d, kdT_d, sig_d, v_d, y_d) = scr
    NKC = NK // 128
    NCC = CTX // 128
    KD = os.environ.get("KDBG", "")
    irf = C.sb([128, 512], F32, "irf")
    irep = C.sb([128, 512], BF16, "irep")
    P.dma("sp", irf[:], rwc_d[:, RWC_IREP:RWC_IREP + 512], w=[irf])
    P.op("dve", lambda e: e.tensor_copy(out=irep[:], in_=irf[:]), r=[irf], w=[irep])
    triI = C.sb([128, 128], F32, "triI")
    triE = C.sb([128, 128], F32, "triE")
    mS = C.sb([128, 512], F32, "mS")
    mST = C.sb([128, 512], F32, "mST")
    mI = C.sb([128, 512], F32, "mI")
    mSb = C.sb([128, 512], BF16, "mSb")
    mIb = C.sb([128, 512], BF16, "mIb")
    lvm = C.sb([128, 14, 512], BF16, "lvm")
    lvf = C.sb([128, 2, 512], F32, "lvf")
    NLB = 2
    ld = [dict(r=C.sb([128, 8, 128], BF16, "l_r"), nk=C.sb([128, 8, 128], BF16, "l_nk"),
               b=C.sb([128, 8, 128], BF16, "l_b"), kd=C.sb([128, 8, 128], BF16, "l_kd"),
               sig=C.sb([128, D], F32, "l_sig"), v=C.sb([128, D], BF16, "l_v")) for _ in range(NLB)]
    eL = C.sb([128, D], F32, "eL")
    eLx = C.sb([128, D], F32, "eLx")
    enL = C.sb([128, D], F32, "enL")
    pre = []
    for i in range(2):
        d_ = dict(rt=C.sb([128, D], BF16, "rt"), at=C.sb([128, D], BF16, "at"), bt=C.sb([128, D], BF16, "bt"),
                  kt=C.sb([128, D], BF16, "kt"),
                  bpA=C.sb([128, 8, 128], BF16, "bpA"), bpB=C.sb([128, 8, 128], BF16, "bpB"),
                  kpA=C.sb([128, 8, 128], BF16, "kpA"), kpB=C.sb([128, 8, 128], BF16, "kpB"),
                  T=[GBuf(C.sb([128, 16, 128], BF16, "T"), 4) for _ in range(2)],
                  Tt=[GBuf(C.sb([128, 16, 128], BF16, "Tt"), 4) for _ in range(2)],
                  Mak=GBuf(C.sb([128, 16, 128], BF16, "Mak"), 4), Mbr=GBuf(C.sb([128, 16, 128], BF16, "Mbr"), 4),
                  Mkr=GBuf(C.sb([128, 16, 128], BF16, "Mkr"), 4), gam=C.sb([128, 8], F32, "gam"))
        for nm in ("bpA", "bpB", "kpA", "kpB"):
            P.op("pool", lambda e, b_=d_[nm]: e.memset(b_[:], 0.0), w=[d_[nm]])
        pre.append(d_)
    Pb = [GBuf(C.sb([128, 16, 128], BF16, "Pb"), 4) for _ in range(2)]
    Qb = [GBuf(C.sb([128, 16, 128], BF16, "Qb"), 4) for _ in range(2)]
    Sf = C.sb([128, 8, 64], F32, "Sf")
    Sb = C.sb([128, 8, 64], BF16, "Sb")
    Stmp = C.sb([128, 8, 64], F32, "Stmp")
    XT = GBuf(C.sb([128, 16, 64], BF16, "XT"), 2)
    UT = GBuf(C.sb([128, 16, 64], BF16, "UT"), 2)
    yt = [C.sb([128, D], F32, "yt") for _ in range(2)]
    pbF = [C.ps([128, 512], F32, "pbF") for _ in range(6)]
    pbT = [C.ps([128, D], BF16, "pbT") for _ in range(2)]
    cnt = dict(f=0, t=0, y=0)

    def nf():
        i = cnt["f"]
        cnt["f"] = i + 1
        return pbF[i % 6]

    def nt():
        i = cnt["t"]
        cnt["t"] = i + 1
        return pbT[i % 2]

    rv = rT_d.rearrange("(c p) t -> p c t", p=128)
    nkv = nkkT_d.rearrange("(c p) t -> p c t", p=128)

    for d in range(2):
        base = RWC_TRI + d * RWC_DIR
        P.dma("sp", triI[:], rwc_d[:, base:base + 128], w=[triI])
        P.dma("sp", triE[:], rwc_d[:, base + 128:base + 256], w=[triE])
        P.dma("sp", mS[:], rwc_d[:, base + 256:base + 768], w=[mS])
        P.dma("sp", mST[:], rwc_d[:, base + 768:base + 1280], w=[mST])
        P.dma("sp", mI[:], rwc_d[:, base + 1280:base + 1792], w=[mI])
        P.op("dve", lambda e: e.tensor_copy(out=mSb[:], in_=mS[:]), r=[mS], w=[mSb])
        P.op("dve", lambda e: e.tensor_copy(out=mIb[:], in_=mI[:]), r=[mI], w=[mIb])
        for q_ in range(7):
            o = base + RWC_LV + q_ * 1024
            P.dma("sp", lvf[:], rwc_d[:, o:o + 1024].rearrange("p (a b) -> p a b", a=2), w=[lvf])
            P.op("dve", lambda e: e.tensor_copy(out=lvm[:, 2 * q_:2 * q_ + 2, :], in_=lvf[:]), r=[lvf], w=[lvm])
        P.op("dve", lambda e: e.memset(Sf[:], 0.0), w=[Sf])
        P.op("dve", lambda e: e.memset(Sb[:], 0.0), w=[Sb])
        if d == 0:
            order = list(range(NKC))
        else:
            order = list(range(NCC - 1, -1, -1)) + list(range(NKC - 1, NCC - 1, -1))
        last = 127 if d == 0 else 0
        bv = bT_d[d].rearrange("(c p) t -> p c t", p=128)
        kv = kdT_d[d].rearrange("(c p) t -> p c t", p=128)

        def load(i):
            c = order[i]
            L = ld[i % NLB]
            t0 = c * 128
            P.dma("sp", L["r"][:], rv[:, :, t0:t0 + 128], w=[L["r"]])
            P.dma("sp", L["nk"][:], nkv[:, :, t0:t0 + 128], w=[L["nk"]])
            P.dma("sp", L["b"][:], bv[:, :, t0:t0 + 128], w=[L["b"]])
            P.dma("sp", L["kd"][:], kv[:, :, t0:t0 + 128], w=[L["kd"]])
            P.dma("sp", L["sig"][:], sig_d[d, t0:t0 + 128, :], w=[L["sig"]])
            P.dma("sp", L["v"][:], v_d[t0:t0 + 128, :], w=[L["v"]])

        def precompute(i):
            L = ld[i % NLB]
            R = pre[i % 2]
            bL = [nf(), nf()]
            for p in range(8):
                P.op("pe", lambda e: e.matmul(bL[p // 4][:, (p % 4) * 128:(p % 4 + 1) * 128],
                                              lhsT=L["sig"][:, p * 128:(p + 1) * 128], rhs=triI[:],
                                              start=True, stop=True), r=[L["sig"], triI], w=[bL[p // 4]])
            for hf in range(2):
                sl = slice(hf * 512, (hf + 1) * 512)
                P.op("act", lambda e: e.activation(out=eL[:, sl], in_=bL[hf][:], func=AF.Exp), r=[bL[hf]], w=[eL])
                P.op("act", lambda e: e.activation(out=enL[:, sl], in_=bL[hf][:], func=AF.Exp, scale=-1.0),
                     r=[bL[hf]], w=[enL])
            yield
            fl = lambda b_: b_[:].rearrange("p c t -> p (c t)")
            P.op("dve", lambda e: e.tensor_tensor(out=R["rt"][:], in0=fl(L["r"]), in1=eL[:], op=ALU.mult),
                 r=[L["r"], eL], w=[R["rt"]])
            at3 = R["at"][:].rearrange("p (c t) -> p c t", t=128)
            eL3 = eL[:].rearrange("p (c t) -> p c t", t=128)
            if d == 0:
                P.op("pool", lambda e: e.tensor_tensor(out=at3[:, :, 1:128], in0=L["nk"][:, :, 1:128],
                                                       in1=eL3[:, :, 0:127], op=ALU.mult), r=[L["nk"], eL], w=[R["at"]])
                P.op("pool", lambda e: e.tensor_copy(out=at3[:, :, 0:1], in_=L["nk"][:, :, 0:1]), r=[L["nk"]], w=[R["at"]])
            else:
                P.op("pool", lambda e: e.tensor_tensor(out=at3[:, :, 0:127], in0=L["nk"][:, :, 0:127],
                                                       in1=eL3[:, :, 1:128], op=ALU.mult), r=[L["nk"], eL], w=[R["at"]])
                P.op("pool", lambda e: e.tensor_copy(out=at3[:, :, 127:128], in_=L["nk"][:, :, 127:128]),
                     r=[L["nk"]], w=[R["at"]])
            P.op("dve", lambda e: e.tensor_tensor(out=R["bt"][:], in0=fl(L["b"]), in1=enL[:], op=ALU.mult),
                 r=[L["b"], enL], w=[R["bt"]])
            P.op("pool", lambda e: e.tensor_tensor(out=R["kt"][:], in0=fl(L["kd"]), in1=enL[:], op=ALU.mult),
                 r=[L["kd"], enL], w=[R["kt"]])
            P.op("dve", lambda e: e.tensor_copy(out=R["gam"][:],
                                                in_=eL[:].rearrange("p (c t) -> p c t", t=128)[:, :, last]),
                 r=[eL], w=[R["gam"]])
            yield
            for (src, dA, dB) in ((R["bt"], R["bpA"], R["bpB"]), (R["kt"], R["kpA"], R["kpB"])):
                pt = nt()
                for p in range(8):
                    P.op("pe", lambda e: e.transpose(out=pt[:, p * 128:(p + 1) * 128],
                                                     in_=src[:, p * 128:(p + 1) * 128], identity=ident[:]),
                         r=[src, ident], w=[pt])
                ptv = pt[:].rearrange("p (c k) -> p c k", k=128)
                P.op("act", lambda e: e.copy(out=dA[:, :, 0:64], in_=ptv[:, :, 0:64]), r=[pt], w=[dA])
                P.op("dve", lambda e: e.tensor_copy(out=dB[:, :, 64:128], in_=ptv[:, :, 64:128]), r=[pt], w=[dB])

            def hm(lhs, rhs, mask, dst, eng):
                for G in range(2):
                    pbs = (nf(), nf())
                    for j in range(8):
                        h = G * 8 + j
                        p, q = h // 2, h % 2
                        rows = slice(q * 64, q * 64 + 64)
                        pb = pbs[q]
                        P.op("pe", lambda e: e.matmul(pb[:, (j // 2) * 128:(j // 2 + 1) * 128],
                                                      lhsT=lhs[rows, p * 128:(p + 1) * 128],
                                                      rhs=rhs[rows, p * 128:(p + 1) * 128], start=True, stop=True),
                             r=[lhs, rhs], w=[pb])
                    for q in range(2):
                        dv = dst[:, G * 8 + q:G * 8 + 8:2, :]
                        m3 = mask[:].rearrange("p (h t) -> p h t", t=128)
                        dg = [dst.g[2 * G], dst.g[2 * G + 1]]
                        if eng == "dve":
                            P.op("dve", lambda e: e.tensor_tensor(
                                out=dv, in0=pbs[q][:].rearrange("p (h t) -> p h t", t=128), in1=m3, op=ALU.mult),
                                 r=[pbs[q], mask], w=dg)
                        else:
                            P.op("act", lambda e: e.copy(out=dv, in_=pbs[q][:].rearrange("p (h t) -> p h t", t=128)),
                                 r=[pbs[q]], w=dg)
                            P.op("pool", lambda e: e.tensor_tensor(out=dv, in0=dv, in1=m3, op=ALU.mult),
                                 r=dg + [mask], w=dg)

            yield
            hm(R["bt"], R["at"], mS, Pb[0], "dve")
            yield
            hm(R["at"], R["bt"], mST, Qb[0], "dve")
            yield
            hm(R["kt"], R["at"], mS, R["Mak"], "dve")
            yield
            hm(R["bt"], R["rt"], mIb, R["Mbr"], "actpool")
            yield
            hm(R["kt"], R["rt"], mIb, R["Mkr"], "actpool")
            yield
            T, Tt = R["T"][0], R["Tt"][0]
            g4 = lambda b_, g: b_[:, g * 4:(g + 1) * 4, :].rearrange("p h t -> p (h t)")
            for g in range(4):
                P.op("dve", lambda e: e.tensor_tensor(out=g4(T, g), in0=g4(Pb[0], g), in1=lvm[:, 0, :], op=ALU.mult),
                     r=[Pb[0].g[g], lvm], w=[T.g[g]])
                P.op("pool", lambda e: e.tensor_tensor(out=g4(T, g), in0=g4(T, g), in1=irep[:], op=ALU.add),
                     r=[T.g[g], irep], w=[T.g[g]])
                P.op("dve", lambda e: e.tensor_tensor(out=g4(Tt, g), in0=g4(Qb[0], g), in1=lvm[:, 1, :], op=ALU.mult),
                     r=[Qb[0].g[g], lvm], w=[Tt.g[g]])
                P.op("pool", lambda e: e.tensor_tensor(out=g4(Tt, g), in0=g4(Tt, g), in1=irep[:], op=ALU.add),
                     r=[Tt.g[g], irep], w=[Tt.g[g]])
            yield
            Wb = Pb[1]
            cur = 0
            for li in range(6):
                mk = lvm[:, 2 + 2 * li, :]
                T, Tt = R["T"][cur], R["Tt"][cur]
                Tn, Ttn = R["T"][1 - cur], R["Tt"][1 - cur]
                for g in range(4):
                    pw = nf()
                    for j in range(4):
                        h = g * 4 + j
                        P.op("pe", lambda e: e.matmul(pw[:, j * 128:(j + 1) * 128], lhsT=Qb[0][:, h, :], rhs=T[:, h, :],
                                                      start=True, stop=True), r=[Qb[0].g[g], T.g[g]], w=[pw])
                    if g < 3:
                        P.op("dve", lambda e: e.tensor_tensor(out=g4(Wb, g), in0=pw[:], in1=mk, op=ALU.mult),
                             r=[pw, lvm], w=[Wb.g[g]])
                    else:
                        P.op("act", lambda e: e.copy(out=g4(Wb, g), in_=pw[:]), r=[pw], w=[Wb.g[g]])
                        P.op("pool", lambda e: e.tensor_tensor(out=g4(Wb, g), in0=g4(Wb, g), in1=mk, op=ALU.mult),
                             r=[Wb.g[g], lvm], w=[Wb.g[g]])
                    if g % 2 == 1:
                        yield
                for g in range(4):
                    pm = nf()
                    for j in range(4):
                        h = g * 4 + j
                        P.op("pe", lambda e: e.matmul(pm[:, j * 128:(j + 1) * 128], lhsT=Tt[:, h, :], rhs=Wb[:, h, :],
                                                      start=True, stop=True), r=[Tt.g[g], Wb.g[g]], w=[pm])
                    P.op("dve", lambda e: e.tensor_tensor(out=g4(Tn, g), in0=pm[:], in1=g4(T, g), op=ALU.add),
                         r=[pm, T.g[g]], w=[Tn.g[g]])
                    if g % 2 == 1:
                        yield
                if li < 5:
                    for g in range(4):
                        ptt = nt()
                        for j in range(4):
                            h = g * 4 + j
                            P.op("pe", lambda e: e.transpose(out=ptt[:, j * 128:(j + 1) * 128], in_=Tn[:, h, :],
                                                             identity=ident[:]), r=[Tn.g[g], ident], w=[ptt])
                        P.op("act", lambda e: e.copy(out=g4(Ttn, g), in_=ptt[:, 0:512]), r=[ptt], w=[Ttn.g[g]])
                        if g % 2 == 1:
                            yield
                cur = 1 - cur
            R["Tf"] = R["T"][cur]

        def chain(i):
            c = order[i]
            L = ld[i % NLB]
            R = pre[i % 2]
            V = L["v"]
            T = R["Tf"]
            for g in range(2):
                pb = nf()
                for j in range(8):
                    h = g * 8 + j
                    p, q = h // 2, h % 2
                    rows = slice(q * 64, q * 64 + 64)
                    P.op("pe", lambda e: e.matmul(pb[:, j * 64:(j + 1) * 64], lhsT=R["at"][rows, p * 128:(p + 1) * 128],
                                                  rhs=Sb[rows, p, :], start=True, stop=False), r=[R["at"], Sb], w=[pb])
                    P.op("pe", lambda e: e.matmul(pb[:, j * 64:(j + 1) * 64], lhsT=R["Mak"][:, h, :],
                                                  rhs=V[:, h * 64:(h + 1) * 64], start=False, stop=True),
                         r=[R["Mak"].g[h // 4], V], w=[pb])
                P.op("act", lambda e: e.copy(out=XT[:, g * 8:(g + 1) * 8, :].rearrange("p h v -> p (h v)"), in_=pb[:]),
                     r=[pb], w=[XT.g[g]])
            yield
            for g in range(2):
                pb = nf()
                for j in range(8):
                    h = g * 8 + j
                    P.op("pe", lambda e: e.matmul(pb[:, j * 64:(j + 1) * 64], lhsT=T[:, h, :], rhs=XT[:, h, :],
                                                  start=True, stop=True), r=[T.g[h // 4], XT.g[g]], w=[pb])
                P.op("dve", lambda e: e.tensor_copy(out=UT[:, g * 8:(g + 1) * 8, :].rearrange("p h v -> p (h v)"),
                                                    in_=pb[:]), r=[pb], w=[UT.g[g]])
            yield
            y_ = yt[cnt["y"] % 2]
            cnt["y"] += 1
            for g in range(2):
                pb = nf()
                for j in range(8):
                    h = g * 8 + j
                    p, q = h // 2, h % 2
                    rows = slice(q * 64, q * 64 + 64)
                    P.op("pe", lambda e: e.matmul(pb[:, j * 64:(j + 1) * 64], lhsT=R["rt"][rows, p * 128:(p + 1) * 128],
                                                  rhs=Sb[rows, p, :], start=True, stop=False), r=[R["rt"], Sb], w=[pb])
                    P.op("pe", lambda e: e.matmul(pb[:, j * 64:(j + 1) * 64], lhsT=R["Mbr"][:, h, :], rhs=UT[:, h, :],
                                                  start=False, stop=False), r=[R["Mbr"].g[h // 4], UT.g[g]], w=[pb])
                    P.op("pe", lambda e: e.matmul(pb[:, j * 64:(j + 1) * 64], lhsT=R["Mkr"][:, h, :],
                                                  rhs=V[:, h * 64:(h + 1) * 64], start=False, stop=True),
                         r=[R["Mkr"].g[h // 4], V], w=[pb])
                P.op("act", lambda e: e.copy(out=y_[:, g * 512:(g + 1) * 512], in_=pb[:]), r=[pb], w=[y_])
            P.dma("act", y_d[d, c * 128:(c + 1) * 128, :], y_[:], r=[y_])
            yield
            pb = nf()
            for p in range(8):
                o_ = pb[:, p * 64:(p + 1) * 64]
                P.op("pe", lambda e: e.matmul(o_, lhsT=R["bpA"][:, p, :], rhs=UT[:, 2 * p, :], start=True, stop=False),
                     r=[R["bpA"]] + UT.g, w=[pb])
                P.op("pe", lambda e: e.matmul(o_, lhsT=R["bpB"][:, p, :], rhs=UT[:, 2 * p + 1, :], start=False,
                                              stop=False), r=[R["bpB"]] + UT.g, w=[pb])
                P.op("pe", lambda e: e.matmul(o_, lhsT=R["kpA"][:, p, :], rhs=V[:, (2 * p) * 64:(2 * p + 1) * 64],
                                              start=False, stop=False), r=[R["kpA"], V], w=[pb])
                P.op("pe", lambda e: e.matmul(o_, lhsT=R["kpB"][:, p, :], rhs=V[:, (2 * p + 1) * 64:(2 * p + 2) * 64],
                                              start=False, stop=True), r=[R["kpB"], V], w=[pb])
            P.op("dve", lambda e: e.tensor_tensor(out=Stmp[:].rearrange("p c v -> p (c v)"),
                                                  in0=pb[:], in1=Sf[:].rearrange("p c v -> p (c v)"), op=ALU.add),
                 r=[pb, Sf], w=[Stmp])
            P.op("dve", lambda e: e.tensor_tensor(out=Sf[:], in0=Stmp[:], in1=bc3(R["gam"][:], 64), op=ALU.mult),
                 r=[Stmp, R["gam"]], w=[Sf])
            P.op("act", lambda e: e.copy(out=Sb[:], in_=Sf[:]), r=[Sf], w=[Sb])
            yield

        def run2(gp, gc, ratio):
            done_p, done_c, k = gp is None, False, 0
            while not (done_p and done_c):
                if not done_p:
                    try:
                        next(gp)
                    except StopIteration:
                        done_p = True
                k += 1
                if not done_c and (done_p or k % ratio == 0):
                    try:
                        next(gc)
                    except StopIteration:
                        done_c = True

        n_it = len(order)
        load(0)
        if n_it > 1:
            load(1)
        for _ in precompute(0):
            pass
        for i in range(n_it):
            run2(precompute(i + 1) if i + 1 < n_it else None, chain(i), 6)
            if i + 2 < n_it:
                load(i + 2)
    C.pop()


def stage_rwkv_out(C, segs, wd, scr, mods_d, l, consts):
    P = C.P
    C.push()
    ident, epsb = consts
    (g2_d, w_o_d, ln_g_d, ln_b_d) = wd
    (y_d, v_d, sbon_d, sigG_d) = scr
    G2 = C.sb([128, D], BF16, "G2")
    Wo = C.sb([128, 8, D], BF16, "Wo")
    P.dma("pool", G2[:], g2_d, w=[G2])
    P.dma("pool", Wo[:], w_o_d.rearrange("(k p) n -> p k n", p=128), w=[Wo])
    lng = C.sb([128, D], F32, "lng")
    lnb = C.sb([128, D], F32, "lnb")
    gt = C.sb([128, D], F32, "gt")
    load_bc(C, lng, ln_g_d, "sp")
    load_bc(C, lnb, ln_b_d, "sp")
    gneps = C.sb([128, 1], F32, "gneps")
    P.op("dve", lambda e: e.memset(gneps[:], GN_EPS), w=[gneps])
    y0 = [C.sb([128, D], F32, "y0") for _ in range(2)]
    y1 = [C.sb([128, D], F32, "y1") for _ in range(2)]
    vb = [C.sb([128, D], BF16, "vb") for _ in range(2)]
    sb_ = [C.sb([128, 16], F32, "sb") for _ in range(2)]
    sg = [C.sb([128, 128], BF16, "sg") for _ in range(2)]
    xs_ = [C.sb([128, D], F32, "xs") for _ in range(2)]
    yc = C.sb([128, D], F32, "yc")
    sq = C.sb([128, D], F32, "sq")
    mean = C.sb([128, 16], F32, "mean")
    var = C.sb([128, 16], F32, "var")
    rstd = C.sb([128, 16], F32, "rstd")
    bon = C.sb([128, D], F32, "bon")
    zb = C.sb([128, D], BF16, "zb")
    zT = C.sb([128, 8, 128], BF16, "zT")
    tmp = C.sb([128, D], F32, "tmp")
    psG = [C.ps([128, 512], F32, "psG") for _ in range(2)]
    psO = [C.ps([128, 512], F32, "psO") for _ in range(2)]
    psT = C.ps([128, D], BF16, "psT")
    it = 0
    cur_which = None
    v3 = lambda b_: b_[:].rearrange("p (h e) -> p h e", e=64)
    for (xd, ntok, which, koff) in segs:
        if which != cur_which:
            cur_which = which
            load_bc(C, gt, mods_d[l, which, 5 * D:6 * D], "act")
        for r0 in range(0, ntok, 128):
            i = it % 2
            it += 1
            g0 = koff + r0
            P.dma("sp", y0[i][:], y_d[0, g0:g0 + 128, :], w=[y0[i]])
            P.dma("sp", y1[i][:], y_d[1, g0:g0 + 128, :], w=[y1[i]])
            P.dma("sp", vb[i][:], v_d[g0:g0 + 128, :], w=[vb[i]])
            P.dma("sp", sb_[i][:], sbon_d[g0:g0 + 128, :], w=[sb_[i]])
            P.dma("sp", sg[i][:], sigG_d[:, g0:g0 + 128], w=[sg[i]])
            P.dma("sp", xs_[i][:], xd[r0:r0 + 128, :], w=[xs_[i]])
            Y = y0[i]
            P.op("dve", lambda e: e.tensor_tensor(out=Y[:], in0=Y[:], in1=y1[i][:], op=ALU.add), r=[Y, y1[i]], w=[Y])
            P.op("dve", lambda e: e.tensor_reduce(out=mean[:], in_=v3(Y), axis=AX.X, op=ALU.add), r=[Y], w=[mean])
            P.op("dve", lambda e: e.tensor_scalar_mul(out=mean[:], in0=mean[:], scalar1=1.0 / 64), r=[mean], w=[mean])
            P.op("dve", lambda e: e.tensor_tensor(out=v3(yc), in0=v3(Y), in1=bc3(mean[:], 64), op=ALU.subtract),
                 r=[Y, mean], w=[yc])
            P.op("act", lambda e: e.activation(out=sq[:], in_=yc[:], func=AF.Square), r=[yc], w=[sq])
            P.op("dve", lambda e: e.tensor_reduce(out=var[:], in_=v3(sq), axis=AX.X, op=ALU.add), r=[sq], w=[var])
            P.op("act", lambda e: e.activation(out=var[:], in_=var[:], func=AF.Sqrt, bias=gneps[:], scale=1.0 / 64),
                 r=[var, gneps], w=[var])
            P.op("dve", lambda e: e.reciprocal(out=rstd[:], in_=var[:]), r=[var], w=[rstd])
            P.op("dve", lambda e: e.tensor_tensor(out=v3(yc), in0=v3(yc), in1=bc3(rstd[:], 64), op=ALU.mult),
                 r=[yc, rstd], w=[yc])
            P.op("pool", lambda e: e.tensor_tensor(out=yc[:], in0=yc[:], in1=lng[:], op=ALU.mult), r=[yc, lng], w=[yc])
            P.op("pool", lambda e: e.tensor_tensor(out=yc[:], in0=yc[:], in1=lnb[:], op=ALU.add), r=[yc, lnb], w=[yc])
            P.op("dve", lambda e: e.tensor_tensor(out=v3(bon), in0=v3(vb[i]), in1=bc3(sb_[i][:], 64), op=ALU.mult),
                 r=[vb[i], sb_[i]], w=[bon])
            P.op("pool", lambda e: e.tensor_tensor(out=yc[:], in0=yc[:], in1=bon[:], op=ALU.add), r=[yc, bon], w=[yc])
            for dh in range(2):
                pg = psG[dh]
                P.op("pe", lambda e: e.matmul(pg[:], lhsT=sg[i][:], rhs=G2[:, dh * 512:(dh + 1) * 512],
                                              start=True, stop=True), r=[sg[i], G2], w=[pg])
                P.op("dve", lambda e: e.tensor_tensor(out=zb[:, dh * 512:(dh + 1) * 512],
                                                      in0=pg[:], in1=yc[:, dh * 512:(dh + 1) * 512], op=ALU.mult),
                     r=[pg, yc], w=[zb])
            for kc in range(8):
                P.op("pe", lambda e: e.transpose(out=psT[:, kc * 128:(kc + 1) * 128], in_=zb[:, kc * 128:(kc + 1) * 128],
                                                 identity=ident[:]), r=[zb, ident], w=[psT])
            P.op("act", lambda e: e.copy(out=zT[:], in_=psT[:].rearrange("p (k t) -> p k t", k=8)), r=[psT], w=[zT])
            xs = xs_[i]
            for dh in range(2):
                po = psO[dh]
                for kc in range(8):
                    P.op("pe", lambda e: e.matmul(po[:], lhsT=zT[:, kc, :], rhs=Wo[:, kc, dh * 512:(dh + 1) * 512],
                                                  start=(kc == 0), stop=(kc == 7)), r=[zT, Wo], w=[po])
                P.op("dve", lambda e: e.tensor_tensor(out=tmp[:, dh * 512:(dh + 1) * 512], in0=po[:],
                                                      in1=gt[:, dh * 512:(dh + 1) * 512], op=ALU.mult),
                     r=[po, gt], w=[tmp])
                P.op("pool", lambda e: e.tensor_tensor(out=xs[:, dh * 512:(dh + 1) * 512],
                                                       in0=tmp[:, dh * 512:(dh + 1) * 512],
                                                       in1=xs[:, dh * 512:(dh + 1) * 512], op=ALU.add),
                     r=[tmp, xs], w=[xs])
            P.dma("act", xd[r0:r0 + 128, :], xs[:], r=[xs])
    C.pop()


def host_consts(S):
    pos = np.arange(S)
    row = (pos // 64).astype(np.float32)
    col = (pos % 64).astype(np.float32)
    inv = (10000.0 ** (-np.arange(8, dtype=np.float32) / 8)).astype(np.float32)
    ang_r = row[None, :] * inv[:, None]
    ang_c = col[None, :] * inv[:, None]
    cos = np.concatenate([np.cos(ang_r), np.cos(ang_r), np.cos(ang_c), np.cos(ang_c)], 0).astype(np.float32)
    sin = np.concatenate([-np.sin(ang_r), np.sin(ang_r), -np.sin(ang_c), np.sin(ang_c)], 0).astype(np.float32)
    prot = np.zeros((32, 32), np.float32)
    for i in range(32):
        j = i + 8 if (i % 16) < 8 else i - 8
        prot[j, i] = 1.0
    rwc = np.zeros((128, RWC_N), np.float32)
    rwc[0:64, 0:64] = 1.0
    rwc[64:128, 64:128] = 1.0
    rwc[0:64, 128] = 1.0
    rwc[64:128, 129] = 1.0
    ii = np.arange(128)
    for d in range(2):
        before = (ii[:, None] < ii[None, :]) if d == 0 else (ii[:, None] > ii[None, :])
        incl = before | np.eye(128, dtype=bool)
        base = RWC_TRI + d * RWC_DIR
        rwc[:, base:base + 128] = incl * DEC_C
        rwc[:, base + 128:base + 256] = before * DEC_C
        rwc[:, base + 256:base + 768] = np.tile(before.astype(np.float32), (1, 4))
        rwc[:, base + 768:base + 1280] = np.tile(before.T.astype(np.float32), (1, 4))
        rwc[:, base + 1280:base + 1792] = np.tile(incl.astype(np.float32), (1, 4))
        blkid = lambda m: (ii[:, None] // m) == (ii[None, :] // m)
        m1 = before & blkid(2)
        o = base + RWC_LV
        rwc[:, o:o + 512] = np.tile(m1.astype(np.float32), (1, 4))
        rwc[:, o + 512:o + 1024] = np.tile(m1.T.astype(np.float32), (1, 4))
        for li in range(6):
            m = 2 << li
            mm = before & blkid(2 * m) & ~blkid(m)
            o2 = o + 1024 + li * 1024
            rwc[:, o2:o2 + 512] = np.tile(mm.astype(np.float32), (1, 4))
            rwc[:, o2 + 512:o2 + 1024] = np.tile(mm.T.astype(np.float32), (1, 4))
    rwc[:, RWC_IREP:RWC_IREP + 512] = np.tile(np.eye(128, dtype=np.float32), (1, 4))
    return dict(ident=np.eye(128, dtype=np.float32), ones=np.ones((128, 128), np.float32),
                cos=np.ascontiguousarray(cos), sin=np.ascontiguousarray(sin), prot=prot, rwc=rwc)


def host_layout(inp, b):
    f = np.float32
    cond = np.stack([inp["c"][b].reshape(8, 128).T, inp["c_ctx"].reshape(8, 128).T], axis=-1).astype(f)
    mlac = np.zeros((128, 144), f)
    mlac[:, 0:3] = inp["mla_q_norm"][0].reshape(3, 128).T
    mlac[:, 3] = inp["mla_kv_norm"][0]
    mlac[0:64, 4] = inp["mla_qk_norm_q"][0][0:64]
    mlac[0:32, 5] = inp["mla_qk_norm_q"][0][64:96]
    mlac[0:64, 6] = inp["mla_qk_norm_k"][0][0:64]
    mlac[0:32, 7] = inp["mla_qk_norm_k"][0][64:96]
    dw = inp["conv_dw_w"][0][:, 0, :]
    mlac[:, 8:132] = dw.T.reshape(4, 128, 31).transpose(1, 0, 2).reshape(128, 124)
    mlac[:, 132:136] = inp["conv_dw_b"][0].reshape(4, 128).T
    mlac[:, 136:140] = inp["conv_norm_g"][0].reshape(4, 128).T
    mlac[:, 140:144] = inp["conv_norm_b"][0].reshape(4, 128).T
    d = dict(x=inp["x"][b], ctx=inp["ctx"][b], cond=np.ascontiguousarray(cond), mlac=mlac)
    for k in ("ada_w", "ada_b", "ffn_w1", "ffn_w3", "ffn_w2"):
        d[k] = inp[k]
    d["mla_w_in"] = inp["mla_w_in"][0]
    d["mla_w_qb"] = inp["mla_w_qb"][0]
    d["mla_w_kvb"] = inp["mla_w_kvb"][0]
    d["mix_w_out"] = inp["mix_w_out"][0]
    pp = lambda v: v.reshape(8, 128).T
    rwv = np.zeros((128, 88), f)
    for j in range(6):
        rwv[:, j * 8:(j + 1) * 8] = pp(inp["rwkv_x_mix"][0][j])
    rwv[:, 48:56] = pp(inp["rwkv_k_k"][0])
    rwv[:, 56:64] = pp(inp["rwkv_k_a"][0])
    rwv[:, 64:72] = pp(inp["rwkv_r_k"][0].reshape(-1))
    rwv[:, 72:80] = pp(inp["rwkv_a0"][0][0])
    rwv[:, 80:88] = pp(inp["rwkv_a0"][0][1])
    d["rwv"] = rwv
    for k in ("rwkv_w_r", "rwkv_w_k", "rwkv_w_v", "rwkv_w0", "rwkv_w1", "rwkv_w2", "rwkv_a1", "rwkv_a2",
              "rwkv_g1", "rwkv_g2", "rwkv_ln_g", "rwkv_ln_b", "rwkv_w_o"):
        d[k] = inp[k][0]
    return d


def build(S, stages=("ada", "ffn")):
    nc = bass.Bass("TRN2", target_bir_lowering=False)

    def din(name, shape, dt=F32):
        return nc.dram_tensor(name, list(shape), dt, kind="ExternalInput").ap()

    x_d = din("x", [S, D])
    ctx_d = din("ctx", [CTX, D])
    cond_d = din("cond", [128, 8, 2])
    ada_w_d = din("ada_w", [2, D, 9 * D])
    ada_b_d = din("ada_b", [2, 9 * D])
    w1_d = din("ffn_w1", [2, 2, D, DFF])
    w3_d = din("ffn_w3", [2, 2, D, DFF])
    w2_d = din("ffn_w2", [2, 2, DFF, D])
    ident_d = din("ident", [128, 128])
    ones_d = din("ones", [128, 128])
    cos_d = din("cos", [32, S])
    sin_d = din("sin", [32, S])
    prot_d = din("prot", [32, 32])
    mlac_d = din("mlac", [128, 144])
    w_in_d = din("mla_w_in", [D, 1568])
    w_qb_d = din("mla_w_qb", [384, 768])
    w_kvb_d = din("mla_w_kvb", [128, 1024])
    w_out_d = din("mix_w_out", [D, D])
    rwv_d = din("rwv", [128, 88])
    rwc_d = din("rwc", [128, RWC_N])
    rw = {k: din("rwkv_" + k, shp) for k, shp in (
        ("w_r", [D, D]), ("w_k", [D, D]), ("w_v", [D, D]), ("w0", [2, D]), ("w1", [2, D, 64]), ("w2", [2, 64, D]),
        ("a1", [2, D, 64]), ("a2", [2, 64, D]), ("g1", [D, 128]), ("g2", [128, D]), ("ln_g", [D]), ("ln_b", [D]),
        ("w_o", [D, D]))}
    out_d = nc.dram_tensor("out", [S, D], F32, kind="ExternalOutput").ap()
    C = Ctx(nc)
    P = C.P
    NK = CTX + S
    mods_d = C.dram("mods", [2, 2, 9 * D], F32)
    octx_d = C.dram("octx", [CTX, D], F32)
    qT_d = C.dram("qT", [NH, 96, S], BF16)
    qcT_d = C.dram("qcT", [NH, 96, CTX], BF16)
    kT_d = C.dram("kT", [NH, 96, NK], BF16)
    v_d = C.dram("v", [NK, 512], BF16)
    glu_l_d = C.dram("glu_l", [512, S + 30], F32)
    glu_c_d = C.dram("glu_c", [512, CTX + 30], F32)
    mixT_d = C.dram("mixT", [D, NK], BF16)
    dbg = {}
    if "dbg" in stages:
        dbg["octx"] = nc.dram_tensor("octx_o", [CTX, D], F32, kind="ExternalOutput").ap()
    C.push()
    consts = make_consts(C, ident_d)
    C.push()
    cpb = [C.sb([128, 4, D], F32) for _ in range(2)]
    i = 0
    for (src, dst, ntok) in ((x_d, out_d, S), (ctx_d, octx_d, CTX)):
        for t0 in range(0, ntok, 512):
            n = min(512, ntok - t0)
            b = cpb[i % 2]
            i += 1
            P.dma("sp", b[:, 0:n // 128, :], src[t0:t0 + n, :].rearrange("(s p) d -> p s d", p=128), w=[b])
            P.dma("sp", dst[t0:t0 + n, :].rearrange("(s p) d -> p s d", p=128), b[:, 0:n // 128, :], r=[b])
    C.pop()
    stage_ada(C, cond_d, ada_w_d, ada_b_d, mods_d)
    both = [(octx_d, CTX, 1), (out_d, S, 0)]
    if "ffn" in stages:
        stage_ffn(C, both, w1_d[0, 0], w3_d[0, 0], w2_d[0, 0], mods_d, 0, 0, consts)
    if "mla" in stages:
        wd = (w_in_d, w_qb_d, w_kvb_d, mlac_d, cos_d, sin_d, prot_d, ones_d)
        sub = [x for x in stages if x.startswith("mla_")] or ["mla_proj", "mla_conv", "mla_attn", "mla_out"]
        if "mla_proj" in sub:
            stage_mla_proj(C, [(octx_d, CTX, 1, 0, False), (out_d, S, 0, CTX, True)], S, wd, mods_d, 0, consts,
                           (qT_d, qcT_d, kT_d, v_d, glu_l_d, glu_c_d))
        if "mla_conv" in sub:
            stage_conv(C, [(glu_c_d, CTX, 0), (glu_l_d, S, CTX)], mlac_d, ones_d, mixT_d, consts)
        if "mla_attn" in sub:
            stage_attn(C, S, (qT_d, qcT_d, kT_d, v_d, mixT_d), ones_d)
        if "mla_out" in sub:
            stage_outproj(C, [(octx_d, CTX, 1, 0), (out_d, S, 0, CTX)], w_out_d, mixT_d, mods_d, 0)
    if "ffn2" in stages:
        stage_ffn(C, both, w1_d[0, 1], w3_d[0, 1], w2_d[0, 1], mods_d, 0, 6, consts)
    if "l1ffn" in stages:
        stage_ffn(C, both, w1_d[1, 0], w3_d[1, 0], w2_d[1, 0], mods_d, 1, 0, consts)
    if "rwkv" in stages:
        hT_d = C.dram("hTr", [D, NK + 4], BF16)
        rT_d = C.dram("rT", [D, NK], BF16)
        nkkT_d = C.dram("nkkT", [D, NK], BF16)
        bT_d = C.dram("bT", [2, D, NK], BF16)
        kdT_d = C.dram("kdT", [2, D, NK], BF16)
        sig_d = C.dram("sigw", [2, NK, D], F32)
        vr_d = C.dram("vr", [NK, D], BF16)
        sbon_d = C.dram("sbon", [NK, 16], F32)
        sigG_d = C.dram("sigG", [128, NK], BF16)
        y_d = C.dram("yscan", [2, NK, D], F32)
        sub = [x for x in stages if x.startswith("rw_")] or ["rw_h", "rw_proj", "rw_scan", "rw_out"]
        if "rw_h" in sub:
            stage_rwkv_h(C, [(octx_d, CTX, 1, 1), (out_d, S, 0, CTX + 3)], mods_d, 1, consts, hT_d)
        if "rw_proj" in sub:
            stage_rwkv_proj(C, [(CTX, 1, 0), (S, CTX + 3, CTX)],
                            (rw["w_r"], rw["w_k"], rw["w_v"], rw["w1"], rw["w2"], rw["a1"], rw["a2"], rw["g1"],
                             rw["w0"], rwv_d, rwc_d), hT_d,
                            (rT_d, nkkT_d, bT_d, kdT_d, sig_d, vr_d, sbon_d, sigG_d))
        if "rw_scan" in sub:
            stage_rwkv_scan(C, NK, (rT_d, nkkT_d, bT_d, kdT_d, sig_d, vr_d, y_d), rwc_d, consts)
        if "rw_out" in sub:
            stage_rwkv_out(C, [(out_d, S, 0, CTX)], (rw["g2"], rw["w_o"], rw["ln_g"], rw["ln_b"]),
                           (y_d, vr_d, sbon_d, sigG_d), mods_d, 1, consts)
    if "l1ffn2" in stages:
        stage_ffn(C, [(out_d, S, 0)], w1_d[1, 1], w3_d[1, 1], w2_d[1, 1], mods_d, 1, 6, consts)
    if "dbg" in stages:
        C.push()
        b = C.sb([128, 2, D], F32)
        P.dma("sp", b[:], octx_d.rearrange("(s p) d -> p s d", p=128), w=[b])
        P.dma("sp", dbg["octx"].rearrange("(s p) d -> p s d", p=128), b[:], r=[b])
        C.pop()
    C.pop()
    P.finish()
    return nc


ALL_STAGES = ("ada", "ffn", "mla", "ffn2", "l1ffn", "rwkv", "l1ffn2")
S_FULL = 8192


def kernel(**inputs):
    inp = {k: np.asarray(v) for k, v in inputs.items()}
    B = inp["x"].shape[0]
    S = inp["x"].shape[1]
    nc = build(S, ALL_STAGES)
    hc = host_consts(S)
    in_maps = []
    for b in range(B):
        d = host_layout(inp, b)
        d.update(hc)
        in_maps.append({k: np.ascontiguousarray(v, dtype=np.float32) for k, v in d.items()})
    res = run_bass_kernel_spmd(nc, in_maps, core_ids=list(range(B)))
    return np.stack([np.asarray(r["out"], dtype=np.float32) for r in res.results], axis=0)
```

```python
import os
import numpy as np
import concourse.bass as bass
import concourse.mybir as mybir
from concourse.bass_utils import run_bass_kernel_spmd

F32 = mybir.dt.float32
BF16 = mybir.dt.bfloat16
AF = mybir.ActivationFunctionType
ALU = mybir.AluOpType
AX = mybir.AxisListType

D = 1024
DFF = 2816
NFF = DFF // 128
EPS = 1e-6
CTX = 256


class Tk:
    __slots__ = ("w", "r")

    def __init__(self):
        self.w = None
        self.r = {}


class Buf:
    def __init__(self, t, k=None, excl=False):
        self.t = t
        self.k = k if k is not None else Tk()
        self.excl = excl

    def __getitem__(self, key):
        return self.t[key]


class GBuf:
    def __init__(self, buf, ngroups):
        self.t = buf.t
        self.g = [Buf(buf.t) for _ in range(ngroups)]

    def __getitem__(self, key):
        return self.t[key]


class Prog:
    def __init__(self, nc):
        self.nc = nc
        self.engs = {}
        for name, obj in [("pe", nc.tensor), ("act", nc.scalar), ("dve", nc.vector),
                          ("pool", nc.gpsimd), ("sp", nc.sync)]:
            self.engs[name] = dict(name=name, obj=obj, sem=nc.alloc_semaphore("s_" + name), cnt=0,
                                   waited={}, dsems=None)
        for name, n in [("sp", 12), ("pool", 8), ("act", 4)]:
            e = self.engs[name]
            e["dsems"] = [nc.alloc_semaphore(f"d_{name}{i}") for i in range(n)]
            e["dvals"] = [0] * n
            e["rr"] = 0
        self.nops = 0

    def _wait(self, e, tok):
        sem, val = tok
        key = sem.num
        if e["waited"].get(key, 0) >= val:
            return
        if e["name"] == "pe" and sem is e["sem"]:
            return
        e["obj"].wait_ge(sem, val)
        e["waited"][key] = val
        self.nops += 1

    def _deps(self, e, r, w):
        for b in r:
            k = b.k
            if k.w is not None:
                self._wait(e, k.w)
            if b.excl:
                for tok in k.r.values():
                    self._wait(e, tok)
        for b in w:
            k = b.k
            if k.w is not None:
                self._wait(e, k.w)
            for tok in k.r.values():
                self._wait(e, tok)

    def _update(self, tok, r, w):
        sem, val = tok
        for b in r:
            b.k.r[sem.num] = tok
        for b in w:
            b.k.w = tok
            b.k.r = {}

    def op(self, eng, fn, r=(), w=()):
        e = self.engs[eng]
        self._deps(e, r, w)
        ins = fn(e["obj"])
        e["cnt"] += 1
        ins.then_inc(e["sem"], 1)
        tok = (e["sem"], e["cnt"])
        self._update(tok, r, w)
        self.nops += 1
        return tok

    def dma(self, eng, out, in_, r=(), w=(), **kw):
        e = self.engs[eng]
        self._deps(e, r, w)
        i = e["rr"]
        e["rr"] = (i + 1) % len(e["dsems"])
        sem = e["dsems"][i]
        if e["dvals"][i] > 0:
            self._wait(e, (sem, e["dvals"][i]))
        ins = e["obj"].dma_start(out=out, in_=in_, **kw)
        e["dvals"][i] += 16
        ins.then_inc(sem, 16)
        tok = (sem, e["dvals"][i])
        self._update(tok, r, w)
        self.nops += 1
        return tok

    def barrier(self):
        toks = []
        for e in self.engs.values():
            if e["cnt"] > 0:
                toks.append((e["sem"], e["cnt"]))
            if e["dsems"]:
                for s, v in zip(e["dsems"], e["dvals"]):
                    if v > 0:
                        toks.append((s, v))
        for e in self.engs.values():
            for t in toks:
                if t[0] is e["sem"]:
                    continue
                self._wait(e, t)

    def finish(self):
        self.barrier()


class Ctx:
    def __init__(self, nc):
        self.nc = nc
        self.P = Prog(nc)
        self.uid = 0
        self.stack = []

    def sb(self, shape, dt, name=None):
        self.uid += 1
        cm = self.nc.sbuf_tensor(f"{name or 'sb'}_{self.uid}", list(shape), dt)
        t = cm.__enter__()
        self.stack[-1].append(cm)
        return Buf(t)

    def ps(self, shape, dt, name=None):
        self.uid += 1
        cm = self.nc.psum_tensor(f"{name or 'ps'}_{self.uid}", list(shape), dt)
        t = cm.__enter__()
        self.stack[-1].append(cm)
        return Buf(t, excl=True)

    def push(self):
        self.stack.append([])

    def pop(self):
        self.P.barrier()
        for cm in reversed(self.stack.pop()):
            cm.__exit__(None, None, None)

    def dram(self, name, shape, dt):
        return self.nc.dram_tensor(name, list(shape), dt, kind="Internal").ap()


def stage_ada(C, cond_d, ada_w_d, ada_b_d, mods_d):
    P = C.P
    C.push()
    cond = C.sb([128, 8, 2], F32)
    scond = C.sb([128, 8, 2], F32)
    wb = [C.sb([128, 8, 512], F32) for _ in range(3)]
    bb = [C.sb([2, 512], F32) for _ in range(3)]
    mrow = [C.sb([2, 9 * D], F32) for _ in range(2)]
    pss = [C.ps([2, 512], F32) for _ in range(2)]
    P.dma("sp", cond[:], cond_d, w=[cond])
    P.op("act", lambda e: e.activation(out=scond[:], in_=cond[:], func=AF.Silu), r=[cond], w=[scond])
    it = 0
    for l in range(2):
        wv = ada_w_d[l].rearrange("(k p) n -> p k n", p=128)
        for n in range(18):
            w = wb[it % 3]
            b = bb[it % 3]
            ps = pss[it % 2]
            P.dma("sp", w[:], wv[:, :, n * 512:(n + 1) * 512], w=[w])
            P.dma("sp", b[:], ada_b_d[l, n * 512:(n + 1) * 512].partition_broadcast(2), w=[b])
            for kc in range(8):
                P.op("pe", lambda e, kc=kc, w=w, ps=ps: e.matmul(ps[:], lhsT=scond[:, kc, :], rhs=w[:, kc, :],
                                                              start=(kc == 0), stop=(kc == 7)),
                     r=[scond, w], w=[ps])
            P.op("dve", lambda e, ps=ps, b=b, l=l, n=n: e.tensor_tensor(out=mrow[l][:, n * 512:(n + 1) * 512],
                                                                      in0=ps[:], in1=b[:], op=ALU.add),
                 r=[ps, b], w=[mrow[l]])
            it += 1
        P.dma("sp", mods_d[l], mrow[l][:], r=[mrow[l]])
    C.pop()


def load_bc(C, dst, src_row, eng="sp"):
    C.P.dma(eng, dst[:], src_row.partition_broadcast(128), w=[dst])


def make_hT_sub(C, xs, sub, sc1, sh, hT, tmp, hb, psT, ident, stat, epsb):
    P = C.P
    junk, ssq, rt, rstd = stat
    P.op("act", lambda e: e.activation(out=junk[:], in_=xs[:], func=AF.Square, accum_out=ssq[:]),
         r=[xs], w=[junk, ssq])
    P.op("act", lambda e: e.activation(out=rt[:], in_=ssq[:], func=AF.Sqrt, bias=epsb[:], scale=1.0 / D),
         r=[ssq, epsb], w=[rt])
    P.op("dve", lambda e: e.reciprocal(out=rstd[:], in_=rt[:]), r=[rt], w=[rstd])
    P.op("dve", lambda e: e.scalar_tensor_tensor(out=tmp[:], in0=xs[:], scalar=rstd[:, 0:1],
                                                 in1=sc1[:], op0=ALU.mult, op1=ALU.mult),
         r=[xs, rstd, sc1], w=[tmp])
    P.op("pool", lambda e: e.tensor_tensor(out=hb[:], in0=tmp[:], in1=sh[:], op=ALU.add),
         r=[tmp, sh], w=[hb])
    for kc in range(8):
        P.op("pe", lambda e, kc=kc: e.transpose(out=psT[:, kc * 128:(kc + 1) * 128],
                                                in_=hb[:, kc * 128:(kc + 1) * 128], identity=ident[:]),
             r=[hb, ident], w=[psT])
    P.op("act", lambda e: e.copy(out=hT[:, :, sub * 128:(sub + 1) * 128],
                                 in_=psT[:].rearrange("p (k t) -> p k t", k=8)),
         r=[psT], w=[hT])


def stage_ffn(C, segs, w1_d, w3_d, w2_d, mods_d, l, mbase, consts):
    P = C.P
    C.push()
    W1 = C.sb([128, 8, DFF], BF16, "W1")
    W3 = C.sb([128, 8, DFF], BF16, "W3")
    W2 = C.sb([128, NFF, D], BF16, "W2")
    P.dma("pool", W1[:], w1_d.rearrange("(k p) n -> p k n", p=128), w=[W1])
    P.dma("pool", W3[:], w3_d.rearrange("(k p) n -> p k n", p=128), w=[W3])
    P.dma("pool", W2[:], w2_d.rearrange("(k p) n -> p k n", p=128), w=[W2])
    ident, epsb = consts
    sc1 = C.sb([128, D], F32)
    sh = C.sb([128, D], F32)
    gt = C.sb([128, D], F32)
    NXB = 3
    xbs = [C.sb([128, D], F32, "xs") for _ in range(NXB)]
    hT = C.sb([128, 8, 512], BF16, "hT")
    gT = C.sb([128, NFF, 512], BF16, "gT")
    tmp = C.sb([128, D], F32)
    hb = C.sb([128, D], BF16)
    junk = C.sb([128, D], BF16)
    ssq = C.sb([128, 1], F32)
    rt = C.sb([128, 1], F32)
    rstd = C.sb([128, 1], F32)
    sil = [C.sb([128, 512], BF16, "sil") for _ in range(2)]
    psT = C.ps([128, D], BF16, "psT")
    ps1 = [C.ps([128, 512], F32, "ps1") for _ in range(2)]
    ps3 = [C.ps([128, 512], F32, "ps3") for _ in range(2)]
    pso = [C.ps([128, 512], F32, "pso") for _ in range(2)]
    stat = (junk, ssq, rt, rstd)
    cur_which = None
    tiles = []
    for (xd, ntok, which) in segs:
        t0 = 0
        while t0 < ntok:
            n = min(512, ntok - t0)
            tiles.append((xd, t0, n, which))
            t0 += n
    loads = []
    for (xd, t0, n, which) in tiles:
        for ph in range(2):
            for sub in range(n // 128):
                loads.append((xd, t0 + sub * 128))
    lstate = dict(i=0)

    def issue_load():
        i = lstate["i"]
        if i >= len(loads):
            return
        xd, r0 = loads[i]
        xb = xbs[i % NXB]
        P.dma("sp", xb[:], xd[r0:r0 + 128, :], w=[xb])
        lstate["i"] = i + 1

    cons = dict(i=0)

    def next_x():
        i = cons["i"]
        cons["i"] = i + 1
        while lstate["i"] < min(i + NXB - 1, len(loads)):
            issue_load()
        if lstate["i"] <= i:
            issue_load()
        return xbs[i % NXB]

    it = 0
    oi = 0
    for (xd, t0, n, which) in tiles:
        nsub = n // 128
        if which != cur_which:
            cur_which = which
            load_bc(C, sh, mods_d[l, which, (mbase + 0) * D:(mbase + 1) * D], "act")
            load_bc(C, sc1, mods_d[l, which, (mbase + 1) * D:(mbase + 2) * D], "act")
            load_bc(C, gt, mods_d[l, which, (mbase + 2) * D:(mbase + 3) * D], "act")
            P.op("dve", lambda e: e.tensor_scalar_add(out=sc1[:], in0=sc1[:], scalar1=1.0), r=[sc1], w=[sc1])
            P.op("dve", lambda e: e.tensor_scalar_mul(out=gt[:], in0=gt[:], scalar1=0.5), r=[gt], w=[gt])
        for sub in range(nsub):
            xs = next_x()
            make_hT_sub(C, xs, sub, sc1, sh, hT, tmp, hb, psT, ident, stat, epsb)
        for f in range(NFF):
            p1 = ps1[it % 2]
            p3 = ps3[it % 2]
            sl = sil[it % 2]
            it += 1
            for kc in range(8):
                P.op("pe", lambda e, kc=kc, f=f, p1=p1: e.matmul(p1[:, 0:n], lhsT=W1[:, kc, f * 128:(f + 1) * 128],
                                                                rhs=hT[:, kc, 0:n], start=(kc == 0), stop=(kc == 7)),
                     r=[W1, hT], w=[p1])
            for kc in range(8):
                P.op("pe", lambda e, kc=kc, f=f, p3=p3: e.matmul(p3[:, 0:n], lhsT=W3[:, kc, f * 128:(f + 1) * 128],
                                                                rhs=hT[:, kc, 0:n], start=(kc == 0), stop=(kc == 7)),
                     r=[W3, hT], w=[p3])
            P.op("act", lambda e, p1=p1, sl=sl: e.activation(out=sl[:, 0:n], in_=p1[:, 0:n], func=AF.Silu),
                 r=[p1], w=[sl])
            P.op("dve", lambda e, p3=p3, sl=sl, f=f: e.tensor_tensor(out=gT[:, f, 0:n], in0=sl[:, 0:n], in1=p3[:, 0:n],
                                                                    op=ALU.mult), r=[sl, p3], w=[gT])
        for sub in range(nsub):
            xs = next_x()
            for dh in range(2):
                po = pso[oi % 2]
                oi += 1
                for f in range(NFF):
                    P.op("pe", lambda e, f=f, sub=sub, dh=dh, po=po: e.matmul(
                        po[:], lhsT=gT[:, f, sub * 128:(sub + 1) * 128], rhs=W2[:, f, dh * 512:(dh + 1) * 512],
                        start=(f == 0), stop=(f == NFF - 1)), r=[gT, W2], w=[po])
                P.op("dve", lambda e, po=po, dh=dh: e.tensor_tensor(out=tmp[:, dh * 512:(dh + 1) * 512], in0=po[:],
                                                                   in1=gt[:, dh * 512:(dh + 1) * 512],
                                                                   op=ALU.mult), r=[po, gt], w=[tmp])
                P.op("pool", lambda e, dh=dh, xs=xs: e.tensor_tensor(
                    out=xs[:, dh * 512:(dh + 1) * 512], in0=tmp[:, dh * 512:(dh + 1) * 512],
                    in1=xs[:, dh * 512:(dh + 1) * 512], op=ALU.add), r=[tmp, xs], w=[xs])
            P.dma("act", xd[t0 + sub * 128:t0 + (sub + 1) * 128, :], xs[:], r=[xs])
    C.pop()


ATT_SCALE = 96 ** -0.5
NH = 8


def rms_bc(C, ss_ps, n_feat, rows, n, rt, rstd, epsb):
    P = C.P
    P.op("act", lambda e: e.activation(out=rt[0:rows, 0:n], in_=ss_ps[0:rows, 0:n], func=AF.Ln,
                                       bias=epsb[0:rows, :], scale=1.0 / n_feat), r=[ss_ps, epsb], w=[rt])
    P.op("act", lambda e: e.activation(out=rstd[0:rows, 0:n], in_=rt[0:rows, 0:n], func=AF.Exp, scale=-0.5),
         r=[rt], w=[rstd])


def stage_mla_proj(C, segs, S, wd, mods_d, l, consts, scr):
    P = C.P
    C.push()
    ident, epsb = consts
    w_in_d, w_qb_d, w_kvb_d, mlac_d, cos_d, sin_d, prot_d, ones_d = wd
    qT_d, qcT_d, kT_d, v_d, glu_l_d, glu_c_d = scr
    Win = C.sb([128, 8, 1568], BF16, "Win")
    Wqb = C.sb([128, 3, 768], BF16, "Wqb")
    Wkvb = C.sb([128, 1024], BF16, "Wkvb")
    P.dma("pool", Win[:], w_in_d.rearrange("(k p) n -> p k n", p=128), w=[Win])
    P.dma("pool", Wqb[:], w_qb_d.rearrange("(k p) n -> p k n", p=128), w=[Wqb])
    P.dma("pool", Wkvb[:], w_kvb_d, w=[Wkvb])
    mlac = C.sb([128, 144], F32, "mlac")
    P.dma("sp", mlac[:], mlac_d, w=[mlac])
    onesf = C.sb([128, 128], F32, "onesf")
    onesb = C.sb([128, 128], BF16, "onesb")
    prot = C.sb([32, 32], F32, "prot")
    P.dma("sp", onesf[:], ones_d, w=[onesf])
    P.dma("sp", prot[:], prot_d, w=[prot])
    P.op("dve", lambda e: e.tensor_copy(out=onesb[:], in_=onesf[:]), r=[onesf], w=[onesb])
    zero = C.sb([128, 4, 16], F32, "zero")
    P.op("dve", lambda e: e.memset(zero[:], 0.0), w=[zero])
    if "stop1" in os.environ.get("KDBG", ""):
        C.pop()
        return
    KDBG = os.environ.get("KDBG", "")
    for (gd, ntok) in ((glu_l_d, S), (glu_c_d, CTX)):
        if "nopad" in KDBG:
            break
        gv = gd.rearrange("(c p) t -> p c t", p=128)
        P.dma("sp", gv[:, :, 0:15], zero[:, :, 0:15], r=[zero])
        P.dma("sp", gv[:, :, 15 + ntok:30 + ntok], zero[:, :, 0:15], r=[zero])
    sc1 = C.sb([128, D], F32)
    sh = C.sb([128, D], F32)
    NXB = 3
    xbs = [C.sb([128, D], F32, "xs") for _ in range(NXB)]
    hT = C.sb([128, 8, 512], BF16, "hT")
    tmp = C.sb([128, D], F32)
    hb = C.sb([128, D], BF16)
    junk = C.sb([128, D], BF16)
    ssq = C.sb([128, 1], F32)
    rt1 = C.sb([128, 1], F32)
    rstd1 = C.sb([128, 1], F32)
    stat = (junk, ssq, rt1, rstd1)
    zq = C.sb([128, 3, 512], F32, "zq")
    sqb = [C.sb([128, 512], BF16, "sqb") for _ in range(2)]
    cqn = C.sb([128, 3, 512], BF16, "cqn")
    zkv = C.sb([128, 512], F32, "zkv")
    ckvn = C.sb([128, 512], BF16, "ckvn")
    kr_raw = C.sb([32, 512], F32, "kr_raw")
    sq_kr = C.sb([32, 512], BF16, "sq_kr")
    rts = [C.sb([128, 512], F32, "rt") for _ in range(3)]
    rstds = [C.sb([128, 512], F32, "rstd") for _ in range(3)]
    rr = dict(i=0)

    def nrs():
        rr["i"] += 1
        return rts[rr["i"] % 3], rstds[rr["i"] % 3]
    sig = [C.sb([128, 512], F32, "sig") for _ in range(2)]
    glu = [C.sb([128, 512], F32, "glu") for _ in range(2)]
    cos = C.sb([32, 512], F32, "cos")
    sin = C.sb([32, 512], F32, "sin")
    hn_o = [C.sb([64, 512], BF16, "hn_o") for _ in range(2)]
    hr = [C.sb([32, 512], F32, "hr") for _ in range(2)]
    t1 = [C.sb([32, 512], F32, "t1") for _ in range(2)]
    t2 = [C.sb([32, 512], F32, "t2") for _ in range(2)]
    hr_o = [C.sb([32, 512], BF16, "hr_o") for _ in range(2)]
    sqn = [C.sb([64, 512], BF16, "sqn") for _ in range(2)]
    sqr = [C.sb([32, 512], BF16, "sqr") for _ in range(2)]
    vt = [C.sb([128, 512], BF16, "vt") for _ in range(2)]
    psT = C.ps([128, D], BF16, "psT")
    psA = [C.ps([128, 512], F32, "psA") for _ in range(int(os.environ.get("NPSA", "3")))]
    psB = [C.ps([128, 512], F32, "psB") for _ in range(2)]
    psR = [C.ps([32, 512], F32, "psR") for _ in range(2)]
    cnt = dict(a=0, b=0, r=0, g=0, h=0, v=0, s=0)

    def nxt(lst, key):
        i = cnt[key]
        cnt[key] = i + 1
        return lst[i % len(lst)]

    cur_which = None
    for (xd, ntok, which, koff, is_lat) in segs:
        for t0 in range(0, ntok, 512):
            n = min(512, ntok - t0)
            nsub = n // 128
            if which != cur_which:
                cur_which = which
                load_bc(C, sh, mods_d[l, which, 3 * D:4 * D], "act")
                load_bc(C, sc1, mods_d[l, which, 4 * D:5 * D], "act")
                P.op("dve", lambda e: e.tensor_scalar_add(out=sc1[:], in0=sc1[:], scalar1=1.0), r=[sc1], w=[sc1])
            if is_lat:
                P.dma("sp", cos[:, 0:n], cos_d[:, t0:t0 + n], w=[cos])
                P.dma("sp", sin[:, 0:n], sin_d[:, t0:t0 + n], w=[sin])
            for sub in range(nsub):
                xs = xbs[cnt["s"] % NXB]
                cnt["s"] += 1
                P.dma("sp", xs[:], xd[t0 + sub * 128:t0 + (sub + 1) * 128, :], w=[xs])
                make_hT_sub(C, xs, sub, sc1, sh, hT, tmp, hb, psT, ident, stat, epsb)

            def proj(ps, col0, ncol):
                for kc in range(8):
                    P.op("pe", lambda e, kc=kc: e.matmul(ps[0:ncol, 0:n], lhsT=Win[:, kc, col0:col0 + ncol],
                                                         rhs=hT[:, kc, 0:n], start=(kc == 0), stop=(kc == 7)),
                         r=[Win, hT], w=[ps])

            if "stop2" in KDBG:
                continue
            pss = nxt(psB, "b")
            for c in range(3):
                ps = nxt(psA, "a")
                proj(ps, c * 128, 128)
                sq = nxt(sqb, "g")
                if "nosq" not in KDBG:
                    P.op("act", lambda e, ps=ps, sq=sq: e.activation(out=sq[:, 0:n], in_=ps[:, 0:n], func=AF.Square),
                         r=[ps], w=[sq])
                if "nocp" not in KDBG:
                    P.op("dve", lambda e, ps=ps, c=c: e.tensor_copy(out=zq[:, c, 0:n], in_=ps[:, 0:n]), r=[ps], w=[zq])
                if "cq1" in KDBG:
                    continue
                P.op("pe", lambda e, sq=sq, c=c: e.matmul(pss[:, 0:n], lhsT=onesb[:], rhs=sq[:, 0:n],
                                                         start=(c == 0), stop=(c == 2)), r=[onesb, sq], w=[pss])
            if "cq1" in KDBG or "cq2" in KDBG:
                continue
            rt, rstd = nrs()
            rms_bc(C, pss, 384, 128, n, rt, rstd, epsb)
            if "cq3" in KDBG:
                continue
            for c in range(3):
                P.op("dve", lambda e, c=c: e.scalar_tensor_tensor(out=cqn[:, c, 0:n], in0=zq[:, c, 0:n],
                                                                  scalar=mlac[:, c:c + 1], in1=rstd[:, 0:n],
                                                                  op0=ALU.mult, op1=ALU.mult),
                     r=[zq, mlac, rstd], w=[cqn])
            if "stop3" in KDBG:
                continue
            ps = nxt(psA, "a")
            proj(ps, 384, 128)
            sq = nxt(sqb, "g")
            P.op("act", lambda e, ps=ps, sq=sq: e.activation(out=sq[:, 0:n], in_=ps[:, 0:n], func=AF.Square),
                 r=[ps], w=[sq])
            P.op("dve", lambda e, ps=ps: e.tensor_copy(out=zkv[:, 0:n], in_=ps[:, 0:n]), r=[ps], w=[zkv])
            pss = nxt(psB, "b")
            P.op("pe", lambda e, sq=sq: e.matmul(pss[:, 0:n], lhsT=onesb[:], rhs=sq[:, 0:n], start=True, stop=True),
                 r=[onesb, sq], w=[pss])
            rt, rstd = nrs()
            rms_bc(C, pss, 128, 128, n, rt, rstd, epsb)
            P.op("dve", lambda e: e.scalar_tensor_tensor(out=ckvn[:, 0:n], in0=zkv[:, 0:n], scalar=mlac[:, 3:4],
                                                         in1=rstd[:, 0:n], op0=ALU.mult, op1=ALU.mult),
                 r=[zkv, mlac, rstd], w=[ckvn])
            if "stop4" in KDBG:
                continue
            ps = nxt(psA, "a")
            proj(ps, 512, 32)
            P.op("act", lambda e, ps=ps: e.activation(out=sq_kr[:, 0:n], in_=ps[0:32, 0:n], func=AF.Square),
                 r=[ps], w=[sq_kr])
            P.op("dve", lambda e, ps=ps: e.tensor_copy(out=kr_raw[:, 0:n], in_=ps[0:32, 0:n]), r=[ps], w=[kr_raw])
            gd = glu_l_d if is_lat else glu_c_d
            for c in range(4):
                if "noglu" in KDBG:
                    break
                pa = nxt(psA, "a")
                proj(pa, 544 + c * 128, 128)
                pg = nxt(psA, "a")
                proj(pg, 1056 + c * 128, 128)
                sg = nxt(sig, "h")
                gl = glu[cnt["h"] % 2]
                P.op("act", lambda e, pg=pg, sg=sg: e.activation(out=sg[:, 0:n], in_=pg[:, 0:n], func=AF.Sigmoid),
                     r=[pg], w=[sg])
                P.op("dve", lambda e, pa=pa, sg=sg, gl=gl: e.tensor_tensor(out=gl[:, 0:n], in0=sg[:, 0:n],
                                                                          in1=pa[:, 0:n], op=ALU.mult),
                     r=[sg, pa], w=[gl])
                P.dma("act", gd[c * 128:(c + 1) * 128, 15 + t0:15 + t0 + n], gl[:, 0:n], r=[gl])

            def head_side(ps_n, ps_r_or_raw, raw_is_sbuf, sq_r_shared, gcol_n, gcol_r, dst, dcol0, rope):
                sn = nxt(sqn, "v")
                P.op("act", lambda e: e.activation(out=sn[:, 0:n], in_=ps_n[0:64, 0:n], func=AF.Square),
                     r=[ps_n], w=[sn])
                if sq_r_shared is None:
                    sr = sqr[cnt["v"] % 2]
                    P.op("act", lambda e: e.activation(out=sr[:, 0:n], in_=ps_r_or_raw[0:32, 0:n], func=AF.Square),
                         r=[ps_r_or_raw], w=[sr])
                else:
                    sr = sq_r_shared
                pss = nxt(psB, "b")
                P.op("pe", lambda e: e.matmul(pss[0:64, 0:n], lhsT=onesb[0:64, 0:64], rhs=sn[:, 0:n],
                                              start=True, stop=False), r=[onesb, sn], w=[pss])
                P.op("pe", lambda e: e.matmul(pss[0:64, 0:n], lhsT=onesb[0:32, 0:64], rhs=sr[:, 0:n],
                                              start=False, stop=True), r=[onesb, sr], w=[pss])
                rt, rstd = nrs()
                rms_bc(C, pss, 96, 64, n, rt, rstd, epsb)
                ho = nxt(hn_o, "r")
                P.op("dve", lambda e: e.scalar_tensor_tensor(out=ho[:, 0:n], in0=ps_n[0:64, 0:n],
                                                             scalar=mlac[0:64, gcol_n:gcol_n + 1],
                                                             in1=rstd[0:64, 0:n], op0=ALU.mult, op1=ALU.mult),
                     r=[ps_n, mlac, rstd], w=[ho])
                P.dma("act", dst[0:64, dcol0:dcol0 + n], ho[:, 0:n], r=[ho])
                i = cnt["r"]
                h_r = hr[i % 2]
                ro = hr_o[i % 2]
                if rope:
                    P.op("dve", lambda e: e.scalar_tensor_tensor(out=h_r[:, 0:n], in0=ps_r_or_raw[0:32, 0:n],
                                                                 scalar=mlac[0:32, gcol_r:gcol_r + 1],
                                                                 in1=rstd[0:32, 0:n], op0=ALU.mult, op1=ALU.mult),
                         r=[ps_r_or_raw, mlac, rstd], w=[h_r])
                    pr = nxt(psR, "s")
                    P.op("pe", lambda e: e.matmul(pr[:, 0:n], lhsT=prot[:], rhs=h_r[:, 0:n], start=True, stop=True),
                         r=[prot, h_r], w=[pr])
                    a1 = t1[i % 2]
                    a2 = t2[i % 2]
                    P.op("pool", lambda e: e.tensor_tensor(out=a1[:, 0:n], in0=h_r[:, 0:n], in1=cos[:, 0:n],
                                                           op=ALU.mult), r=[h_r, cos], w=[a1])
                    P.op("dve", lambda e: e.tensor_tensor(out=a2[:, 0:n], in0=pr[:, 0:n], in1=sin[:, 0:n],
                                                          op=ALU.mult), r=[pr, sin], w=[a2])
                    P.op("pool", lambda e: e.tensor_tensor(out=ro[:, 0:n], in0=a1[:, 0:n], in1=a2[:, 0:n],
                                                           op=ALU.add), r=[a1, a2], w=[ro])
                else:
                    P.op("dve", lambda e: e.scalar_tensor_tensor(out=ro[:, 0:n], in0=ps_r_or_raw[0:32, 0:n],
                                                                 scalar=mlac[0:32, gcol_r:gcol_r + 1],
                                                                 in1=rstd[0:32, 0:n], op0=ALU.mult, op1=ALU.mult),
                         r=[ps_r_or_raw, mlac, rstd], w=[ro])
                P.dma("act", dst[64:96, dcol0:dcol0 + n], ro[:, 0:n], r=[ro])

            for h in range(NH):
                if "noheads" in KDBG:
                    break
                pqn = nxt(psA, "a")
                for c in range(3):
                    P.op("pe", lambda e, c=c: e.matmul(pqn[0:64, 0:n], lhsT=Wqb[:, c, h * 96:h * 96 + 64],
                                                       rhs=cqn[:, c, 0:n], start=(c == 0), stop=(c == 2)),
                         r=[Wqb, cqn], w=[pqn])
                pqr = nxt(psA, "a")
                for c in range(3):
                    P.op("pe", lambda e, c=c: e.matmul(pqr[0:32, 0:n], lhsT=Wqb[:, c, h * 96 + 64:h * 96 + 96],
                                                       rhs=cqn[:, c, 0:n], start=(c == 0), stop=(c == 2)),
                         r=[Wqb, cqn], w=[pqr])
                if is_lat:
                    head_side(pqn, pqr, False, None, 4, 5, qT_d[h], t0, True)
                else:
                    head_side(pqn, pqr, False, None, 4, 5, qcT_d[h], t0, False)
                pkn = nxt(psA, "a")
                P.op("pe", lambda e: e.matmul(pkn[0:64, 0:n], lhsT=Wkvb[:, h * 128:h * 128 + 64], rhs=ckvn[:, 0:n],
                                              start=True, stop=True), r=[Wkvb, ckvn], w=[pkn])
                head_side(pkn, kr_raw, True, sq_kr, 6, 7, kT_d[h], koff + t0, is_lat)
            for sub in range(nsub):
                if "nov" in KDBG:
                    break
                pv = nxt(psA, "a")
                P.op("pe", lambda e, sub=sub: e.matmul(
                    pv[:].rearrange("p (h e) -> p h e", e=64), lhsT=ckvn[:, sub * 128:(sub + 1) * 128],
                    rhs=Wkvb[:].rearrange("p (h e) -> p h e", e=128)[:, :, 64:128], start=True, stop=True),
                     r=[ckvn, Wkvb], w=[pv])
                vb = nxt(vt, "v")
                P.op("act", lambda e: e.copy(out=vb[:], in_=pv[:]), r=[pv], w=[vb])
                r0 = koff + t0 + sub * 128
                P.dma("act", v_d[r0:r0 + 128, :], vb[:], r=[vb])
    C.pop()


def stage_conv(C, segs, mlac_d, ones_d, scr, consts):
    P = C.P
    C.push()
    ident, epsb = consts
    mixT_d = scr
    mlac = C.sb([128, 144], F32, "mlac")
    onesf = C.sb([128, 128], F32, "onesf")
    P.dma("sp", mlac[:], mlac_d, w=[mlac])
    P.dma("sp", onesf[:], ones_d, w=[onesf])
    G = [C.sb([128, 4, 542], F32, "G") for _ in range(2)]
    Gb = [C.sb([128, 4, 542], BF16, "Gb") for _ in range(2)]
    dg = C.sb([128, 124, 128], BF16, "dg")
    identb = C.sb([128, 128], BF16, "identb")
    P.op("dve", lambda e: e.tensor_copy(out=identb[:], in_=ident[:]), r=[ident], w=[identb])
    for cj in range(124):
        P.op("dve" if cj % 2 == 0 else "pool",
             lambda e: e.tensor_scalar_mul(out=dg[:, cj, :], in0=identb[:], scalar1=mlac[:, 8 + cj:9 + cj]),
             r=[identb, mlac], w=[dg])
    psc = [C.ps([128, 512], F32, "psc") for _ in range(4)]
    acc = [C.sb([128, 512], F32, "acc") for _ in range(4)]
    sq = [C.sb([128, 512], F32, "sq") for _ in range(2)]
    mean = C.sb([128, 512], F32, "mean")
    m2 = C.sb([128, 512], F32, "m2")
    var = C.sb([128, 512], F32, "var")
    rt = C.sb([128, 512], F32, "rt")
    rstd = C.sb([128, 512], F32, "rstd")
    tt = [C.sb([128, 512], F32, "tt") for _ in range(2)]
    ob = [C.sb([128, 512], BF16, "ob") for _ in range(2)]
    ps1 = C.ps([128, 512], F32, "ps1")
    ps2 = C.ps([128, 512], F32, "ps2")
    W0 = 8
    it = 0
    for (gd, ntok, koff) in segs:
        gv = gd.rearrange("(c p) t -> p c t", p=128)
        for t0 in range(0, ntok, 512):
            n = min(512, ntok - t0)
            g = G[it % 2]
            it += 1
            P.dma("sp", g[:, :, 0:n + 30], gv[:, :, t0:t0 + n + 30], w=[g])
            gb = Gb[it % 2]
            P.op("act", lambda e: e.copy(out=gb[:, :, 0:n + 30], in_=g[:, :, 0:n + 30]), r=[g], w=[gb])
            for c in range(4):
                a = acc[c]
                pc = psc[c]
                for j in range(31):
                    P.op("pe", lambda e: e.matmul(pc[:, 0:n], lhsT=dg[:, c * 31 + j, :], rhs=gb[:, c, j:j + n],
                                                  start=(j == 0), stop=(j == 30)), r=[dg, gb], w=[pc])
                P.op("act", lambda e: e.activation(out=a[:, 0:n], in_=pc[:, 0:n], func=AF.Identity,
                                                   bias=mlac[:, 132 + c:133 + c], scale=1.0), r=[pc, mlac], w=[a])
            for c in range(4):
                a = acc[c]
                s_ = sq[c % 2]
                P.op("act", lambda e, a=a, s_=s_: e.activation(out=s_[:, 0:n], in_=a[:, 0:n], func=AF.Square),
                     r=[a], w=[s_])
                P.op("pe", lambda e, a=a, c=c: e.matmul(ps1[:, 0:n], lhsT=onesf[:], rhs=a[:, 0:n],
                                                       start=(c == 0), stop=(c == 3)), r=[onesf, a], w=[ps1])
                P.op("pe", lambda e, s_=s_, c=c: e.matmul(ps2[:, 0:n], lhsT=onesf[:], rhs=s_[:, 0:n],
                                                         start=(c == 0), stop=(c == 3)), r=[onesf, s_], w=[ps2])
            P.op("act", lambda e: e.activation(out=mean[:, 0:n], in_=ps1[:, 0:n], func=AF.Copy, scale=1.0 / 512),
                 r=[ps1], w=[mean])
            P.op("dve", lambda e: e.tensor_tensor(out=m2[:, 0:n], in0=mean[:, 0:n], in1=mean[:, 0:n], op=ALU.mult),
                 r=[mean], w=[m2])
            P.op("dve", lambda e: e.scalar_tensor_tensor(out=var[:, 0:n], in0=ps2[:, 0:n], scalar=1.0 / 512,
                                                         in1=m2[:, 0:n], op0=ALU.mult, op1=ALU.subtract),
                 r=[ps2, m2], w=[var])
            P.op("act", lambda e: e.activation(out=rt[:, 0:n], in_=var[:, 0:n], func=AF.Ln, bias=epsb[:],
                                               scale=1.0), r=[var, epsb], w=[rt])
            P.op("act", lambda e: e.activation(out=rstd[:, 0:n], in_=rt[:, 0:n], func=AF.Exp, scale=-0.5),
                 r=[rt], w=[rstd])
            for c in range(4):
                a = acc[c]
                t_ = tt[c % 2]
                o_ = ob[c % 2]
                P.op("dve", lambda e, a=a, t_=t_: e.tensor_tensor(out=t_[:, 0:n], in0=a[:, 0:n], in1=mean[:, 0:n],
                                                                 op=ALU.subtract), r=[a, mean], w=[t_])
                P.op("pool", lambda e, t_=t_: e.tensor_tensor(out=t_[:, 0:n], in0=t_[:, 0:n], in1=rstd[:, 0:n],
                                                             op=ALU.mult), r=[t_, rstd], w=[t_])
                P.op("act", lambda e, t_=t_, o_=o_, c=c: e.activation(out=o_[:, 0:n], in_=t_[:, 0:n], func=AF.Silu,
                                                                     bias=mlac[:, 140 + c:141 + c],
                                                                     scale=mlac[:, 136 + c:137 + c]),
                     r=[t_, mlac], w=[o_])
                P.dma("act", mixT_d[512 + c * 128:512 + (c + 1) * 128, koff + t0:koff + t0 + n], o_[:, 0:n], r=[o_])
    C.pop()


def stage_attn(C, S, scr, ones_d, do_ctx=True):
    P = C.P
    C.push()
    qT_d, qcT_d, kT_d, v_d, mixT_d = scr
    NK = CTX + S
    NKC = NK // 128
    onesf = C.sb([128, 128], F32, "onesf")
    P.dma("sp", onesf[:], ones_d, w=[onesf])
    kTs = [C.sb([96, NK], BF16, "kT") for _ in range(2)]
    Vs = [C.sb([128, NKC, 65], BF16, "V") for _ in range(2)]
    for V in Vs:
        P.op("dve", lambda e, V=V: e.memset(V[:, :, 64:65], 1.0), w=[V])
    qs = [C.sb([96, 512], BF16, "q") for _ in range(2)]
    pTs = [C.sb([128, 512], BF16, "pT") for _ in range(3)]
    oT = [C.sb([65, 512], F32, "oT") for _ in range(2)]
    rden = [C.sb([65, 512], F32, "rden") for _ in range(2)]
    att = [C.sb([64, 512], BF16, "att") for _ in range(2)]
    psS = [C.ps([128, 512], F32, "psS") for _ in range(3)]
    psO = [C.ps([65, 512], F32, "psO") for _ in range(2)]
    psB = [C.ps([64, 512], F32, "psB") for _ in range(2)]
    vv = v_d.rearrange("(kc p) (h e) -> p kc h e", p=128, e=64)
    si = 0
    qi = 0
    for h in range(NH):
        kT = kTs[h % 2]
        V = Vs[h % 2]
        P.dma("sp", kT[:], kT_d[h], w=[kT])
        P.dma("pool", V[:, :, 0:64], vv[:, :, h, :], w=[V])
        qtiles = [(qT_d, t0, 512, 0, NKC, CTX + t0) for t0 in range(0, S, 512)]
        if do_ctx:
            qtiles.append((qcT_d, 0, CTX, 0, CTX // 128, 0))
        for (qd, t0, n, kc0, kc1, ocol) in qtiles:
            q = qs[qi % 2]
            po = psO[qi % 2]
            o_ = oT[qi % 2]
            rd = rden[qi % 2]
            pb = psB[qi % 2]
            at = att[qi % 2]
            qi += 1
            P.dma("sp", q[:, 0:n], qd[h, :, t0:t0 + n], w=[q])
            prev = None
            for kc in range(kc0, kc1):
                ps = psS[si % 3]
                pT = pTs[si % 3]
                si += 1
                P.op("pe", lambda e: e.matmul(ps[:, 0:n], lhsT=kT[:, kc * 128:(kc + 1) * 128],
                                              rhs=q[:, 0:n], start=True, stop=True), r=[kT, q], w=[ps])
                P.op("act", lambda e: e.activation(out=pT[:, 0:n], in_=ps[:, 0:n], func=AF.Exp,
                                                   scale=ATT_SCALE), r=[ps], w=[pT])
                if prev is not None:
                    pk, ppT = prev
                    P.op("pe", lambda e: e.matmul(po[:, 0:n], lhsT=V[:, pk, :], rhs=ppT[:, 0:n],
                                                  start=(pk == kc0), stop=False), r=[V, ppT], w=[po])
                prev = (kc, pT)
            pk, ppT = prev
            P.op("pe", lambda e: e.matmul(po[:, 0:n], lhsT=V[:, pk, :], rhs=ppT[:, 0:n],
                                          start=(pk == kc0), stop=True), r=[V, ppT], w=[po])
            P.op("dve", lambda e: e.tensor_copy(out=o_[:, 0:n], in_=po[:, 0:n]), r=[po], w=[o_])
            P.op("act", lambda e: e.activation(out=rd[64:65, 0:n], in_=o_[64:65, 0:n], func=AF.Ln), r=[o_], w=[rd])
            P.op("act", lambda e: e.activation(out=rd[64:65, 0:n], in_=rd[64:65, 0:n], func=AF.Exp, scale=-1.0),
                 r=[rd], w=[rd])
            P.op("pe", lambda e: e.matmul(pb[:, 0:n], lhsT=onesf[64:65, 0:64], rhs=rd[64:65, 0:n],
                                          start=True, stop=True), r=[onesf, rd], w=[pb])
            P.op("dve", lambda e: e.tensor_tensor(out=at[:, 0:n], in0=o_[0:64, 0:n], in1=pb[:, 0:n], op=ALU.mult),
                 r=[o_, pb], w=[at])
            P.dma("pool", mixT_d[h * 64:(h + 1) * 64, ocol:ocol + n], at[:, 0:n], r=[at])
    C.pop()


def stage_outproj(C, segs, w_out_d, mixT_d, mods_d, l, row0=0):
    P = C.P
    C.push()
    Wout = C.sb([128, 8, D], BF16, "Wout")
    P.dma("pool", Wout[:], w_out_d.rearrange("(k p) n -> p k n", p=128), w=[Wout])
    gt = C.sb([128, D], F32)
    mixs = [C.sb([128, 8, 512], BF16, "mix") for _ in range(2)]
    xbs = [C.sb([128, D], F32, "xs") for _ in range(3)]
    tmp = C.sb([128, D], F32)
    pso = [C.ps([128, 512], F32, "pso") for _ in range(2)]
    mv = mixT_d.rearrange("(c p) t -> p c t", p=128)
    cur_which = None
    it = 0
    xi = 0
    oi = 0
    for (xd, ntok, which, koff) in segs:
        for t0 in range(0, ntok, 512):
            n = min(512, ntok - t0)
            if which != cur_which:
                cur_which = which
                load_bc(C, gt, mods_d[l, which, 5 * D:6 * D], "act")
            mx = mixs[it % 2]
            it += 1
            P.dma("sp", mx[:, :, 0:n], mv[:, :, koff + t0:koff + t0 + n], w=[mx])
            for sub in range(n // 128):
                xs = xbs[xi % 3]
                xi += 1
                r0 = t0 + sub * 128
                P.dma("sp", xs[:], xd[r0:r0 + 128, :], w=[xs])
                for dh in range(2):
                    po = pso[oi % 2]
                    oi += 1
                    for c in range(8):
                        P.op("pe", lambda e, c=c, po=po: e.matmul(po[:], lhsT=mx[:, c, sub * 128:(sub + 1) * 128],
                                                                 rhs=Wout[:, c, dh * 512:(dh + 1) * 512],
                                                                 start=(c == 0), stop=(c == 7)), r=[mx, Wout], w=[po])
                    P.op("dve", lambda e, po=po: e.tensor_tensor(out=tmp[:, dh * 512:(dh + 1) * 512], in0=po[:],
                                                                in1=gt[:, dh * 512:(dh + 1) * 512], op=ALU.mult),
                         r=[po, gt], w=[tmp])
                    P.op("pool", lambda e, xs=xs: e.tensor_tensor(out=xs[:, dh * 512:(dh + 1) * 512],
                                                                 in0=tmp[:, dh * 512:(dh + 1) * 512],
                                                                 in1=xs[:, dh * 512:(dh + 1) * 512], op=ALU.add),
                         r=[tmp, xs], w=[xs])
                P.dma("act", xd[r0:r0 + 128, :], xs[:], r=[xs])
    C.pop()


def make_consts(C, ident_d):
    P = C.P
    ident = C.sb([128, 128], BF16, "ident")
    identf = C.sb([128, 128], F32, "identf")
    epsb = C.sb([128, 1], F32, "epsb")
    P.dma("sp", identf[:], ident_d, w=[identf])
    P.op("dve", lambda e: e.tensor_copy(out=ident[:], in_=identf[:]), r=[identf], w=[ident])
    P.op("dve", lambda e: e.memset(epsb[:], EPS), w=[epsb])
    return ident, epsb


NHR = 16
GN_EPS = 64 * 1e-5
DEC_C = -float(np.exp(-0.5))


def bc3(ap2, n):
    return ap2.unsqueeze(2).to_broadcast([ap2.shape[0], ap2.shape[1], n])


def stage_rwkv_h(C, segs, mods_d, l, consts, hT_d):
    P = C.P
    C.push()
    ident, epsb = consts
    sc1 = C.sb([128, D], F32)
    sh = C.sb([128, D], F32)
    xbs = [C.sb([128, D], F32, "xs") for _ in range(3)]
    hTs = [C.sb([128, 8, 512], BF16, "hT") for _ in range(2)]
    tmp = C.sb([128, D], F32)
    hb = C.sb([128, D], BF16)
    junk = C.sb([128, D], BF16)
    ssq = C.sb([128, 1], F32)
    rt1 = C.sb([128, 1], F32)
    rstd1 = C.sb([128, 1], F32)
    stat = (junk, ssq, rt1, rstd1)
    zero = C.sb([128, 8, 1], BF16, "zero")
    P.op("dve", lambda e: e.memset(zero[:], 0.0), w=[zero])
    psT = C.ps([128, D], BF16, "psT")
    hv = hT_d.rearrange("(c p) t -> p c t", p=128)
    xi = 0
    ti = 0
    cur_which = None
    with C.nc.allow_non_contiguous_dma(reason="tiny zero pad columns"):
        for (xd, ntok, which, col0) in segs:
            P.dma("sp", hv[:, :, col0 - 1:col0], zero[:], r=[zero])
            P.dma("sp", hv[:, :, col0 + ntok:col0 + ntok + 1], zero[:], r=[zero])
    for (xd, ntok, which, col0) in segs:
        for t0 in range(0, ntok, 512):
            n = min(512, ntok - t0)
            if which != cur_which:
                cur_which = which
                load_bc(C, sh, mods_d[l, which, 3 * D:4 * D], "act")
                load_bc(C, sc1, mods_d[l, which, 4 * D:5 * D], "act")
                P.op("dve", lambda e: e.tensor_scalar_add(out=sc1[:], in0=sc1[:], scalar1=1.0), r=[sc1], w=[sc1])
            hT = hTs[ti % 2]
            ti += 1
            for sub in range(n // 128):
                xs = xbs[xi % 3]
                xi += 1
                P.dma("sp", xs[:], xd[t0 + sub * 128:t0 + (sub + 1) * 128, :], w=[xs])
                make_hT_sub(C, xs, sub, sc1, sh, hT, tmp, hb, psT, ident, stat, epsb)
            P.dma("act", hv[:, :, col0 + t0:col0 + t0 + n], hT[:, :, 0:n], r=[hT])
    C.pop()


def stage_rwkv_proj(C, segs, wd, hT_d, scr):
    P = C.P
    C.push()
    (w_r_d, w_k_d, w_v_d, w1_d, w2_d, a1_d, a2_d, g1_d, w0_d, rwv_d, rwc_d) = wd
    (rT_d, nkkT_d, bT_d, kdT_d, sig_d, v_d, sbon_d, sigG_d) = scr
    Wr = C.sb([128, 8, D], BF16, "Wr")
    Wk = C.sb([128, 8, D], BF16, "Wk")
    Wv = C.sb([128, 8, D], BF16, "Wv")
    for W, wdram in ((Wr, w_r_d), (Wk, w_k_d), (Wv, w_v_d)):
        P.dma("pool", W[:], wdram.rearrange("(k p) n -> p k n", p=128), w=[W])
    W1 = [C.sb([128, 8, 64], BF16, "W1") for _ in range(2)]
    A1 = [C.sb([128, 8, 64], BF16, "A1") for _ in range(2)]
    W2 = [C.sb([64, D], BF16, "W2") for _ in range(2)]
    A2 = [C.sb([64, D], BF16, "A2") for _ in range(2)]
    G1 = C.sb([128, 8, 128], BF16, "G1")
    w0bc = [C.sb([128, D], F32, "w0bc") for _ in range(2)]
    for d in range(2):
        P.dma("pool", W1[d][:], w1_d[d].rearrange("(k p) n -> p k n", p=128), w=[W1[d]])
        P.dma("pool", A1[d][:], a1_d[d].rearrange("(k p) n -> p k n", p=128), w=[A1[d]])
        P.dma("pool", W2[d][:], w2_d[d], w=[W2[d]])
        P.dma("pool", A2[d][:], a2_d[d], w=[A2[d]])
        load_bc(C, w0bc[d], w0_d[d], "sp")
    P.dma("pool", G1[:], g1_d.rearrange("(k p) n -> p k n", p=128), w=[G1])
    rwv = C.sb([128, 88], F32, "rwv")
    P.dma("sp", rwv[:], rwv_d, w=[rwv])
    XM, KK, KA, RK, A0 = 0, 48, 56, 64, 72
    omk = C.sb([128, 8], F32, "omk")
    rkh = C.sb([128, 8], F32, "rkh")
    P.op("dve", lambda e: e.tensor_scalar(out=omk[:], in0=rwv[:, KA:KA + 8], scalar1=-1.0, scalar2=1.0,
                                          op0=ALU.mult, op1=ALU.add), r=[rwv], w=[omk])
    P.op("dve", lambda e: e.tensor_scalar_mul(out=rkh[:], in0=rwv[:, RK:RK + 8], scalar1=0.5), r=[rwv], w=[rkh])
    blk = C.sb([128, 128], BF16, "blk")
    hsel = C.sb([128, 2], BF16, "hsel")
    blkf = C.sb([128, 130], F32, "blkf")
    P.dma("sp", blkf[:], rwc_d[:, 0:130], w=[blkf])
    P.op("dve", lambda e: e.tensor_copy(out=blk[:], in_=blkf[:, 0:128]), r=[blkf], w=[blk])
    P.op("dve", lambda e: e.tensor_copy(out=hsel[:], in_=blkf[:, 128:130]), r=[blkf], w=[hsel])
    hTh = [C.sb([128, 8, 514], BF16, "hTh") for _ in range(2)]
    tt = GBuf(C.sb([128, 8, 512], F32, "tt"), 8)
    xx = C.sb([128, 8, 512], F32, "xx")
    xj = [C.sb([128, 8, 512], BF16, "xj") for _ in range(2)]
    kT = GBuf(C.sb([128, 8, 512], F32, "kT"), 8)
    kkT = GBuf(C.sb([128, 8, 512], BF16, "kkT"), 8)
    rTs = GBuf(C.sb([128, 8, 512], BF16, "rTs"), 8)
    kdsum = tt
    ob = [C.sb([128, 512], BF16, "ob") for _ in range(3)]
    of = [C.sb([128, 512], F32, "of") for _ in range(3)]
    hid = [C.sb([128, 512], BF16, "hid") for _ in range(2)]
    sbo = [C.sb([128, 16], F32, "sbo") for _ in range(2)]
    psA = [C.ps([128, 512], F32, "psA") for _ in range(5)]
    psS = [C.ps([128, 512], F32, "psS") for _ in range(2)]
    cnt = dict(a=0, o=0, f=0, h=0, x=0, t=0, s=0)

    def nxt(lst, key):
        i = cnt[key]
        cnt[key] = i + 1
        return lst[i % len(lst)]

    hv = hT_d.rearrange("(c p) t -> p c t", p=128)
    for (ntok, col0, koff) in segs:
        for t0 in range(0, ntok, 512):
            n = min(512, ntok - t0)
            nsub = n // 128
            hh = nxt(hTh, "t")
            P.dma("sp", hh[:, :, 0:n + 2], hv[:, :, col0 + t0 - 1:col0 + t0 + n + 1], w=[hh])
            hc = hh[:, :, 1:n + 1]
            P.op("dve", lambda e: e.tensor_tensor(out=tt[:, :, 0:n], in0=hh[:, :, 0:n], in1=hh[:, :, 2:n + 2],
                                                  op=ALU.add), r=[hh], w=tt.g)
            P.op("dve", lambda e: e.scalar_tensor_tensor(out=xx[:, :, 0:n], in0=tt[:, :, 0:n], scalar=0.5, in1=hc,
                                                         op0=ALU.mult, op1=ALU.subtract), r=tt.g + [hh], w=[xx])

            def mix(j):
                x_ = nxt(xj, "x")
                P.op("dve", lambda e: e.tensor_tensor(out=tt[:, :, 0:n], in0=xx[:, :, 0:n],
                                                      in1=bc3(rwv[:, XM + j * 8:XM + j * 8 + 8], n), op=ALU.mult),
                     r=[xx, rwv], w=tt.g)
                P.op("pool", lambda e: e.tensor_tensor(out=x_[:, :, 0:n], in0=tt[:, :, 0:n], in1=hc, op=ALU.add),
                     r=tt.g + [hh], w=[x_])
                return x_

            def projT(W, x_, p, ncol=128, col0_=None):
                ps = nxt(psA, "a")
                c0 = p * 128 if col0_ is None else col0_
                for kc in range(8):
                    P.op("pe", lambda e, kc=kc: e.matmul(ps[0:ncol, 0:n], lhsT=W[:, kc, c0:c0 + ncol],
                                                         rhs=x_[:, kc, 0:n], start=(kc == 0), stop=(kc == 7)),
                         r=[W, x_], w=[ps])
                return ps

            tok0 = koff + t0
            x_ = mix(0)
            for p in range(8):
                ps = projT(Wr, x_, p)
                P.op("act", lambda e: e.copy(out=rTs[:, p, 0:n], in_=ps[:, 0:n]), r=[ps], w=[rTs.g[p]])
            P.dma("act", rT_d.rearrange("(c p) t -> p c t", p=128)[:, :, tok0:tok0 + n], rTs[:, :, 0:n], r=rTs.g)
            x_ = mix(2)
            for p in range(8):
                ps = projT(Wk, x_, p)
                P.op("act", lambda e: e.copy(out=kT[:, p, 0:n], in_=ps[:, 0:n]), r=[ps], w=[kT.g[p]])
                kr = nxt(of, "f")
                P.op("dve", lambda e: e.tensor_scalar_mul(out=kr[:, 0:n], in0=kT[:, p, 0:n],
                                                          scalar1=rwv[:, KK + p:KK + p + 1]), r=[kT.g[p], rwv], w=[kr])
                sq = nxt(ob, "o")
                P.op("act", lambda e: e.activation(out=sq[:, 0:n], in_=kr[:, 0:n], func=AF.Square), r=[kr], w=[sq])
                pss = nxt(psA, "a")
                P.op("pe", lambda e: e.matmul(pss[:, 0:n], lhsT=blk[:], rhs=sq[:, 0:n], start=True, stop=True),
                     r=[blk, sq], w=[pss])
                nr = nxt(of, "f")
                P.op("dve", lambda e: e.tensor_scalar_max(out=nr[:, 0:n], in0=pss[:, 0:n], scalar1=1e-24),
                     r=[pss], w=[nr])
                P.op("act", lambda e: e.activation(out=nr[:, 0:n], in_=nr[:, 0:n], func=AF.Ln), r=[nr], w=[nr])
                P.op("act", lambda e: e.activation(out=nr[:, 0:n], in_=nr[:, 0:n], func=AF.Exp, scale=-0.5),
                     r=[nr], w=[nr])
                P.op("dve", lambda e: e.tensor_tensor(out=kkT[:, p, 0:n], in0=kr[:, 0:n], in1=nr[:, 0:n], op=ALU.mult),
                     r=[kr, nr], w=[kkT.g[p]])
                nk = nxt(ob, "o")
                P.op("pool", lambda e: e.tensor_scalar_mul(out=nk[:, 0:n], in0=kkT[:, p, 0:n], scalar1=-1.0),
                     r=[kkT.g[p]], w=[nk])
                P.dma("act", nkkT_d[p * 128:(p + 1) * 128, tok0:tok0 + n], nk[:, 0:n], r=[nk])
            x_ = mix(3)
            for sub in range(nsub):
                for dh in range(2):
                    ps = nxt(psA, "a")
                    for kc in range(8):
                        P.op("pe", lambda e, kc=kc: e.matmul(ps[:], lhsT=x_[:, kc, sub * 128:(sub + 1) * 128],
                                                             rhs=Wv[:, kc, dh * 512:(dh + 1) * 512],
                                                             start=(kc == 0), stop=(kc == 7)), r=[x_, Wv], w=[ps])
                    vb = nxt(ob, "o")
                    P.op("act", lambda e: e.copy(out=vb[:], in_=ps[:]), r=[ps], w=[vb])
                    P.dma("act", v_d[tok0 + sub * 128:tok0 + (sub + 1) * 128, dh * 512:(dh + 1) * 512], vb[:], r=[vb])
            x_ = mix(1)
            for d in range(2):
                ps = projT(W1[d], x_, 0, 64, 0)
                hd = nxt(hid, "h")
                P.op("act", lambda e: e.activation(out=hd[0:64, 0:n], in_=ps[0:64, 0:n], func=AF.Tanh), r=[ps], w=[hd])
                for sub in range(nsub):
                    for dh in range(2):
                        ps2 = nxt(psA, "a")
                        P.op("pe", lambda e: e.matmul(ps2[:], lhsT=hd[0:64, sub * 128:(sub + 1) * 128],
                                                      rhs=W2[d][:, dh * 512:(dh + 1) * 512], start=True, stop=True),
                             r=[hd, W2[d]], w=[ps2])
                        o1 = nxt(of, "f")
                        P.op("dve", lambda e: e.tensor_tensor(out=o1[:], in0=ps2[:],
                                                              in1=w0bc[d][:, dh * 512:(dh + 1) * 512], op=ALU.add),
                             r=[ps2, w0bc[d]], w=[o1])
                        P.op("act", lambda e: e.activation(out=o1[:], in_=o1[:], func=AF.Sigmoid), r=[o1], w=[o1])
                        P.dma("act", sig_d[d, tok0 + sub * 128:tok0 + (sub + 1) * 128, dh * 512:(dh + 1) * 512],
                              o1[:], r=[o1])
            x_ = mix(5)
            ps = projT(G1, x_, 0, 128, 0)
            sg = nxt(ob, "o")
            P.op("act", lambda e: e.activation(out=sg[:, 0:n], in_=ps[:, 0:n], func=AF.Sigmoid), r=[ps], w=[sg])
            P.dma("act", sigG_d[:, tok0:tok0 + n], sg[:, 0:n], r=[sg])
            x_ = mix(4)
            for d in range(2):
                ps = projT(A1[d], x_, 0, 64, 0)
                hd = nxt(hid, "h")
                P.op("act", lambda e: e.copy(out=hd[0:64, 0:n], in_=ps[0:64, 0:n]), r=[ps], w=[hd])
                for p in range(8):
                    ps2 = nxt(psA, "a")
                    P.op("pe", lambda e: e.matmul(ps2[:, 0:n], lhsT=A2[d][:, p * 128:(p + 1) * 128], rhs=hd[0:64, 0:n],
                                                  start=True, stop=True), r=[A2[d], hd], w=[ps2])
                    av = nxt(of, "f")
                    P.op("act", lambda e: e.activation(out=av[:, 0:n], in_=ps2[:, 0:n], func=AF.Sigmoid,
                                                       bias=rwv[:, A0 + d * 8 + p:A0 + d * 8 + p + 1]),
                         r=[ps2, rwv], w=[av])
                    bb = nxt(ob, "o")
                    P.op("dve", lambda e: e.tensor_tensor(out=bb[:, 0:n], in0=kkT[:, p, 0:n], in1=av[:, 0:n],
                                                          op=ALU.mult), r=[kkT.g[p], av], w=[bb])
                    P.dma("act", bT_d[d, p * 128:(p + 1) * 128, tok0:tok0 + n], bb[:, 0:n], r=[bb])
                    P.op("dve", lambda e: e.tensor_scalar(out=av[:, 0:n], in0=av[:, 0:n],
                                                          scalar1=rwv[:, KA + p:KA + p + 1], scalar2=omk[:, p:p + 1],
                                                          op0=ALU.mult, op1=ALU.add), r=[av, rwv, omk], w=[av])
                    kd = nxt(ob, "o")
                    P.op("dve", lambda e: e.tensor_tensor(out=kd[:, 0:n], in0=kT[:, p, 0:n], in1=av[:, 0:n],
                                                          op=ALU.mult), r=[kT.g[p], av], w=[kd])
                    P.dma("act", kdT_d[d, p * 128:(p + 1) * 128, tok0:tok0 + n], kd[:, 0:n], r=[kd])
                    if d == 0:
                        P.op("pool", lambda e: e.tensor_tensor(out=kdsum[:, p, 0:n], in0=kT[:, p, 0:n], in1=av[:, 0:n],
                                                               op=ALU.mult), r=[kT.g[p], av], w=[kdsum.g[p]])
                    else:
                        P.op("dve", lambda e: e.tensor_tensor(out=av[:, 0:n], in0=kT[:, p, 0:n], in1=av[:, 0:n],
                                                              op=ALU.mult), r=[kT.g[p], av], w=[av])
                        P.op("dve", lambda e: e.tensor_tensor(out=kdsum[:, p, 0:n], in0=kdsum[:, p, 0:n],
                                                              in1=av[:, 0:n], op=ALU.add), r=[kdsum.g[p], av], w=[kdsum.g[p]])
            for p in range(8):
                P.op("dve", lambda e: e.scalar_tensor_tensor(out=kdsum[:, p, 0:n], in0=kdsum[:, p, 0:n],
                                                             scalar=rkh[:, p:p + 1], in1=rTs[:, p, 0:n],
                                                             op0=ALU.mult, op1=ALU.mult),
                     r=[kdsum.g[p], rkh, rTs.g[p]], w=[kdsum.g[p]])
            prod = nxt(xj, "x")
            P.op("act", lambda e: e.copy(out=prod[:, :, 0:n], in_=kdsum[:, :, 0:n]), r=kdsum.g, w=[prod])
            for sub in range(nsub):
                pb = nxt(psS, "s")
                for p in range(8):
                    P.op("pe", lambda e: e.matmul(pb[:, 2 * p:2 * p + 2], lhsT=prod[:, p, sub * 128:(sub + 1) * 128],
                                                  rhs=hsel[:], start=True, stop=True), r=[prod, hsel], w=[pb])
                so = sbo[sub % 2]
                P.op("dve", lambda e: e.tensor_copy(out=so[:], in_=pb[:, 0:16]), r=[pb], w=[so])
                P.dma("act", sbon_d[tok0 + sub * 128:tok0 + (sub + 1) * 128, :], so[:], r=[so])
    C.pop()


RWC_TRI = 130
RWC_LV = 256 + 1536
RWC_DIR = 256 + 1536 + 1024 + 12 * 512
RWC_IREP = 130 + 2 * RWC_DIR
RWC_N = RWC_IREP + 512


def stage_rwkv_scan(C, NK, scr, rwc_d, consts):
    P = C.P
    C.push()
    ident, epsb = consts
    (rT_d, nkkT_d, bT_d, kdT_d, sig_d, v_d, y_d) = scr
    NKC = NK // 128
    NCC = CTX // 128
    KD = os.environ.get("KDBG", "")
    irf = C.sb([128, 512], F32, "irf")
    irep = C.sb([128, 512], BF16, "irep")
    P.dma("sp", irf[:], rwc_d[:, RWC_IREP:RWC_IREP + 512], w=[irf])
    P.op("dve", lambda e: e.tensor_copy(out=irep[:], in_=irf[:]), r=[irf], w=[irep])
    triI = C.sb([128, 128], F32, "triI")
    triE = C.sb([128, 128], F32, "triE")
    mS = C.sb([128, 512], F32, "mS")
    mST = C.sb([128, 512], F32, "mST")
    mI = C.sb([128, 512], F32, "mI")
    mSb = C.sb([128, 512], BF16, "mSb")
    mIb = C.sb([128, 512], BF16, "mIb")
    lvm = C.sb([128, 14, 512], BF16, "lvm")
    lvf = C.sb([128, 2, 512], F32, "lvf")
    NLB = 2
    ld = [dict(r=C.sb([128, 8, 128], BF16, "l_r"), nk=C.sb([128, 8, 128], BF16, "l_nk"),
               b=C.sb([128, 8, 128], BF16, "l_b"), kd=C.sb([128, 8, 128], BF16, "l_kd"),
               sig=C.sb([128, D], F32, "l_sig"), v=C.sb([128, D], BF16, "l_v")) for _ in range(NLB)]
    eL = C.sb([128, D], F32, "eL")
    eLx = C.sb([128, D], F32, "eLx")
    enL = C.sb([128, D], F32, "enL")
    pre = []
    for i in range(2):
        d_ = dict(rt=C.sb([128, D], BF16, "rt"), at=C.sb([128, D], BF16, "at"), bt=C.sb([128, D], BF16, "bt"),
                  kt=C.sb([128, D], BF16, "kt"),
                  bpA=C.sb([128, 8, 128], BF16, "bpA"), bpB=C.sb([128, 8, 128], BF16, "bpB"),
                  kpA=C.sb([128, 8, 128], BF16, "kpA"), kpB=C.sb([128, 8, 128], BF16, "kpB"),
                  T=[GBuf(C.sb([128, 16, 128], BF16, "T"), 4) for _ in range(2)],
                  Tt=[GBuf(C.sb([128, 16, 128], BF16, "Tt"), 4) for _ in range(2)],
                  Mak=GBuf(C.sb([128, 16, 128], BF16, "Mak"), 4), Mbr=GBuf(C.sb([128, 16, 128], BF16, "Mbr"), 4),
                  Mkr=GBuf(C.sb([128, 16, 128], BF16, "Mkr"), 4), gam=C.sb([128, 8], F32, "gam"))
        for nm in ("bpA", "bpB", "kpA", "kpB"):
            P.op("pool", lambda e, b_=d_[nm]: e.memset(b_[:], 0.0), w=[d_[nm]])
        pre.append(d_)
    Pb = [GBuf(C.sb([128, 16, 128], BF16, "Pb"), 4) for _ in range(2)]
    Qb = [GBuf(C.sb([128, 16, 128], BF16, "Qb"), 4) for _ in range(2)]
    Sf = C.sb([128, 8, 64], F32, "Sf")
    Sb = C.sb([128, 8, 64], BF16, "Sb")
    Stmp = C.sb([128, 8, 64], F32, "Stmp")
    XT = GBuf(C.sb([128, 16, 64], BF16, "XT"), 2)
    UT = GBuf(C.sb([128, 16, 64], BF16, "UT"), 2)
    yt = [C.sb([128, D], F32, "yt") for _ in range(2)]
    pbF = [C.ps([128, 512], F32, "pbF") for _ in range(6)]
    pbT = [C.ps([128, D], BF16, "pbT") for _ in range(2)]
    cnt = dict(f=0, t=0, y=0)

    def nf():
        i = cnt["f"]
        cnt["f"] = i + 1
        return pbF[i % 6]

    def nt():
        i = cnt["t"]
        cnt["t"] = i + 1
        return pbT[i % 2]

    rv = rT_d.rearrange("(c p) t -> p c t", p=128)
    nkv = nkkT_d.rearrange("(c p) t -> p c t", p=128)

    for d in range(2):
        base = RWC_TRI + d * RWC_DIR
        P.dma("sp", triI[:], rwc_d[:, base:base + 128], w=[triI])
        P.dma("sp", triE[:], rwc_d[:, base + 128:base + 256], w=[triE])
        P.dma("sp", mS[:], rwc_d[:, base + 256:base + 768], w=[mS])
        P.dma("sp", mST[:], rwc_d[:, base + 768:base + 1280], w=[mST])
        P.dma("sp", mI[:], rwc_d[:, base + 1280:base + 1792], w=[mI])
        P.op("dve", lambda e: e.tensor_copy(out=mSb[:], in_=mS[:]), r=[mS], w=[mSb])
        P.op("dve", lambda e: e.tensor_copy(out=mIb[:], in_=mI[:]), r=[mI], w=[mIb])
        for q_ in range(7):
            o = base + RWC_LV + q_ * 1024
            P.dma("sp", lvf[:], rwc_d[:, o:o + 1024].rearrange("p (a b) -> p a b", a=2), w=[lvf])
            P.op("dve", lambda e: e.tensor_copy(out=lvm[:, 2 * q_:2 * q_ + 2, :], in_=lvf[:]), r=[lvf], w=[lvm])
        P.op("dve", lambda e: e.memset(Sf[:], 0.0), w=[Sf])
        P.op("dve", lambda e: e.memset(Sb[:], 0.0), w=[Sb])
        if d == 0:
            order = list(range(NKC))
        else:
            order = list(range(NCC - 1, -1, -1)) + list(range(NKC - 1, NCC - 1, -1))
        last = 127 if d == 0 else 0
        bv = bT_d[d].rearrange("(c p) t -> p c t", p=128)
        kv = kdT_d[d].rearrange("(c p) t -> p c t", p=128)

        def load(i):
            c = order[i]
            L = ld[i % NLB]
            t0 = c * 128
            P.dma("sp", L["r"][:], rv[:, :, t0:t0 + 128], w=[L["r"]])
            P.dma("sp", L["nk"][:], nkv[:, :, t0:t0 + 128], w=[L["nk"]])
            P.dma("sp", L["b"][:], bv[:, :, t0:t0 + 128], w=[L["b"]])
            P.dma("sp", L["kd"][:], kv[:, :, t0:t0 + 128], w=[L["kd"]])
            P.dma("sp", L["sig"][:], sig_d[d, t0:t0 + 128, :], w=[L["sig"]])
            P.dma("sp", L["v"][:], v_d[t0:t0 + 128, :], w=[L["v"]])

        def precompute(i):
            L = ld[i % NLB]
            R = pre[i % 2]
            bL = [nf(), nf()]
            for p in range(8):
                P.op("pe", lambda e: e.matmul(bL[p // 4][:, (p % 4) * 128:(p % 4 + 1) * 128],
                                              lhsT=L["sig"][:, p * 128:(p + 1) * 128], rhs=triI[:],
                                              start=True, stop=True), r=[L["sig"], triI], w=[bL[p // 4]])
            for hf in range(2):
                sl = slice(hf * 512, (hf + 1) * 512)
                P.op("act", lambda e: e.activation(out=eL[:, sl], in_=bL[hf][:], func=AF.Exp), r=[bL[hf]], w=[eL])
                P.op("act", lambda e: e.activation(out=enL[:, sl], in_=bL[hf][:], func=AF.Exp, scale=-1.0),
                     r=[bL[hf]], w=[enL])
            yield
            fl = lambda b_: b_[:].rearrange("p c t -> p (c t)")
            P.op("dve", lambda e: e.tensor_tensor(out=R["rt"][:], in0=fl(L["r"]), in1=eL[:], op=ALU.mult),
                 r=[L["r"], eL], w=[R["rt"]])
            at3 = R["at"][:].rearrange("p (c t) -> p c t", t=128)
            eL3 = eL[:].rearrange("p (c t) -> p c t", t=128)
            if d == 0:
                P.op("pool", lambda e: e.tensor_tensor(out=at3[:, :, 1:128], in0=L["nk"][:, :, 1:128],
                                                       in1=eL3[:, :, 0:127], op=ALU.mult), r=[L["nk"], eL], w=[R["at"]])
                P.op("pool", lambda e: e.tensor_copy(out=at3[:, :, 0:1], in_=L["nk"][:, :, 0:1]), r=[L["nk"]], w=[R["at"]])
            else:
                P.op("pool", lambda e: e.tensor_tensor(out=at3[:, :, 0:127], in0=L["nk"][:, :, 0:127],
                                                       in1=eL3[:, :, 1:128], op=ALU.mult), r=[L["nk"], eL], w=[R["at"]])
                P.op("pool", lambda e: e.tensor_copy(out=at3[:, :, 127:128], in_=L["nk"][:, :, 127:128]),
                     r=[L["nk"]], w=[R["at"]])
            P.op("dve", lambda e: e.tensor_tensor(out=R["bt"][:], in0=fl(L["b"]), in1=enL[:], op=ALU.mult),
                 r=[L["b"], enL], w=[R["bt"]])
            P.op("pool", lambda e: e.tensor_tensor(out=R["kt"][:], in0=fl(L["kd"]), in1=enL[:], op=ALU.mult),
                 r=[L["kd"], enL], w=[R["kt"]])
            P.op("dve", lambda e: e.tensor_copy(out=R["gam"][:],
                                                in_=eL[:].rearrange("p (c t) -> p c t", t=128)[:, :, last]),
                 r=[eL], w=[R["gam"]])
            yield
            for (src, dA, dB) in ((R["bt"], R["bpA"], R["bpB"]), (R["kt"], R["kpA"], R["kpB"])):
                pt = nt()
                for p in range(8):
                    P.op("pe", lambda e: e.transpose(out=pt[:, p * 128:(p + 1) * 128],
                                                     in_=src[:, p * 128:(p + 1) * 128], identity=ident[:]),
                         r=[src, ident], w=[pt])
                ptv = pt[:].rearrange("p (c k) -> p c k", k=128)
                P.op("act", lambda e: e.copy(out=dA[:, :, 0:64], in_=ptv[:, :, 0:64]), r=[pt], w=[dA])
                P.op("dve", lambda e: e.tensor_copy(out=dB[:, :, 64:128], in_=ptv[:, :, 64:128]), r=[pt], w=[dB])

            def hm(lhs, rhs, mask, dst, eng):
                for G in range(2):
                    pbs = (nf(), nf())
                    for j in range(8):
                        h = G * 8 + j
                        p, q = h // 2, h % 2
                        rows = slice(q * 64, q * 64 + 64)
                        pb = pbs[q]
                        P.op("pe", lambda e: e.matmul(pb[:, (j // 2) * 128:(j // 2 + 1) * 128],
                                                      lhsT=lhs[rows, p * 128:(p + 1) * 128],
                                                      rhs=rhs[rows, p * 128:(p + 1) * 128], start=True, stop=True),
                             r=[lhs, rhs], w=[pb])
                    for q in range(2):
                        dv = dst[:, G * 8 + q:G * 8 + 8:2, :]
                        m3 = mask[:].rearrange("p (h t) -> p h t", t=128)
                        dg = [dst.g[2 * G], dst.g[2 * G + 1]]
                        if eng == "dve":
                            P.op("dve", lambda e: e.tensor_tensor(
                                out=dv, in0=pbs[q][:].rearrange("p (h t) -> p h t", t=128), in1=m3, op=ALU.mult),
                                 r=[pbs[q], mask], w=dg)
                        else:
                            P.op("act", lambda e: e.copy(out=dv, in_=pbs[q][:].rearrange("p (h t) -> p h t", t=128)),
                                 r=[pbs[q]], w=dg)
                            P.op("pool", lambda e: e.tensor_tensor(out=dv, in0=dv, in1=m3, op=ALU.mult),
                                 r=dg + [mask], w=dg)

            yield
            hm(R["bt"], R["at"], mS, Pb[0], "dve")
            yield
            hm(R["at"], R["bt"], mST, Qb[0], "dve")
            yield
            hm(R["kt"], R["at"], mS, R["Mak"], "dve")
            yield
            hm(R["bt"], R["rt"], mIb, R["Mbr"], "actpool")
            yield
            hm(R["kt"], R["rt"], mIb, R["Mkr"], "actpool")
            yield
            T, Tt = R["T"][0], R["Tt"][0]
            g4 = lambda b_, g: b_[:, g * 4:(g + 1) * 4, :].rearrange("p h t -> p (h t)")
            for g in range(4):
                P.op("dve", lambda e: e.tensor_tensor(out=g4(T, g), in0=g4(Pb[0], g), in1=lvm[:, 0, :], op=ALU.mult),
                     r=[Pb[0].g[g], lvm], w=[T.g[g]])
                P.op("pool", lambda e: e.tensor_tensor(out=g4(T, g), in0=g4(T, g), in1=irep[:], op=ALU.add),
                     r=[T.g[g], irep], w=[T.g[g]])
                P.op("dve", lambda e: e.tensor_tensor(out=g4(Tt, g), in0=g4(Qb[0], g), in1=lvm[:, 1, :], op=ALU.mult),
                     r=[Qb[0].g[g], lvm], w=[Tt.g[g]])
                P.op("pool", lambda e: e.tensor_tensor(out=g4(Tt, g), in0=g4(Tt, g), in1=irep[:], op=ALU.add),
                     r=[Tt.g[g], irep], w=[Tt.g[g]])
            yield
            Wb = Pb[1]
            cur = 0
            for li in range(6):
                mk = lvm[:, 2 + 2 * li, :]
                T, Tt = R["T"][cur], R["Tt"][cur]
                Tn, Ttn = R["T"][1 - cur], R["Tt"][1 - cur]
                for g in range(4):
                    pw = nf()
                    for j in range(4):
                        h = g * 4 + j
                        P.op("pe", lambda e: e.matmul(pw[:, j * 128:(j + 1) * 128], lhsT=Qb[0][:, h, :], rhs=T[:, h, :],
                                                      start=True, stop=True), r=[Qb[0].g[g], T.g[g]], w=[pw])
                    if g < 3:
                        P.op("dve", lambda e: e.tensor_tensor(out=g4(Wb, g), in0=pw[:], in1=mk, op=ALU.mult),
                             r=[pw, lvm], w=[Wb.g[g]])
                    else:
                        P.op("act", lambda e: e.copy(out=g4(Wb, g), in_=pw[:]), r=[pw], w=[Wb.g[g]])
                        P.op("pool", lambda e: e.tensor_tensor(out=g4(Wb, g), in0=g4(Wb, g), in1=mk, op=ALU.mult),
                             r=[Wb.g[g], lvm], w=[Wb.g[g]])
                    if g % 2 == 1:
                        yield
                for g in range(4):
                    pm = nf()
                    for j in range(4):
                        h = g * 4 + j
                        P.op("pe", lambda e: e.matmul(pm[:, j * 128:(j + 1) * 128], lhsT=Tt[:, h, :], rhs=Wb[:, h, :],
                                                      start=True, stop=True), r=[Tt.g[g], Wb.g[g]], w=[pm])
                    P.op("dve", lambda e: e.tensor_tensor(out=g4(Tn, g), in0=pm[:], in1=g4(T, g), op=ALU.add),
                         r=[pm, T.g[g]], w=[Tn.g[g]])
                    if g % 2 == 1:
                        yield
                if li < 5:
                    for g in range(4):
                        ptt = nt()
                        for j in range(4):
                            h = g * 4 + j
                            P.op("pe", lambda e: e.transpose(out=ptt[:, j * 128:(j + 1) * 128], in_=Tn[:, h, :],
                                                             identity=ident[:]), r=[Tn.g[g], ident], w=[ptt])
                        P.op("act", lambda e: e.copy(out=g4(Ttn, g), in_=ptt[:, 0:512]), r=[ptt], w=[Ttn.g[g]])
                        if g % 2 == 1:
                            yield
                cur = 1 - cur
            R["Tf"] = R["T"][cur]

        def chain(i):
            c = order[i]
            L = ld[i % NLB]
            R = pre[i % 2]
            V = L["v"]
            T = R["Tf"]
            for g in range(2):
                pb = nf()
                for j in range(8):
                    h = g * 8 + j
                    p, q = h // 2, h % 2
                    rows = slice(q * 64, q * 64 + 64)
                    P.op("pe", lambda e: e.matmul(pb[:, j * 64:(j + 1) * 64], lhsT=R["at"][rows, p * 128:(p + 1) * 128],
                                                  rhs=Sb[rows, p, :], start=True, stop=False), r=[R["at"], Sb], w=[pb])
                    P.op("pe", lambda e: e.matmul(pb[:, j * 64:(j + 1) * 64], lhsT=R["Mak"][:, h, :],
                                                  rhs=V[:, h * 64:(h + 1) * 64], start=False, stop=True),
                         r=[R["Mak"].g[h // 4], V], w=[pb])
                P.op("act", lambda e: e.copy(out=XT[:, g * 8:(g + 1) * 8, :].rearrange("p h v -> p (h v)"), in_=pb[:]),
                     r=[pb], w=[XT.g[g]])
            yield
            for g in range(2):
                pb = nf()
                for j in range(8):
                    h = g * 8 + j
                    P.op("pe", lambda e: e.matmul(pb[:, j * 64:(j + 1) * 64], lhsT=T[:, h, :], rhs=XT[:, h, :],
                                                  start=True, stop=True), r=[T.g[h // 4], XT.g[g]], w=[pb])
                P.op("dve", lambda e: e.tensor_copy(out=UT[:, g * 8:(g + 1) * 8, :].rearrange("p h v -> p (h v)"),
                                                    in_=pb[:]), r=[pb], w=[UT.g[g]])
            yield
            y_ = yt[cnt["y"] % 2]
            cnt["y"] += 1
            for g in range(2):
                pb = nf()
                for j in range(8):
                    h = g * 8 + j
                    p, q = h // 2, h % 2
                    rows = slice(q * 64, q * 64 + 64)
                    P.op("pe", lambda e: e.matmul(pb[:, j * 64:(j + 1) * 64], lhsT=R["rt"][rows, p * 128:(p + 1) * 128],
                                                  rhs=Sb[rows, p, :], start=True, stop=False), r=[R["rt"], Sb], w=[pb])
                    P.op("pe", lambda e: e.matmul(pb[:, j * 64:(j + 1) * 64], lhsT=R["Mbr"][:, h, :], rhs=UT[:, h, :],
                                                  start=False, stop=False), r=[R["Mbr"].g[h // 4], UT.g[g]], w=[pb])
                    P.op("pe", lambda e: e.matmul(pb[:, j * 64:(j + 1) * 64], lhsT=R["Mkr"][:, h, :],
                                                  rhs=V[:, h * 64:(h + 1) * 64], start=False, stop=True),
                         r=[R["Mkr"].g[h // 4], V], w=[pb])
                P.op("act", lambda e: e.copy(out=y_[:, g * 512:(g + 1) * 512], in_=pb[:]), r=[pb], w=[y_])
            P.dma("act", y_d[d, c * 128:(c + 1) * 128, :], y_[:], r=[y_])
            yield
            pb = nf()
            for p in range(8):
                o_ = pb[:, p * 64:(p + 1) * 64]
                P.op("pe", lambda e: e.matmul(o_, lhsT=R["bpA"][:, p, :], rhs=UT[:, 2 * p, :], start=True, stop=False),
                     r=[R["bpA"]] + UT.g, w=[pb])
                P.op("pe", lambda e: e.matmul(o_, lhsT=R["bpB"][:, p, :], rhs=UT[:, 2 * p + 1, :], start=False,
                                              stop=False), r=[R["bpB"]] + UT.g, w=[pb])
                P.op("pe", lambda e: e.matmul(o_, lhsT=R["kpA"][:, p, :], rhs=V[:, (2 * p) * 64:(2 * p + 1) * 64],
                                              start=False, stop=False), r=[R["kpA"], V], w=[pb])
                P.op("pe", lambda e: e.matmul(o_, lhsT=R["kpB"][:, p, :], rhs=V[:, (2 * p + 1) * 64:(2 * p + 2) * 64],
                                              start=False, stop=True), r=[R["kpB"], V], w=[pb])
            P.op("dve", lambda e: e.tensor_tensor(out=Stmp[:].rearrange("p c v -> p (c v)"),
                                                  in0=pb[:], in1=Sf[:].rearrange("p c v -> p (c v)"), op=ALU.add),
                 r=[pb, Sf], w=[Stmp])
            P.op("dve", lambda e: e.tensor_tensor(out=Sf[:], in0=Stmp[:], in1=bc3(R["gam"][:], 64), op=ALU.mult),
                 r=[Stmp, R["gam"]], w=[Sf])
            P.op("act", lambda e: e.copy(out=Sb[:], in_=Sf[:]), r=[Sf], w=[Sb])
            yield

        def run2(gp, gc, ratio):
            done_p, done_c, k = gp is None, False, 0
            while not (done_p and done_c):
                if not done_p:
                    try:
                        next(gp)
                    except StopIteration:
                        done_p = True
                k += 1
                if not done_c and (done_p or k % ratio == 0):
                    try:
                        next(gc)
                    except StopIteration:
                        done_c = True

        n_it = len(order)
        load(0)
        if n_it > 1:
            load(1)
        for _ in precompute(0):
            pass
        for i in range(n_it):
            run2(precompute(i + 1) if i + 1 < n_it else None, chain(i), 6)
            if i + 2 < n_it:
                load(i + 2)
    C.pop()


def stage_rwkv_out(C, segs, wd, scr, mods_d, l, consts):
    P = C.P
    C.push()
    ident, epsb = consts
    (g2_d, w_o_d, ln_g_d, ln_b_d) = wd
    (y_d, v_d, sbon_d, sigG_d) = scr
    G2 = C.sb([128, D], BF16, "G2")
    Wo = C.sb([128, 8, D], BF16, "Wo")
    P.dma("pool", G2[:], g2_d, w=[G2])
    P.dma("pool", Wo[:], w_o_d.rearrange("(k p) n -> p k n", p=128), w=[Wo])
    lng = C.sb([128, D], F32, "lng")
    lnb = C.sb([128, D], F32, "lnb")
    gt = C.sb([128, D], F32, "gt")
    load_bc(C, lng, ln_g_d, "sp")
    load_bc(C, lnb, ln_b_d, "sp")
    gneps = C.sb([128, 1], F32, "gneps")
    P.op("dve", lambda e: e.memset(gneps[:], GN_EPS), w=[gneps])
    y0 = [C.sb([128, D], F32, "y0") for _ in range(2)]
    y1 = [C.sb([128, D], F32, "y1") for _ in range(2)]
    vb = [C.sb([128, D], BF16, "vb") for _ in range(2)]
    sb_ = [C.sb([128, 16], F32, "sb") for _ in range(2)]
    sg = [C.sb([128, 128], BF16, "sg") for _ in range(2)]
    xs_ = [C.sb([128, D], F32, "xs") for _ in range(2)]
    yc = C.sb([128, D], F32, "yc")
    sq = C.sb([128, D], F32, "sq")
    mean = C.sb([128, 16], F32, "mean")
    var = C.sb([128, 16], F32, "var")
    rstd = C.sb([128, 16], F32, "rstd")
    bon = C.sb([128, D], F32, "bon")
    zb = C.sb([128, D], BF16, "zb")
    zT = C.sb([128, 8, 128], BF16, "zT")
    tmp = C.sb([128, D], F32, "tmp")
    psG = [C.ps([128, 512], F32, "psG") for _ in range(2)]
    psO = [C.ps([128, 512], F32, "psO") for _ in range(2)]
    psT = C.ps([128, D], BF16, "psT")
    it = 0
    cur_which = None
    v3 = lambda b_: b_[:].rearrange("p (h e) -> p h e", e=64)
    for (xd, ntok, which, koff) in segs:
        if which != cur_which:
            cur_which = which
            load_bc(C, gt, mods_d[l, which, 5 * D:6 * D], "act")
        for r0 in range(0, ntok, 128):
            i = it % 2
            it += 1
            g0 = koff + r0
            P.dma("sp", y0[i][:], y_d[0, g0:g0 + 128, :], w=[y0[i]])
            P.dma("sp", y1[i][:], y_d[1, g0:g0 + 128, :], w=[y1[i]])
            P.dma("sp", vb[i][:], v_d[g0:g0 + 128, :], w=[vb[i]])
            P.dma("sp", sb_[i][:], sbon_d[g0:g0 + 128, :], w=[sb_[i]])
            P.dma("sp", sg[i][:], sigG_d[:, g0:g0 + 128], w=[sg[i]])
            P.dma("sp", xs_[i][:], xd[r0:r0 + 128, :], w=[xs_[i]])
            Y = y0[i]
            P.op("dve", lambda e: e.tensor_tensor(out=Y[:], in0=Y[:], in1=y1[i][:], op=ALU.add), r=[Y, y1[i]], w=[Y])
            P.op("dve", lambda e: e.tensor_reduce(out=mean[:], in_=v3(Y), axis=AX.X, op=ALU.add), r=[Y], w=[mean])
            P.op("dve", lambda e: e.tensor_scalar_mul(out=mean[:], in0=mean[:], scalar1=1.0 / 64), r=[mean], w=[mean])
            P.op("dve", lambda e: e.tensor_tensor(out=v3(yc), in0=v3(Y), in1=bc3(mean[:], 64), op=ALU.subtract),
                 r=[Y, mean], w=[yc])
            P.op("act", lambda e: e.activation(out=sq[:], in_=yc[:], func=AF.Square), r=[yc], w=[sq])
            P.op("dve", lambda e: e.tensor_reduce(out=var[:], in_=v3(sq), axis=AX.X, op=ALU.add), r=[sq], w=[var])
            P.op("act", lambda e: e.activation(out=var[:], in_=var[:], func=AF.Sqrt, bias=gneps[:], scale=1.0 / 64),
                 r=[var, gneps], w=[var])
            P.op("dve", lambda e: e.reciprocal(out=rstd[:], in_=var[:]), r=[var], w=[rstd])
            P.op("dve", lambda e: e.tensor_tensor(out=v3(yc), in0=v3(yc), in1=bc3(rstd[:], 64), op=ALU.mult),
                 r=[yc, rstd], w=[yc])
            P.op("pool", lambda e: e.tensor_tensor(out=yc[:], in0=yc[:], in1=lng[:], op=ALU.mult), r=[yc, lng], w=[yc])
            P.op("pool", lambda e: e.tensor_tensor(out=yc[:], in0=yc[:], in1=lnb[:], op=ALU.add), r=[yc, lnb], w=[yc])
            P.op("dve", lambda e: e.tensor_tensor(out=v3(bon), in0=v3(vb[i]), in1=bc3(sb_[i][:], 64), op=ALU.mult),
                 r=[vb[i], sb_[i]], w=[bon])
            P.op("pool", lambda e: e.tensor_tensor(out=yc[:], in0=yc[:], in1=bon[:], op=ALU.add), r=[yc, bon], w=[yc])
            for dh in range(2):
                pg = psG[dh]
                P.op("pe", lambda e: e.matmul(pg[:], lhsT=sg[i][:], rhs=G2[:, dh * 512:(dh + 1) * 512],
                                              start=True, stop=True), r=[sg[i], G2], w=[pg])
                P.op("dve", lambda e: e.tensor_tensor(out=zb[:, dh * 512:(dh + 1) * 512],
                                                      in0=pg[:], in1=yc[:, dh * 512:(dh + 1) * 512], op=ALU.mult),
                     r=[pg, yc], w=[zb])
            for kc in range(8):
                P.op("pe", lambda e: e.transpose(out=psT[:, kc * 128:(kc + 1) * 128], in_=zb[:, kc * 128:(kc + 1) * 128],
                                                 identity=ident[:]), r=[zb, ident], w=[psT])
            P.op("act", lambda e: e.copy(out=zT[:], in_=psT[:].rearrange("p (k t) -> p k t", k=8)), r=[psT], w=[zT])
            xs = xs_[i]
            for dh in range(2):
                po = psO[dh]
                for kc in range(8):
                    P.op("pe", lambda e: e.matmul(po[:], lhsT=zT[:, kc, :], rhs=Wo[:, kc, dh * 512:(dh + 1) * 512],
                                                  start=(kc == 0), stop=(kc == 7)), r=[zT, Wo], w=[po])
                P.op("dve", lambda e: e.tensor_tensor(out=tmp[:, dh * 512:(dh + 1) * 512], in0=po[:],
                                                      in1=gt[:, dh * 512:(dh + 1) * 512], op=ALU.mult),
                     r=[po, gt], w=[tmp])
                P.op("pool", lambda e: e.tensor_tensor(out=xs[:, dh * 512:(dh + 1) * 512],
                                                       in0=tmp[:, dh * 512:(dh + 1) * 512],
                                                       in1=xs[:, dh * 512:(dh + 1) * 512], op=ALU.add),
                     r=[tmp, xs], w=[xs])
            P.dma("act", xd[r0:r0 + 128, :], xs[:], r=[xs])
    C.pop()


def host_consts(S):
    pos = np.arange(S)
    row = (pos // 64).astype(np.float32)
    col = (pos % 64).astype(np.float32)
    inv = (10000.0 ** (-np.arange(8, dtype=np.float32) / 8)).astype(np.float32)
    ang_r = row[None, :] * inv[:, None]
    ang_c = col[None, :] * inv[:, None]
    cos = np.concatenate([np.cos(ang_r), np.cos(ang_r), np.cos(ang_c), np.cos(ang_c)], 0).astype(np.float32)
    sin = np.concatenate([-np.sin(ang_r), np.sin(ang_r), -np.sin(ang_c), np.sin(ang_c)], 0).astype(np.float32)
    prot = np.zeros((32, 32), np.float32)
    for i in range(32):
        j = i + 8 if (i % 16) < 8 else i - 8
        prot[j, i] = 1.0
    rwc = np.zeros((128, RWC_N), np.float32)
    rwc[0:64, 0:64] = 1.0
    rwc[64:128, 64:128] = 1.0
    rwc[0:64, 128] = 1.0
    rwc[64:128, 129] = 1.0
    ii = np.arange(128)
    for d in range(2):
        before = (ii[:, None] < ii[None, :]) if d == 0 else (ii[:, None] > ii[None, :])
        incl = before | np.eye(128, dtype=bool)
        base = RWC_TRI + d * RWC_DIR
        rwc[:, base:base + 128] = incl * DEC_C
        rwc[:, base + 128:base + 256] = before * DEC_C
        rwc[:, base + 256:base + 768] = np.tile(before.astype(np.float32), (1, 4))
        rwc[:, base + 768:base + 1280] = np.tile(before.T.astype(np.float32), (1, 4))
        rwc[:, base + 1280:base + 1792] = np.tile(incl.astype(np.float32), (1, 4))
        blkid = lambda m: (ii[:, None] // m) == (ii[None, :] // m)
        m1 = before & blkid(2)
        o = base + RWC_LV
        rwc[:, o:o + 512] = np.tile(m1.astype(np.float32), (1, 4))
        rwc[:, o + 512:o + 1024] = np.tile(m1.T.astype(np.float32), (1, 4))
        for li in range(6):
            m = 2 << li
            mm = before & blkid(2 * m) & ~blkid(m)
            o2 = o + 1024 + li * 1024
            rwc[:, o2:o2 + 512] = np.tile(mm.astype(np.float32), (1, 4))
            rwc[:, o2 + 512:o2 + 1024] = np.tile(mm.T.astype(np.float32), (1, 4))
    rwc[:, RWC_IREP:RWC_IREP + 512] = np.tile(np.eye(128, dtype=np.float32), (1, 4))
    return dict(ident=np.eye(128, dtype=np.float32), ones=np.ones((128, 128), np.float32),
                cos=np.ascontiguousarray(cos), sin=np.ascontiguousarray(sin), prot=prot, rwc=rwc)


def host_layout(inp, b):
    f = np.float32
    cond = np.stack([inp["c"][b].reshape(8, 128).T, inp["c_ctx"].reshape(8, 128).T], axis=-1).astype(f)
    mlac = np.zeros((128, 144), f)
    mlac[:, 0:3] = inp["mla_q_norm"][0].reshape(3, 128).T
    mlac[:, 3] = inp["mla_kv_norm"][0]
    mlac[0:64, 4] = inp["mla_qk_norm_q"][0][0:64]
    mlac[0:32, 5] = inp["mla_qk_norm_q"][0][64:96]
    mlac[0:64, 6] = inp["mla_qk_norm_k"][0][0:64]
    mlac[0:32, 7] = inp["mla_qk_norm_k"][0][64:96]
    dw = inp["conv_dw_w"][0][:, 0, :]
    mlac[:, 8:132] = dw.T.reshape(4, 128, 31).transpose(1, 0, 2).reshape(128, 124)
    mlac[:, 132:136] = inp["conv_dw_b"][0].reshape(4, 128).T
    mlac[:, 136:140] = inp["conv_norm_g"][0].reshape(4, 128).T
    mlac[:, 140:144] = inp["conv_norm_b"][0].reshape(4, 128).T
    d = dict(x=inp["x"][b], ctx=inp["ctx"][b], cond=np.ascontiguousarray(cond), mlac=mlac)
    for k in ("ada_w", "ada_b", "ffn_w1", "ffn_w3", "ffn_w2"):
        d[k] = inp[k]
    d["mla_w_in"] = inp["mla_w_in"][0]
    d["mla_w_qb"] = inp["mla_w_qb"][0]
    d["mla_w_kvb"] = inp["mla_w_kvb"][0]
    d["mix_w_out"] = inp["mix_w_out"][0]
    pp = lambda v: v.reshape(8, 128).T
    rwv = np.zeros((128, 88), f)
    for j in range(6):
        rwv[:, j * 8:(j + 1) * 8] = pp(inp["rwkv_x_mix"][0][j])
    rwv[:, 48:56] = pp(inp["rwkv_k_k"][0])
    rwv[:, 56:64] = pp(inp["rwkv_k_a"][0])
    rwv[:, 64:72] = pp(inp["rwkv_r_k"][0].reshape(-1))
    rwv[:, 72:80] = pp(inp["rwkv_a0"][0][0])
    rwv[:, 80:88] = pp(inp["rwkv_a0"][0][1])
    d["rwv"] = rwv
    for k in ("rwkv_w_r", "rwkv_w_k", "rwkv_w_v", "rwkv_w0", "rwkv_w1", "rwkv_w2", "rwkv_a1", "rwkv_a2",
              "rwkv_g1", "rwkv_g2", "rwkv_ln_g", "rwkv_ln_b", "rwkv_w_o"):
        d[k] = inp[k][0]
    return d


def build(S, stages=("ada", "ffn")):
    nc = bass.Bass("TRN2", target_bir_lowering=False)

    def din(name, shape, dt=F32):
        return nc.dram_tensor(name, list(shape), dt, kind="ExternalInput").ap()

    x_d = din("x", [S, D])
    ctx_d = din("ctx", [CTX, D])
    cond_d = din("cond", [128, 8, 2])
    ada_w_d = din("ada_w", [2, D, 9 * D])
    ada_b_d = din("ada_b", [2, 9 * D])
    w1_d = din("ffn_w1", [2, 2, D, DFF])
    w3_d = din("ffn_w3", [2, 2, D, DFF])
    w2_d = din("ffn_w2", [2, 2, DFF, D])
    ident_d = din("ident", [128, 128])
    ones_d = din("ones", [128, 128])
    cos_d = din("cos", [32, S])
    sin_d = din("sin", [32, S])
    prot_d = din("prot", [32, 32])
    mlac_d = din("mlac", [128, 144])
    w_in_d = din("mla_w_in", [D, 1568])
    w_qb_d = din("mla_w_qb", [384, 768])
    w_kvb_d = din("mla_w_kvb", [128, 1024])
    w_out_d = din("mix_w_out", [D, D])
    rwv_d = din("rwv", [128, 88])
    rwc_d = din("rwc", [128, RWC_N])
    rw = {k: din("rwkv_" + k, shp) for k, shp in (
        ("w_r", [D, D]), ("w_k", [D, D]), ("w_v", [D, D]), ("w0", [2, D]), ("w1", [2, D, 64]), ("w2", [2, 64, D]),
        ("a1", [2, D, 64]), ("a2", [2, 64, D]), ("g1", [D, 128]), ("g2", [128, D]), ("ln_g", [D]), ("ln_b", [D]),
        ("w_o", [D, D]))}
    out_d = nc.dram_tensor("out", [S, D], F32, kind="ExternalOutput").ap()
    C = Ctx(nc)
    P = C.P
    NK = CTX + S
    mods_d = C.dram("mods", [2, 2, 9 * D], F32)
    octx_d = C.dram("octx", [CTX, D], F32)
    qT_d = C.dram("qT", [NH, 96, S], BF16)
    qcT_d = C.dram("qcT", [NH, 96, CTX], BF16)
    kT_d = C.dram("kT", [NH, 96, NK], BF16)
    v_d = C.dram("v", [NK, 512], BF16)
    glu_l_d = C.dram("glu_l", [512, S + 30], F32)
    glu_c_d = C.dram("glu_c", [512, CTX + 30], F32)
    mixT_d = C.dram("mixT", [D, NK], BF16)
    dbg = {}
    if "dbg" in stages:
        dbg["octx"] = nc.dram_tensor("octx_o", [CTX, D], F32, kind="ExternalOutput").ap()
    C.push()
    consts = make_consts(C, ident_d)
    C.push()
    cpb = [C.sb([128, 4, D], F32) for _ in range(2)]
    i = 0
    for (src, dst, ntok) in ((x_d, out_d, S), (ctx_d, octx_d, CTX)):
        for t0 in range(0, ntok, 512):
            n = min(512, ntok - t0)
            b = cpb[i % 2]
            i += 1
            P.dma("sp", b[:, 0:n // 128, :], src[t0:t0 + n, :].rearrange("(s p) d -> p s d", p=128), w=[b])
            P.dma("sp", dst[t0:t0 + n, :].rearrange("(s p) d -> p s d", p=128), b[:, 0:n // 128, :], r=[b])
    C.pop()
    stage_ada(C, cond_d, ada_w_d, ada_b_d, mods_d)
    both = [(octx_d, CTX, 1), (out_d, S, 0)]
    if "ffn" in stages:
        stage_ffn(C, both, w1_d[0, 0], w3_d[0, 0], w2_d[0, 0], mods_d, 0, 0, consts)
    if "mla" in stages:
        wd = (w_in_d, w_qb_d, w_kvb_d, mlac_d, cos_d, sin_d, prot_d, ones_d)
        sub = [x for x in stages if x.startswith("mla_")] or ["mla_proj", "mla_conv", "mla_attn", "mla_out"]
        if "mla_proj" in sub:
            stage_mla_proj(C, [(octx_d, CTX, 1, 0, False), (out_d, S, 0, CTX, True)], S, wd, mods_d, 0, consts,
                           (qT_d, qcT_d, kT_d, v_d, glu_l_d, glu_c_d))
        if "mla_conv" in sub:
            stage_conv(C, [(glu_c_d, CTX, 0), (glu_l_d, S, CTX)], mlac_d, ones_d, mixT_d, consts)
        if "mla_attn" in sub:
            stage_attn(C, S, (qT_d, qcT_d, kT_d, v_d, mixT_d), ones_d)
        if "mla_out" in sub:
            stage_outproj(C, [(octx_d, CTX, 1, 0), (out_d, S, 0, CTX)], w_out_d, mixT_d, mods_d, 0)
    if "ffn2" in stages:
        stage_ffn(C, both, w1_d[0, 1], w3_d[0, 1], w2_d[0, 1], mods_d, 0, 6, consts)
    if "l1ffn" in stages:
        stage_ffn(C, both, w1_d[1, 0], w3_d[1, 0], w2_d[1, 0], mods_d, 1, 0, consts)
    if "rwkv" in stages:
        hT_d = C.dram("hTr", [D, NK + 4], BF16)
        rT_d = C.dram("rT", [D, NK], BF16)
        nkkT_d = C.dram("nkkT", [D, NK], BF16)
        bT_d = C.dram("bT", [2, D, NK], BF16)
        kdT_d = C.dram("kdT", [2, D, NK], BF16)
        sig_d = C.dram("sigw", [2, NK, D], F32)
        vr_d = C.dram("vr", [NK, D], BF16)
        sbon_d = C.dram("sbon", [NK, 16], F32)
        sigG_d = C.dram("sigG", [128, NK], BF16)
        y_d = C.dram("yscan", [2, NK, D], F32)
        sub = [x for x in stages if x.startswith("rw_")] or ["rw_h", "rw_proj", "rw_scan", "rw_out"]
        if "rw_h" in sub:
            stage_rwkv_h(C, [(octx_d, CTX, 1, 1), (out_d, S, 0, CTX + 3)], mods_d, 1, consts, hT_d)
        if "rw_proj" in sub:
            stage_rwkv_proj(C, [(CTX, 1, 0), (S, CTX + 3, CTX)],
                            (rw["w_r"], rw["w_k"], rw["w_v"], rw["w1"], rw["w2"], rw["a1"], rw["a2"], rw["g1"],
                             rw["w0"], rwv_d, rwc_d), hT_d,
                            (rT_d, nkkT_d, bT_d, kdT_d, sig_d, vr_d, sbon_d, sigG_d))
        if "rw_scan" in sub:
            stage_rwkv_scan(C, NK, (rT_d, nkkT_d, bT_d, kdT_d, sig_d, vr_d, y_d), rwc_d, consts)
        if "rw_out" in sub:
            stage_rwkv_out(C, [(out_d, S, 0, CTX)], (rw["g2"], rw["w_o"], rw["ln_g"], rw["ln_b"]),
                           (y_d, vr_d, sbon_d, sigG_d), mods_d, 1, consts)
    if "l1ffn2" in stages:
        stage_ffn(C, [(out_d, S, 0)], w1_d[1, 1], w3_d[1, 1], w2_d[1, 1], mods_d, 1, 6, consts)
    if "dbg" in stages:
        C.push()
        b = C.sb([128, 2, D], F32)
        P.dma("sp", b[:], octx_d.rearrange("(s p) d -> p s d", p=128), w=[b])
        P.dma("sp", dbg["octx"].rearrange("(s p) d -> p s d", p=128), b[:], r=[b])
        C.pop()
    C.pop()
    P.finish()
    return nc


ALL_STAGES = ("ada", "ffn", "mla", "ffn2", "l1ffn", "rwkv", "l1ffn2")
S_FULL = 8192


def kernel(**inputs):
    inp = {k: np.asarray(v) for k, v in inputs.items()}
    B = inp["x"].shape[0]
    S = inp["x"].shape[1]
    nc = build(S, ALL_STAGES)
    hc = host_consts(S)
    in_maps = []
    for b in range(B):
        d = host_layout(inp, b)
        d.update(hc)
        in_maps.append({k: np.ascontiguousarray(v, dtype=np.float32) for k, v in d.items()})
    res = run_bass_kernel_spmd(nc, in_maps, core_ids=list(range(B)))
    return np.stack([np.asarray(r["out"], dtype=np.float32) for r in res.results], axis=0)
```

```python
import os
import numpy as np
import concourse.bass as bass
import concourse.mybir as mybir
from concourse.bass_utils import run_bass_kernel_spmd

F32 = mybir.dt.float32
BF16 = mybir.dt.bfloat16
AF = mybir.ActivationFunctionType
ALU = mybir.AluOpType
AX = mybir.AxisListType

D = 1024
DFF = 2816
NFF = DFF // 128
EPS = 1e-6
CTX = 256


class Tk:
    __slots__ = ("w", "r")

    def __init__(self):
        self.w = None
        self.r = {}


class Buf:
    def __init__(self, t, k=None, excl=False):
        self.t = t
        self.k = k if k is not None else Tk()
        self.excl = excl

    def __getitem__(self, key):
        return self.t[key]


class GBuf:
    def __init__(self, buf, ngroups):
        self.t = buf.t
        self.g = [Buf(buf.t) for _ in range(ngroups)]

    def __getitem__(self, key):
        return self.t[key]


class Prog:
    def __init__(self, nc):
        self.nc = nc
        self.engs = {}
        for name, obj in [("pe", nc.tensor), ("act", nc.scalar), ("dve", nc.vector),
                          ("pool", nc.gpsimd), ("sp", nc.sync)]:
            self.engs[name] = dict(name=name, obj=obj, sem=nc.alloc_semaphore("s_" + name), cnt=0,
                                   waited={}, dsems=None)
        for name, n in [("sp", 12), ("pool", 8), ("act", 4)]:
            e = self.engs[name]
            e["dsems"] = [nc.alloc_semaphore(f"d_{name}{i}") for i in range(n)]
            e["dvals"] = [0] * n
            e["rr"] = 0
        self.nops = 0

    def _wait(self, e, tok):
        sem, val = tok
        key = sem.num
        if e["waited"].get(key, 0) >= val:
            return
        if e["name"] == "pe" and sem is e["sem"]:
            return
        e["obj"].wait_ge(sem, val)
        e["waited"][key] = val
        self.nops += 1

    def _deps(self, e, r, w):
        for b in r:
            k = b.k
            if k.w is not None:
                self._wait(e, k.w)
            if b.excl:
                for tok in k.r.values():
                    self._wait(e, tok)
        for b in w:
            k = b.k
            if k.w is not None:
                self._wait(e, k.w)
            for tok in k.r.values():
                self._wait(e, tok)

    def _update(self, tok, r, w):
        sem, val = tok
        for b in r:
            b.k.r[sem.num] = tok
        for b in w:
            b.k.w = tok
            b.k.r = {}

    def op(self, eng, fn, r=(), w=()):
        e = self.engs[eng]
        self._deps(e, r, w)
        ins = fn(e["obj"])
        e["cnt"] += 1
        ins.then_inc(e["sem"], 1)
        tok = (e["sem"], e["cnt"])
        self._update(tok, r, w)
        self.nops += 1
        return tok

    def dma(self, eng, out, in_, r=(), w=(), **kw):
        e = self.engs[eng]
        self._deps(e, r, w)
        i = e["rr"]
        e["rr"] = (i + 1) % len(e["dsems"])
        sem = e["dsems"][i]
        if e["dvals"][i] > 0:
            self._wait(e, (sem, e["dvals"][i]))
        ins = e["obj"].dma_start(out=out, in_=in_, **kw)
        e["dvals"][i] += 16
        ins.then_inc(sem, 16)
        tok = (sem, e["dvals"][i])
        self._update(tok, r, w)
        self.nops += 1
        return tok

    def barrier(self):
        toks = []
        for e in self.engs.values():
            if e["cnt"] > 0:
                toks.append((e["sem"], e["cnt"]))
            if e["dsems"]:
                for s, v in zip(e["dsems"], e["dvals"]):
                    if v > 0:
                        toks.append((s, v))
        for e in self.engs.values():
            for t in toks:
                if t[0] is e["sem"]:
                    continue
                self._wait(e, t)

    def finish(self):
        self.barrier()


class Ctx:
    def __init__(self, nc):
        self.nc = nc
        self.P = Prog(nc)
        self.uid = 0
        self.stack = []

    def sb(self, shape, dt, name=None):
        self.uid += 1
        cm = self.nc.sbuf_tensor(f"{name or 'sb'}_{self.uid}", list(shape), dt)
        t = cm.__enter__()
        self.stack[-1].append(cm)
        return Buf(t)

    def ps(self, shape, dt, name=None):
        self.uid += 1
        cm = self.nc.psum_tensor(f"{name or 'ps'}_{self.uid}", list(shape), dt)
        t = cm.__enter__()
        self.stack[-1].append(cm)
        return Buf(t, excl=True)

    def push(self):
        self.stack.append([])

    def pop(self):
        self.P.barrier()
        for cm in reversed(self.stack.pop()):
            cm.__exit__(None, None, None)

    def dram(self, name, shape, dt):
        return self.nc.dram_tensor(name, list(shape), dt, kind="Internal").ap()


def stage_ada(C, cond_d, ada_w_d, ada_b_d, mods_d):
    P = C.P
    C.push()
    cond = C.sb([128, 8, 2], F32)
    scond = C.sb([128, 8, 2], F32)
    wb = [C.sb([128, 8, 512], F32) for _ in range(3)]
    bb = [C.sb([2, 512], F32) for _ in range(3)]
    mrow = [C.sb([2, 9 * D], F32) for _ in range(2)]
    pss = [C.ps([2, 512], F32) for _ in range(2)]
    P.dma("sp", cond[:], cond_d, w=[cond])
    P.op("act", lambda e: e.activation(out=scond[:], in_=cond[:], func=AF.Silu), r=[cond], w=[scond])
    it = 0
    for l in range(2):
        wv = ada_w_d[l].rearrange("(k p) n -> p k n", p=128)
        for n in range(18):
            w = wb[it % 3]
            b = bb[it % 3]
            ps = pss[it % 2]
            P.dma("sp", w[:], wv[:, :, n * 512:(n + 1) * 512], w=[w])
            P.dma("sp", b[:], ada_b_d[l, n * 512:(n + 1) * 512].partition_broadcast(2), w=[b])
            for kc in range(8):
                P.op("pe", lambda e, kc=kc, w=w, ps=ps: e.matmul(ps[:], lhsT=scond[:, kc, :], rhs=w[:, kc, :],
                                                              start=(kc == 0), stop=(kc == 7)),
                     r=[scond, w], w=[ps])
            P.op("dve", lambda e, ps=ps, b=b, l=l, n=n: e.tensor_tensor(out=mrow[l][:, n * 512:(n + 1) * 512],
                                                                      in0=ps[:], in1=b[:], op=ALU.add),
                 r=[ps, b], w=[mrow[l]])
            it += 1
        P.dma("sp", mods_d[l], mrow[l][:], r=[mrow[l]])
    C.pop()


def load_bc(C, dst, src_row, eng="sp"):
    C.P.dma(eng, dst[:], src_row.partition_broadcast(128), w=[dst])


def hT_elem(C, xs, sc1, sh, tmp, hb, stat, epsb):
    P = C.P
    junk, ssq, rt, rstd = stat
    P.op("act", lambda e: e.activation(out=junk[:], in_=xs[:], func=AF.Square, accum_out=ssq[:]),
         r=[xs], w=[junk, ssq])
    P.op("act", lambda e: e.activation(out=rt[:], in_=ssq[:], func=AF.Sqrt, bias=epsb[:], scale=1.0 / D),
         r=[ssq, epsb], w=[rt])
    P.op("dve", lambda e: e.reciprocal(out=rstd[:], in_=rt[:]), r=[rt], w=[rstd])
    P.op("dve", lambda e: e.scalar_tensor_tensor(out=tmp[:], in0=xs[:], scalar=rstd[:, 0:1],
                                                 in1=sc1[:], op0=ALU.mult, op1=ALU.mult),
         r=[xs, rstd, sc1], w=[tmp])
    P.op("pool", lambda e: e.tensor_tensor(out=hb[:], in0=tmp[:], in1=sh[:], op=ALU.add),
         r=[tmp, sh], w=[hb])


def hT_tr(C, sub, hT, hb, psT, ident):
    P = C.P
    for kc in range(8):
        P.op("pe", lambda e, kc=kc: e.transpose(out=psT[:, kc * 128:(kc + 1) * 128],
                                                in_=hb[:, kc * 128:(kc + 1) * 128], identity=ident[:]),
             r=[hb, ident], w=[psT])
    P.op("act", lambda e: e.copy(out=hT[:, :, sub * 128:(sub + 1) * 128],
                                 in_=psT[:].rearrange("p (k t) -> p k t", k=8)),
         r=[psT], w=[hT])


def make_hT_sub(C, xs, sub, sc1, sh, hT, tmp, hb, psT, ident, stat, epsb):
    hT_elem(C, xs, sc1, sh, tmp, hb, stat, epsb)
    hT_tr(C, sub, hT, hb, psT, ident)


def stage_ffn(C, segs, w1_d, w3_d, w2_d, mods_d, l, mbase, consts):
    P = C.P
    C.push()
    W1 = C.sb([128, 8, DFF], BF16, "W1")
    W3 = C.sb([128, 8, DFF], BF16, "W3")
    W2 = C.sb([128, NFF, D], BF16, "W2")
    P.dma("pool", W1[:], w1_d.rearrange("(k p) n -> p k n", p=128), w=[W1])
    P.dma("pool", W3[:], w3_d.rearrange("(k p) n -> p k n", p=128), w=[W3])
    P.dma("pool", W2[:], w2_d.rearrange("(k p) n -> p k n", p=128), w=[W2])
    ident, epsb = consts
    sc1 = C.sb([128, D], F32)
    sh = C.sb([128, D], F32)
    gt = C.sb([128, D], F32)
    NXB = 3
    xbs = [C.sb([128, D], F32, "xs") for _ in range(NXB)]
    hT = C.sb([128, 8, 512], BF16, "hT")
    gT = C.sb([128, NFF, 512], BF16, "gT")
    tmp = C.sb([128, D], F32)
    hb = C.sb([128, D], BF16)
    junk = C.sb([128, D], BF16)
    ssq = C.sb([128, 1], F32)
    rt = C.sb([128, 1], F32)
    rstd = C.sb([128, 1], F32)
    sil = [C.sb([128, 512], BF16, "sil") for _ in range(2)]
    psT = C.ps([128, D], BF16, "psT")
    ps1 = [C.ps([128, 512], F32, "ps1") for _ in range(2)]
    ps3 = [C.ps([128, 512], F32, "ps3") for _ in range(2)]
    pso = [C.ps([128, 512], F32, "pso") for _ in range(2)]
    stat = (junk, ssq, rt, rstd)
    cur_which = None
    tiles = []
    for (xd, ntok, which) in segs:
        t0 = 0
        while t0 < ntok:
            n = min(512, ntok - t0)
            tiles.append((xd, t0, n, which))
            t0 += n
    xk = dict(i=0)

    def get_x(xd, r0):
        xb = xbs[xk["i"] % NXB]
        xk["i"] += 1
        P.dma("sp", xb[:], xd[r0:r0 + 128, :], w=[xb])
        return xb

    hTs = [hT, C.sb([128, 8, 512], BF16, "hT2")]
    ELEM_AT = {2: 0, 7: 1, 12: 2, 17: 3}
    TR_AT = {5: 0, 10: 1, 15: 2, 20: 3}
    it = 0
    oi = 0
    prefetched = False
    for ti, (xd, t0, n, which) in enumerate(tiles):
        nsub = n // 128
        hTc = hTs[ti % 2]
        if which != cur_which:
            cur_which = which
            load_bc(C, sh, mods_d[l, which, (mbase + 0) * D:(mbase + 1) * D], "act")
            load_bc(C, sc1, mods_d[l, which, (mbase + 1) * D:(mbase + 2) * D], "act")
            load_bc(C, gt, mods_d[l, which, (mbase + 2) * D:(mbase + 3) * D], "act")
            P.op("dve", lambda e: e.tensor_scalar_add(out=sc1[:], in0=sc1[:], scalar1=1.0), r=[sc1], w=[sc1])
            P.op("dve", lambda e: e.tensor_scalar_mul(out=gt[:], in0=gt[:], scalar1=0.5), r=[gt], w=[gt])
        if not prefetched:
            for sub in range(nsub):
                xs = get_x(xd, t0 + sub * 128)
                make_hT_sub(C, xs, sub, sc1, sh, hTc, tmp, hb, psT, ident, stat, epsb)
        nxt_t = tiles[ti + 1] if ti + 1 < len(tiles) else None
        do_pf = nxt_t is not None and nxt_t[3] == which
        for f in range(NFF):
            p1 = ps1[it % 2]
            p3 = ps3[it % 2]
            sl = sil[it % 2]
            it += 1
            for kc in range(8):
                P.op("pe", lambda e, kc=kc, f=f, p1=p1: e.matmul(p1[:, 0:n], lhsT=W1[:, kc, f * 128:(f + 1) * 128],
                                                                rhs=hTc[:, kc, 0:n], start=(kc == 0), stop=(kc == 7)),
                     r=[W1, hTc], w=[p1])
            for kc in range(8):
                P.op("pe", lambda e, kc=kc, f=f, p3=p3: e.matmul(p3[:, 0:n], lhsT=W3[:, kc, f * 128:(f + 1) * 128],
                                                                rhs=hTc[:, kc, 0:n], start=(kc == 0), stop=(kc == 7)),
                     r=[W3, hTc], w=[p3])
            P.op("act", lambda e, p1=p1, sl=sl: e.activation(out=sl[:, 0:n], in_=p1[:, 0:n], func=AF.Silu),
                 r=[p1], w=[sl])
            P.op("dve", lambda e, p3=p3, sl=sl, f=f: e.tensor_tensor(out=gT[:, f, 0:n], in0=sl[:, 0:n], in1=p3[:, 0:n],
                                                                    op=ALU.mult), r=[sl, p3], w=[gT])
            if do_pf:
                nxd, nt0, nn, _ = nxt_t
                if f in ELEM_AT and ELEM_AT[f] < nn // 128:
                    xs = get_x(nxd, nt0 + ELEM_AT[f] * 128)
                    hT_elem(C, xs, sc1, sh, tmp, hb, stat, epsb)
                if f in TR_AT and TR_AT[f] < nn // 128:
                    hT_tr(C, TR_AT[f], hTs[(ti + 1) % 2], hb, psT, ident)
        prefetched = do_pf
        for sub in range(nsub):
            xs = get_x(xd, t0 + sub * 128)
            for dh in range(2):
                po = pso[oi % 2]
                oi += 1
                for f in range(NFF):
                    P.op("pe", lambda e, f=f, sub=sub, dh=dh, po=po: e.matmul(
                        po[:], lhsT=gT[:, f, sub * 128:(sub + 1) * 128], rhs=W2[:, f, dh * 512:(dh + 1) * 512],
                        start=(f == 0), stop=(f == NFF - 1)), r=[gT, W2], w=[po])
                P.op("dve", lambda e, po=po, dh=dh: e.tensor_tensor(out=tmp[:, dh * 512:(dh + 1) * 512], in0=po[:],
                                                                   in1=gt[:, dh * 512:(dh + 1) * 512],
                                                                   op=ALU.mult), r=[po, gt], w=[tmp])
                P.op("pool", lambda e, dh=dh, xs=xs: e.tensor_tensor(
                    out=xs[:, dh * 512:(dh + 1) * 512], in0=tmp[:, dh * 512:(dh + 1) * 512],
                    in1=xs[:, dh * 512:(dh + 1) * 512], op=ALU.add), r=[tmp, xs], w=[xs])
            P.dma("act", xd[t0 + sub * 128:t0 + (sub + 1) * 128, :], xs[:], r=[xs])
    C.pop()


ATT_SCALE = 96 ** -0.5
NH = 8


def rms_bc(C, ss_ps, n_feat, rows, n, rt, rstd, epsb):
    P = C.P
    P.op("act", lambda e: e.activation(out=rt[0:rows, 0:n], in_=ss_ps[0:rows, 0:n], func=AF.Ln,
                                       bias=epsb[0:rows, :], scale=1.0 / n_feat), r=[ss_ps, epsb], w=[rt])
    P.op("act", lambda e: e.activation(out=rstd[0:rows, 0:n], in_=rt[0:rows, 0:n], func=AF.Exp, scale=-0.5),
         r=[rt], w=[rstd])


def stage_mla_proj(C, segs, S, wd, mods_d, l, consts, scr):
    P = C.P
    C.push()
    ident, epsb = consts
    w_in_d, w_qb_d, w_kvb_d, mlac_d, cos_d, sin_d, prot_d, ones_d = wd
    qT_d, qcT_d, kT_d, v_d, glu_l_d, glu_c_d = scr
    Win = C.sb([128, 8, 1568], BF16, "Win")
    Wqb = C.sb([128, 3, 768], BF16, "Wqb")
    Wkvb = C.sb([128, 1024], BF16, "Wkvb")
    P.dma("pool", Win[:], w_in_d.rearrange("(k p) n -> p k n", p=128), w=[Win])
    P.dma("pool", Wqb[:], w_qb_d.rearrange("(k p) n -> p k n", p=128), w=[Wqb])
    P.dma("pool", Wkvb[:], w_kvb_d, w=[Wkvb])
    mlac = C.sb([128, 144], F32, "mlac")
    P.dma("sp", mlac[:], mlac_d, w=[mlac])
    onesf = C.sb([128, 128], F32, "onesf")
    onesb = C.sb([128, 128], BF16, "onesb")
    prot = C.sb([32, 32], F32, "prot")
    P.dma("sp", onesf[:], ones_d, w=[onesf])
    P.dma("sp", prot[:], prot_d, w=[prot])
    P.op("dve", lambda e: e.tensor_copy(out=onesb[:], in_=onesf[:]), r=[onesf], w=[onesb])
    zero = C.sb([128, 4, 16], F32, "zero")
    P.op("dve", lambda e: e.memset(zero[:], 0.0), w=[zero])
    if "stop1" in os.environ.get("KDBG", ""):
        C.pop()
        return
    KDBG = os.environ.get("KDBG", "")
    for (gd, ntok) in ((glu_l_d, S), (glu_c_d, CTX)):
        if "nopad" in KDBG:
            break
        gv = gd.rearrange("(c p) t -> p c t", p=128)
        P.dma("sp", gv[:, :, 0:15], zero[:, :, 0:15], r=[zero])
        P.dma("sp", gv[:, :, 15 + ntok:30 + ntok], zero[:, :, 0:15], r=[zero])
    sc1 = C.sb([128, D], F32)
    sh = C.sb([128, D], F32)
    NXB = 3
    xbs = [C.sb([128, D], F32, "xs") for _ in range(NXB)]
    hT = C.sb([128, 8, 512], BF16, "hT")
    tmp = C.sb([128, D], F32)
    hb = C.sb([128, D], BF16)
    junk = C.sb([128, D], BF16)
    ssq = C.sb([128, 1], F32)
    rt1 = C.sb([128, 1], F32)
    rstd1 = C.sb([128, 1], F32)
    stat = (junk, ssq, rt1, rstd1)
    zq = C.sb([128, 3, 512], F32, "zq")
    sqb = [C.sb([128, 512], BF16, "sqb") for _ in range(2)]
    cqn = C.sb([128, 3, 512], BF16, "cqn")
    zkv = C.sb([128, 512], F32, "zkv")
    ckvn = C.sb([128, 512], BF16, "ckvn")
    kr_raw = C.sb([32, 512], F32, "kr_raw")
    sq_kr = C.sb([32, 512], BF16, "sq_kr")
    rts = [C.sb([128, 512], F32, "rt") for _ in range(3)]
    rstds = [C.sb([128, 512], F32, "rstd") for _ in range(3)]
    rr = dict(i=0)

    def nrs():
        rr["i"] += 1
        return rts[rr["i"] % 3], rstds[rr["i"] % 3]
    sig = [C.sb([128, 512], F32, "sig") for _ in range(2)]
    glu = [C.sb([128, 512], F32, "glu") for _ in range(2)]
    cos = C.sb([32, 512], F32, "cos")
    sin = C.sb([32, 512], F32, "sin")
    hn_o = [C.sb([64, 512], BF16, "hn_o") for _ in range(2)]
    hr = [C.sb([32, 512], F32, "hr") for _ in range(2)]
    t1 = [C.sb([32, 512], F32, "t1") for _ in range(2)]
    t2 = [C.sb([32, 512], F32, "t2") for _ in range(2)]
    hr_o = [C.sb([32, 512], BF16, "hr_o") for _ in range(2)]
    sqn = [C.sb([64, 512], BF16, "sqn") for _ in range(2)]
    sqr = [C.sb([32, 512], BF16, "sqr") for _ in range(2)]
    vt = [C.sb([128, 512], BF16, "vt") for _ in range(2)]
    psT = C.ps([128, D], BF16, "psT")
    psA = [C.ps([128, 512], F32, "psA") for _ in range(int(os.environ.get("NPSA", "3")))]
    psB = [C.ps([128, 512], F32, "psB") for _ in range(2)]
    psR = [C.ps([32, 512], F32, "psR") for _ in range(2)]
    cnt = dict(a=0, b=0, r=0, g=0, h=0, v=0, s=0)

    def nxt(lst, key):
        i = cnt[key]
        cnt[key] = i + 1
        return lst[i % len(lst)]

    cur_which = None
    for (xd, ntok, which, koff, is_lat) in segs:
        for t0 in range(0, ntok, 512):
            n = min(512, ntok - t0)
            nsub = n // 128
            if which != cur_which:
                cur_which = which
                load_bc(C, sh, mods_d[l, which, 3 * D:4 * D], "act")
                load_bc(C, sc1, mods_d[l, which, 4 * D:5 * D], "act")
                P.op("dve", lambda e: e.tensor_scalar_add(out=sc1[:], in0=sc1[:], scalar1=1.0), r=[sc1], w=[sc1])
            if is_lat:
                P.dma("sp", cos[:, 0:n], cos_d[:, t0:t0 + n], w=[cos])
                P.dma("sp", sin[:, 0:n], sin_d[:, t0:t0 + n], w=[sin])
            for sub in range(nsub):
                xs = xbs[cnt["s"] % NXB]
                cnt["s"] += 1
                P.dma("sp", xs[:], xd[t0 + sub * 128:t0 + (sub + 1) * 128, :], w=[xs])
                make_hT_sub(C, xs, sub, sc1, sh, hT, tmp, hb, psT, ident, stat, epsb)

            def proj(ps, col0, ncol):
                for kc in range(8):
                    P.op("pe", lambda e, kc=kc: e.matmul(ps[0:ncol, 0:n], lhsT=Win[:, kc, col0:col0 + ncol],
                                                         rhs=hT[:, kc, 0:n], start=(kc == 0), stop=(kc == 7)),
                         r=[Win, hT], w=[ps])

            if "stop2" in KDBG:
                continue
            pss = nxt(psB, "b")
            for c in range(3):
                ps = nxt(psA, "a")
                proj(ps, c * 128, 128)
                sq = nxt(sqb, "g")
                if "nosq" not in KDBG:
                    P.op("act", lambda e, ps=ps, sq=sq: e.activation(out=sq[:, 0:n], in_=ps[:, 0:n], func=AF.Square),
                         r=[ps], w=[sq])
                if "nocp" not in KDBG:
                    P.op("dve", lambda e, ps=ps, c=c: e.tensor_copy(out=zq[:, c, 0:n], in_=ps[:, 0:n]), r=[ps], w=[zq])
                if "cq1" in KDBG:
                    continue
                P.op("pe", lambda e, sq=sq, c=c: e.matmul(pss[:, 0:n], lhsT=onesb[:], rhs=sq[:, 0:n],
                                                         start=(c == 0), stop=(c == 2)), r=[onesb, sq], w=[pss])
            if "cq1" in KDBG or "cq2" in KDBG:
                continue
            rt, rstd = nrs()
            rms_bc(C, pss, 384, 128, n, rt, rstd, epsb)
            if "cq3" in KDBG:
                continue
            for c in range(3):
                P.op("dve", lambda e, c=c: e.scalar_tensor_tensor(out=cqn[:, c, 0:n], in0=zq[:, c, 0:n],
                                                                  scalar=mlac[:, c:c + 1], in1=rstd[:, 0:n],
                                                                  op0=ALU.mult, op1=ALU.mult),
                     r=[zq, mlac, rstd], w=[cqn])
            if "stop3" in KDBG:
                continue
            ps = nxt(psA, "a")
            proj(ps, 384, 128)
            sq = nxt(sqb, "g")
            P.op("act", lambda e, ps=ps, sq=sq: e.activation(out=sq[:, 0:n], in_=ps[:, 0:n], func=AF.Square),
                 r=[ps], w=[sq])
            P.op("dve", lambda e, ps=ps: e.tensor_copy(out=zkv[:, 0:n], in_=ps[:, 0:n]), r=[ps], w=[zkv])
            pss = nxt(psB, "b")
            P.op("pe", lambda e, sq=sq: e.matmul(pss[:, 0:n], lhsT=onesb[:], rhs=sq[:, 0:n], start=True, stop=True),
                 r=[onesb, sq], w=[pss])
            rt, rstd = nrs()
            rms_bc(C, pss, 128, 128, n, rt, rstd, epsb)
            P.op("dve", lambda e: e.scalar_tensor_tensor(out=ckvn[:, 0:n], in0=zkv[:, 0:n], scalar=mlac[:, 3:4],
                                                         in1=rstd[:, 0:n], op0=ALU.mult, op1=ALU.mult),
                 r=[zkv, mlac, rstd], w=[ckvn])
            if "stop4" in KDBG:
                continue
            ps = nxt(psA, "a")
            proj(ps, 512, 32)
            P.op("act", lambda e, ps=ps: e.activation(out=sq_kr[:, 0:n], in_=ps[0:32, 0:n], func=AF.Square),
                 r=[ps], w=[sq_kr])
            P.op("dve", lambda e, ps=ps: e.tensor_copy(out=kr_raw[:, 0:n], in_=ps[0:32, 0:n]), r=[ps], w=[kr_raw])
            gd = glu_l_d if is_lat else glu_c_d
            for c in range(4):
                if "noglu" in KDBG:
                    break
                pa = nxt(psA, "a")
                proj(pa, 544 + c * 128, 128)
                pg = nxt(psA, "a")
                proj(pg, 1056 + c * 128, 128)
                sg = nxt(sig, "h")
                gl = glu[cnt["h"] % 2]
                P.op("act", lambda e, pg=pg, sg=sg: e.activation(out=sg[:, 0:n], in_=pg[:, 0:n], func=AF.Sigmoid),
                     r=[pg], w=[sg])
                P.op("dve", lambda e, pa=pa, sg=sg, gl=gl: e.tensor_tensor(out=gl[:, 0:n], in0=sg[:, 0:n],
                                                                          in1=pa[:, 0:n], op=ALU.mult),
                     r=[sg, pa], w=[gl])
                P.dma("act", gd[c * 128:(c + 1) * 128, 15 + t0:15 + t0 + n], gl[:, 0:n], r=[gl])

            def head_side(ps_n, ps_r_or_raw, raw_is_sbuf, sq_r_shared, gcol_n, gcol_r, dst, dcol0, rope):
                sn = nxt(sqn, "v")
                P.op("act", lambda e: e.activation(out=sn[:, 0:n], in_=ps_n[0:64, 0:n], func=AF.Square),
                     r=[ps_n], w=[sn])
                if sq_r_shared is None:
                    sr = sqr[cnt["v"] % 2]
                    P.op("act", lambda e: e.activation(out=sr[:, 0:n], in_=ps_r_or_raw[0:32, 0:n], func=AF.Square),
                         r=[ps_r_or_raw], w=[sr])
                else:
                    sr = sq_r_shared
                pss = nxt(psB, "b")
                P.op("pe", lambda e: e.matmul(pss[0:64, 0:n], lhsT=onesb[0:64, 0:64], rhs=sn[:, 0:n],
                                              start=True, stop=False), r=[onesb, sn], w=[pss])
                P.op("pe", lambda e: e.matmul(pss[0:64, 0:n], lhsT=onesb[0:32, 0:64], rhs=sr[:, 0:n],
                                              start=False, stop=True), r=[onesb, sr], w=[pss])
                rt, rstd = nrs()
                rms_bc(C, pss, 96, 64, n, rt, rstd, epsb)
                ho = nxt(hn_o, "r")
                P.op("dve", lambda e: e.scalar_tensor_tensor(out=ho[:, 0:n], in0=ps_n[0:64, 0:n],
                                                             scalar=mlac[0:64, gcol_n:gcol_n + 1],
                                                             in1=rstd[0:64, 0:n], op0=ALU.mult, op1=ALU.mult),
                     r=[ps_n, mlac, rstd], w=[ho])
                P.dma("act", dst[0:64, dcol0:dcol0 + n], ho[:, 0:n], r=[ho])
                i = cnt["r"]
                h_r = hr[i % 2]
                ro = hr_o[i % 2]
                if rope:
                    P.op("dve", lambda e: e.scalar_tensor_tensor(out=h_r[:, 0:n], in0=ps_r_or_raw[0:32, 0:n],
                                                                 scalar=mlac[0:32, gcol_r:gcol_r + 1],
                                                                 in1=rstd[0:32, 0:n], op0=ALU.mult, op1=ALU.mult),
                         r=[ps_r_or_raw, mlac, rstd], w=[h_r])
                    pr = nxt(psR, "s")
                    P.op("pe", lambda e: e.matmul(pr[:, 0:n], lhsT=prot[:], rhs=h_r[:, 0:n], start=True, stop=True),
                         r=[prot, h_r], w=[pr])
                    a1 = t1[i % 2]
                    a2 = t2[i % 2]
                    P.op("pool", lambda e: e.tensor_tensor(out=a1[:, 0:n], in0=h_r[:, 0:n], in1=cos[:, 0:n],
                                                           op=ALU.mult), r=[h_r, cos], w=[a1])
                    P.op("dve", lambda e: e.tensor_tensor(out=a2[:, 0:n], in0=pr[:, 0:n], in1=sin[:, 0:n],
                                                          op=ALU.mult), r=[pr, sin], w=[a2])
                    P.op("pool", lambda e: e.tensor_tensor(out=ro[:, 0:n], in0=a1[:, 0:n], in1=a2[:, 0:n],
                                                           op=ALU.add), r=[a1, a2], w=[ro])
                else:
                    P.op("dve", lambda e: e.scalar_tensor_tensor(out=ro[:, 0:n], in0=ps_r_or_raw[0:32, 0:n],
                                                                 scalar=mlac[0:32, gcol_r:gcol_r + 1],
                                                                 in1=rstd[0:32, 0:n], op0=ALU.mult, op1=ALU.mult),
                         r=[ps_r_or_raw, mlac, rstd], w=[ro])
                P.dma("act", dst[64:96, dcol0:dcol0 + n], ro[:, 0:n], r=[ro])

            for h in range(NH):
                if "noheads" in KDBG:
                    break
                pqn = nxt(psA, "a")
                for c in range(3):
                    P.op("pe", lambda e, c=c: e.matmul(pqn[0:64, 0:n], lhsT=Wqb[:, c, h * 96:h * 96 + 64],
                                                       rhs=cqn[:, c, 0:n], start=(c == 0), stop=(c == 2)),
                         r=[Wqb, cqn], w=[pqn])
                pqr = nxt(psA, "a")
                for c in range(3):
                    P.op("pe", lambda e, c=c: e.matmul(pqr[0:32, 0:n], lhsT=Wqb[:, c, h * 96 + 64:h * 96 + 96],
                                                       rhs=cqn[:, c, 0:n], start=(c == 0), stop=(c == 2)),
                         r=[Wqb, cqn], w=[pqr])
                if is_lat:
                    head_side(pqn, pqr, False, None, 4, 5, qT_d[h], t0, True)
                else:
                    head_side(pqn, pqr, False, None, 4, 5, qcT_d[h], t0, False)
                pkn = nxt(psA, "a")
                P.op("pe", lambda e: e.matmul(pkn[0:64, 0:n], lhsT=Wkvb[:, h * 128:h * 128 + 64], rhs=ckvn[:, 0:n],
                                              start=True, stop=True), r=[Wkvb, ckvn], w=[pkn])
                head_side(pkn, kr_raw, True, sq_kr, 6, 7, kT_d[h], koff + t0, is_lat)
            for sub in range(nsub):
                if "nov" in KDBG:
                    break
                pv = nxt(psA, "a")
                P.op("pe", lambda e, sub=sub: e.matmul(
                    pv[:].rearrange("p (h e) -> p h e", e=64), lhsT=ckvn[:, sub * 128:(sub + 1) * 128],
                    rhs=Wkvb[:].rearrange("p (h e) -> p h e", e=128)[:, :, 64:128], start=True, stop=True),
                     r=[ckvn, Wkvb], w=[pv])
                vb = nxt(vt, "v")
                P.op("act", lambda e: e.copy(out=vb[:], in_=pv[:]), r=[pv], w=[vb])
                r0 = koff + t0 + sub * 128
                P.dma("act", v_d[r0:r0 + 128, :], vb[:], r=[vb])
    C.pop()


def stage_conv(C, segs, mlac_d, ones_d, scr, consts):
    P = C.P
    C.push()
    ident, epsb = consts
    mixT_d = scr
    mlac = C.sb([128, 144], F32, "mlac")
    onesf = C.sb([128, 128], F32, "onesf")
    P.dma("sp", mlac[:], mlac_d, w=[mlac])
    P.dma("sp", onesf[:], ones_d, w=[onesf])
    G = [C.sb([128, 4, 542], F32, "G") for _ in range(2)]
    Gb = [C.sb([128, 4, 542], BF16, "Gb") for _ in range(2)]
    dg = C.sb([128, 124, 128], BF16, "dg")
    identb = C.sb([128, 128], BF16, "identb")
    P.op("dve", lambda e: e.tensor_copy(out=identb[:], in_=ident[:]), r=[ident], w=[identb])
    for cj in range(124):
        P.op("dve" if cj % 2 == 0 else "pool",
             lambda e: e.tensor_scalar_mul(out=dg[:, cj, :], in0=identb[:], scalar1=mlac[:, 8 + cj:9 + cj]),
             r=[identb, mlac], w=[dg])
    psc = [C.ps([128, 512], F32, "psc") for _ in range(4)]
    acc = [C.sb([128, 512], F32, "acc") for _ in range(4)]
    sq = [C.sb([128, 512], F32, "sq") for _ in range(2)]
    mean = C.sb([128, 512], F32, "mean")
    m2 = C.sb([128, 512], F32, "m2")
    var = C.sb([128, 512], F32, "var")
    rt = C.sb([128, 512], F32, "rt")
    rstd = C.sb([128, 512], F32, "rstd")
    tt = [C.sb([128, 512], F32, "tt") for _ in range(2)]
    ob = [C.sb([128, 512], BF16, "ob") for _ in range(2)]
    ps1 = C.ps([128, 512], F32, "ps1")
    ps2 = C.ps([128, 512], F32, "ps2")
    W0 = 8
    it = 0
    for (gd, ntok, koff) in segs:
        gv = gd.rearrange("(c p) t -> p c t", p=128)
        for t0 in range(0, ntok, 512):
            n = min(512, ntok - t0)
            g = G[it % 2]
            it += 1
            P.dma("sp", g[:, :, 0:n + 30], gv[:, :, t0:t0 + n + 30], w=[g])
            gb = Gb[it % 2]
            P.op("act", lambda e: e.copy(out=gb[:, :, 0:n + 30], in_=g[:, :, 0:n + 30]), r=[g], w=[gb])
            for c in range(4):
                a = acc[c]
                pc = psc[c]
                for j in range(31):
                    P.op("pe", lambda e: e.matmul(pc[:, 0:n], lhsT=dg[:, c * 31 + j, :], rhs=gb[:, c, j:j + n],
                                                  start=(j == 0), stop=(j == 30)), r=[dg, gb], w=[pc])
                P.op("act", lambda e: e.activation(out=a[:, 0:n], in_=pc[:, 0:n], func=AF.Identity,
                                                   bias=mlac[:, 132 + c:133 + c], scale=1.0), r=[pc, mlac], w=[a])
            for c in range(4):
                a = acc[c]
                s_ = sq[c % 2]
                P.op("act", lambda e, a=a, s_=s_: e.activation(out=s_[:, 0:n], in_=a[:, 0:n], func=AF.Square),
                     r=[a], w=[s_])
                P.op("pe", lambda e, a=a, c=c: e.matmul(ps1[:, 0:n], lhsT=onesf[:], rhs=a[:, 0:n],
                                                       start=(c == 0), stop=(c == 3)), r=[onesf, a], w=[ps1])
                P.op("pe", lambda e, s_=s_, c=c: e.matmul(ps2[:, 0:n], lhsT=onesf[:], rhs=s_[:, 0:n],
                                                         start=(c == 0), stop=(c == 3)), r=[onesf, s_], w=[ps2])
            P.op("act", lambda e: e.activation(out=mean[:, 0:n], in_=ps1[:, 0:n], func=AF.Copy, scale=1.0 / 512),
                 r=[ps1], w=[mean])
            P.op("dve", lambda e: e.tensor_tensor(out=m2[:, 0:n], in0=mean[:, 0:n], in1=mean[:, 0:n], op=ALU.mult),
                 r=[mean], w=[m2])
            P.op("dve", lambda e: e.scalar_tensor_tensor(out=var[:, 0:n], in0=ps2[:, 0:n], scalar=1.0 / 512,
                                                         in1=m2[:, 0:n], op0=ALU.mult, op1=ALU.subtract),
                 r=[ps2, m2], w=[var])
            P.op("act", lambda e: e.activation(out=rt[:, 0:n], in_=var[:, 0:n], func=AF.Ln, bias=epsb[:],
                                               scale=1.0), r=[var, epsb], w=[rt])
            P.op("act", lambda e: e.activation(out=rstd[:, 0:n], in_=rt[:, 0:n], func=AF.Exp, scale=-0.5),
                 r=[rt], w=[rstd])
            for c in range(4):
                a = acc[c]
                t_ = tt[c % 2]
                o_ = ob[c % 2]
                P.op("dve", lambda e, a=a, t_=t_: e.tensor_tensor(out=t_[:, 0:n], in0=a[:, 0:n], in1=mean[:, 0:n],
                                                                 op=ALU.subtract), r=[a, mean], w=[t_])
                P.op("pool", lambda e, t_=t_: e.tensor_tensor(out=t_[:, 0:n], in0=t_[:, 0:n], in1=rstd[:, 0:n],
                                                             op=ALU.mult), r=[t_, rstd], w=[t_])
                P.op("act", lambda e, t_=t_, o_=o_, c=c: e.activation(out=o_[:, 0:n], in_=t_[:, 0:n], func=AF.Silu,
                                                                     bias=mlac[:, 140 + c:141 + c],
                                                                     scale=mlac[:, 136 + c:137 + c]),
                     r=[t_, mlac], w=[o_])
                P.dma("act", mixT_d[512 + c * 128:512 + (c + 1) * 128, koff + t0:koff + t0 + n], o_[:, 0:n], r=[o_])
    C.pop()


def stage_attn(C, S, scr, ones_d, do_ctx=True):
    P = C.P
    C.push()
    qT_d, qcT_d, kT_d, v_d, mixT_d = scr
    NK = CTX + S
    NKC = NK // 128
    onesf = C.sb([128, 128], F32, "onesf")
    P.dma("sp", onesf[:], ones_d, w=[onesf])
    kTs = [C.sb([96, NK], BF16, "kT") for _ in range(2)]
    Vs = [C.sb([128, NKC, 65], BF16, "V") for _ in range(2)]
    for V in Vs:
        P.op("dve", lambda e, V=V: e.memset(V[:, :, 64:65], 1.0), w=[V])
    qs = [C.sb([96, 512], BF16, "q") for _ in range(2)]
    pTs = [C.sb([128, 512], BF16, "pT") for _ in range(3)]
    oT = [C.sb([65, 512], F32, "oT") for _ in range(2)]
    rden = [C.sb([65, 512], F32, "rden") for _ in range(2)]
    att = [C.sb([64, 512], BF16, "att") for _ in range(2)]
    psS = [C.ps([128, 512], F32, "psS") for _ in range(3)]
    psO = [C.ps([65, 512], F32, "psO") for _ in range(2)]
    psB = [C.ps([64, 512], F32, "psB") for _ in range(2)]
    vv = v_d.rearrange("(kc p) (h e) -> p kc h e", p=128, e=64)
    si = 0
    qi = 0
    for h in range(NH):
        kT = kTs[h % 2]
        V = Vs[h % 2]
        P.dma("sp", kT[:], kT_d[h], w=[kT])
        P.dma("pool", V[:, :, 0:64], vv[:, :, h, :], w=[V])
        qtiles = [(qT_d, t0, 512, 0, NKC, CTX + t0) for t0 in range(0, S, 512)]
        if do_ctx:
            qtiles.append((qcT_d, 0, CTX, 0, CTX // 128, 0))
        for (qd, t0, n, kc0, kc1, ocol) in qtiles:
            q = qs[qi % 2]
            po = psO[qi % 2]
            o_ = oT[qi % 2]
            rd = rden[qi % 2]
            pb = psB[qi % 2]
            at = att[qi % 2]
            qi += 1
            P.dma("sp", q[:, 0:n], qd[h, :, t0:t0 + n], w=[q])
            prev = None
            for kc in range(kc0, kc1):
                ps = psS[si % 3]
                pT = pTs[si % 3]
                si += 1
                P.op("pe", lambda e: e.matmul(ps[:, 0:n], lhsT=kT[:, kc * 128:(kc + 1) * 128],
                                              rhs=q[:, 0:n], start=True, stop=True), r=[kT, q], w=[ps])
                P.op("act", lambda e: e.activation(out=pT[:, 0:n], in_=ps[:, 0:n], func=AF.Exp,
                                                   scale=ATT_SCALE), r=[ps], w=[pT])
                if prev is not None:
                    pk, ppT = prev
                    P.op("pe", lambda e: e.matmul(po[:, 0:n], lhsT=V[:, pk, :], rhs=ppT[:, 0:n],
                                                  start=(pk == kc0), stop=False), r=[V, ppT], w=[po])
                prev = (kc, pT)
            pk, ppT = prev
            P.op("pe", lambda e: e.matmul(po[:, 0:n], lhsT=V[:, pk, :], rhs=ppT[:, 0:n],
                                          start=(pk == kc0), stop=True), r=[V, ppT], w=[po])
            P.op("dve", lambda e: e.tensor_copy(out=o_[:, 0:n], in_=po[:, 0:n]), r=[po], w=[o_])
            P.op("act", lambda e: e.activation(out=rd[64:65, 0:n], in_=o_[64:65, 0:n], func=AF.Ln), r=[o_], w=[rd])
            P.op("act", lambda e: e.activation(out=rd[64:65, 0:n], in_=rd[64:65, 0:n], func=AF.Exp, scale=-1.0),
                 r=[rd], w=[rd])
            P.op("pe", lambda e: e.matmul(pb[:, 0:n], lhsT=onesf[64:65, 0:64], rhs=rd[64:65, 0:n],
                                          start=True, stop=True), r=[onesf, rd], w=[pb])
            P.op("dve", lambda e: e.tensor_tensor(out=at[:, 0:n], in0=o_[0:64, 0:n], in1=pb[:, 0:n], op=ALU.mult),
                 r=[o_, pb], w=[at])
            P.dma("pool", mixT_d[h * 64:(h + 1) * 64, ocol:ocol + n], at[:, 0:n], r=[at])
    C.pop()


def stage_outproj(C, segs, w_out_d, mixT_d, mods_d, l, row0=0):
    P = C.P
    C.push()
    Wout = C.sb([128, 8, D], BF16, "Wout")
    P.dma("pool", Wout[:], w_out_d.rearrange("(k p) n -> p k n", p=128), w=[Wout])
    gt = C.sb([128, D], F32)
    mixs = [C.sb([128, 8, 512], BF16, "mix") for _ in range(2)]
    xbs = [C.sb([128, D], F32, "xs") for _ in range(3)]
    tmp = C.sb([128, D], F32)
    pso = [C.ps([128, 512], F32, "pso") for _ in range(2)]
    mv = mixT_d.rearrange("(c p) t -> p c t", p=128)
    cur_which = None
    it = 0
    xi = 0
    oi = 0
    for (xd, ntok, which, koff) in segs:
        for t0 in range(0, ntok, 512):
            n = min(512, ntok - t0)
            if which != cur_which:
                cur_which = which
                load_bc(C, gt, mods_d[l, which, 5 * D:6 * D], "act")
            mx = mixs[it % 2]
            it += 1
            P.dma("sp", mx[:, :, 0:n], mv[:, :, koff + t0:koff + t0 + n], w=[mx])
            for sub in range(n // 128):
                xs = xbs[xi % 3]
                xi += 1
                r0 = t0 + sub * 128
                P.dma("sp", xs[:], xd[r0:r0 + 128, :], w=[xs])
                for dh in range(2):
                    po = pso[oi % 2]
                    oi += 1
                    for c in range(8):
                        P.op("pe", lambda e, c=c, po=po: e.matmul(po[:], lhsT=mx[:, c, sub * 128:(sub + 1) * 128],
                                                                 rhs=Wout[:, c, dh * 512:(dh + 1) * 512],
                                                                 start=(c == 0), stop=(c == 7)), r=[mx, Wout], w=[po])
                    P.op("dve", lambda e, po=po: e.tensor_tensor(out=tmp[:, dh * 512:(dh + 1) * 512], in0=po[:],
                                                                in1=gt[:, dh * 512:(dh + 1) * 512], op=ALU.mult),
                         r=[po, gt], w=[tmp])
                    P.op("pool", lambda e, xs=xs: e.tensor_tensor(out=xs[:, dh * 512:(dh + 1) * 512],
                                                                 in0=tmp[:, dh * 512:(dh + 1) * 512],
                                                                 in1=xs[:, dh * 512:(dh + 1) * 512], op=ALU.add),
                         r=[tmp, xs], w=[xs])
                P.dma("act", xd[r0:r0 + 128, :], xs[:], r=[xs])
    C.pop()


def make_consts(C, ident_d):
    P = C.P
    ident = C.sb([128, 128], BF16, "ident")
    identf = C.sb([128, 128], F32, "identf")
    epsb = C.sb([128, 1], F32, "epsb")
    P.dma("sp", identf[:], ident_d, w=[identf])
    P.op("dve", lambda e: e.tensor_copy(out=ident[:], in_=identf[:]), r=[identf], w=[ident])
    P.op("dve", lambda e: e.memset(epsb[:], EPS), w=[epsb])
    return ident, epsb


NHR = 16
GN_EPS = 64 * 1e-5
DEC_C = -float(np.exp(-0.5))


def bc3(ap2, n):
    return ap2.unsqueeze(2).to_broadcast([ap2.shape[0], ap2.shape[1], n])


def stage_rwkv_h(C, segs, mods_d, l, consts, hT_d):
    P = C.P
    C.push()
    ident, epsb = consts
    sc1 = C.sb([128, D], F32)
    sh = C.sb([128, D], F32)
    xbs = [C.sb([128, D], F32, "xs") for _ in range(3)]
    hTs = [C.sb([128, 8, 512], BF16, "hT") for _ in range(2)]
    tmp = C.sb([128, D], F32)
    hb = C.sb([128, D], BF16)
    junk = C.sb([128, D], BF16)
    ssq = C.sb([128, 1], F32)
    rt1 = C.sb([128, 1], F32)
    rstd1 = C.sb([128, 1], F32)
    stat = (junk, ssq, rt1, rstd1)
    zero = C.sb([128, 8, 1], BF16, "zero")
    P.op("dve", lambda e: e.memset(zero[:], 0.0), w=[zero])
    psT = C.ps([128, D], BF16, "psT")
    hv = hT_d.rearrange("(c p) t -> p c t", p=128)
    xi = 0
    ti = 0
    cur_which = None
    with C.nc.allow_non_contiguous_dma(reason="tiny zero pad columns"):
        for (xd, ntok, which, col0) in segs:
            P.dma("sp", hv[:, :, col0 - 1:col0], zero[:], r=[zero])
            P.dma("sp", hv[:, :, col0 + ntok:col0 + ntok + 1], zero[:], r=[zero])
    for (xd, ntok, which, col0) in segs:
        for t0 in range(0, ntok, 512):
            n = min(512, ntok - t0)
            if which != cur_which:
                cur_which = which
                load_bc(C, sh, mods_d[l, which, 3 * D:4 * D], "act")
                load_bc(C, sc1, mods_d[l, which, 4 * D:5 * D], "act")
                P.op("dve", lambda e: e.tensor_scalar_add(out=sc1[:], in0=sc1[:], scalar1=1.0), r=[sc1], w=[sc1])
            hT = hTs[ti % 2]
            ti += 1
            for sub in range(n // 128):
                xs = xbs[xi % 3]
                xi += 1
                P.dma("sp", xs[:], xd[t0 + sub * 128:t0 + (sub + 1) * 128, :], w=[xs])
                make_hT_sub(C, xs, sub, sc1, sh, hT, tmp, hb, psT, ident, stat, epsb)
            P.dma("act", hv[:, :, col0 + t0:col0 + t0 + n], hT[:, :, 0:n], r=[hT])
    C.pop()


def stage_rwkv_proj(C, segs, wd, hT_d, scr):
    P = C.P
    C.push()
    (w_r_d, w_k_d, w_v_d, w1_d, w2_d, a1_d, a2_d, g1_d, w0_d, rwv_d, rwc_d) = wd
    (rT_d, nkkT_d, bT_d, kdT_d, sig_d, v_d, sbon_d, sigG_d) = scr
    Wr = C.sb([128, 8, D], BF16, "Wr")
    Wk = C.sb([128, 8, D], BF16, "Wk")
    Wv = C.sb([128, 8, D], BF16, "Wv")
    for W, wdram in ((Wr, w_r_d), (Wk, w_k_d), (Wv, w_v_d)):
        P.dma("pool", W[:], wdram.rearrange("(k p) n -> p k n", p=128), w=[W])
    W1 = [C.sb([128, 8, 64], BF16, "W1") for _ in range(2)]
    A1 = [C.sb([128, 8, 64], BF16, "A1") for _ in range(2)]
    W2 = [C.sb([64, D], BF16, "W2") for _ in range(2)]
    A2 = [C.sb([64, D], BF16, "A2") for _ in range(2)]
    G1 = C.sb([128, 8, 128], BF16, "G1")
    w0bc = [C.sb([128, D], F32, "w0bc") for _ in range(2)]
    for d in range(2):
        P.dma("pool", W1[d][:], w1_d[d].rearrange("(k p) n -> p k n", p=128), w=[W1[d]])
        P.dma("pool", A1[d][:], a1_d[d].rearrange("(k p) n -> p k n", p=128), w=[A1[d]])
        P.dma("pool", W2[d][:], w2_d[d], w=[W2[d]])
        P.dma("pool", A2[d][:], a2_d[d], w=[A2[d]])
        load_bc(C, w0bc[d], w0_d[d], "sp")
    P.dma("pool", G1[:], g1_d.rearrange("(k p) n -> p k n", p=128), w=[G1])
    rwv = C.sb([128, 88], F32, "rwv")
    P.dma("sp", rwv[:], rwv_d, w=[rwv])
    XM, KK, KA, RK, A0 = 0, 48, 56, 64, 72
    omk = C.sb([128, 8], F32, "omk")
    rkh = C.sb([128, 8], F32, "rkh")
    P.op("dve", lambda e: e.tensor_scalar(out=omk[:], in0=rwv[:, KA:KA + 8], scalar1=-1.0, scalar2=1.0,
                                          op0=ALU.mult, op1=ALU.add), r=[rwv], w=[omk])
    P.op("dve", lambda e: e.tensor_scalar_mul(out=rkh[:], in0=rwv[:, RK:RK + 8], scalar1=0.5), r=[rwv], w=[rkh])
    blk = C.sb([128, 128], BF16, "blk")
    hsel = C.sb([128, 2], BF16, "hsel")
    blkf = C.sb([128, 130], F32, "blkf")
    P.dma("sp", blkf[:], rwc_d[:, 0:130], w=[blkf])
    P.op("dve", lambda e: e.tensor_copy(out=blk[:], in_=blkf[:, 0:128]), r=[blkf], w=[blk])
    P.op("dve", lambda e: e.tensor_copy(out=hsel[:], in_=blkf[:, 128:130]), r=[blkf], w=[hsel])
    hTh = [C.sb([128, 8, 514], BF16, "hTh") for _ in range(2)]
    tt = GBuf(C.sb([128, 8, 512], F32, "tt"), 8)
    xx = C.sb([128, 8, 512], F32, "xx")
    xj = [C.sb([128, 8, 512], BF16, "xj") for _ in range(2)]
    kT = GBuf(C.sb([128, 8, 512], F32, "kT"), 8)
    kkT = GBuf(C.sb([128, 8, 512], BF16, "kkT"), 8)
    rTs = GBuf(C.sb([128, 8, 512], BF16, "rTs"), 8)
    kdsum = tt
    ob = [C.sb([128, 512], BF16, "ob") for _ in range(3)]
    of = [C.sb([128, 512], F32, "of") for _ in range(3)]
    hid = [C.sb([128, 512], BF16, "hid") for _ in range(2)]
    sbo = [C.sb([128, 16], F32, "sbo") for _ in range(2)]
    psA = [C.ps([128, 512], F32, "psA") for _ in range(5)]
    psS = [C.ps([128, 512], F32, "psS") for _ in range(2)]
    cnt = dict(a=0, o=0, f=0, h=0, x=0, t=0, s=0)

    def nxt(lst, key):
        i = cnt[key]
        cnt[key] = i + 1
        return lst[i % len(lst)]

    hv = hT_d.rearrange("(c p) t -> p c t", p=128)
    for (ntok, col0, koff) in segs:
        for t0 in range(0, ntok, 512):
            n = min(512, ntok - t0)
            nsub = n // 128
            hh = nxt(hTh, "t")
            P.dma("sp", hh[:, :, 0:n + 2], hv[:, :, col0 + t0 - 1:col0 + t0 + n + 1], w=[hh])
            hc = hh[:, :, 1:n + 1]
            P.op("dve", lambda e: e.tensor_tensor(out=tt[:, :, 0:n], in0=hh[:, :, 0:n], in1=hh[:, :, 2:n + 2],
                                                  op=ALU.add), r=[hh], w=tt.g)
            P.op("dve", lambda e: e.scalar_tensor_tensor(out=xx[:, :, 0:n], in0=tt[:, :, 0:n], scalar=0.5, in1=hc,
                                                         op0=ALU.mult, op1=ALU.subtract), r=tt.g + [hh], w=[xx])

            def mix(j):
                x_ = nxt(xj, "x")
                P.op("dve", lambda e: e.tensor_tensor(out=tt[:, :, 0:n], in0=xx[:, :, 0:n],
                                                      in1=bc3(rwv[:, XM + j * 8:XM + j * 8 + 8], n), op=ALU.mult),
                     r=[xx, rwv], w=tt.g)
                P.op("pool", lambda e: e.tensor_tensor(out=x_[:, :, 0:n], in0=tt[:, :, 0:n], in1=hc, op=ALU.add),
                     r=tt.g + [hh], w=[x_])
                return x_

            def projT(W, x_, p, ncol=128, col0_=None):
                ps = nxt(psA, "a")
                c0 = p * 128 if col0_ is None else col0_
                for kc in range(8):
                    P.op("pe", lambda e, kc=kc: e.matmul(ps[0:ncol, 0:n], lhsT=W[:, kc, c0:c0 + ncol],
                                                         rhs=x_[:, kc, 0:n], start=(kc == 0), stop=(kc == 7)),
                         r=[W, x_], w=[ps])
                return ps

            tok0 = koff + t0
            x_ = mix(0)
            for p in range(8):
                ps = projT(Wr, x_, p)
                P.op("act", lambda e: e.copy(out=rTs[:, p, 0:n], in_=ps[:, 0:n]), r=[ps], w=[rTs.g[p]])
            P.dma("act", rT_d.rearrange("(c p) t -> p c t", p=128)[:, :, tok0:tok0 + n], rTs[:, :, 0:n], r=rTs.g)
            x_ = mix(2)
            for p in range(8):
                ps = projT(Wk, x_, p)
                P.op("act", lambda e: e.copy(out=kT[:, p, 0:n], in_=ps[:, 0:n]), r=[ps], w=[kT.g[p]])
                kr = nxt(of, "f")
                P.op("dve", lambda e: e.tensor_scalar_mul(out=kr[:, 0:n], in0=kT[:, p, 0:n],
                                                          scalar1=rwv[:, KK + p:KK + p + 1]), r=[kT.g[p], rwv], w=[kr])
                sq = nxt(ob, "o")
                P.op("act", lambda e: e.activation(out=sq[:, 0:n], in_=kr[:, 0:n], func=AF.Square), r=[kr], w=[sq])
                pss = nxt(psA, "a")
                P.op("pe", lambda e: e.matmul(pss[:, 0:n], lhsT=blk[:], rhs=sq[:, 0:n], start=True, stop=True),
                     r=[blk, sq], w=[pss])
                nr = nxt(of, "f")
                P.op("dve", lambda e: e.tensor_scalar_max(out=nr[:, 0:n], in0=pss[:, 0:n], scalar1=1e-24),
                     r=[pss], w=[nr])
                P.op("act", lambda e: e.activation(out=nr[:, 0:n], in_=nr[:, 0:n], func=AF.Ln), r=[nr], w=[nr])
                P.op("act", lambda e: e.activation(out=nr[:, 0:n], in_=nr[:, 0:n], func=AF.Exp, scale=-0.5),
                     r=[nr], w=[nr])
                P.op("dve", lambda e: e.tensor_tensor(out=kkT[:, p, 0:n], in0=kr[:, 0:n], in1=nr[:, 0:n], op=ALU.mult),
                     r=[kr, nr], w=[kkT.g[p]])
                nk = nxt(ob, "o")
                P.op("pool", lambda e: e.tensor_scalar_mul(out=nk[:, 0:n], in0=kkT[:, p, 0:n], scalar1=-1.0),
                     r=[kkT.g[p]], w=[nk])
                P.dma("act", nkkT_d[p * 128:(p + 1) * 128, tok0:tok0 + n], nk[:, 0:n], r=[nk])
            x_ = mix(3)
            for sub in range(nsub):
                for dh in range(2):
                    ps = nxt(psA, "a")
                    for kc in range(8):
                        P.op("pe", lambda e, kc=kc: e.matmul(ps[:], lhsT=x_[:, kc, sub * 128:(sub + 1) * 128],
                                                             rhs=Wv[:, kc, dh * 512:(dh + 1) * 512],
                                                             start=(kc == 0), stop=(kc == 7)), r=[x_, Wv], w=[ps])
                    vb = nxt(ob, "o")
                    P.op("act", lambda e: e.copy(out=vb[:], in_=ps[:]), r=[ps], w=[vb])
                    P.dma("act", v_d[tok0 + sub * 128:tok0 + (sub + 1) * 128, dh * 512:(dh + 1) * 512], vb[:], r=[vb])
            x_ = mix(1)
            for d in range(2):
                ps = projT(W1[d], x_, 0, 64, 0)
                hd = nxt(hid, "h")
                P.op("act", lambda e: e.activation(out=hd[0:64, 0:n], in_=ps[0:64, 0:n], func=AF.Tanh), r=[ps], w=[hd])
                for sub in range(nsub):
                    for dh in range(2):
                        ps2 = nxt(psA, "a")
                        P.op("pe", lambda e: e.matmul(ps2[:], lhsT=hd[0:64, sub * 128:(sub + 1) * 128],
                                                      rhs=W2[d][:, dh * 512:(dh + 1) * 512], start=True, stop=True),
                             r=[hd, W2[d]], w=[ps2])
                        o1 = nxt(of, "f")
                        P.op("dve", lambda e: e.tensor_tensor(out=o1[:], in0=ps2[:],
                                                              in1=w0bc[d][:, dh * 512:(dh + 1) * 512], op=ALU.add),
                             r=[ps2, w0bc[d]], w=[o1])
                        P.op("act", lambda e: e.activation(out=o1[:], in_=o1[:], func=AF.Sigmoid), r=[o1], w=[o1])
                        P.dma("act", sig_d[d, tok0 + sub * 128:tok0 + (sub + 1) * 128, dh * 512:(dh + 1) * 512],
                              o1[:], r=[o1])
            x_ = mix(5)
            ps = projT(G1, x_, 0, 128, 0)
            sg = nxt(ob, "o")
            P.op("act", lambda e: e.activation(out=sg[:, 0:n], in_=ps[:, 0:n], func=AF.Sigmoid), r=[ps], w=[sg])
            P.dma("act", sigG_d[:, tok0:tok0 + n], sg[:, 0:n], r=[sg])
            x_ = mix(4)
            for d in range(2):
                ps = projT(A1[d], x_, 0, 64, 0)
                hd = nxt(hid, "h")
                P.op("act", lambda e: e.copy(out=hd[0:64, 0:n], in_=ps[0:64, 0:n]), r=[ps], w=[hd])
                for p in range(8):
                    ps2 = nxt(psA, "a")
                    P.op("pe", lambda e: e.matmul(ps2[:, 0:n], lhsT=A2[d][:, p * 128:(p + 1) * 128], rhs=hd[0:64, 0:n],
                                                  start=True, stop=True), r=[A2[d], hd], w=[ps2])
                    av = nxt(of, "f")
                    P.op("act", lambda e: e.activation(out=av[:, 0:n], in_=ps2[:, 0:n], func=AF.Sigmoid,
                                                       bias=rwv[:, A0 + d * 8 + p:A0 + d * 8 + p + 1]),
                         r=[ps2, rwv], w=[av])
                    bb = nxt(ob, "o")
                    P.op("dve", lambda e: e.tensor_tensor(out=bb[:, 0:n], in0=kkT[:, p, 0:n], in1=av[:, 0:n],
                                                          op=ALU.mult), r=[kkT.g[p], av], w=[bb])
                    P.dma("act", bT_d[d, p * 128:(p + 1) * 128, tok0:tok0 + n], bb[:, 0:n], r=[bb])
                    P.op("dve", lambda e: e.tensor_scalar(out=av[:, 0:n], in0=av[:, 0:n],
                                                          scalar1=rwv[:, KA + p:KA + p + 1], scalar2=omk[:, p:p + 1],
                                                          op0=ALU.mult, op1=ALU.add), r=[av, rwv, omk], w=[av])
                    kd = nxt(ob, "o")
                    P.op("dve", lambda e: e.tensor_tensor(out=kd[:, 0:n], in0=kT[:, p, 0:n], in1=av[:, 0:n],
                                                          op=ALU.mult), r=[kT.g[p], av], w=[kd])
                    P.dma("act", kdT_d[d, p * 128:(p + 1) * 128, tok0:tok0 + n], kd[:, 0:n], r=[kd])
                    if d == 0:
                        P.op("pool", lambda e: e.tensor_tensor(out=kdsum[:, p, 0:n], in0=kT[:, p, 0:n], in1=av[:, 0:n],
                                                               op=ALU.mult), r=[kT.g[p], av], w=[kdsum.g[p]])
                    else:
                        P.op("dve", lambda e: e.tensor_tensor(out=av[:, 0:n], in0=kT[:, p, 0:n], in1=av[:, 0:n],
                                                              op=ALU.mult), r=[kT.g[p], av], w=[av])
                        P.op("dve", lambda e: e.tensor_tensor(out=kdsum[:, p, 0:n], in0=kdsum[:, p, 0:n],
                                                              in1=av[:, 0:n], op=ALU.add), r=[kdsum.g[p], av], w=[kdsum.g[p]])
            for p in range(8):
                P.op("dve", lambda e: e.scalar_tensor_tensor(out=kdsum[:, p, 0:n], in0=kdsum[:, p, 0:n],
                                                             scalar=rkh[:, p:p + 1], in1=rTs[:, p, 0:n],
                                                             op0=ALU.mult, op1=ALU.mult),
                     r=[kdsum.g[p], rkh, rTs.g[p]], w=[kdsum.g[p]])
            prod = nxt(xj, "x")
            P.op("act", lambda e: e.copy(out=prod[:, :, 0:n], in_=kdsum[:, :, 0:n]), r=kdsum.g, w=[prod])
            for sub in range(nsub):
                pb = nxt(psS, "s")
                for p in range(8):
                    P.op("pe", lambda e: e.matmul(pb[:, 2 * p:2 * p + 2], lhsT=prod[:, p, sub * 128:(sub + 1) * 128],
                                                  rhs=hsel[:], start=True, stop=True), r=[prod, hsel], w=[pb])
                so = sbo[sub % 2]
                P.op("dve", lambda e: e.tensor_copy(out=so[:], in_=pb[:, 0:16]), r=[pb], w=[so])
                P.dma("act", sbon_d[tok0 + sub * 128:tok0 + (sub + 1) * 128, :], so[:], r=[so])
    C.pop()


RWC_TRI = 130
RWC_LV = 256 + 1536
RWC_DIR = 256 + 1536 + 1024 + 12 * 512
RWC_IREP = 130 + 2 * RWC_DIR
RWC_N = RWC_IREP + 512


def stage_rwkv_scan(C, NK, scr, rwc_d, consts):
    P = C.P
    C.push()
    ident, epsb = consts
    (rT_d, nkkT_d, bT_d, kdT_d, sig_d, v_d, y_d) = scr
    NKC = NK // 128
    NCC = CTX // 128
    KD = os.environ.get("KDBG", "")
    irf = C.sb([128, 512], F32, "irf")
    irep = C.sb([128, 512], BF16, "irep")
    P.dma("sp", irf[:], rwc_d[:, RWC_IREP:RWC_IREP + 512], w=[irf])
    P.op("dve", lambda e: e.tensor_copy(out=irep[:], in_=irf[:]), r=[irf], w=[irep])
    triI = C.sb([128, 128], F32, "triI")
    triE = C.sb([128, 128], F32, "triE")
    mS = C.sb([128, 512], F32, "mS")
    mST = C.sb([128, 512], F32, "mST")
    mI = C.sb([128, 512], F32, "mI")
    mSb = C.sb([128, 512], BF16, "mSb")
    mIb = C.sb([128, 512], BF16, "mIb")
    lvm = C.sb([128, 14, 512], BF16, "lvm")
    lvf = C.sb([128, 2, 512], F32, "lvf")
    NLB = 2
    ld = [dict(r=C.sb([128, 8, 128], BF16, "l_r"), nk=C.sb([128, 8, 128], BF16, "l_nk"),
               b=C.sb([128, 8, 128], BF16, "l_b"), kd=C.sb([128, 8, 128], BF16, "l_kd"),
               sig=C.sb([128, D], F32, "l_sig"), v=C.sb([128, D], BF16, "l_v")) for _ in range(NLB)]
    eL = C.sb([128, D], F32, "eL")
    eLx = C.sb([128, D], F32, "eLx")
    enL = C.sb([128, D], F32, "enL")
    pre = []
    for i in range(2):
        d_ = dict(rt=C.sb([128, D], BF16, "rt"), at=C.sb([128, D], BF16, "at"), bt=C.sb([128, D], BF16, "bt"),
                  kt=C.sb([128, D], BF16, "kt"),
                  bpA=C.sb([128, 8, 128], BF16, "bpA"), bpB=C.sb([128, 8, 128], BF16, "bpB"),
                  kpA=C.sb([128, 8, 128], BF16, "kpA"), kpB=C.sb([128, 8, 128], BF16, "kpB"),
                  T=[GBuf(C.sb([128, 16, 128], BF16, "T"), 4) for _ in range(2)],
                  Tt=[GBuf(C.sb([128, 16, 128], BF16, "Tt"), 4) for _ in range(2)],
                  Mak=GBuf(C.sb([128, 16, 128], BF16, "Mak"), 4), Mbr=GBuf(C.sb([128, 16, 128], BF16, "Mbr"), 4),
                  Mkr=GBuf(C.sb([128, 16, 128], BF16, "Mkr"), 4), gam=C.sb([128, 8], F32, "gam"))
        for nm in ("bpA", "bpB", "kpA", "kpB"):
            P.op("pool", lambda e, b_=d_[nm]: e.memset(b_[:], 0.0), w=[d_[nm]])
        pre.append(d_)
    Pb = [GBuf(C.sb([128, 16, 128], BF16, "Pb"), 4) for _ in range(2)]
    Qb = [GBuf(C.sb([128, 16, 128], BF16, "Qb"), 4) for _ in range(2)]
    Sf = C.sb([128, 8, 64], F32, "Sf")
    Sb = C.sb([128, 8, 64], BF16, "Sb")
    Stmp = C.sb([128, 8, 64], F32, "Stmp")
    XT = GBuf(C.sb([128, 16, 64], BF16, "XT"), 2)
    UT = GBuf(C.sb([128, 16, 64], BF16, "UT"), 2)
    yt = [C.sb([128, D], F32, "yt") for _ in range(2)]
    pbF = [C.ps([128, 512], F32, "pbF") for _ in range(6)]
    pbT = [C.ps([128, D], BF16, "pbT") for _ in range(2)]
    cnt = dict(f=0, t=0, y=0)

    def nf():
        i = cnt["f"]
        cnt["f"] = i + 1
        return pbF[i % 6]

    def nt():
        i = cnt["t"]
        cnt["t"] = i + 1
        return pbT[i % 2]

    rv = rT_d.rearrange("(c p) t -> p c t", p=128)
    nkv = nkkT_d.rearrange("(c p) t -> p c t", p=128)

    for d in range(2):
        base = RWC_TRI + d * RWC_DIR
        P.dma("sp", triI[:], rwc_d[:, base:base + 128], w=[triI])
        P.dma("sp", triE[:], rwc_d[:, base + 128:base + 256], w=[triE])
        P.dma("sp", mS[:], rwc_d[:, base + 256:base + 768], w=[mS])
        P.dma("sp", mST[:], rwc_d[:, base + 768:base + 1280], w=[mST])
        P.dma("sp", mI[:], rwc_d[:, base + 1280:base + 1792], w=[mI])
        P.op("dve", lambda e: e.tensor_copy(out=mSb[:], in_=mS[:]), r=[mS], w=[mSb])
        P.op("dve", lambda e: e.tensor_copy(out=mIb[:], in_=mI[:]), r=[mI], w=[mIb])
        for q_ in range(7):
            o = base + RWC_LV + q_ * 1024
            P.dma("sp", lvf[:], rwc_d[:, o:o + 1024].rearrange("p (a b) -> p a b", a=2), w=[lvf])
            P.op("dve", lambda e: e.tensor_copy(out=lvm[:, 2 * q_:2 * q_ + 2, :], in_=lvf[:]), r=[lvf], w=[lvm])
        P.op("dve", lambda e: e.memset(Sf[:], 0.0), w=[Sf])
        P.op("dve", lambda e: e.memset(Sb[:], 0.0), w=[Sb])
        if d == 0:
            order = list(range(NKC))
        else:
            order = list(range(NCC - 1, -1, -1)) + list(range(NKC - 1, NCC - 1, -1))
        last = 127 if d == 0 else 0
        bv = bT_d[d].rearrange("(c p) t -> p c t", p=128)
        kv = kdT_d[d].rearrange("(c p) t -> p c t", p=128)

        def load(i):
            c = order[i]
            L = ld[i % NLB]
            t0 = c * 128
            P.dma("sp", L["r"][:], rv[:, :, t0:t0 + 128], w=[L["r"]])
            P.dma("sp", L["nk"][:], nkv[:, :, t0:t0 + 128], w=[L["nk"]])
            P.dma("sp", L["b"][:], bv[:, :, t0:t0 + 128], w=[L["b"]])
            P.dma("sp", L["kd"][:], kv[:, :, t0:t0 + 128], w=[L["kd"]])
            P.dma("sp", L["sig"][:], sig_d[d, t0:t0 + 128, :], w=[L["sig"]])
            P.dma("sp", L["v"][:], v_d[t0:t0 + 128, :], w=[L["v"]])

        def precompute(i):
            L = ld[i % NLB]
            R = pre[i % 2]
            bL = [nf(), nf()]
            for p in range(8):
                P.op("pe", lambda e: e.matmul(bL[p // 4][:, (p % 4) * 128:(p % 4 + 1) * 128],
                                              lhsT=L["sig"][:, p * 128:(p + 1) * 128], rhs=triI[:],
                                              start=True, stop=True), r=[L["sig"], triI], w=[bL[p // 4]])
            for hf in range(2):
                sl = slice(hf * 512, (hf + 1) * 512)
                P.op("act", lambda e: e.activation(out=eL[:, sl], in_=bL[hf][:], func=AF.Exp), r=[bL[hf]], w=[eL])
                P.op("act", lambda e: e.activation(out=enL[:, sl], in_=bL[hf][:], func=AF.Exp, scale=-1.0),
                     r=[bL[hf]], w=[enL])
            yield
            fl = lambda b_: b_[:].rearrange("p c t -> p (c t)")
            P.op("dve", lambda e: e.tensor_tensor(out=R["rt"][:], in0=fl(L["r"]), in1=eL[:], op=ALU.mult),
                 r=[L["r"], eL], w=[R["rt"]])
            at3 = R["at"][:].rearrange("p (c t) -> p c t", t=128)
            eL3 = eL[:].rearrange("p (c t) -> p c t", t=128)
            if d == 0:
                P.op("pool", lambda e: e.tensor_tensor(out=at3[:, :, 1:128], in0=L["nk"][:, :, 1:128],
                                                       in1=eL3[:, :, 0:127], op=ALU.mult), r=[L["nk"], eL], w=[R["at"]])
                P.op("pool", lambda e: e.tensor_copy(out=at3[:, :, 0:1], in_=L["nk"][:, :, 0:1]), r=[L["nk"]], w=[R["at"]])
            else:
                P.op("pool", lambda e: e.tensor_tensor(out=at3[:, :, 0:127], in0=L["nk"][:, :, 0:127],
                                                       in1=eL3[:, :, 1:128], op=ALU.mult), r=[L["nk"], eL], w=[R["at"]])
                P.op("pool", lambda e: e.tensor_copy(out=at3[:, :, 127:128], in_=L["nk"][:, :, 127:128]),
                     r=[L["nk"]], w=[R["at"]])
            P.op("dve", lambda e: e.tensor_tensor(out=R["bt"][:], in0=fl(L["b"]), in1=enL[:], op=ALU.mult),
                 r=[L["b"], enL], w=[R["bt"]])
            P.op("pool", lambda e: e.tensor_tensor(out=R["kt"][:], in0=fl(L["kd"]), in1=enL[:], op=ALU.mult),
                 r=[L["kd"], enL], w=[R["kt"]])
            P.op("dve", lambda e: e.tensor_copy(out=R["gam"][:],
                                                in_=eL[:].rearrange("p (c t) -> p c t", t=128)[:, :, last]),
                 r=[eL], w=[R["gam"]])
            yield
            for (src, dA, dB) in ((R["bt"], R["bpA"], R["bpB"]), (R["kt"], R["kpA"], R["kpB"])):
                pt = nt()
                for p in range(8):
                    P.op("pe", lambda e: e.transpose(out=pt[:, p * 128:(p + 1) * 128],
                                                     in_=src[:, p * 128:(p + 1) * 128], identity=ident[:]),
                         r=[src, ident], w=[pt])
                ptv = pt[:].rearrange("p (c k) -> p c k", k=128)
                P.op("act", lambda e: e.copy(out=dA[:, :, 0:64], in_=ptv[:, :, 0:64]), r=[pt], w=[dA])
                P.op("dve", lambda e: e.tensor_copy(out=dB[:, :, 64:128], in_=ptv[:, :, 64:128]), r=[pt], w=[dB])

            def hm(lhs, rhs, mask, dst, eng):
                for G in range(2):
                    pbs = (nf(), nf())
                    for j in range(8):
                        h = G * 8 + j
                        p, q = h // 2, h % 2
                        rows = slice(q * 64, q * 64 + 64)
                        pb = pbs[q]
                        P.op("pe", lambda e: e.matmul(pb[:, (j // 2) * 128:(j // 2 + 1) * 128],
                                                      lhsT=lhs[rows, p * 128:(p + 1) * 128],
                                                      rhs=rhs[rows, p * 128:(p + 1) * 128], start=True, stop=True),
                             r=[lhs, rhs], w=[pb])
                    for q in range(2):
                        dv = dst[:, G * 8 + q:G * 8 + 8:2, :]
                        m3 = mask[:].rearrange("p (h t) -> p h t", t=128)
                        dg = [dst.g[2 * G], dst.g[2 * G + 1]]
                        if eng == "dve":
                            P.op("dve", lambda e: e.tensor_tensor(
                                out=dv, in0=pbs[q][:].rearrange("p (h t) -> p h t", t=128), in1=m3, op=ALU.mult),
                                 r=[pbs[q], mask], w=dg)
                        else:
                            P.op("act", lambda e: e.copy(out=dv, in_=pbs[q][:].rearrange("p (h t) -> p h t", t=128)),
                                 r=[pbs[q]], w=dg)
                            P.op("pool", lambda e: e.tensor_tensor(out=dv, in0=dv, in1=m3, op=ALU.mult),
                                 r=dg + [mask], w=dg)

            yield
            hm(R["bt"], R["at"], mS, Pb[0], "dve")
            yield
            hm(R["at"], R["bt"], mST, Qb[0], "dve")
            yield
            hm(R["kt"], R["at"], mS, R["Mak"], "dve")
            yield
            hm(R["bt"], R["rt"], mIb, R["Mbr"], "actpool")
            yield
            hm(R["kt"], R["rt"], mIb, R["Mkr"], "actpool")
            yield
            T, Tt = R["T"][0], R["Tt"][0]
            g4 = lambda b_, g: b_[:, g * 4:(g + 1) * 4, :].rearrange("p h t -> p (h t)")
            for g in range(4):
                P.op("dve", lambda e: e.tensor_tensor(out=g4(T, g), in0=g4(Pb[0], g), in1=lvm[:, 0, :], op=ALU.mult),
                     r=[Pb[0].g[g], lvm], w=[T.g[g]])
                P.op("pool", lambda e: e.tensor_tensor(out=g4(T, g), in0=g4(T, g), in1=irep[:], op=ALU.add),
                     r=[T.g[g], irep], w=[T.g[g]])
                P.op("dve", lambda e: e.tensor_tensor(out=g4(Tt, g), in0=g4(Qb[0], g), in1=lvm[:, 1, :], op=ALU.mult),
                     r=[Qb[0].g[g], lvm], w=[Tt.g[g]])
                P.op("pool", lambda e: e.tensor_tensor(out=g4(Tt, g), in0=g4(Tt, g), in1=irep[:], op=ALU.add),
                     r=[Tt.g[g], irep], w=[Tt.g[g]])
            yield
            Wb = Pb[1]
            cur = 0
            for li in range(6):
                mk = lvm[:, 2 + 2 * li, :]
                T, Tt = R["T"][cur], R["Tt"][cur]
                Tn, Ttn = R["T"][1 - cur], R["Tt"][1 - cur]
                for g in range(4):
                    pw = nf()
                    for j in range(4):
                        h = g * 4 + j
                        P.op("pe", lambda e: e.matmul(pw[:, j * 128:(j + 1) * 128], lhsT=Qb[0][:, h, :], rhs=T[:, h, :],
                                                      start=True, stop=True), r=[Qb[0].g[g], T.g[g]], w=[pw])
                    if g < 3:
                        P.op("dve", lambda e: e.tensor_tensor(out=g4(Wb, g), in0=pw[:], in1=mk, op=ALU.mult),
                             r=[pw, lvm], w=[Wb.g[g]])
                    else:
                        P.op("act", lambda e: e.copy(out=g4(Wb, g), in_=pw[:]), r=[pw], w=[Wb.g[g]])
                        P.op("pool", lambda e: e.tensor_tensor(out=g4(Wb, g), in0=g4(Wb, g), in1=mk, op=ALU.mult),
                             r=[Wb.g[g], lvm], w=[Wb.g[g]])
                    if g % 2 == 1:
                        yield
                for g in range(4):
                    pm = nf()
                    for j in range(4):
                        h = g * 4 + j
                        P.op("pe", lambda e: e.matmul(pm[:, j * 128:(j + 1) * 128], lhsT=Tt[:, h, :], rhs=Wb[:, h, :],
                                                      start=True, stop=True), r=[Tt.g[g], Wb.g[g]], w=[pm])
                    P.op("dve", lambda e: e.tensor_tensor(out=g4(Tn, g), in0=pm[:], in1=g4(T, g), op=ALU.add),
                         r=[pm, T.g[g]], w=[Tn.g[g]])
                    if g % 2 == 1:
                        yield
                if li < 5:
                    for g in range(4):
                        ptt = nt()
                        for j in range(4):
                            h = g * 4 + j
                            P.op("pe", lambda e: e.transpose(out=ptt[:, j * 128:(j + 1) * 128], in_=Tn[:, h, :],
                                                             identity=ident[:]), r=[Tn.g[g], ident], w=[ptt])
                        P.op("act", lambda e: e.copy(out=g4(Ttn, g), in_=ptt[:, 0:512]), r=[ptt], w=[Ttn.g[g]])
                        if g % 2 == 1:
                            yield
                cur = 1 - cur
            R["Tf"] = R["T"][cur]

        def chain(i):
            c = order[i]
            L = ld[i % NLB]
            R = pre[i % 2]
            V = L["v"]
            T = R["Tf"]
            for g in range(2):
                pb = nf()
                for j in range(8):
                    h = g * 8 + j
                    p, q = h // 2, h % 2
                    rows = slice(q * 64, q * 64 + 64)
                    P.op("pe", lambda e: e.matmul(pb[:, j * 64:(j + 1) * 64], lhsT=R["at"][rows, p * 128:(p + 1) * 128],
                                                  rhs=Sb[rows, p, :], start=True, stop=False), r=[R["at"], Sb], w=[pb])
                    P.op("pe", lambda e: e.matmul(pb[:, j * 64:(j + 1) * 64], lhsT=R["Mak"][:, h, :],
                                                  rhs=V[:, h * 64:(h + 1) * 64], start=False, stop=True),
                         r=[R["Mak"].g[h // 4], V], w=[pb])
                P.op("act", lambda e: e.copy(out=XT[:, g * 8:(g + 1) * 8, :].rearrange("p h v -> p (h v)"), in_=pb[:]),
                     r=[pb], w=[XT.g[g]])
            yield
            for g in range(2):
                pb = nf()
                for j in range(8):
                    h = g * 8 + j
                    P.op("pe", lambda e: e.matmul(pb[:, j * 64:(j + 1) * 64], lhsT=T[:, h, :], rhs=XT[:, h, :],
                                                  start=True, stop=True), r=[T.g[h // 4], XT.g[g]], w=[pb])
                P.op("dve", lambda e: e.tensor_copy(out=UT[:, g * 8:(g + 1) * 8, :].rearrange("p h v -> p (h v)"),
                                                    in_=pb[:]), r=[pb], w=[UT.g[g]])
            yield
            y_ = yt[cnt["y"] % 2]
            cnt["y"] += 1
            for g in range(2):
                pb = nf()
                for j in range(8):
                    h = g * 8 + j
                    p, q = h // 2, h % 2
                    rows = slice(q * 64, q * 64 + 64)
                    P.op("pe", lambda e: e.matmul(pb[:, j * 64:(j + 1) * 64], lhsT=R["rt"][rows, p * 128:(p + 1) * 128],
                                                  rhs=Sb[rows, p, :], start=True, stop=False), r=[R["rt"], Sb], w=[pb])
                    P.op("pe", lambda e: e.matmul(pb[:, j * 64:(j + 1) * 64], lhsT=R["Mbr"][:, h, :], rhs=UT[:, h, :],
                                                  start=False, stop=False), r=[R["Mbr"].g[h // 4], UT.g[g]], w=[pb])
                    P.op("pe", lambda e: e.matmul(pb[:, j * 64:(j + 1) * 64], lhsT=R["Mkr"][:, h, :],
                                                  rhs=V[:, h * 64:(h + 1) * 64], start=False, stop=True),
                         r=[R["Mkr"].g[h // 4], V], w=[pb])
                P.op("act", lambda e: e.copy(out=y_[:, g * 512:(g + 1) * 512], in_=pb[:]), r=[pb], w=[y_])
            P.dma("act", y_d[d, c * 128:(c + 1) * 128, :], y_[:], r=[y_])
            yield
            pb = nf()
            for p in range(8):
                o_ = pb[:, p * 64:(p + 1) * 64]
                P.op("pe", lambda e: e.matmul(o_, lhsT=R["bpA"][:, p, :], rhs=UT[:, 2 * p, :], start=True, stop=False),
                     r=[R["bpA"]] + UT.g, w=[pb])
                P.op("pe", lambda e: e.matmul(o_, lhsT=R["bpB"][:, p, :], rhs=UT[:, 2 * p + 1, :], start=False,
                                              stop=False), r=[R["bpB"]] + UT.g, w=[pb])
                P.op("pe", lambda e: e.matmul(o_, lhsT=R["kpA"][:, p, :], rhs=V[:, (2 * p) * 64:(2 * p + 1) * 64],
                                              start=False, stop=False), r=[R["kpA"], V], w=[pb])
                P.op("pe", lambda e: e.matmul(o_, lhsT=R["kpB"][:, p, :], rhs=V[:, (2 * p + 1) * 64:(2 * p + 2) * 64],
                                              start=False, stop=True), r=[R["kpB"], V], w=[pb])
            P.op("dve", lambda e: e.tensor_tensor(out=Stmp[:].rearrange("p c v -> p (c v)"),
                                                  in0=pb[:], in1=Sf[:].rearrange("p c v -> p (c v)"), op=ALU.add),
                 r=[pb, Sf], w=[Stmp])
            P.op("dve", lambda e: e.tensor_tensor(out=Sf[:], in0=Stmp[:], in1=bc3(R["gam"][:], 64), op=ALU.mult),
                 r=[Stmp, R["gam"]], w=[Sf])
            P.op("act", lambda e: e.copy(out=Sb[:], in_=Sf[:]), r=[Sf], w=[Sb])
            yield

        def run2(gp, gc, ratio):
            done_p, done_c, k = gp is None, False, 0
            while not (done_p and done_c):
                if not done_p:
                    try:
                        next(gp)
                    except StopIteration:
                        done_p = True
                k += 1
                if not done_c and (done_p or k % ratio == 0):
                    try:
                        next(gc)
                    except StopIteration:
                        done_c = True

        n_it = len(order)
        load(0)
        if n_it > 1:
            load(1)
        for _ in precompute(0):
            pass
        for i in range(n_it):
            run2(precompute(i + 1) if i + 1 < n_it else None, chain(i), 6)
            if i + 2 < n_it:
                load(i + 2)
    C.pop()


def stage_rwkv_out(C, segs, wd, scr, mods_d, l, consts):
    P = C.P
    C.push()
    ident, epsb = consts
    (g2_d, w_o_d, ln_g_d, ln_b_d) = wd
    (y_d, v_d, sbon_d, sigG_d) = scr
    G2 = C.sb([128, D], BF16, "G2")
    Wo = C.sb([128, 8, D], BF16, "Wo")
    P.dma("pool", G2[:], g2_d, w=[G2])
    P.dma("pool", Wo[:], w_o_d.rearrange("(k p) n -> p k n", p=128), w=[Wo])
    lng = C.sb([128, D], F32, "lng")
    lnb = C.sb([128, D], F32, "lnb")
    gt = C.sb([128, D], F32, "gt")
    load_bc(C, lng, ln_g_d, "sp")
    load_bc(C, lnb, ln_b_d, "sp")
    gneps = C.sb([128, 1], F32, "gneps")
    P.op("dve", lambda e: e.memset(gneps[:], GN_EPS), w=[gneps])
    y0 = [C.sb([128, D], F32, "y0") for _ in range(2)]
    y1 = [C.sb([128, D], F32, "y1") for _ in range(2)]
    vb = [C.sb([128, D], BF16, "vb") for _ in range(2)]
    sb_ = [C.sb([128, 16], F32, "sb") for _ in range(2)]
    sg = [C.sb([128, 128], BF16, "sg") for _ in range(2)]
    xs_ = [C.sb([128, D], F32, "xs") for _ in range(2)]
    yc = C.sb([128, D], F32, "yc")
    sq = C.sb([128, D], F32, "sq")
    mean = C.sb([128, 16], F32, "mean")
    var = C.sb([128, 16], F32, "var")
    rstd = C.sb([128, 16], F32, "rstd")
    bon = C.sb([128, D], F32, "bon")
    zb = C.sb([128, D], BF16, "zb")
    zT = C.sb([128, 8, 128], BF16, "zT")
    tmp = C.sb([128, D], F32, "tmp")
    psG = [C.ps([128, 512], F32, "psG") for _ in range(2)]
    psO = [C.ps([128, 512], F32, "psO") for _ in range(2)]
    psT = C.ps([128, D], BF16, "psT")
    it = 0
    cur_which = None
    v3 = lambda b_: b_[:].rearrange("p (h e) -> p h e", e=64)
    for (xd, ntok, which, koff) in segs:
        if which != cur_which:
            cur_which = which
            load_bc(C, gt, mods_d[l, which, 5 * D:6 * D], "act")
        for r0 in range(0, ntok, 128):
            i = it % 2
            it += 1
            g0 = koff + r0
            P.dma("sp", y0[i][:], y_d[0, g0:g0 + 128, :], w=[y0[i]])
            P.dma("sp", y1[i][:], y_d[1, g0:g0 + 128, :], w=[y1[i]])
            P.dma("sp", vb[i][:], v_d[g0:g0 + 128, :], w=[vb[i]])
            P.dma("sp", sb_[i][:], sbon_d[g0:g0 + 128, :], w=[sb_[i]])
            P.dma("sp", sg[i][:], sigG_d[:, g0:g0 + 128], w=[sg[i]])
            P.dma("sp", xs_[i][:], xd[r0:r0 + 128, :], w=[xs_[i]])
            Y = y0[i]
            P.op("dve", lambda e: e.tensor_tensor(out=Y[:], in0=Y[:], in1=y1[i][:], op=ALU.add), r=[Y, y1[i]], w=[Y])
            P.op("dve", lambda e: e.tensor_reduce(out=mean[:], in_=v3(Y), axis=AX.X, op=ALU.add), r=[Y], w=[mean])
            P.op("dve", lambda e: e.tensor_scalar_mul(out=mean[:], in0=mean[:], scalar1=1.0 / 64), r=[mean], w=[mean])
            P.op("dve", lambda e: e.tensor_tensor(out=v3(yc), in0=v3(Y), in1=bc3(mean[:], 64), op=ALU.subtract),
                 r=[Y, mean], w=[yc])
            P.op("act", lambda e: e.activation(out=sq[:], in_=yc[:], func=AF.Square), r=[yc], w=[sq])
            P.op("dve", lambda e: e.tensor_reduce(out=var[:], in_=v3(sq), axis=AX.X, op=ALU.add), r=[sq], w=[var])
            P.op("act", lambda e: e.activation(out=var[:], in_=var[:], func=AF.Sqrt, bias=gneps[:], scale=1.0 / 64),
                 r=[var, gneps], w=[var])
            P.op("dve", lambda e: e.reciprocal(out=rstd[:], in_=var[:]), r=[var], w=[rstd])
            P.op("dve", lambda e: e.tensor_tensor(out=v3(yc), in0=v3(yc), in1=bc3(rstd[:], 64), op=ALU.mult),
                 r=[yc, rstd], w=[yc])
            P.op("pool", lambda e: e.tensor_tensor(out=yc[:], in0=yc[:], in1=lng[:], op=ALU.mult), r=[yc, lng], w=[yc])
            P.op("pool", lambda e: e.tensor_tensor(out=yc[:], in0=yc[:], in1=lnb[:], op=ALU.add), r=[yc, lnb], w=[yc])
            P.op("dve", lambda e: e.tensor_tensor(out=v3(bon), in0=v3(vb[i]), in1=bc3(sb_[i][:], 64), op=ALU.mult),
                 r=[vb[i], sb_[i]], w=[bon])
            P.op("pool", lambda e: e.tensor_tensor(out=yc[:], in0=yc[:], in1=bon[:], op=ALU.add), r=[yc, bon], w=[yc])
            for dh in range(2):
                pg = psG[dh]
                P.op("pe", lambda e: e.matmul(pg[:], lhsT=sg[i][:], rhs=G2[:, dh * 512:(dh + 1) * 512],
                                              start=True, stop=True), r=[sg[i], G2], w=[pg])
                P.op("dve", lambda e: e.tensor_tensor(out=zb[:, dh * 512:(dh + 1) * 512],
                                                      in0=pg[:], in1=yc[:, dh * 512:(dh + 1) * 512], op=ALU.mult),
                     r=[pg, yc], w=[zb])
            for kc in range(8):
                P.op("pe", lambda e: e.transpose(out=psT[:, kc * 128:(kc + 1) * 128], in_=zb[:, kc * 128:(kc + 1) * 128],
                                                 identity=ident[:]), r=[zb, ident], w=[psT])
            P.op("act", lambda e: e.copy(out=zT[:], in_=psT[:].rearrange("p (k t) -> p k t", k=8)), r=[psT], w=[zT])
            xs = xs_[i]
            for dh in range(2):
                po = psO[dh]
                for kc in range(8):
                    P.op("pe", lambda e: e.matmul(po[:], lhsT=zT[:, kc, :], rhs=Wo[:, kc, dh * 512:(dh + 1) * 512],
                                                  start=(kc == 0), stop=(kc == 7)), r=[zT, Wo], w=[po])
                P.op("dve", lambda e: e.tensor_tensor(out=tmp[:, dh * 512:(dh + 1) * 512], in0=po[:],
                                                      in1=gt[:, dh * 512:(dh + 1) * 512], op=ALU.mult),
                     r=[po, gt], w=[tmp])
                P.op("pool", lambda e: e.tensor_tensor(out=xs[:, dh * 512:(dh + 1) * 512],
                                                       in0=tmp[:, dh * 512:(dh + 1) * 512],
                                                       in1=xs[:, dh * 512:(dh + 1) * 512], op=ALU.add),
                     r=[tmp, xs], w=[xs])
            P.dma("act", xd[r0:r0 + 128, :], xs[:], r=[xs])
    C.pop()


def host_consts(S):
    pos = np.arange(S)
    row = (pos // 64).astype(np.float32)
    col = (pos % 64).astype(np.float32)
    inv = (10000.0 ** (-np.arange(8, dtype=np.float32) / 8)).astype(np.float32)
    ang_r = row[None, :] * inv[:, None]
    ang_c = col[None, :] * inv[:, None]
    cos = np.concatenate([np.cos(ang_r), np.cos(ang_r), np.cos(ang_c), np.cos(ang_c)], 0).astype(np.float32)
    sin = np.concatenate([-np.sin(ang_r), np.sin(ang_r), -np.sin(ang_c), np.sin(ang_c)], 0).astype(np.float32)
    prot = np.zeros((32, 32), np.float32)
    for i in range(32):
        j = i + 8 if (i % 16) < 8 else i - 8
        prot[j, i] = 1.0
    rwc = np.zeros((128, RWC_N), np.float32)
    rwc[0:64, 0:64] = 1.0
    rwc[64:128, 64:128] = 1.0
    rwc[0:64, 128] = 1.0
    rwc[64:128, 129] = 1.0
    ii = np.arange(128)
    for d in range(2):
        before = (ii[:, None] < ii[None, :]) if d == 0 else (ii[:, None] > ii[None, :])
        incl = before | np.eye(128, dtype=bool)
        base = RWC_TRI + d * RWC_DIR
        rwc[:, base:base + 128] = incl * DEC_C
        rwc[:, base + 128:base + 256] = before * DEC_C
        rwc[:, base + 256:base + 768] = np.tile(before.astype(np.float32), (1, 4))
        rwc[:, base + 768:base + 1280] = np.tile(before.T.astype(np.float32), (1, 4))
        rwc[:, base + 1280:base + 1792] = np.tile(incl.astype(np.float32), (1, 4))
        blkid = lambda m: (ii[:, None] // m) == (ii[None, :] // m)
        m1 = before & blkid(2)
        o = base + RWC_LV
        rwc[:, o:o + 512] = np.tile(m1.astype(np.float32), (1, 4))
        rwc[:, o + 512:o + 1024] = np.tile(m1.T.astype(np.float32), (1, 4))
        for li in range(6):
            m = 2 << li
            mm = before & blkid(2 * m) & ~blkid(m)
            o2 = o + 1024 + li * 1024
            rwc[:, o2:o2 + 512] = np.tile(mm.astype(np.float32), (1, 4))
            rwc[:, o2 + 512:o2 + 1024] = np.tile(mm.T.astype(np.float32), (1, 4))
    rwc[:, RWC_IREP:RWC_IREP + 512] = np.tile(np.eye(128, dtype=np.float32), (1, 4))
    return dict(ident=np.eye(128, dtype=np.float32), ones=np.ones((128, 128), np.float32),
                cos=np.ascontiguousarray(cos), sin=np.ascontiguousarray(sin), prot=prot, rwc=rwc)


def host_layout(inp, b):
    f = np.float32
    cond = np.stack([inp["c"][b].reshape(8, 128).T, inp["c_ctx"].reshape(8, 128).T], axis=-1).astype(f)
    mlac = np.zeros((128, 144), f)
    mlac[:, 0:3] = inp["mla_q_norm"][0].reshape(3, 128).T
    mlac[:, 3] = inp["mla_kv_norm"][0]
    mlac[0:64, 4] = inp["mla_qk_norm_q"][0][0:64]
    mlac[0:32, 5] = inp["mla_qk_norm_q"][0][64:96]
    mlac[0:64, 6] = inp["mla_qk_norm_k"][0][0:64]
    mlac[0:32, 7] = inp["mla_qk_norm_k"][0][64:96]
    dw = inp["conv_dw_w"][0][:, 0, :]
    mlac[:, 8:132] = dw.T.reshape(4, 128, 31).transpose(1, 0, 2).reshape(128, 124)
    mlac[:, 132:136] = inp["conv_dw_b"][0].reshape(4, 128).T
    mlac[:, 136:140] = inp["conv_norm_g"][0].reshape(4, 128).T
    mlac[:, 140:144] = inp["conv_norm_b"][0].reshape(4, 128).T
    d = dict(x=inp["x"][b], ctx=inp["ctx"][b], cond=np.ascontiguousarray(cond), mlac=mlac)
    for k in ("ada_w", "ada_b", "ffn_w1", "ffn_w3", "ffn_w2"):
        d[k] = inp[k]
    d["mla_w_in"] = inp["mla_w_in"][0]
    d["mla_w_qb"] = inp["mla_w_qb"][0]
    d["mla_w_kvb"] = inp["mla_w_kvb"][0]
    d["mix_w_out"] = inp["mix_w_out"][0]
    pp = lambda v: v.reshape(8, 128).T
    rwv = np.zeros((128, 88), f)
    for j in range(6):
        rwv[:, j * 8:(j + 1) * 8] = pp(inp["rwkv_x_mix"][0][j])
    rwv[:, 48:56] = pp(inp["rwkv_k_k"][0])
    rwv[:, 56:64] = pp(inp["rwkv_k_a"][0])
    rwv[:, 64:72] = pp(inp["rwkv_r_k"][0].reshape(-1))
    rwv[:, 72:80] = pp(inp["rwkv_a0"][0][0])
    rwv[:, 80:88] = pp(inp["rwkv_a0"][0][1])
    d["rwv"] = rwv
    for k in ("rwkv_w_r", "rwkv_w_k", "rwkv_w_v", "rwkv_w0", "rwkv_w1", "rwkv_w2", "rwkv_a1", "rwkv_a2",
              "rwkv_g1", "rwkv_g2", "rwkv_ln_g", "rwkv_ln_b", "rwkv_w_o"):
        d[k] = inp[k][0]
    return d


def build(S, stages=("ada", "ffn")):
    nc = bass.Bass("TRN2", target_bir_lowering=False)

    def din(name, shape, dt=F32):
        return nc.dram_tensor(name, list(shape), dt, kind="ExternalInput").ap()

    x_d = din("x", [S, D])
    ctx_d = din("ctx", [CTX, D])
    cond_d = din("cond", [128, 8, 2])
    ada_w_d = din("ada_w", [2, D, 9 * D])
    ada_b_d = din("ada_b", [2, 9 * D])
    w1_d = din("ffn_w1", [2, 2, D, DFF])
    w3_d = din("ffn_w3", [2, 2, D, DFF])
    w2_d = din("ffn_w2", [2, 2, DFF, D])
    ident_d = din("ident", [128, 128])
    ones_d = din("ones", [128, 128])
    cos_d = din("cos", [32, S])
    sin_d = din("sin", [32, S])
    prot_d = din("prot", [32, 32])
    mlac_d = din("mlac", [128, 144])
    w_in_d = din("mla_w_in", [D, 1568])
    w_qb_d = din("mla_w_qb", [384, 768])
    w_kvb_d = din("mla_w_kvb", [128, 1024])
    w_out_d = din("mix_w_out", [D, D])
    rwv_d = din("rwv", [128, 88])
    rwc_d = din("rwc", [128, RWC_N])
    rw = {k: din("rwkv_" + k, shp) for k, shp in (
        ("w_r", [D, D]), ("w_k", [D, D]), ("w_v", [D, D]), ("w0", [2, D]), ("w1", [2, D, 64]), ("w2", [2, 64, D]),
        ("a1", [2, D, 64]), ("a2", [2, 64, D]), ("g1", [D, 128]), ("g2", [128, D]), ("ln_g", [D]), ("ln_b", [D]),
        ("w_o", [D, D]))}
    out_d = nc.dram_tensor("out", [S, D], F32, kind="ExternalOutput").ap()
    C = Ctx(nc)
    P = C.P
    NK = CTX + S
    mods_d = C.dram("mods", [2, 2, 9 * D], F32)
    octx_d = C.dram("octx", [CTX, D], F32)
    qT_d = C.dram("qT", [NH, 96, S], BF16)
    qcT_d = C.dram("qcT", [NH, 96, CTX], BF16)
    kT_d = C.dram("kT", [NH, 96, NK], BF16)
    v_d = C.dram("v", [NK, 512], BF16)
    glu_l_d = C.dram("glu_l", [512, S + 30], F32)
    glu_c_d = C.dram("glu_c", [512, CTX + 30], F32)
    mixT_d = C.dram("mixT", [D, NK], BF16)
    dbg = {}
    if "dbg" in stages:
        dbg["octx"] = nc.dram_tensor("octx_o", [CTX, D], F32, kind="ExternalOutput").ap()
    C.push()
    consts = make_consts(C, ident_d)
    C.push()
    cpb = [C.sb([128, 4, D], F32) for _ in range(2)]
    i = 0
    for (src, dst, ntok) in ((x_d, out_d, S), (ctx_d, octx_d, CTX)):
        for t0 in range(0, ntok, 512):
            n = min(512, ntok - t0)
            b = cpb[i % 2]
            i += 1
            P.dma("sp", b[:, 0:n // 128, :], src[t0:t0 + n, :].rearrange("(s p) d -> p s d", p=128), w=[b])
            P.dma("sp", dst[t0:t0 + n, :].rearrange("(s p) d -> p s d", p=128), b[:, 0:n // 128, :], r=[b])
    C.pop()
    stage_ada(C, cond_d, ada_w_d, ada_b_d, mods_d)
    both = [(octx_d, CTX, 1), (out_d, S, 0)]
    if "ffn" in stages:
        stage_ffn(C, both, w1_d[0, 0], w3_d[0, 0], w2_d[0, 0], mods_d, 0, 0, consts)
    if "mla" in stages:
        wd = (w_in_d, w_qb_d, w_kvb_d, mlac_d, cos_d, sin_d, prot_d, ones_d)
        sub = [x for x in stages if x.startswith("mla_")] or ["mla_proj", "mla_conv", "mla_attn", "mla_out"]
        if "mla_proj" in sub:
            stage_mla_proj(C, [(octx_d, CTX, 1, 0, False), (out_d, S, 0, CTX, True)], S, wd, mods_d, 0, consts,
                           (qT_d, qcT_d, kT_d, v_d, glu_l_d, glu_c_d))
        if "mla_conv" in sub:
            stage_conv(C, [(glu_c_d, CTX, 0), (glu_l_d, S, CTX)], mlac_d, ones_d, mixT_d, consts)
        if "mla_attn" in sub:
            stage_attn(C, S, (qT_d, qcT_d, kT_d, v_d, mixT_d), ones_d)
        if "mla_out" in sub:
            stage_outproj(C, [(octx_d, CTX, 1, 0), (out_d, S, 0, CTX)], w_out_d, mixT_d, mods_d, 0)
    if "ffn2" in stages:
        stage_ffn(C, both, w1_d[0, 1], w3_d[0, 1], w2_d[0, 1], mods_d, 0, 6, consts)
    if "l1ffn" in stages:
        stage_ffn(C, both, w1_d[1, 0], w3_d[1, 0], w2_d[1, 0], mods_d, 1, 0, consts)
    if "rwkv" in stages:
        hT_d = C.dram("hTr", [D, NK + 4], BF16)
        rT_d = C.dram("rT", [D, NK], BF16)
        nkkT_d = C.dram("nkkT", [D, NK], BF16)
        bT_d = C.dram("bT", [2, D, NK], BF16)
        kdT_d = C.dram("kdT", [2, D, NK], BF16)
        sig_d = C.dram("sigw", [2, NK, D], F32)
        vr_d = C.dram("vr", [NK, D], BF16)
        sbon_d = C.dram("sbon", [NK, 16], F32)
        sigG_d = C.dram("sigG", [128, NK], BF16)
        y_d = C.dram("yscan", [2, NK, D], F32)
        sub = [x for x in stages if x.startswith("rw_")] or ["rw_h", "rw_proj", "rw_scan", "rw_out"]
        if "rw_h" in sub:
            stage_rwkv_h(C, [(octx_d, CTX, 1, 1), (out_d, S, 0, CTX + 3)], mods_d, 1, consts, hT_d)
        if "rw_proj" in sub:
            stage_rwkv_proj(C, [(CTX, 1, 0), (S, CTX + 3, CTX)],
                            (rw["w_r"], rw["w_k"], rw["w_v"], rw["w1"], rw["w2"], rw["a1"], rw["a2"], rw["g1"],
                             rw["w0"], rwv_d, rwc_d), hT_d,
                            (rT_d, nkkT_d, bT_d, kdT_d, sig_d, vr_d, sbon_d, sigG_d))
        if "rw_scan" in sub:
            stage_rwkv_scan(C, NK, (rT_d, nkkT_d, bT_d, kdT_d, sig_d, vr_d, y_d), rwc_d, consts)
        if "rw_out" in sub:
            stage_rwkv_out(C, [(out_d, S, 0, CTX)], (rw["g2"], rw["w_o"], rw["ln_g"], rw["ln_b"]),
                           (y_d, vr_d, sbon_d, sigG_d), mods_d, 1, consts)
    if "l1ffn2" in stages:
        stage_ffn(C, [(out_d, S, 0)], w1_d[1, 1], w3_d[1, 1], w2_d[1, 1], mods_d, 1, 6, consts)
    if "dbg" in stages:
        C.push()
        b = C.sb([128, 2, D], F32)
        P.dma("sp", b[:], octx_d.rearrange("(s p) d -> p s d", p=128), w=[b])
        P.dma("sp", dbg["octx"].rearrange("(s p) d -> p s d", p=128), b[:], r=[b])
        C.pop()
    C.pop()
    P.finish()
    return nc


ALL_STAGES = ("ada", "ffn", "mla", "ffn2", "l1ffn", "rwkv", "l1ffn2")
S_FULL = 8192


def kernel(**inputs):
    inp = {k: np.asarray(v) for k, v in inputs.items()}
    B = inp["x"].shape[0]
    S = inp["x"].shape[1]
    nc = build(S, ALL_STAGES)
    hc = host_consts(S)
    in_maps = []
    for b in range(B):
        d = host_layout(inp, b)
        d.update(hc)
        in_maps.append({k: np.ascontiguousarray(v, dtype=np.float32) for k, v in d.items()})
    res = run_bass_kernel_spmd(nc, in_maps, core_ids=list(range(B)))
    return np.stack([np.asarray(r["out"], dtype=np.float32) for r in res.results], axis=0)
```

```python
import os
import numpy as np
import concourse.bass as bass
import concourse.mybir as mybir
from concourse.bass_utils import run_bass_kernel_spmd

F32 = mybir.dt.float32
BF16 = mybir.dt.bfloat16
AF = mybir.ActivationFunctionType
ALU = mybir.AluOpType
AX = mybir.AxisListType

D = 1024
DFF = 2816
NFF = DFF // 128
EPS = 1e-6
CTX = 256


class Tk:
    __slots__ = ("w", "r")

    def __init__(self):
        self.w = None
        self.r = {}


class Buf:
    def __init__(self, t, k=None, excl=False):
        self.t = t
        self.k = k if k is not None else Tk()
        self.excl = excl

    def __getitem__(self, key):
        return self.t[key]


class GBuf:
    def __init__(self, buf, ngroups):
        self.t = buf.t
        self.g = [Buf(buf.t) for _ in range(ngroups)]

    def __getitem__(self, key):
        return self.t[key]


class Prog:
    def __init__(self, nc):
        self.nc = nc
        self.engs = {}
        for name, obj in [("pe", nc.tensor), ("act", nc.scalar), ("dve", nc.vector),
                          ("pool", nc.gpsimd), ("sp", nc.sync)]:
            self.engs[name] = dict(name=name, obj=obj, sem=nc.alloc_semaphore("s_" + name), cnt=0,
                                   waited={}, dsems=None)
        for name, n in [("sp", 12), ("pool", 8), ("act", 4)]:
            e = self.engs[name]
            e["dsems"] = [nc.alloc_semaphore(f"d_{name}{i}") for i in range(n)]
            e["dvals"] = [0] * n
            e["rr"] = 0
        self.nops = 0

    def _wait(self, e, tok):
        sem, val = tok
        key = sem.num
        if e["waited"].get(key, 0) >= val:
            return
        if e["name"] == "pe" and sem is e["sem"]:
            return
        e["obj"].wait_ge(sem, val)
        e["waited"][key] = val
        self.nops += 1

    def _deps(self, e, r, w):
        for b in r:
            k = b.k
            if k.w is not None:
                self._wait(e, k.w)
            if b.excl:
                for tok in k.r.values():
                    self._wait(e, tok)
        for b in w:
            k = b.k
            if k.w is not None:
                self._wait(e, k.w)
            for tok in k.r.values():
                self._wait(e, tok)

    def _update(self, tok, r, w):
        sem, val = tok
        for b in r:
            b.k.r[sem.num] = tok
        for b in w:
            b.k.w = tok
            b.k.r = {}

    def op(self, eng, fn, r=(), w=()):
        e = self.engs[eng]
        self._deps(e, r, w)
        ins = fn(e["obj"])
        e["cnt"] += 1
        ins.then_inc(e["sem"], 1)
        tok = (e["sem"], e["cnt"])
        self._update(tok, r, w)
        self.nops += 1
        return tok

    def dma(self, eng, out, in_, r=(), w=(), **kw):
        e = self.engs[eng]
        self._deps(e, r, w)
        i = e["rr"]
        e["rr"] = (i + 1) % len(e["dsems"])
        sem = e["dsems"][i]
        if e["dvals"][i] > 0:
            self._wait(e, (sem, e["dvals"][i]))
        ins = e["obj"].dma_start(out=out, in_=in_, **kw)
        e["dvals"][i] += 16
        ins.then_inc(sem, 16)
        tok = (sem, e["dvals"][i])
        self._update(tok, r, w)
        self.nops += 1
        return tok

    def barrier(self):
        toks = []
        for e in self.engs.values():
            if e["cnt"] > 0:
                toks.append((e["sem"], e["cnt"]))
            if e["dsems"]:
                for s, v in zip(e["dsems"], e["dvals"]):
                    if v > 0:
                        toks.append((s, v))
        for e in self.engs.values():
            for t in toks:
                if t[0] is e["sem"]:
                    continue
                self._wait(e, t)

    def finish(self):
        self.barrier()


class Ctx:
    def __init__(self, nc):
        self.nc = nc
        self.P = Prog(nc)
        self.uid = 0
        self.stack = []

    def sb(self, shape, dt, name=None):
        self.uid += 1
        cm = self.nc.sbuf_tensor(f"{name or 'sb'}_{self.uid}", list(shape), dt)
        t = cm.__enter__()
        self.stack[-1].append(cm)
        return Buf(t)

    def ps(self, shape, dt, name=None):
        self.uid += 1
        cm = self.nc.psum_tensor(f"{name or 'ps'}_{self.uid}", list(shape), dt)
        t = cm.__enter__()
        self.stack[-1].append(cm)
        return Buf(t, excl=True)

    def push(self):
        self.stack.append([])

    def pop(self):
        self.P.barrier()
        for cm in reversed(self.stack.pop()):
            cm.__exit__(None, None, None)

    def dram(self, name, shape, dt):
        return self.nc.dram_tensor(name, list(shape), dt, kind="Internal").ap()


def stage_ada(C, cond_d, ada_w_d, ada_b_d, mods_d):
    P = C.P
    C.push()
    cond = C.sb([128, 8, 2], F32)
    scond = C.sb([128, 8, 2], F32)
    wb = [C.sb([128, 8, 512], F32) for _ in range(3)]
    bb = [C.sb([2, 512], F32) for _ in range(3)]
    mrow = [C.sb([2, 9 * D], F32) for _ in range(2)]
    pss = [C.ps([2, 512], F32) for _ in range(2)]
    P.dma("sp", cond[:], cond_d, w=[cond])
    P.op("act", lambda e: e.activation(out=scond[:], in_=cond[:], func=AF.Silu), r=[cond], w=[scond])
    it = 0
    for l in range(2):
        wv = ada_w_d[l].rearrange("(k p) n -> p k n", p=128)
        for n in range(18):
            w = wb[it % 3]
            b = bb[it % 3]
            ps = pss[it % 2]
            P.dma("sp", w[:], wv[:, :, n * 512:(n + 1) * 512], w=[w])
            P.dma("sp", b[:], ada_b_d[l, n * 512:(n + 1) * 512].partition_broadcast(2), w=[b])
            for kc in range(8):
                P.op("pe", lambda e, kc=kc, w=w, ps=ps: e.matmul(ps[:], lhsT=scond[:, kc, :], rhs=w[:, kc, :],
                                                              start=(kc == 0), stop=(kc == 7)),
                     r=[scond, w], w=[ps])
            P.op("dve", lambda e, ps=ps, b=b, l=l, n=n: e.tensor_tensor(out=mrow[l][:, n * 512:(n + 1) * 512],
                                                                      in0=ps[:], in1=b[:], op=ALU.add),
                 r=[ps, b], w=[mrow[l]])
            it += 1
        P.dma("sp", mods_d[l], mrow[l][:], r=[mrow[l]])
    C.pop()


def load_bc(C, dst, src_row, eng="sp"):
    C.P.dma(eng, dst[:], src_row.partition_broadcast(128), w=[dst])


def hT_elem(C, xs, sc1, sh, tmp, hb, stat, epsb):
    P = C.P
    junk, ssq, rt, rstd = stat
    P.op("act", lambda e: e.activation(out=junk[:], in_=xs[:], func=AF.Square, accum_out=ssq[:]),
         r=[xs], w=[junk, ssq])
    P.op("act", lambda e: e.activation(out=rt[:], in_=ssq[:], func=AF.Sqrt, bias=epsb[:], scale=1.0 / D),
         r=[ssq, epsb], w=[rt])
    P.op("dve", lambda e: e.reciprocal(out=rstd[:], in_=rt[:]), r=[rt], w=[rstd])
    P.op("dve", lambda e: e.scalar_tensor_tensor(out=tmp[:], in0=xs[:], scalar=rstd[:, 0:1],
                                                 in1=sc1[:], op0=ALU.mult, op1=ALU.mult),
         r=[xs, rstd, sc1], w=[tmp])
    P.op("pool", lambda e: e.tensor_tensor(out=hb[:], in0=tmp[:], in1=sh[:], op=ALU.add),
         r=[tmp, sh], w=[hb])


def hT_tr(C, sub, hT, hb, psT, ident):
    P = C.P
    for kc in range(8):
        P.op("pe", lambda e, kc=kc: e.transpose(out=psT[:, kc * 128:(kc + 1) * 128],
                                                in_=hb[:, kc * 128:(kc + 1) * 128], identity=ident[:]),
             r=[hb, ident], w=[psT])
    P.op("act", lambda e: e.copy(out=hT[:, :, sub * 128:(sub + 1) * 128],
                                 in_=psT[:].rearrange("p (k t) -> p k t", k=8)),
         r=[psT], w=[hT])


def make_hT_sub(C, xs, sub, sc1, sh, hT, tmp, hb, psT, ident, stat, epsb):
    hT_elem(C, xs, sc1, sh, tmp, hb, stat, epsb)
    hT_tr(C, sub, hT, hb, psT, ident)


def stage_ffn(C, segs, w1_d, w3_d, w2_d, mods_d, l, mbase, consts):
    P = C.P
    C.push()
    W1 = C.sb([128, 8, DFF], BF16, "W1")
    W3 = C.sb([128, 8, DFF], BF16, "W3")
    W2 = C.sb([128, NFF, D], BF16, "W2")
    P.dma("pool", W1[:], w1_d.rearrange("(k p) n -> p k n", p=128), w=[W1])
    P.dma("pool", W3[:], w3_d.rearrange("(k p) n -> p k n", p=128), w=[W3])
    P.dma("pool", W2[:], w2_d.rearrange("(k p) n -> p k n", p=128), w=[W2])
    ident, epsb = consts
    sc1 = C.sb([128, D], F32)
    sh = C.sb([128, D], F32)
    gt = C.sb([128, D], F32)
    NXB = 3
    xbs = [C.sb([128, D], F32, "xs") for _ in range(NXB)]
    hT = C.sb([128, 8, 512], BF16, "hT")
    gT = C.sb([128, NFF, 512], BF16, "gT")
    tmp = C.sb([128, D], F32)
    hb = C.sb([128, D], BF16)
    junk = C.sb([128, D], BF16)
    ssq = C.sb([128, 1], F32)
    rt = C.sb([128, 1], F32)
    rstd = C.sb([128, 1], F32)
    sil = [C.sb([128, 512], BF16, "sil") for _ in range(2)]
    psT = C.ps([128, D], BF16, "psT")
    ps1 = [C.ps([128, 512], F32, "ps1") for _ in range(2)]
    ps3 = [C.ps([128, 512], F32, "ps3") for _ in range(2)]
    pso = [C.ps([128, 512], F32, "pso") for _ in range(2)]
    stat = (junk, ssq, rt, rstd)
    cur_which = None
    tiles = []
    for (xd, ntok, which) in segs:
        t0 = 0
        while t0 < ntok:
            n = min(512, ntok - t0)
            tiles.append((xd, t0, n, which))
            t0 += n
    xk = dict(i=0)

    def get_x(xd, r0):
        xb = xbs[xk["i"] % NXB]
        xk["i"] += 1
        P.dma("sp", xb[:], xd[r0:r0 + 128, :], w=[xb])
        return xb

    hTs = [hT, C.sb([128, 8, 512], BF16, "hT2")]
    ELEM_AT = {2: 0, 7: 1, 12: 2, 17: 3}
    TR_AT = {5: 0, 10: 1, 15: 2, 20: 3}
    it = 0
    oi = 0
    prefetched = False
    for ti, (xd, t0, n, which) in enumerate(tiles):
        nsub = n // 128
        hTc = hTs[ti % 2]
        if which != cur_which:
            cur_which = which
            load_bc(C, sh, mods_d[l, which, (mbase + 0) * D:(mbase + 1) * D], "act")
            load_bc(C, sc1, mods_d[l, which, (mbase + 1) * D:(mbase + 2) * D], "act")
            load_bc(C, gt, mods_d[l, which, (mbase + 2) * D:(mbase + 3) * D], "act")
            P.op("dve", lambda e: e.tensor_scalar_add(out=sc1[:], in0=sc1[:], scalar1=1.0), r=[sc1], w=[sc1])
            P.op("dve", lambda e: e.tensor_scalar_mul(out=gt[:], in0=gt[:], scalar1=0.5), r=[gt], w=[gt])
        if not prefetched:
            for sub in range(nsub):
                xs = get_x(xd, t0 + sub * 128)
                make_hT_sub(C, xs, sub, sc1, sh, hTc, tmp, hb, psT, ident, stat, epsb)
        nxt_t = tiles[ti + 1] if ti + 1 < len(tiles) else None
        do_pf = nxt_t is not None and nxt_t[3] == which
        for f in range(NFF):
            p1 = ps1[it % 2]
            p3 = ps3[it % 2]
            sl = sil[it % 2]
            it += 1
            for kc in range(8):
                P.op("pe", lambda e, kc=kc, f=f, p1=p1: e.matmul(p1[:, 0:n], lhsT=W1[:, kc, f * 128:(f + 1) * 128],
                                                                rhs=hTc[:, kc, 0:n], start=(kc == 0), stop=(kc == 7)),
                     r=[W1, hTc], w=[p1])
            for kc in range(8):
                P.op("pe", lambda e, kc=kc, f=f, p3=p3: e.matmul(p3[:, 0:n], lhsT=W3[:, kc, f * 128:(f + 1) * 128],
                                                                rhs=hTc[:, kc, 0:n], start=(kc == 0), stop=(kc == 7)),
                     r=[W3, hTc], w=[p3])
            P.op("act", lambda e, p1=p1, sl=sl: e.activation(out=sl[:, 0:n], in_=p1[:, 0:n], func=AF.Silu),
                 r=[p1], w=[sl])
            P.op("dve", lambda e, p3=p3, sl=sl, f=f: e.tensor_tensor(out=gT[:, f, 0:n], in0=sl[:, 0:n], in1=p3[:, 0:n],
                                                                    op=ALU.mult), r=[sl, p3], w=[gT])
            if do_pf:
                nxd, nt0, nn, _ = nxt_t
                if f in ELEM_AT and ELEM_AT[f] < nn // 128:
                    xs = get_x(nxd, nt0 + ELEM_AT[f] * 128)
                    hT_elem(C, xs, sc1, sh, tmp, hb, stat, epsb)
                if f in TR_AT and TR_AT[f] < nn // 128:
                    hT_tr(C, TR_AT[f], hTs[(ti + 1) % 2], hb, psT, ident)
        prefetched = do_pf
        for sub in range(nsub):
            xs = get_x(xd, t0 + sub * 128)
            for dh in range(2):
                po = pso[oi % 2]
                oi += 1
                for f in range(NFF):
                    P.op("pe", lambda e, f=f, sub=sub, dh=dh, po=po: e.matmul(
                        po[:], lhsT=gT[:, f, sub * 128:(sub + 1) * 128], rhs=W2[:, f, dh * 512:(dh + 1) * 512],
                        start=(f == 0), stop=(f == NFF - 1)), r=[gT, W2], w=[po])
                P.op("dve", lambda e, po=po, dh=dh: e.tensor_tensor(out=tmp[:, dh * 512:(dh + 1) * 512], in0=po[:],
                                                                   in1=gt[:, dh * 512:(dh + 1) * 512],
                                                                   op=ALU.mult), r=[po, gt], w=[tmp])
                P.op("pool", lambda e, dh=dh, xs=xs: e.tensor_tensor(
                    out=xs[:, dh * 512:(dh + 1) * 512], in0=tmp[:, dh * 512:(dh + 1) * 512],
                    in1=xs[:, dh * 512:(dh + 1) * 512], op=ALU.add), r=[tmp, xs], w=[xs])
            P.dma("act", xd[t0 + sub * 128:t0 + (sub + 1) * 128, :], xs[:], r=[xs])
    C.pop()


ATT_SCALE = 96 ** -0.5
NH = 8


def rms_bc(C, ss_ps, n_feat, rows, n, rt, rstd, epsb):
    P = C.P
    P.op("act", lambda e: e.activation(out=rt[0:rows, 0:n], in_=ss_ps[0:rows, 0:n], func=AF.Ln,
                                       bias=epsb[0:rows, :], scale=1.0 / n_feat), r=[ss_ps, epsb], w=[rt])
    P.op("act", lambda e: e.activation(out=rstd[0:rows, 0:n], in_=rt[0:rows, 0:n], func=AF.Exp, scale=-0.5),
         r=[rt], w=[rstd])


def stage_mla_proj(C, segs, S, wd, mods_d, l, consts, scr):
    P = C.P
    C.push()
    ident, epsb = consts
    w_in_d, w_qb_d, w_kvb_d, mlac_d, cos_d, sin_d, prot_d, ones_d = wd
    qT_d, qcT_d, kT_d, v_d, glu_l_d, glu_c_d = scr
    Win = C.sb([128, 8, 1568], BF16, "Win")
    Wqb = C.sb([128, 3, 768], BF16, "Wqb")
    Wkvb = C.sb([128, 1024], BF16, "Wkvb")
    P.dma("pool", Win[:], w_in_d.rearrange("(k p) n -> p k n", p=128), w=[Win])
    P.dma("pool", Wqb[:], w_qb_d.rearrange("(k p) n -> p k n", p=128), w=[Wqb])
    P.dma("pool", Wkvb[:], w_kvb_d, w=[Wkvb])
    mlac = C.sb([128, 144], F32, "mlac")
    P.dma("sp", mlac[:], mlac_d, w=[mlac])
    onesf = C.sb([128, 128], F32, "onesf")
    onesb = C.sb([128, 128], BF16, "onesb")
    prot = C.sb([32, 32], F32, "prot")
    P.dma("sp", onesf[:], ones_d, w=[onesf])
    P.dma("sp", prot[:], prot_d, w=[prot])
    P.op("dve", lambda e: e.tensor_copy(out=onesb[:], in_=onesf[:]), r=[onesf], w=[onesb])
    zero = C.sb([128, 4, 16], F32, "zero")
    P.op("dve", lambda e: e.memset(zero[:], 0.0), w=[zero])
    if "stop1" in os.environ.get("KDBG", ""):
        C.pop()
        return
    KDBG = os.environ.get("KDBG", "")
    for (gd, ntok) in ((glu_l_d, S), (glu_c_d, CTX)):
        if "nopad" in KDBG:
            break
        gv = gd.rearrange("(c p) t -> p c t", p=128)
        P.dma("sp", gv[:, :, 0:15], zero[:, :, 0:15], r=[zero])
        P.dma("sp", gv[:, :, 15 + ntok:30 + ntok], zero[:, :, 0:15], r=[zero])
    sc1 = C.sb([128, D], F32)
    sh = C.sb([128, D], F32)
    NXB = 3
    xbs = [C.sb([128, D], F32, "xs") for _ in range(NXB)]
    hT = C.sb([128, 8, 512], BF16, "hT")
    tmp = C.sb([128, D], F32)
    hb = C.sb([128, D], BF16)
    junk = C.sb([128, D], BF16)
    ssq = C.sb([128, 1], F32)
    rt1 = C.sb([128, 1], F32)
    rstd1 = C.sb([128, 1], F32)
    stat = (junk, ssq, rt1, rstd1)
    zq = C.sb([128, 3, 512], F32, "zq")
    sqb = [C.sb([128, 512], BF16, "sqb") for _ in range(2)]
    cqn = C.sb([128, 3, 512], BF16, "cqn")
    zkv = C.sb([128, 512], F32, "zkv")
    ckvn = C.sb([128, 512], BF16, "ckvn")
    kr_raw = C.sb([32, 512], F32, "kr_raw")
    sq_kr = C.sb([32, 512], BF16, "sq_kr")
    rts = [C.sb([128, 512], F32, "rt") for _ in range(3)]
    rstds = [C.sb([128, 512], F32, "rstd") for _ in range(3)]
    rr = dict(i=0)

    def nrs():
        rr["i"] += 1
        return rts[rr["i"] % 3], rstds[rr["i"] % 3]
    sig = [C.sb([128, 512], F32, "sig") for _ in range(2)]
    glu = [C.sb([128, 512], F32, "glu") for _ in range(2)]
    cos = C.sb([32, 512], F32, "cos")
    sin = C.sb([32, 512], F32, "sin")
    hn_o = [C.sb([64, 512], BF16, "hn_o") for _ in range(2)]
    hr = [C.sb([32, 512], F32, "hr") for _ in range(2)]
    t1 = [C.sb([32, 512], F32, "t1") for _ in range(2)]
    t2 = [C.sb([32, 512], F32, "t2") for _ in range(2)]
    hr_o = [C.sb([32, 512], BF16, "hr_o") for _ in range(2)]
    sqn = [C.sb([64, 512], BF16, "sqn") for _ in range(2)]
    sqr = [C.sb([32, 512], BF16, "sqr") for _ in range(2)]
    vt = [C.sb([128, 512], BF16, "vt") for _ in range(2)]
    psT = C.ps([128, D], BF16, "psT")
    psA = [C.ps([128, 512], F32, "psA") for _ in range(int(os.environ.get("NPSA", "3")))]
    psB = [C.ps([128, 512], F32, "psB") for _ in range(2)]
    psR = [C.ps([32, 512], F32, "psR") for _ in range(2)]
    cnt = dict(a=0, b=0, r=0, g=0, h=0, v=0, s=0)

    def nxt(lst, key):
        i = cnt[key]
        cnt[key] = i + 1
        return lst[i % len(lst)]

    cur_which = None
    for (xd, ntok, which, koff, is_lat) in segs:
        for t0 in range(0, ntok, 512):
            n = min(512, ntok - t0)
            nsub = n // 128
            if which != cur_which:
                cur_which = which
                load_bc(C, sh, mods_d[l, which, 3 * D:4 * D], "act")
                load_bc(C, sc1, mods_d[l, which, 4 * D:5 * D], "act")
                P.op("dve", lambda e: e.tensor_scalar_add(out=sc1[:], in0=sc1[:], scalar1=1.0), r=[sc1], w=[sc1])
            if is_lat:
                P.dma("sp", cos[:, 0:n], cos_d[:, t0:t0 + n], w=[cos])
                P.dma("sp", sin[:, 0:n], sin_d[:, t0:t0 + n], w=[sin])
            for sub in range(nsub):
                xs = xbs[cnt["s"] % NXB]
                cnt["s"] += 1
                P.dma("sp", xs[:], xd[t0 + sub * 128:t0 + (sub + 1) * 128, :], w=[xs])
                make_hT_sub(C, xs, sub, sc1, sh, hT, tmp, hb, psT, ident, stat, epsb)

            def proj(ps, col0, ncol):
                for kc in range(8):
                    P.op("pe", lambda e, kc=kc: e.matmul(ps[0:ncol, 0:n], lhsT=Win[:, kc, col0:col0 + ncol],
                                                         rhs=hT[:, kc, 0:n], start=(kc == 0), stop=(kc == 7)),
                         r=[Win, hT], w=[ps])

            if "stop2" in KDBG:
                continue
            pss = nxt(psB, "b")
            for c in range(3):
                ps = nxt(psA, "a")
                proj(ps, c * 128, 128)
                sq = nxt(sqb, "g")
                if "nosq" not in KDBG:
                    P.op("act", lambda e, ps=ps, sq=sq: e.activation(out=sq[:, 0:n], in_=ps[:, 0:n], func=AF.Square),
                         r=[ps], w=[sq])
                if "nocp" not in KDBG:
                    P.op("dve", lambda e, ps=ps, c=c: e.tensor_copy(out=zq[:, c, 0:n], in_=ps[:, 0:n]), r=[ps], w=[zq])
                if "cq1" in KDBG:
                    continue
                P.op("pe", lambda e, sq=sq, c=c: e.matmul(pss[:, 0:n], lhsT=onesb[:], rhs=sq[:, 0:n],
                                                         start=(c == 0), stop=(c == 2)), r=[onesb, sq], w=[pss])
            if "cq1" in KDBG or "cq2" in KDBG:
                continue
            rt, rstd = nrs()
            rms_bc(C, pss, 384, 128, n, rt, rstd, epsb)
            if "cq3" in KDBG:
                continue
            for c in range(3):
                P.op("dve", lambda e, c=c: e.scalar_tensor_tensor(out=cqn[:, c, 0:n], in0=zq[:, c, 0:n],
                                                                  scalar=mlac[:, c:c + 1], in1=rstd[:, 0:n],
                                                                  op0=ALU.mult, op1=ALU.mult),
                     r=[zq, mlac, rstd], w=[cqn])
            if "stop3" in KDBG:
                continue
            ps = nxt(psA, "a")
            proj(ps, 384, 128)
            sq = nxt(sqb, "g")
            P.op("act", lambda e, ps=ps, sq=sq: e.activation(out=sq[:, 0:n], in_=ps[:, 0:n], func=AF.Square),
                 r=[ps], w=[sq])
            P.op("dve", lambda e, ps=ps: e.tensor_copy(out=zkv[:, 0:n], in_=ps[:, 0:n]), r=[ps], w=[zkv])
            pss = nxt(psB, "b")
            P.op("pe", lambda e, sq=sq: e.matmul(pss[:, 0:n], lhsT=onesb[:], rhs=sq[:, 0:n], start=True, stop=True),
                 r=[onesb, sq], w=[pss])
            rt, rstd = nrs()
            rms_bc(C, pss, 128, 128, n, rt, rstd, epsb)
            P.op("dve", lambda e: e.scalar_tensor_tensor(out=ckvn[:, 0:n], in0=zkv[:, 0:n], scalar=mlac[:, 3:4],
                                                         in1=rstd[:, 0:n], op0=ALU.mult, op1=ALU.mult),
                 r=[zkv, mlac, rstd], w=[ckvn])
            if "stop4" in KDBG:
                continue
            ps = nxt(psA, "a")
            proj(ps, 512, 32)
            P.op("act", lambda e, ps=ps: e.activation(out=sq_kr[:, 0:n], in_=ps[0:32, 0:n], func=AF.Square),
                 r=[ps], w=[sq_kr])
            P.op("dve", lambda e, ps=ps: e.tensor_copy(out=kr_raw[:, 0:n], in_=ps[0:32, 0:n]), r=[ps], w=[kr_raw])
            gd = glu_l_d if is_lat else glu_c_d
            for c in range(4):
                if "noglu" in KDBG:
                    break
                pa = nxt(psA, "a")
                proj(pa, 544 + c * 128, 128)
                pg = nxt(psA, "a")
                proj(pg, 1056 + c * 128, 128)
                sg = nxt(sig, "h")
                gl = glu[cnt["h"] % 2]
                P.op("act", lambda e, pg=pg, sg=sg: e.activation(out=sg[:, 0:n], in_=pg[:, 0:n], func=AF.Sigmoid),
                     r=[pg], w=[sg])
                P.op("dve", lambda e, pa=pa, sg=sg, gl=gl: e.tensor_tensor(out=gl[:, 0:n], in0=sg[:, 0:n],
                                                                          in1=pa[:, 0:n], op=ALU.mult),
                     r=[sg, pa], w=[gl])
                P.dma("act", gd[c * 128:(c + 1) * 128, 15 + t0:15 + t0 + n], gl[:, 0:n], r=[gl])

            def head_side(ps_n, ps_r_or_raw, raw_is_sbuf, sq_r_shared, gcol_n, gcol_r, dst, dcol0, rope):
                sn = nxt(sqn, "v")
                P.op("act", lambda e: e.activation(out=sn[:, 0:n], in_=ps_n[0:64, 0:n], func=AF.Square),
                     r=[ps_n], w=[sn])
                if sq_r_shared is None:
                    sr = sqr[cnt["v"] % 2]
                    P.op("act", lambda e: e.activation(out=sr[:, 0:n], in_=ps_r_or_raw[0:32, 0:n], func=AF.Square),
                         r=[ps_r_or_raw], w=[sr])
                else:
                    sr = sq_r_shared
                pss = nxt(psB, "b")
                P.op("pe", lambda e: e.matmul(pss[0:64, 0:n], lhsT=onesb[0:64, 0:64], rhs=sn[:, 0:n],
                                              start=True, stop=False), r=[onesb, sn], w=[pss])
                P.op("pe", lambda e: e.matmul(pss[0:64, 0:n], lhsT=onesb[0:32, 0:64], rhs=sr[:, 0:n],
                                              start=False, stop=True), r=[onesb, sr], w=[pss])
                rt, rstd = nrs()
                rms_bc(C, pss, 96, 64, n, rt, rstd, epsb)
                ho = nxt(hn_o, "r")
                P.op("dve", lambda e: e.scalar_tensor_tensor(out=ho[:, 0:n], in0=ps_n[0:64, 0:n],
                                                             scalar=mlac[0:64, gcol_n:gcol_n + 1],
                                                             in1=rstd[0:64, 0:n], op0=ALU.mult, op1=ALU.mult),
                     r=[ps_n, mlac, rstd], w=[ho])
                P.dma("act", dst[0:64, dcol0:dcol0 + n], ho[:, 0:n], r=[ho])
                i = cnt["r"]
                h_r = hr[i % 2]
                ro = hr_o[i % 2]
                if rope:
                    P.op("dve", lambda e: e.scalar_tensor_tensor(out=h_r[:, 0:n], in0=ps_r_or_raw[0:32, 0:n],
                                                                 scalar=mlac[0:32, gcol_r:gcol_r + 1],
                                                                 in1=rstd[0:32, 0:n], op0=ALU.mult, op1=ALU.mult),
                         r=[ps_r_or_raw, mlac, rstd], w=[h_r])
                    pr = nxt(psR, "s")
                    P.op("pe", lambda e: e.matmul(pr[:, 0:n], lhsT=prot[:], rhs=h_r[:, 0:n], start=True, stop=True),
                         r=[prot, h_r], w=[pr])
                    a1 = t1[i % 2]
                    a2 = t2[i % 2]
                    P.op("pool", lambda e: e.tensor_tensor(out=a1[:, 0:n], in0=h_r[:, 0:n], in1=cos[:, 0:n],
                                                           op=ALU.mult), r=[h_r, cos], w=[a1])
                    P.op("dve", lambda e: e.tensor_tensor(out=a2[:, 0:n], in0=pr[:, 0:n], in1=sin[:, 0:n],
                                                          op=ALU.mult), r=[pr, sin], w=[a2])
                    P.op("pool", lambda e: e.tensor_tensor(out=ro[:, 0:n], in0=a1[:, 0:n], in1=a2[:, 0:n],
                                                           op=ALU.add), r=[a1, a2], w=[ro])
                else:
                    P.op("dve", lambda e: e.scalar_tensor_tensor(out=ro[:, 0:n], in0=ps_r_or_raw[0:32, 0:n],
                                                                 scalar=mlac[0:32, gcol_r:gcol_r + 1],
                                                                 in1=rstd[0:32, 0:n], op0=ALU.mult, op1=ALU.mult),
                         r=[ps_r_or_raw, mlac, rstd], w=[ro])
                P.dma("act", dst[64:96, dcol0:dcol0 + n], ro[:, 0:n], r=[ro])

            for h in range(NH):
                if "noheads" in KDBG:
                    break
                pqn = nxt(psA, "a")
                for c in range(3):
                    P.op("pe", lambda e, c=c: e.matmul(pqn[0:64, 0:n], lhsT=Wqb[:, c, h * 96:h * 96 + 64],
                                                       rhs=cqn[:, c, 0:n], start=(c == 0), stop=(c == 2)),
                         r=[Wqb, cqn], w=[pqn])
                pqr = nxt(psA, "a")
                for c in range(3):
                    P.op("pe", lambda e, c=c: e.matmul(pqr[0:32, 0:n], lhsT=Wqb[:, c, h * 96 + 64:h * 96 + 96],
                                                       rhs=cqn[:, c, 0:n], start=(c == 0), stop=(c == 2)),
                         r=[Wqb, cqn], w=[pqr])
                if is_lat:
                    head_side(pqn, pqr, False, None, 4, 5, qT_d[h], t0, True)
                else:
                    head_side(pqn, pqr, False, None, 4, 5, qcT_d[h], t0, False)
                pkn = nxt(psA, "a")
                P.op("pe", lambda e: e.matmul(pkn[0:64, 0:n], lhsT=Wkvb[:, h * 128:h * 128 + 64], rhs=ckvn[:, 0:n],
                                              start=True, stop=True), r=[Wkvb, ckvn], w=[pkn])
                head_side(pkn, kr_raw, True, sq_kr, 6, 7, kT_d[h], koff + t0, is_lat)
            for sub in range(nsub):
                if "nov" in KDBG:
                    break
                pv = nxt(psA, "a")
                P.op("pe", lambda e, sub=sub: e.matmul(
                    pv[:].rearrange("p (h e) -> p h e", e=64), lhsT=ckvn[:, sub * 128:(sub + 1) * 128],
                    rhs=Wkvb[:].rearrange("p (h e) -> p h e", e=128)[:, :, 64:128], start=True, stop=True),
                     r=[ckvn, Wkvb], w=[pv])
                vb = nxt(vt, "v")
                P.op("act", lambda e: e.copy(out=vb[:], in_=pv[:]), r=[pv], w=[vb])
                r0 = koff + t0 + sub * 128
                P.dma("act", v_d[r0:r0 + 128, :], vb[:], r=[vb])
    C.pop()


def stage_conv(C, segs, mlac_d, ones_d, scr, consts):
    P = C.P
    C.push()
    ident, epsb = consts
    mixT_d = scr
    mlac = C.sb([128, 144], F32, "mlac")
    onesf = C.sb([128, 128], F32, "onesf")
    P.dma("sp", mlac[:], mlac_d, w=[mlac])
    P.dma("sp", onesf[:], ones_d, w=[onesf])
    G = [C.sb([128, 4, 542], F32, "G") for _ in range(2)]
    Gb = [C.sb([128, 4, 542], BF16, "Gb") for _ in range(2)]
    dg = C.sb([128, 124, 128], BF16, "dg")
    identb = C.sb([128, 128], BF16, "identb")
    P.op("dve", lambda e: e.tensor_copy(out=identb[:], in_=ident[:]), r=[ident], w=[identb])
    for cj in range(124):
        P.op("dve" if cj % 2 == 0 else "pool",
             lambda e: e.tensor_scalar_mul(out=dg[:, cj, :], in0=identb[:], scalar1=mlac[:, 8 + cj:9 + cj]),
             r=[identb, mlac], w=[dg])
    psc = [C.ps([128, 512], F32, "psc") for _ in range(4)]
    acc = [C.sb([128, 512], F32, "acc") for _ in range(4)]
    sq = [C.sb([128, 512], F32, "sq") for _ in range(2)]
    mean = C.sb([128, 512], F32, "mean")
    m2 = C.sb([128, 512], F32, "m2")
    var = C.sb([128, 512], F32, "var")
    rt = C.sb([128, 512], F32, "rt")
    rstd = C.sb([128, 512], F32, "rstd")
    tt = [C.sb([128, 512], F32, "tt") for _ in range(2)]
    ob = [C.sb([128, 512], BF16, "ob") for _ in range(2)]
    ps1 = C.ps([128, 512], F32, "ps1")
    ps2 = C.ps([128, 512], F32, "ps2")
    W0 = 8
    it = 0
    for (gd, ntok, koff) in segs:
        gv = gd.rearrange("(c p) t -> p c t", p=128)
        for t0 in range(0, ntok, 512):
            n = min(512, ntok - t0)
            g = G[it % 2]
            it += 1
            P.dma("sp", g[:, :, 0:n + 30], gv[:, :, t0:t0 + n + 30], w=[g])
            gb = Gb[it % 2]
            P.op("act", lambda e: e.copy(out=gb[:, :, 0:n + 30], in_=g[:, :, 0:n + 30]), r=[g], w=[gb])
            for c in range(4):
                a = acc[c]
                pc = psc[c]
                for j in range(31):
                    P.op("pe", lambda e: e.matmul(pc[:, 0:n], lhsT=dg[:, c * 31 + j, :], rhs=gb[:, c, j:j + n],
                                                  start=(j == 0), stop=(j == 30)), r=[dg, gb], w=[pc])
                P.op("act", lambda e: e.activation(out=a[:, 0:n], in_=pc[:, 0:n], func=AF.Identity,
                                                   bias=mlac[:, 132 + c:133 + c], scale=1.0), r=[pc, mlac], w=[a])
            for c in range(4):
                a = acc[c]
                s_ = sq[c % 2]
                P.op("act", lambda e, a=a, s_=s_: e.activation(out=s_[:, 0:n], in_=a[:, 0:n], func=AF.Square),
                     r=[a], w=[s_])
                P.op("pe", lambda e, a=a, c=c: e.matmul(ps1[:, 0:n], lhsT=onesf[:], rhs=a[:, 0:n],
                                                       start=(c == 0), stop=(c == 3)), r=[onesf, a], w=[ps1])
                P.op("pe", lambda e, s_=s_, c=c: e.matmul(ps2[:, 0:n], lhsT=onesf[:], rhs=s_[:, 0:n],
                                                         start=(c == 0), stop=(c == 3)), r=[onesf, s_], w=[ps2])
            P.op("act", lambda e: e.activation(out=mean[:, 0:n], in_=ps1[:, 0:n], func=AF.Copy, scale=1.0 / 512),
                 r=[ps1], w=[mean])
            P.op("dve", lambda e: e.tensor_tensor(out=m2[:, 0:n], in0=mean[:, 0:n], in1=mean[:, 0:n], op=ALU.mult),
                 r=[mean], w=[m2])
            P.op("dve", lambda e: e.scalar_tensor_tensor(out=var[:, 0:n], in0=ps2[:, 0:n], scalar=1.0 / 512,
                                                         in1=m2[:, 0:n], op0=ALU.mult, op1=ALU.subtract),
                 r=[ps2, m2], w=[var])
            P.op("act", lambda e: e.activation(out=rt[:, 0:n], in_=var[:, 0:n], func=AF.Ln, bias=epsb[:],
                                               scale=1.0), r=[var, epsb], w=[rt])
            P.op("act", lambda e: e.activation(out=rstd[:, 0:n], in_=rt[:, 0:n], func=AF.Exp, scale=-0.5),
                 r=[rt], w=[rstd])
            for c in range(4):
                a = acc[c]
                t_ = tt[c % 2]
                o_ = ob[c % 2]
                P.op("dve", lambda e, a=a, t_=t_: e.tensor_tensor(out=t_[:, 0:n], in0=a[:, 0:n], in1=mean[:, 0:n],
                                                                 op=ALU.subtract), r=[a, mean], w=[t_])
                P.op("pool", lambda e, t_=t_: e.tensor_tensor(out=t_[:, 0:n], in0=t_[:, 0:n], in1=rstd[:, 0:n],
                                                             op=ALU.mult), r=[t_, rstd], w=[t_])
                P.op("act", lambda e, t_=t_, o_=o_, c=c: e.activation(out=o_[:, 0:n], in_=t_[:, 0:n], func=AF.Silu,
                                                                     bias=mlac[:, 140 + c:141 + c],
                                                                     scale=mlac[:, 136 + c:137 + c]),
                     r=[t_, mlac], w=[o_])
                P.dma("act", mixT_d[512 + c * 128:512 + (c + 1) * 128, koff + t0:koff + t0 + n], o_[:, 0:n], r=[o_])
    C.pop()


def stage_attn(C, S, scr, ones_d, do_ctx=True):
    P = C.P
    C.push()
    qT_d, qcT_d, kT_d, v_d, mixT_d = scr
    NK = CTX + S
    NKC = NK // 128
    onesf = C.sb([128, 128], F32, "onesf")
    P.dma("sp", onesf[:], ones_d, w=[onesf])
    kTs = [C.sb([96, NK], BF16, "kT") for _ in range(2)]
    Vs = [C.sb([128, NKC, 65], BF16, "V") for _ in range(2)]
    for V in Vs:
        P.op("dve", lambda e, V=V: e.memset(V[:, :, 64:65], 1.0), w=[V])
    qs = [C.sb([96, 512], BF16, "q") for _ in range(2)]
    pTs = [C.sb([128, 512], BF16, "pT") for _ in range(3)]
    oT = [C.sb([65, 512], F32, "oT") for _ in range(2)]
    rden = [C.sb([65, 512], F32, "rden") for _ in range(2)]
    att = [C.sb([64, 512], BF16, "att") for _ in range(2)]
    psS = [C.ps([128, 512], F32, "psS") for _ in range(3)]
    psO = [C.ps([65, 512], F32, "psO") for _ in range(2)]
    psB = [C.ps([64, 512], F32, "psB") for _ in range(2)]
    vv = v_d.rearrange("(kc p) (h e) -> p kc h e", p=128, e=64)
    si = 0
    qi = 0
    for h in range(NH):
        kT = kTs[h % 2]
        V = Vs[h % 2]
        P.dma("sp", kT[:], kT_d[h], w=[kT])
        P.dma("pool", V[:, :, 0:64], vv[:, :, h, :], w=[V])
        qtiles = [(qT_d, t0, 512, 0, NKC, CTX + t0) for t0 in range(0, S, 512)]
        if do_ctx:
            qtiles.append((qcT_d, 0, CTX, 0, CTX // 128, 0))
        for (qd, t0, n, kc0, kc1, ocol) in qtiles:
            q = qs[qi % 2]
            po = psO[qi % 2]
            o_ = oT[qi % 2]
            rd = rden[qi % 2]
            pb = psB[qi % 2]
            at = att[qi % 2]
            qi += 1
            P.dma("sp", q[:, 0:n], qd[h, :, t0:t0 + n], w=[q])
            prev = None
            for kc in range(kc0, kc1):
                ps = psS[si % 3]
                pT = pTs[si % 3]
                si += 1
                P.op("pe", lambda e: e.matmul(ps[:, 0:n], lhsT=kT[:, kc * 128:(kc + 1) * 128],
                                              rhs=q[:, 0:n], start=True, stop=True), r=[kT, q], w=[ps])
                P.op("act", lambda e: e.activation(out=pT[:, 0:n], in_=ps[:, 0:n], func=AF.Exp,
                                                   scale=ATT_SCALE), r=[ps], w=[pT])
                if prev is not None:
                    pk, ppT = prev
                    P.op("pe", lambda e: e.matmul(po[:, 0:n], lhsT=V[:, pk, :], rhs=ppT[:, 0:n],
                                                  start=(pk == kc0), stop=False), r=[V, ppT], w=[po])
                prev = (kc, pT)
            pk, ppT = prev
            P.op("pe", lambda e: e.matmul(po[:, 0:n], lhsT=V[:, pk, :], rhs=ppT[:, 0:n],
                                          start=(pk == kc0), stop=True), r=[V, ppT], w=[po])
            P.op("dve", lambda e: e.tensor_copy(out=o_[:, 0:n], in_=po[:, 0:n]), r=[po], w=[o_])
            P.op("act", lambda e: e.activation(out=rd[64:65, 0:n], in_=o_[64:65, 0:n], func=AF.Ln), r=[o_], w=[rd])
            P.op("act", lambda e: e.activation(out=rd[64:65, 0:n], in_=rd[64:65, 0:n], func=AF.Exp, scale=-1.0),
                 r=[rd], w=[rd])
            P.op("pe", lambda e: e.matmul(pb[:, 0:n], lhsT=onesf[64:65, 0:64], rhs=rd[64:65, 0:n],
                                          start=True, stop=True), r=[onesf, rd], w=[pb])
            P.op("dve", lambda e: e.tensor_tensor(out=at[:, 0:n], in0=o_[0:64, 0:n], in1=pb[:, 0:n], op=ALU.mult),
                 r=[o_, pb], w=[at])
            P.dma("pool", mixT_d[h * 64:(h + 1) * 64, ocol:ocol + n], at[:, 0:n], r=[at])
    C.pop()


def stage_outproj(C, segs, w_out_d, mixT_d, mods_d, l, row0=0):
    P = C.P
    C.push()
    Wout = C.sb([128, 8, D], BF16, "Wout")
    P.dma("pool", Wout[:], w_out_d.rearrange("(k p) n -> p k n", p=128), w=[Wout])
    gt = C.sb([128, D], F32)
    mixs = [C.sb([128, 8, 512], BF16, "mix") for _ in range(2)]
    xbs = [C.sb([128, D], F32, "xs") for _ in range(3)]
    tmp = C.sb([128, D], F32)
    pso = [C.ps([128, 512], F32, "pso") for _ in range(2)]
    mv = mixT_d.rearrange("(c p) t -> p c t", p=128)
    cur_which = None
    it = 0
    xi = 0
    oi = 0
    for (xd, ntok, which, koff) in segs:
        for t0 in range(0, ntok, 512):
            n = min(512, ntok - t0)
            if which != cur_which:
                cur_which = which
                load_bc(C, gt, mods_d[l, which, 5 * D:6 * D], "act")
            mx = mixs[it % 2]
            it += 1
            P.dma("sp", mx[:, :, 0:n], mv[:, :, koff + t0:koff + t0 + n], w=[mx])
            for sub in range(n // 128):
                xs = xbs[xi % 3]
                xi += 1
                r0 = t0 + sub * 128
                P.dma("sp", xs[:], xd[r0:r0 + 128, :], w=[xs])
                for dh in range(2):
                    po = pso[oi % 2]
                    oi += 1
                    for c in range(8):
                        P.op("pe", lambda e, c=c, po=po: e.matmul(po[:], lhsT=mx[:, c, sub * 128:(sub + 1) * 128],
                                                                 rhs=Wout[:, c, dh * 512:(dh + 1) * 512],
                                                                 start=(c == 0), stop=(c == 7)), r=[mx, Wout], w=[po])
                    P.op("dve", lambda e, po=po: e.tensor_tensor(out=tmp[:, dh * 512:(dh + 1) * 512], in0=po[:],
                                                                in1=gt[:, dh * 512:(dh + 1) * 512], op=ALU.mult),
                         r=[po, gt], w=[tmp])
                    P.op("pool", lambda e, xs=xs: e.tensor_tensor(out=xs[:, dh * 512:(dh + 1) * 512],
                                                                 in0=tmp[:, dh * 512:(dh + 1) * 512],
                                                                 in1=xs[:, dh * 512:(dh + 1) * 512], op=ALU.add),
                         r=[tmp, xs], w=[xs])
                P.dma("act", xd[r0:r0 + 128, :], xs[:], r=[xs])
    C.pop()


def make_consts(C, ident_d):
    P = C.P
    ident = C.sb([128, 128], BF16, "ident")
    identf = C.sb([128, 128], F32, "identf")
    epsb = C.sb([128, 1], F32, "epsb")
    P.dma("sp", identf[:], ident_d, w=[identf])
    P.op("dve", lambda e: e.tensor_copy(out=ident[:], in_=identf[:]), r=[identf], w=[ident])
    P.op("dve", lambda e: e.memset(epsb[:], EPS), w=[epsb])
    return ident, epsb


NHR = 16
GN_EPS = 64 * 1e-5
DEC_C = -float(np.exp(-0.5))


def bc3(ap2, n):
    return ap2.unsqueeze(2).to_broadcast([ap2.shape[0], ap2.shape[1], n])


def stage_rwkv_h(C, segs, mods_d, l, consts, hT_d):
    P = C.P
    C.push()
    ident, epsb = consts
    sc1 = C.sb([128, D], F32)
    sh = C.sb([128, D], F32)
    xbs = [C.sb([128, D], F32, "xs") for _ in range(3)]
    hTs = [C.sb([128, 8, 512], BF16, "hT") for _ in range(2)]
    tmp = C.sb([128, D], F32)
    hb = C.sb([128, D], BF16)
    junk = C.sb([128, D], BF16)
    ssq = C.sb([128, 1], F32)
    rt1 = C.sb([128, 1], F32)
    rstd1 = C.sb([128, 1], F32)
    stat = (junk, ssq, rt1, rstd1)
    zero = C.sb([128, 8, 1], BF16, "zero")
    P.op("dve", lambda e: e.memset(zero[:], 0.0), w=[zero])
    psT = C.ps([128, D], BF16, "psT")
    hv = hT_d.rearrange("(c p) t -> p c t", p=128)
    xi = 0
    ti = 0
    cur_which = None
    with C.nc.allow_non_contiguous_dma(reason="tiny zero pad columns"):
        for (xd, ntok, which, col0) in segs:
            P.dma("sp", hv[:, :, col0 - 1:col0], zero[:], r=[zero])
            P.dma("sp", hv[:, :, col0 + ntok:col0 + ntok + 1], zero[:], r=[zero])
    for (xd, ntok, which, col0) in segs:
        for t0 in range(0, ntok, 512):
            n = min(512, ntok - t0)
            if which != cur_which:
                cur_which = which
                load_bc(C, sh, mods_d[l, which, 3 * D:4 * D], "act")
                load_bc(C, sc1, mods_d[l, which, 4 * D:5 * D], "act")
                P.op("dve", lambda e: e.tensor_scalar_add(out=sc1[:], in0=sc1[:], scalar1=1.0), r=[sc1], w=[sc1])
            hT = hTs[ti % 2]
            ti += 1
            for sub in range(n // 128):
                xs = xbs[xi % 3]
                xi += 1
                P.dma("sp", xs[:], xd[t0 + sub * 128:t0 + (sub + 1) * 128, :], w=[xs])
                make_hT_sub(C, xs, sub, sc1, sh, hT, tmp, hb, psT, ident, stat, epsb)
            P.dma("act", hv[:, :, col0 + t0:col0 + t0 + n], hT[:, :, 0:n], r=[hT])
    C.pop()


def stage_rwkv_proj(C, segs, wd, hT_d, scr):
    P = C.P
    C.push()
    (w_r_d, w_k_d, w_v_d, w1_d, w2_d, a1_d, a2_d, g1_d, w0_d, rwv_d, rwc_d) = wd
    (rT_d, nkkT_d, bT_d, kdT_d, sig_d, v_d, sbon_d, sigG_d) = scr
    Wr = C.sb([128, 8, D], BF16, "Wr")
    Wk = C.sb([128, 8, D], BF16, "Wk")
    Wv = C.sb([128, 8, D], BF16, "Wv")
    for W, wdram in ((Wr, w_r_d), (Wk, w_k_d), (Wv, w_v_d)):
        P.dma("pool", W[:], wdram.rearrange("(k p) n -> p k n", p=128), w=[W])
    W1 = [C.sb([128, 8, 64], BF16, "W1") for _ in range(2)]
    A1 = [C.sb([128, 8, 64], BF16, "A1") for _ in range(2)]
    W2 = [C.sb([64, D], BF16, "W2") for _ in range(2)]
    A2 = [C.sb([64, D], BF16, "A2") for _ in range(2)]
    G1 = C.sb([128, 8, 128], BF16, "G1")
    w0bc = [C.sb([128, D], F32, "w0bc") for _ in range(2)]
    for d in range(2):
        P.dma("pool", W1[d][:], w1_d[d].rearrange("(k p) n -> p k n", p=128), w=[W1[d]])
        P.dma("pool", A1[d][:], a1_d[d].rearrange("(k p) n -> p k n", p=128), w=[A1[d]])
        P.dma("pool", W2[d][:], w2_d[d], w=[W2[d]])
        P.dma("pool", A2[d][:], a2_d[d], w=[A2[d]])
        load_bc(C, w0bc[d], w0_d[d], "sp")
    P.dma("pool", G1[:], g1_d.rearrange("(k p) n -> p k n", p=128), w=[G1])
    rwv = C.sb([128, 88], F32, "rwv")
    P.dma("sp", rwv[:], rwv_d, w=[rwv])
    XM, KK, KA, RK, A0 = 0, 48, 56, 64, 72
    omk = C.sb([128, 8], F32, "omk")
    rkh = C.sb([128, 8], F32, "rkh")
    P.op("dve", lambda e: e.tensor_scalar(out=omk[:], in0=rwv[:, KA:KA + 8], scalar1=-1.0, scalar2=1.0,
                                          op0=ALU.mult, op1=ALU.add), r=[rwv], w=[omk])
    P.op("dve", lambda e: e.tensor_scalar_mul(out=rkh[:], in0=rwv[:, RK:RK + 8], scalar1=0.5), r=[rwv], w=[rkh])
    blk = C.sb([128, 128], BF16, "blk")
    hsel = C.sb([128, 2], BF16, "hsel")
    blkf = C.sb([128, 130], F32, "blkf")
    P.dma("sp", blkf[:], rwc_d[:, 0:130], w=[blkf])
    P.op("dve", lambda e: e.tensor_copy(out=blk[:], in_=blkf[:, 0:128]), r=[blkf], w=[blk])
    P.op("dve", lambda e: e.tensor_copy(out=hsel[:], in_=blkf[:, 128:130]), r=[blkf], w=[hsel])
    hTh = [C.sb([128, 8, 514], BF16, "hTh") for _ in range(2)]
    hcAs = [C.sb([128, 8, 512], BF16, "hcA") for _ in range(2)]
    tt = GBuf(C.sb([128, 8, 512], F32, "tt"), 8)
    xx = C.sb([128, 8, 512], F32, "xx")
    xj = [C.sb([128, 8, 512], BF16, "xj") for _ in range(2)]
    kT = GBuf(C.sb([128, 8, 512], F32, "kT"), 8)
    kkT = GBuf(C.sb([128, 8, 512], BF16, "kkT"), 8)
    rTs = GBuf(C.sb([128, 8, 512], BF16, "rTs"), 8)
    kdsum = tt
    ob = [C.sb([128, 512], BF16, "ob") for _ in range(3)]
    of = [C.sb([128, 512], F32, "of") for _ in range(3)]
    hid = [C.sb([128, 512], BF16, "hid") for _ in range(2)]
    sbo = [C.sb([128, 16], F32, "sbo") for _ in range(2)]
    psA = [C.ps([128, 512], F32, "psA") for _ in range(5)]
    psS = [C.ps([128, 512], F32, "psS") for _ in range(2)]
    cnt = dict(a=0, o=0, f=0, h=0, x=0, t=0, s=0)

    def nxt(lst, key):
        i = cnt[key]
        cnt[key] = i + 1
        return lst[i % len(lst)]

    hv = hT_d.rearrange("(c p) t -> p c t", p=128)
    for (ntok, col0, koff) in segs:
        for t0 in range(0, ntok, 512):
            n = min(512, ntok - t0)
            nsub = n // 128
            hh = nxt(hTh, "t")
            P.dma("sp", hh[:, :, 0:n + 2], hv[:, :, col0 + t0 - 1:col0 + t0 + n + 1], w=[hh])
            hc = hh[:, :, 1:n + 1]
            hcA = hcAs[cnt["t"] % 2]
            P.dma("sp", hcA[:, :, 0:n], hv[:, :, col0 + t0:col0 + t0 + n], w=[hcA])
            P.op("dve", lambda e: e.tensor_tensor(out=tt[:, :, 0:n], in0=hh[:, :, 0:n], in1=hh[:, :, 2:n + 2],
                                                  op=ALU.add), r=[hh], w=tt.g)
            P.op("dve", lambda e: e.scalar_tensor_tensor(out=xx[:, :, 0:n], in0=tt[:, :, 0:n], scalar=0.5, in1=hc,
                                                         op0=ALU.mult, op1=ALU.subtract), r=tt.g + [hh], w=[xx])

            def mix(j):
                x_ = nxt(xj, "x")
                P.op("dve", lambda e: e.tensor_tensor(out=x_[:, :, 0:n], in0=xx[:, :, 0:n],
                                                      in1=bc3(rwv[:, XM + j * 8:XM + j * 8 + 8], n), op=ALU.mult),
                     r=[xx, rwv], w=[x_])
                return x_

            def projT(W, x_, p, ncol=128, col0_=None):
                ps = nxt(psA, "a")
                c0 = p * 128 if col0_ is None else col0_
                for kc in range(8):
                    P.op("pe", lambda e, kc=kc: e.matmul(ps[0:ncol, 0:n], lhsT=W[:, kc, c0:c0 + ncol],
                                                         rhs=hcA[:, kc, 0:n], start=(kc == 0), stop=False),
                         r=[W, hcA], w=[ps])
                for kc in range(8):
                    P.op("pe", lambda e, kc=kc: e.matmul(ps[0:ncol, 0:n], lhsT=W[:, kc, c0:c0 + ncol],
                                                         rhs=x_[:, kc, 0:n], start=False, stop=(kc == 7)),
                         r=[W, x_], w=[ps])
                return ps

            tok0 = koff + t0
            x_ = mix(0)
            for p in range(8):
                ps = projT(Wr, x_, p)
                P.op("act", lambda e: e.copy(out=rTs[:, p, 0:n], in_=ps[:, 0:n]), r=[ps], w=[rTs.g[p]])
            P.dma("act", rT_d.rearrange("(c p) t -> p c t", p=128)[:, :, tok0:tok0 + n], rTs[:, :, 0:n], r=rTs.g)
            x_ = mix(2)
            for p in range(8):
                ps = projT(Wk, x_, p)
                P.op("act", lambda e: e.copy(out=kT[:, p, 0:n], in_=ps[:, 0:n]), r=[ps], w=[kT.g[p]])
                kr = nxt(of, "f")
                P.op("dve", lambda e: e.tensor_scalar_mul(out=kr[:, 0:n], in0=kT[:, p, 0:n],
                                                          scalar1=rwv[:, KK + p:KK + p + 1]), r=[kT.g[p], rwv], w=[kr])
                sq = nxt(ob, "o")
                P.op("act", lambda e: e.activation(out=sq[:, 0:n], in_=kr[:, 0:n], func=AF.Square), r=[kr], w=[sq])
                pss = nxt(psA, "a")
                P.op("pe", lambda e: e.matmul(pss[:, 0:n], lhsT=blk[:], rhs=sq[:, 0:n], start=True, stop=True),
                     r=[blk, sq], w=[pss])
                nr = nxt(of, "f")
                P.op("dve", lambda e: e.tensor_scalar_max(out=nr[:, 0:n], in0=pss[:, 0:n], scalar1=1e-24),
                     r=[pss], w=[nr])
                P.op("act", lambda e: e.activation(out=nr[:, 0:n], in_=nr[:, 0:n], func=AF.Ln), r=[nr], w=[nr])
                P.op("act", lambda e: e.activation(out=nr[:, 0:n], in_=nr[:, 0:n], func=AF.Exp, scale=-0.5),
                     r=[nr], w=[nr])
                P.op("dve", lambda e: e.tensor_tensor(out=kkT[:, p, 0:n], in0=kr[:, 0:n], in1=nr[:, 0:n], op=ALU.mult),
                     r=[kr, nr], w=[kkT.g[p]])
                nk = nxt(ob, "o")
                P.op("pool", lambda e: e.tensor_scalar_mul(out=nk[:, 0:n], in0=kkT[:, p, 0:n], scalar1=-1.0),
                     r=[kkT.g[p]], w=[nk])
                P.dma("act", nkkT_d[p * 128:(p + 1) * 128, tok0:tok0 + n], nk[:, 0:n], r=[nk])
            x_ = mix(3)
            for sub in range(nsub):
                for dh in range(2):
                    ps = nxt(psA, "a")
                    for kc in range(8):
                        P.op("pe", lambda e, kc=kc: e.matmul(ps[:], lhsT=hcA[:, kc, sub * 128:(sub + 1) * 128],
                                                             rhs=Wv[:, kc, dh * 512:(dh + 1) * 512],
                                                             start=(kc == 0), stop=False), r=[hcA, Wv], w=[ps])
                    for kc in range(8):
                        P.op("pe", lambda e, kc=kc: e.matmul(ps[:], lhsT=x_[:, kc, sub * 128:(sub + 1) * 128],
                                                             rhs=Wv[:, kc, dh * 512:(dh + 1) * 512],
                                                             start=False, stop=(kc == 7)), r=[x_, Wv], w=[ps])
                    vb = nxt(ob, "o")
                    P.op("act", lambda e: e.copy(out=vb[:], in_=ps[:]), r=[ps], w=[vb])
                    P.dma("act", v_d[tok0 + sub * 128:tok0 + (sub + 1) * 128, dh * 512:(dh + 1) * 512], vb[:], r=[vb])
            x_ = mix(1)
            for d in range(2):
                ps = projT(W1[d], x_, 0, 64, 0)
                hd = nxt(hid, "h")
                P.op("act", lambda e: e.activation(out=hd[0:64, 0:n], in_=ps[0:64, 0:n], func=AF.Tanh), r=[ps], w=[hd])
                for sub in range(nsub):
                    for dh in range(2):
                        ps2 = nxt(psA, "a")
                        P.op("pe", lambda e: e.matmul(ps2[:], lhsT=hd[0:64, sub * 128:(sub + 1) * 128],
                                                      rhs=W2[d][:, dh * 512:(dh + 1) * 512], start=True, stop=True),
                             r=[hd, W2[d]], w=[ps2])
                        o1 = nxt(of, "f")
                        P.op("dve", lambda e: e.tensor_tensor(out=o1[:], in0=ps2[:],
                                                              in1=w0bc[d][:, dh * 512:(dh + 1) * 512], op=ALU.add),
                             r=[ps2, w0bc[d]], w=[o1])
                        P.op("act", lambda e: e.activation(out=o1[:], in_=o1[:], func=AF.Sigmoid), r=[o1], w=[o1])
                        P.dma("act", sig_d[d, tok0 + sub * 128:tok0 + (sub + 1) * 128, dh * 512:(dh + 1) * 512],
                              o1[:], r=[o1])
            x_ = mix(5)
            ps = projT(G1, x_, 0, 128, 0)
            sg = nxt(ob, "o")
            P.op("act", lambda e: e.activation(out=sg[:, 0:n], in_=ps[:, 0:n], func=AF.Sigmoid), r=[ps], w=[sg])
            P.dma("act", sigG_d[:, tok0:tok0 + n], sg[:, 0:n], r=[sg])
            x_ = mix(4)
            for d in range(2):
                ps = projT(A1[d], x_, 0, 64, 0)
                hd = nxt(hid, "h")
                P.op("act", lambda e: e.copy(out=hd[0:64, 0:n], in_=ps[0:64, 0:n]), r=[ps], w=[hd])
                for p in range(8):
                    ps2 = nxt(psA, "a")
                    P.op("pe", lambda e: e.matmul(ps2[:, 0:n], lhsT=A2[d][:, p * 128:(p + 1) * 128], rhs=hd[0:64, 0:n],
                                                  start=True, stop=True), r=[A2[d], hd], w=[ps2])
                    av = nxt(of, "f")
                    P.op("act", lambda e: e.activation(out=av[:, 0:n], in_=ps2[:, 0:n], func=AF.Sigmoid,
                                                       bias=rwv[:, A0 + d * 8 + p:A0 + d * 8 + p + 1]),
                         r=[ps2, rwv], w=[av])
                    bb = nxt(ob, "o")
                    P.op("dve", lambda e: e.tensor_tensor(out=bb[:, 0:n], in0=kkT[:, p, 0:n], in1=av[:, 0:n],
                                                          op=ALU.mult), r=[kkT.g[p], av], w=[bb])
                    P.dma("act", bT_d[d, p * 128:(p + 1) * 128, tok0:tok0 + n], bb[:, 0:n], r=[bb])
                    P.op("dve", lambda e: e.tensor_scalar(out=av[:, 0:n], in0=av[:, 0:n],
                                                          scalar1=rwv[:, KA + p:KA + p + 1], scalar2=omk[:, p:p + 1],
                                                          op0=ALU.mult, op1=ALU.add), r=[av, rwv, omk], w=[av])
                    kd = nxt(ob, "o")
                    P.op("dve", lambda e: e.tensor_tensor(out=kd[:, 0:n], in0=kT[:, p, 0:n], in1=av[:, 0:n],
                                                          op=ALU.mult), r=[kT.g[p], av], w=[kd])
                    P.dma("act", kdT_d[d, p * 128:(p + 1) * 128, tok0:tok0 + n], kd[:, 0:n], r=[kd])
                    if d == 0:
                        P.op("pool", lambda e: e.tensor_tensor(out=kdsum[:, p, 0:n], in0=kT[:, p, 0:n], in1=av[:, 0:n],
                                                               op=ALU.mult), r=[kT.g[p], av], w=[kdsum.g[p]])
                    else:
                        P.op("dve", lambda e: e.tensor_tensor(out=av[:, 0:n], in0=kT[:, p, 0:n], in1=av[:, 0:n],
                                                              op=ALU.mult), r=[kT.g[p], av], w=[av])
                        P.op("dve", lambda e: e.tensor_tensor(out=kdsum[:, p, 0:n], in0=kdsum[:, p, 0:n],
                                                              in1=av[:, 0:n], op=ALU.add), r=[kdsum.g[p], av], w=[kdsum.g[p]])
            for p in range(8):
                P.op("dve", lambda e: e.scalar_tensor_tensor(out=kdsum[:, p, 0:n], in0=kdsum[:, p, 0:n],
                                                             scalar=rkh[:, p:p + 1], in1=rTs[:, p, 0:n],
                                                             op0=ALU.mult, op1=ALU.mult),
                     r=[kdsum.g[p], rkh, rTs.g[p]], w=[kdsum.g[p]])
            prod = nxt(xj, "x")
            P.op("act", lambda e: e.copy(out=prod[:, :, 0:n], in_=kdsum[:, :, 0:n]), r=kdsum.g, w=[prod])
            for sub in range(nsub):
                pb = nxt(psS, "s")
                for p in range(8):
                    P.op("pe", lambda e: e.matmul(pb[:, 2 * p:2 * p + 2], lhsT=prod[:, p, sub * 128:(sub + 1) * 128],
                                                  rhs=hsel[:], start=True, stop=True), r=[prod, hsel], w=[pb])
                so = sbo[sub % 2]
                P.op("dve", lambda e: e.tensor_copy(out=so[:], in_=pb[:, 0:16]), r=[pb], w=[so])
                P.dma("act", sbon_d[tok0 + sub * 128:tok0 + (sub + 1) * 128, :], so[:], r=[so])
    C.pop()


RWC_TRI = 130
RWC_LV = 256 + 1536
RWC_DIR = 256 + 1536 + 1024 + 12 * 512
RWC_IREP = 130 + 2 * RWC_DIR
RWC_N = RWC_IREP + 512


def stage_rwkv_scan(C, NK, scr, rwc_d, consts):
    P = C.P
    C.push()
    ident, epsb = consts
    (rT_d, nkkT_d, bT_d, kdT_d, sig_d, v_d, y_d) = scr
    NKC = NK // 128
    NCC = CTX // 128
    KD = os.environ.get("KDBG", "")
    irf = C.sb([128, 512], F32, "irf")
    irep = C.sb([128, 512], BF16, "irep")
    P.dma("sp", irf[:], rwc_d[:, RWC_IREP:RWC_IREP + 512], w=[irf])
    P.op("dve", lambda e: e.tensor_copy(out=irep[:], in_=irf[:]), r=[irf], w=[irep])
    triI = C.sb([128, 128], F32, "triI")
    triE = C.sb([128, 128], F32, "triE")
    mS = C.sb([128, 512], F32, "mS")
    mST = C.sb([128, 512], F32, "mST")
    mI = C.sb([128, 512], F32, "mI")
    mSb = C.sb([128, 512], BF16, "mSb")
    mIb = C.sb([128, 512], BF16, "mIb")
    lvm = C.sb([128, 14, 512], BF16, "lvm")
    lvf = C.sb([128, 2, 512], F32, "lvf")
    NLB = 2
    ld = [dict(r=C.sb([128, 8, 128], BF16, "l_r"), nk=C.sb([128, 8, 128], BF16, "l_nk"),
               b=C.sb([128, 8, 128], BF16, "l_b"), kd=C.sb([128, 8, 128], BF16, "l_kd"),
               sig=C.sb([128, D], F32, "l_sig"), v=C.sb([128, D], BF16, "l_v")) for _ in range(NLB)]
    eL = C.sb([128, D], F32, "eL")
    eLx = C.sb([128, D], F32, "eLx")
    enL = C.sb([128, D], F32, "enL")
    pre = []
    for i in range(2):
        d_ = dict(rt=C.sb([128, D], BF16, "rt"), at=C.sb([128, D], BF16, "at"), bt=C.sb([128, D], BF16, "bt"),
                  kt=C.sb([128, D], BF16, "kt"),
                  bpA=C.sb([128, 8, 128], BF16, "bpA"), bpB=C.sb([128, 8, 128], BF16, "bpB"),
                  kpA=C.sb([128, 8, 128], BF16, "kpA"), kpB=C.sb([128, 8, 128], BF16, "kpB"),
                  T=[GBuf(C.sb([128, 16, 128], BF16, "T"), 4) for _ in range(2)],
                  Tt=[GBuf(C.sb([128, 16, 128], BF16, "Tt"), 4) for _ in range(2)],
                  Mak=GBuf(C.sb([128, 16, 128], BF16, "Mak"), 4), Mbr=GBuf(C.sb([128, 16, 128], BF16, "Mbr"), 4),
                  Mkr=GBuf(C.sb([128, 16, 128], BF16, "Mkr"), 4), gam=C.sb([128, 8], F32, "gam"))
        for nm in ("bpA", "bpB", "kpA", "kpB"):
            P.op("pool", lambda e, b_=d_[nm]: e.memset(b_[:], 0.0), w=[d_[nm]])
        pre.append(d_)
    Pb = [GBuf(C.sb([128, 16, 128], BF16, "Pb"), 4) for _ in range(2)]
    Qb = [GBuf(C.sb([128, 16, 128], BF16, "Qb"), 4) for _ in range(2)]
    Sf = C.sb([128, 8, 64], F32, "Sf")
    Sb = C.sb([128, 8, 64], BF16, "Sb")
    Stmp = C.sb([128, 8, 64], F32, "Stmp")
    XT = GBuf(C.sb([128, 16, 64], BF16, "XT"), 2)
    UT = GBuf(C.sb([128, 16, 64], BF16, "UT"), 2)
    yt = [C.sb([128, D], F32, "yt") for _ in range(2)]
    pbF = [C.ps([128, 512], F32, "pbF") for _ in range(6)]
    pbT = [C.ps([128, D], BF16, "pbT") for _ in range(2)]
    cnt = dict(f=0, t=0, y=0)

    def nf():
        i = cnt["f"]
        cnt["f"] = i + 1
        return pbF[i % 6]

    def nt():
        i = cnt["t"]
        cnt["t"] = i + 1
        return pbT[i % 2]

    rv = rT_d.rearrange("(c p) t -> p c t", p=128)
    nkv = nkkT_d.rearrange("(c p) t -> p c t", p=128)

    for d in range(2):
        base = RWC_TRI + d * RWC_DIR
        P.dma("sp", triI[:], rwc_d[:, base:base + 128], w=[triI])
        P.dma("sp", triE[:], rwc_d[:, base + 128:base + 256], w=[triE])
        P.dma("sp", mS[:], rwc_d[:, base + 256:base + 768], w=[mS])
        P.dma("sp", mST[:], rwc_d[:, base + 768:base + 1280], w=[mST])
        P.dma("sp", mI[:], rwc_d[:, base + 1280:base + 1792], w=[mI])
        P.op("dve", lambda e: e.tensor_copy(out=mSb[:], in_=mS[:]), r=[mS], w=[mSb])
        P.op("dve", lambda e: e.tensor_copy(out=mIb[:], in_=mI[:]), r=[mI], w=[mIb])
        for q_ in range(7):
            o = base + RWC_LV + q_ * 1024
            P.dma("sp", lvf[:], rwc_d[:, o:o + 1024].rearrange("p (a b) -> p a b", a=2), w=[lvf])
            P.op("dve", lambda e: e.tensor_copy(out=lvm[:, 2 * q_:2 * q_ + 2, :], in_=lvf[:]), r=[lvf], w=[lvm])
        P.op("dve", lambda e: e.memset(Sf[:], 0.0), w=[Sf])
        P.op("dve", lambda e: e.memset(Sb[:], 0.0), w=[Sb])
        if d == 0:
            order = list(range(NKC))
        else:
            order = list(range(NCC - 1, -1, -1)) + list(range(NKC - 1, NCC - 1, -1))
        last = 127 if d == 0 else 0
        bv = bT_d[d].rearrange("(c p) t -> p c t", p=128)
        kv = kdT_d[d].rearrange("(c p) t -> p c t", p=128)

        def load(i):
            c = order[i]
            L = ld[i % NLB]
            t0 = c * 128
            P.dma("sp", L["r"][:], rv[:, :, t0:t0 + 128], w=[L["r"]])
            P.dma("sp", L["nk"][:], nkv[:, :, t0:t0 + 128], w=[L["nk"]])
            P.dma("sp", L["b"][:], bv[:, :, t0:t0 + 128], w=[L["b"]])
            P.dma("sp", L["kd"][:], kv[:, :, t0:t0 + 128], w=[L["kd"]])
            P.dma("sp", L["sig"][:], sig_d[d, t0:t0 + 128, :], w=[L["sig"]])
            P.dma("sp", L["v"][:], v_d[t0:t0 + 128, :], w=[L["v"]])

        def precompute(i):
            L = ld[i % NLB]
            R = pre[i % 2]
            bL = [nf(), nf()]
            for p in range(8):
                P.op("pe", lambda e: e.matmul(bL[p // 4][:, (p % 4) * 128:(p % 4 + 1) * 128],
                                              lhsT=L["sig"][:, p * 128:(p + 1) * 128], rhs=triI[:],
                                              start=True, stop=True), r=[L["sig"], triI], w=[bL[p // 4]])
            for hf in range(2):
                sl = slice(hf * 512, (hf + 1) * 512)
                P.op("act", lambda e: e.activation(out=eL[:, sl], in_=bL[hf][:], func=AF.Exp), r=[bL[hf]], w=[eL])
                P.op("act", lambda e: e.activation(out=enL[:, sl], in_=bL[hf][:], func=AF.Exp, scale=-1.0),
                     r=[bL[hf]], w=[enL])
            yield
            fl = lambda b_: b_[:].rearrange("p c t -> p (c t)")
            P.op("dve", lambda e: e.tensor_tensor(out=R["rt"][:], in0=fl(L["r"]), in1=eL[:], op=ALU.mult),
                 r=[L["r"], eL], w=[R["rt"]])
            at3 = R["at"][:].rearrange("p (c t) -> p c t", t=128)
            eL3 = eL[:].rearrange("p (c t) -> p c t", t=128)
            if d == 0:
                P.op("pool", lambda e: e.tensor_tensor(out=at3[:, :, 1:128], in0=L["nk"][:, :, 1:128],
                                                       in1=eL3[:, :, 0:127], op=ALU.mult), r=[L["nk"], eL], w=[R["at"]])
                P.op("pool", lambda e: e.tensor_copy(out=at3[:, :, 0:1], in_=L["nk"][:, :, 0:1]), r=[L["nk"]], w=[R["at"]])
            else:
                P.op("pool", lambda e: e.tensor_tensor(out=at3[:, :, 0:127], in0=L["nk"][:, :, 0:127],
                                                       in1=eL3[:, :, 1:128], op=ALU.mult), r=[L["nk"], eL], w=[R["at"]])
                P.op("pool", lambda e: e.tensor_copy(out=at3[:, :, 127:128], in_=L["nk"][:, :, 127:128]),
                     r=[L["nk"]], w=[R["at"]])
            P.op("dve", lambda e: e.tensor_tensor(out=R["bt"][:], in0=fl(L["b"]), in1=enL[:], op=ALU.mult),
                 r=[L["b"], enL], w=[R["bt"]])
            P.op("pool", lambda e: e.tensor_tensor(out=R["kt"][:], in0=fl(L["kd"]), in1=enL[:], op=ALU.mult),
                 r=[L["kd"], enL], w=[R["kt"]])
            P.op("dve", lambda e: e.tensor_copy(out=R["gam"][:],
                                                in_=eL[:].rearrange("p (c t) -> p c t", t=128)[:, :, last]),
                 r=[eL], w=[R["gam"]])
            yield
            for (src, dA, dB) in ((R["bt"], R["bpA"], R["bpB"]), (R["kt"], R["kpA"], R["kpB"])):
                pt = nt()
                for p in range(8):
                    P.op("pe", lambda e: e.transpose(out=pt[:, p * 128:(p + 1) * 128],
                                                     in_=src[:, p * 128:(p + 1) * 128], identity=ident[:]),
                         r=[src, ident], w=[pt])
                ptv = pt[:].rearrange("p (c k) -> p c k", k=128)
                P.op("act", lambda e: e.copy(out=dA[:, :, 0:64], in_=ptv[:, :, 0:64]), r=[pt], w=[dA])
                P.op("dve", lambda e: e.tensor_copy(out=dB[:, :, 64:128], in_=ptv[:, :, 64:128]), r=[pt], w=[dB])

            def hm(lhs, rhs, mask, dst, eng):
                for G in range(2):
                    pbs = (nf(), nf())
                    for j in range(8):
                        h = G * 8 + j
                        p, q = h // 2, h % 2
                        rows = slice(q * 64, q * 64 + 64)
                        pb = pbs[q]
                        P.op("pe", lambda e: e.matmul(pb[:, (j // 2) * 128:(j // 2 + 1) * 128],
                                                      lhsT=lhs[rows, p * 128:(p + 1) * 128],
                                                      rhs=rhs[rows, p * 128:(p + 1) * 128], start=True, stop=True),
                             r=[lhs, rhs], w=[pb])
                    for q in range(2):
                        dv = dst[:, G * 8 + q:G * 8 + 8:2, :]
                        m3 = mask[:].rearrange("p (h t) -> p h t", t=128)
                        dg = [dst.g[2 * G], dst.g[2 * G + 1]]
                        if eng == "dve":
                            P.op("dve", lambda e: e.tensor_tensor(
                                out=dv, in0=pbs[q][:].rearrange("p (h t) -> p h t", t=128), in1=m3, op=ALU.mult),
                                 r=[pbs[q], mask], w=dg)
                        else:
                            P.op("act", lambda e: e.copy(out=dv, in_=pbs[q][:].rearrange("p (h t) -> p h t", t=128)),
                                 r=[pbs[q]], w=dg)
                            P.op("pool", lambda e: e.tensor_tensor(out=dv, in0=dv, in1=m3, op=ALU.mult),
                                 r=dg + [mask], w=dg)

            yield
            hm(R["bt"], R["at"], mS, Pb[0], "dve")
            yield
            hm(R["at"], R["bt"], mST, Qb[0], "dve")
            yield
            hm(R["kt"], R["at"], mS, R["Mak"], "dve")
            yield
            hm(R["bt"], R["rt"], mIb, R["Mbr"], "actpool")
            yield
            hm(R["kt"], R["rt"], mIb, R["Mkr"], "actpool")
            yield
            T, Tt = R["T"][0], R["Tt"][0]
            g4 = lambda b_, g: b_[:, g * 4:(g + 1) * 4, :].rearrange("p h t -> p (h t)")
            for g in range(4):
                P.op("dve", lambda e: e.tensor_tensor(out=g4(T, g), in0=g4(Pb[0], g), in1=lvm[:, 0, :], op=ALU.mult),
                     r=[Pb[0].g[g], lvm], w=[T.g[g]])
                P.op("pool", lambda e: e.tensor_tensor(out=g4(T, g), in0=g4(T, g), in1=irep[:], op=ALU.add),
                     r=[T.g[g], irep], w=[T.g[g]])
                P.op("dve", lambda e: e.tensor_tensor(out=g4(Tt, g), in0=g4(Qb[0], g), in1=lvm[:, 1, :], op=ALU.mult),
                     r=[Qb[0].g[g], lvm], w=[Tt.g[g]])
                P.op("pool", lambda e: e.tensor_tensor(out=g4(Tt, g), in0=g4(Tt, g), in1=irep[:], op=ALU.add),
                     r=[Tt.g[g], irep], w=[Tt.g[g]])
            yield
            Wb = Pb[1]
            cur = 0
            for li in range(6):
                mk = lvm[:, 2 + 2 * li, :]
                T, Tt = R["T"][cur], R["Tt"][cur]
                Tn, Ttn = R["T"][1 - cur], R["Tt"][1 - cur]
                for g in range(4):
                    pw = nf()
                    for j in range(4):
                        h = g * 4 + j
                        P.op("pe", lambda e: e.matmul(pw[:, j * 128:(j + 1) * 128], lhsT=Qb[0][:, h, :], rhs=T[:, h, :],
                                                      start=True, stop=True), r=[Qb[0].g[g], T.g[g]], w=[pw])
                    if g < 3:
                        P.op("dve", lambda e: e.tensor_tensor(out=g4(Wb, g), in0=pw[:], in1=mk, op=ALU.mult),
                             r=[pw, lvm], w=[Wb.g[g]])
                    else:
                        P.op("act", lambda e: e.copy(out=g4(Wb, g), in_=pw[:]), r=[pw], w=[Wb.g[g]])
                        P.op("pool", lambda e: e.tensor_tensor(out=g4(Wb, g), in0=g4(Wb, g), in1=mk, op=ALU.mult),
                             r=[Wb.g[g], lvm], w=[Wb.g[g]])
                    if g % 2 == 1:
                        yield
                for g in range(4):
                    pm = nf()
                    for j in range(4):
                        h = g * 4 + j
                        P.op("pe", lambda e: e.matmul(pm[:, j * 128:(j + 1) * 128], lhsT=Tt[:, h, :], rhs=Wb[:, h, :],
                                                      start=True, stop=True), r=[Tt.g[g], Wb.g[g]], w=[pm])
                    P.op("dve", lambda e: e.tensor_tensor(out=g4(Tn, g), in0=pm[:], in1=g4(T, g), op=ALU.add),
                         r=[pm, T.g[g]], w=[Tn.g[g]])
                    if g % 2 == 1:
                        yield
                if li < 5:
                    for g in range(4):
                        ptt = nt()
                        for j in range(4):
                            h = g * 4 + j
                            P.op("pe", lambda e: e.transpose(out=ptt[:, j * 128:(j + 1) * 128], in_=Tn[:, h, :],
                                                             identity=ident[:]), r=[Tn.g[g], ident], w=[ptt])
                        P.op("act", lambda e: e.copy(out=g4(Ttn, g), in_=ptt[:, 0:512]), r=[ptt], w=[Ttn.g[g]])
                        if g % 2 == 1:
                            yield
                cur = 1 - cur
            R["Tf"] = R["T"][cur]

        def chain(i):
            c = order[i]
            L = ld[i % NLB]
            R = pre[i % 2]
            V = L["v"]
            T = R["Tf"]
            for g in range(2):
                pb = nf()
                for j in range(8):
                    h = g * 8 + j
                    p, q = h // 2, h % 2
                    rows = slice(q * 64, q * 64 + 64)
                    P.op("pe", lambda e: e.matmul(pb[:, j * 64:(j + 1) * 64], lhsT=R["at"][rows, p * 128:(p + 1) * 128],
                                                  rhs=Sb[rows, p, :], start=True, stop=False), r=[R["at"], Sb], w=[pb])
                    P.op("pe", lambda e: e.matmul(pb[:, j * 64:(j + 1) * 64], lhsT=R["Mak"][:, h, :],
                                                  rhs=V[:, h * 64:(h + 1) * 64], start=False, stop=True),
                         r=[R["Mak"].g[h // 4], V], w=[pb])
                P.op("act", lambda e: e.copy(out=XT[:, g * 8:(g + 1) * 8, :].rearrange("p h v -> p (h v)"), in_=pb[:]),
                     r=[pb], w=[XT.g[g]])
            yield
            for g in range(2):
                pb = nf()
                for j in range(8):
                    h = g * 8 + j
                    P.op("pe", lambda e: e.matmul(pb[:, j * 64:(j + 1) * 64], lhsT=T[:, h, :], rhs=XT[:, h, :],
                                                  start=True, stop=True), r=[T.g[h // 4], XT.g[g]], w=[pb])
                P.op("dve", lambda e: e.tensor_copy(out=UT[:, g * 8:(g + 1) * 8, :].rearrange("p h v -> p (h v)"),
                                                    in_=pb[:]), r=[pb], w=[UT.g[g]])
            yield
            y_ = yt[cnt["y"] % 2]
            cnt["y"] += 1
            for g in range(2):
                pb = nf()
                for j in range(8):
                    h = g * 8 + j
                    p, q = h // 2, h % 2
                    rows = slice(q * 64, q * 64 + 64)
                    P.op("pe", lambda e: e.matmul(pb[:, j * 64:(j + 1) * 64], lhsT=R["rt"][rows, p * 128:(p + 1) * 128],
                                                  rhs=Sb[rows, p, :], start=True, stop=False), r=[R["rt"], Sb], w=[pb])
                    P.op("pe", lambda e: e.matmul(pb[:, j * 64:(j + 1) * 64], lhsT=R["Mbr"][:, h, :], rhs=UT[:, h, :],
                                                  start=False, stop=False), r=[R["Mbr"].g[h // 4], UT.g[g]], w=[pb])
                    P.op("pe", lambda e: e.matmul(pb[:, j * 64:(j + 1) * 64], lhsT=R["Mkr"][:, h, :],
                                                  rhs=V[:, h * 64:(h + 1) * 64], start=False, stop=True),
                         r=[R["Mkr"].g[h // 4], V], w=[pb])
                P.op("act", lambda e: e.copy(out=y_[:, g * 512:(g + 1) * 512], in_=pb[:]), r=[pb], w=[y_])
            P.dma("act", y_d[d, c * 128:(c + 1) * 128, :], y_[:], r=[y_])
            yield
            pb = nf()
            for p in range(8):
                o_ = pb[:, p * 64:(p + 1) * 64]
                P.op("pe", lambda e: e.matmul(o_, lhsT=R["bpA"][:, p, :], rhs=UT[:, 2 * p, :], start=True, stop=False),
                     r=[R["bpA"]] + UT.g, w=[pb])
                P.op("pe", lambda e: e.matmul(o_, lhsT=R["bpB"][:, p, :], rhs=UT[:, 2 * p + 1, :], start=False,
                                              stop=False), r=[R["bpB"]] + UT.g, w=[pb])
                P.op("pe", lambda e: e.matmul(o_, lhsT=R["kpA"][:, p, :], rhs=V[:, (2 * p) * 64:(2 * p + 1) * 64],
                                              start=False, stop=False), r=[R["kpA"], V], w=[pb])
                P.op("pe", lambda e: e.matmul(o_, lhsT=R["kpB"][:, p, :], rhs=V[:, (2 * p + 1) * 64:(2 * p + 2) * 64],
                                              start=False, stop=True), r=[R["kpB"], V], w=[pb])
            P.op("dve", lambda e: e.tensor_tensor(out=Stmp[:].rearrange("p c v -> p (c v)"),
                                                  in0=pb[:], in1=Sf[:].rearrange("p c v -> p (c v)"), op=ALU.add),
                 r=[pb, Sf], w=[Stmp])
            P.op("dve", lambda e: e.tensor_tensor(out=Sf[:], in0=Stmp[:], in1=bc3(R["gam"][:], 64), op=ALU.mult),
                 r=[Stmp, R["gam"]], w=[Sf])
            P.op("act", lambda e: e.copy(out=Sb[:], in_=Sf[:]), r=[Sf], w=[Sb])
            yield

        def run2(gp, gc, ratio):
            done_p, done_c, k = gp is None, False, 0
            while not (done_p and done_c):
                if not done_p:
                    try:
                        next(gp)
                    except StopIteration:
                        done_p = True
                k += 1
                if not done_c and (done_p or k % ratio == 0):
                    try:
                        next(gc)
                    except StopIteration:
                        done_c = True

        n_it = len(order)
        load(0)
        if n_it > 1:
            load(1)
        for _ in precompute(0):
            pass
        for i in range(n_it):
            run2(precompute(i + 1) if i + 1 < n_it else None, chain(i), 6)
            if i + 2 < n_it:
                load(i + 2)
    C.pop()


def stage_rwkv_out(C, segs, wd, scr, mods_d, l, consts):
    P = C.P
    C.push()
    ident, epsb = consts
    (g2_d, w_o_d, ln_g_d, ln_b_d) = wd
    (y_d, v_d, sbon_d, sigG_d) = scr
    G2 = C.sb([128, D], BF16, "G2")
    Wo = C.sb([128, 8, D], BF16, "Wo")
    P.dma("pool", G2[:], g2_d, w=[G2])
    P.dma("pool", Wo[:], w_o_d.rearrange("(k p) n -> p k n", p=128), w=[Wo])
    lng = C.sb([128, D], F32, "lng")
    lnb = C.sb([128, D], F32, "lnb")
    gt = C.sb([128, D], F32, "gt")
    load_bc(C, lng, ln_g_d, "sp")
    load_bc(C, lnb, ln_b_d, "sp")
    gneps = C.sb([128, 1], F32, "gneps")
    P.op("dve", lambda e: e.memset(gneps[:], GN_EPS), w=[gneps])
    y0 = [C.sb([128, D], F32, "y0") for _ in range(2)]
    y1 = [C.sb([128, D], F32, "y1") for _ in range(2)]
    vb = [C.sb([128, D], BF16, "vb") for _ in range(2)]
    sb_ = [C.sb([128, 16], F32, "sb") for _ in range(2)]
    sg = [C.sb([128, 128], BF16, "sg") for _ in range(2)]
    xs_ = [C.sb([128, D], F32, "xs") for _ in range(2)]
    yc_l = [C.sb([128, D], F32, "yc") for _ in range(2)]
    sq_l = [C.sb([128, D], F32, "sq") for _ in range(2)]
    mean_l = [C.sb([128, 16], F32, "mean") for _ in range(2)]
    var_l = [C.sb([128, 16], F32, "var") for _ in range(2)]
    rstd_l = [C.sb([128, 16], F32, "rstd") for _ in range(2)]
    bon_l = [C.sb([128, D], F32, "bon") for _ in range(2)]
    zb_l = [C.sb([128, D], BF16, "zb") for _ in range(2)]
    zT_l = [C.sb([128, 8, 128], BF16, "zT") for _ in range(2)]
    tmp_l = [C.sb([128, D], F32, "tmp") for _ in range(2)]
    psG = [C.ps([128, 512], F32, "psG") for _ in range(2)]
    psO = [C.ps([128, 512], F32, "psO") for _ in range(2)]
    psT_l = [C.ps([128, D], BF16, "psT") for _ in range(2)]
    it = 0
    cur_which = None
    v3 = lambda b_: b_[:].rearrange("p (h e) -> p h e", e=64)
    for (xd, ntok, which, koff) in segs:
        if which != cur_which:
            cur_which = which
            load_bc(C, gt, mods_d[l, which, 5 * D:6 * D], "act")
        for r0 in range(0, ntok, 128):
            i = it % 2
            it += 1
            g0 = koff + r0
            P.dma("sp", y0[i][:], y_d[0, g0:g0 + 128, :], w=[y0[i]])
            P.dma("sp", y1[i][:], y_d[1, g0:g0 + 128, :], w=[y1[i]])
            P.dma("sp", vb[i][:], v_d[g0:g0 + 128, :], w=[vb[i]])
            P.dma("sp", sb_[i][:], sbon_d[g0:g0 + 128, :], w=[sb_[i]])
            P.dma("sp", sg[i][:], sigG_d[:, g0:g0 + 128], w=[sg[i]])
            P.dma("sp", xs_[i][:], xd[r0:r0 + 128, :], w=[xs_[i]])
            yc, sq, mean, var, rstd, bon, zb, zT, tmp, psT = (yc_l[i], sq_l[i], mean_l[i], var_l[i], rstd_l[i], bon_l[i],
                                                              zb_l[i], zT_l[i], tmp_l[i], psT_l[i])
            Y = y0[i]
            P.op("dve", lambda e: e.tensor_tensor(out=Y[:], in0=Y[:], in1=y1[i][:], op=ALU.add), r=[Y, y1[i]], w=[Y])
            P.op("dve", lambda e: e.tensor_reduce(out=mean[:], in_=v3(Y), axis=AX.X, op=ALU.add), r=[Y], w=[mean])
            P.op("dve", lambda e: e.tensor_scalar_mul(out=mean[:], in0=mean[:], scalar1=1.0 / 64), r=[mean], w=[mean])
            P.op("dve", lambda e: e.tensor_tensor(out=v3(yc), in0=v3(Y), in1=bc3(mean[:], 64), op=ALU.subtract),
                 r=[Y, mean], w=[yc])
            P.op("act", lambda e: e.activation(out=sq[:], in_=yc[:], func=AF.Square), r=[yc], w=[sq])
            P.op("dve", lambda e: e.tensor_reduce(out=var[:], in_=v3(sq), axis=AX.X, op=ALU.add), r=[sq], w=[var])
            P.op("act", lambda e: e.activation(out=var[:], in_=var[:], func=AF.Sqrt, bias=gneps[:], scale=1.0 / 64),
                 r=[var, gneps], w=[var])
            P.op("dve", lambda e: e.reciprocal(out=rstd[:], in_=var[:]), r=[var], w=[rstd])
            P.op("dve", lambda e: e.tensor_tensor(out=v3(yc), in0=v3(yc), in1=bc3(rstd[:], 64), op=ALU.mult),
                 r=[yc, rstd], w=[yc])
            P.op("pool", lambda e: e.tensor_tensor(out=yc[:], in0=yc[:], in1=lng[:], op=ALU.mult), r=[yc, lng], w=[yc])
            P.op("pool", lambda e: e.tensor_tensor(out=yc[:], in0=yc[:], in1=lnb[:], op=ALU.add), r=[yc, lnb], w=[yc])
            P.op("dve", lambda e: e.tensor_tensor(out=v3(bon), in0=v3(vb[i]), in1=bc3(sb_[i][:], 64), op=ALU.mult),
                 r=[vb[i], sb_[i]], w=[bon])
            P.op("pool", lambda e: e.tensor_tensor(out=yc[:], in0=yc[:], in1=bon[:], op=ALU.add), r=[yc, bon], w=[yc])
            for dh in range(2):
                pg = psG[dh]
                P.op("pe", lambda e: e.matmul(pg[:], lhsT=sg[i][:], rhs=G2[:, dh * 512:(dh + 1) * 512],
                                              start=True, stop=True), r=[sg[i], G2], w=[pg])
                P.op("dve", lambda e: e.tensor_tensor(out=zb[:, dh * 512:(dh + 1) * 512],
                                                      in0=pg[:], in1=yc[:, dh * 512:(dh + 1) * 512], op=ALU.mult),
                     r=[pg, yc], w=[zb])
            for kc in range(8):
                P.op("pe", lambda e: e.transpose(out=psT[:, kc * 128:(kc + 1) * 128], in_=zb[:, kc * 128:(kc + 1) * 128],
                                                 identity=ident[:]), r=[zb, ident], w=[psT])
            P.op("act", lambda e: e.copy(out=zT[:], in_=psT[:].rearrange("p (k t) -> p k t", k=8)), r=[psT], w=[zT])
            xs = xs_[i]
            for dh in range(2):
                po = psO[dh]
                for kc in range(8):
                    P.op("pe", lambda e: e.matmul(po[:], lhsT=zT[:, kc, :], rhs=Wo[:, kc, dh * 512:(dh + 1) * 512],
                                                  start=(kc == 0), stop=(kc == 7)), r=[zT, Wo], w=[po])
                P.op("dve", lambda e: e.tensor_tensor(out=tmp[:, dh * 512:(dh + 1) * 512], in0=po[:],
                                                      in1=gt[:, dh * 512:(dh + 1) * 512], op=ALU.mult),
                     r=[po, gt], w=[tmp])
                P.op("pool", lambda e: e.tensor_tensor(out=xs[:, dh * 512:(dh + 1) * 512],
                                                       in0=tmp[:, dh * 512:(dh + 1) * 512],
                                                       in1=xs[:, dh * 512:(dh + 1) * 512], op=ALU.add),
                     r=[tmp, xs], w=[xs])
            P.dma("act", xd[r0:r0 + 128, :], xs[:], r=[xs])
    C.pop()


def host_consts(S):
    pos = np.arange(S)
    row = (pos // 64).astype(np.float32)
    col = (pos % 64).astype(np.float32)
    inv = (10000.0 ** (-np.arange(8, dtype=np.float32) / 8)).astype(np.float32)
    ang_r = row[None, :] * inv[:, None]
    ang_c = col[None, :] * inv[:, None]
    cos = np.concatenate([np.cos(ang_r), np.cos(ang_r), np.cos(ang_c), np.cos(ang_c)], 0).astype(np.float32)
    sin = np.concatenate([-np.sin(ang_r), np.sin(ang_r), -np.sin(ang_c), np.sin(ang_c)], 0).astype(np.float32)
    prot = np.zeros((32, 32), np.float32)
    for i in range(32):
        j = i + 8 if (i % 16) < 8 else i - 8
        prot[j, i] = 1.0
    rwc = np.zeros((128, RWC_N), np.float32)
    rwc[0:64, 0:64] = 1.0
    rwc[64:128, 64:128] = 1.0
    rwc[0:64, 128] = 1.0
    rwc[64:128, 129] = 1.0
    ii = np.arange(128)
    for d in range(2):
        before = (ii[:, None] < ii[None, :]) if d == 0 else (ii[:, None] > ii[None, :])
        incl = before | np.eye(128, dtype=bool)
        base = RWC_TRI + d * RWC_DIR
        rwc[:, base:base + 128] = incl * DEC_C
        rwc[:, base + 128:base + 256] = before * DEC_C
        rwc[:, base + 256:base + 768] = np.tile(before.astype(np.float32), (1, 4))
        rwc[:, base + 768:base + 1280] = np.tile(before.T.astype(np.float32), (1, 4))
        rwc[:, base + 1280:base + 1792] = np.tile(incl.astype(np.float32), (1, 4))
        blkid = lambda m: (ii[:, None] // m) == (ii[None, :] // m)
        m1 = before & blkid(2)
        o = base + RWC_LV
        rwc[:, o:o + 512] = np.tile(m1.astype(np.float32), (1, 4))
        rwc[:, o + 512:o + 1024] = np.tile(m1.T.astype(np.float32), (1, 4))
        for li in range(6):
            m = 2 << li
            mm = before & blkid(2 * m) & ~blkid(m)
            o2 = o + 1024 + li * 1024
            rwc[:, o2:o2 + 512] = np.tile(mm.astype(np.float32), (1, 4))
            rwc[:, o2 + 512:o2 + 1024] = np.tile(mm.T.astype(np.float32), (1, 4))
    rwc[:, RWC_IREP:RWC_IREP + 512] = np.tile(np.eye(128, dtype=np.float32), (1, 4))
    return dict(ident=np.eye(128, dtype=np.float32), ones=np.ones((128, 128), np.float32),
                cos=np.ascontiguousarray(cos), sin=np.ascontiguousarray(sin), prot=prot, rwc=rwc)


def host_layout(inp, b):
    f = np.float32
    cond = np.stack([inp["c"][b].reshape(8, 128).T, inp["c_ctx"].reshape(8, 128).T], axis=-1).astype(f)
    mlac = np.zeros((128, 144), f)
    mlac[:, 0:3] = inp["mla_q_norm"][0].reshape(3, 128).T
    mlac[:, 3] = inp["mla_kv_norm"][0]
    mlac[0:64, 4] = inp["mla_qk_norm_q"][0][0:64]
    mlac[0:32, 5] = inp["mla_qk_norm_q"][0][64:96]
    mlac[0:64, 6] = inp["mla_qk_norm_k"][0][0:64]
    mlac[0:32, 7] = inp["mla_qk_norm_k"][0][64:96]
    dw = inp["conv_dw_w"][0][:, 0, :]
    mlac[:, 8:132] = dw.T.reshape(4, 128, 31).transpose(1, 0, 2).reshape(128, 124)
    mlac[:, 132:136] = inp["conv_dw_b"][0].reshape(4, 128).T
    mlac[:, 136:140] = inp["conv_norm_g"][0].reshape(4, 128).T
    mlac[:, 140:144] = inp["conv_norm_b"][0].reshape(4, 128).T
    d = dict(x=inp["x"][b], ctx=inp["ctx"][b], cond=np.ascontiguousarray(cond), mlac=mlac)
    for k in ("ada_w", "ada_b", "ffn_w1", "ffn_w3", "ffn_w2"):
        d[k] = inp[k]
    d["mla_w_in"] = inp["mla_w_in"][0]
    d["mla_w_qb"] = inp["mla_w_qb"][0]
    d["mla_w_kvb"] = inp["mla_w_kvb"][0]
    d["mix_w_out"] = inp["mix_w_out"][0]
    pp = lambda v: v.reshape(8, 128).T
    rwv = np.zeros((128, 88), f)
    for j in range(6):
        rwv[:, j * 8:(j + 1) * 8] = pp(inp["rwkv_x_mix"][0][j])
    rwv[:, 48:56] = pp(inp["rwkv_k_k"][0])
    rwv[:, 56:64] = pp(inp["rwkv_k_a"][0])
    rwv[:, 64:72] = pp(inp["rwkv_r_k"][0].reshape(-1))
    rwv[:, 72:80] = pp(inp["rwkv_a0"][0][0])
    rwv[:, 80:88] = pp(inp["rwkv_a0"][0][1])
    d["rwv"] = rwv
    for k in ("rwkv_w_r", "rwkv_w_k", "rwkv_w_v", "rwkv_w0", "rwkv_w1", "rwkv_w2", "rwkv_a1", "rwkv_a2",
              "rwkv_g1", "rwkv_g2", "rwkv_ln_g", "rwkv_ln_b", "rwkv_w_o"):
        d[k] = inp[k][0]
    return d


def build(S, stages=("ada", "ffn")):
    nc = bass.Bass("TRN2", target_bir_lowering=False)

    def din(name, shape, dt=F32):
        return nc.dram_tensor(name, list(shape), dt, kind="ExternalInput").ap()

    x_d = din("x", [S, D])
    ctx_d = din("ctx", [CTX, D])
    cond_d = din("cond", [128, 8, 2])
    ada_w_d = din("ada_w", [2, D, 9 * D])
    ada_b_d = din("ada_b", [2, 9 * D])
    w1_d = din("ffn_w1", [2, 2, D, DFF])
    w3_d = din("ffn_w3", [2, 2, D, DFF])
    w2_d = din("ffn_w2", [2, 2, DFF, D])
    ident_d = din("ident", [128, 128])
    ones_d = din("ones", [128, 128])
    cos_d = din("cos", [32, S])
    sin_d = din("sin", [32, S])
    prot_d = din("prot", [32, 32])
    mlac_d = din("mlac", [128, 144])
    w_in_d = din("mla_w_in", [D, 1568])
    w_qb_d = din("mla_w_qb", [384, 768])
    w_kvb_d = din("mla_w_kvb", [128, 1024])
    w_out_d = din("mix_w_out", [D, D])
    rwv_d = din("rwv", [128, 88])
    rwc_d = din("rwc", [128, RWC_N])
    rw = {k: din("rwkv_" + k, shp) for k, shp in (
        ("w_r", [D, D]), ("w_k", [D, D]), ("w_v", [D, D]), ("w0", [2, D]), ("w1", [2, D, 64]), ("w2", [2, 64, D]),
        ("a1", [2, D, 64]), ("a2", [2, 64, D]), ("g1", [D, 128]), ("g2", [128, D]), ("ln_g", [D]), ("ln_b", [D]),
        ("w_o", [D, D]))}
    out_d = nc.dram_tensor("out", [S, D], F32, kind="ExternalOutput").ap()
    C = Ctx(nc)
    P = C.P
    NK = CTX + S
    mods_d = C.dram("mods", [2, 2, 9 * D], F32)
    octx_d = C.dram("octx", [CTX, D], F32)
    qT_d = C.dram("qT", [NH, 96, S], BF16)
    qcT_d = C.dram("qcT", [NH, 96, CTX], BF16)
    kT_d = C.dram("kT", [NH, 96, NK], BF16)
    v_d = C.dram("v", [NK, 512], BF16)
    glu_l_d = C.dram("glu_l", [512, S + 30], F32)
    glu_c_d = C.dram("glu_c", [512, CTX + 30], F32)
    mixT_d = C.dram("mixT", [D, NK], BF16)
    dbg = {}
    if "dbg" in stages:
        dbg["octx"] = nc.dram_tensor("octx_o", [CTX, D], F32, kind="ExternalOutput").ap()
    C.push()
    consts = make_consts(C, ident_d)
    C.push()
    cpb = [C.sb([128, 4, D], F32) for _ in range(2)]
    i = 0
    for (src, dst, ntok) in ((x_d, out_d, S), (ctx_d, octx_d, CTX)):
        for t0 in range(0, ntok, 512):
            n = min(512, ntok - t0)
            b = cpb[i % 2]
            i += 1
            P.dma("sp", b[:, 0:n // 128, :], src[t0:t0 + n, :].rearrange("(s p) d -> p s d", p=128), w=[b])
            P.dma("sp", dst[t0:t0 + n, :].rearrange("(s p) d -> p s d", p=128), b[:, 0:n // 128, :], r=[b])
    C.pop()
    stage_ada(C, cond_d, ada_w_d, ada_b_d, mods_d)
    both = [(octx_d, CTX, 1), (out_d, S, 0)]
    if "ffn" in stages:
        stage_ffn(C, both, w1_d[0, 0], w3_d[0, 0], w2_d[0, 0], mods_d, 0, 0, consts)
    if "mla" in stages:
        wd = (w_in_d, w_qb_d, w_kvb_d, mlac_d, cos_d, sin_d, prot_d, ones_d)
        sub = [x for x in stages if x.startswith("mla_")] or ["mla_proj", "mla_conv", "mla_attn", "mla_out"]
        if "mla_proj" in sub:
            stage_mla_proj(C, [(octx_d, CTX, 1, 0, False), (out_d, S, 0, CTX, True)], S, wd, mods_d, 0, consts,
                           (qT_d, qcT_d, kT_d, v_d, glu_l_d, glu_c_d))
        if "mla_conv" in sub:
            stage_conv(C, [(glu_c_d, CTX, 0), (glu_l_d, S, CTX)], mlac_d, ones_d, mixT_d, consts)
        if "mla_attn" in sub:
            stage_attn(C, S, (qT_d, qcT_d, kT_d, v_d, mixT_d), ones_d)
        if "mla_out" in sub:
            stage_outproj(C, [(octx_d, CTX, 1, 0), (out_d, S, 0, CTX)], w_out_d, mixT_d, mods_d, 0)
    if "ffn2" in stages:
        stage_ffn(C, both, w1_d[0, 1], w3_d[0, 1], w2_d[0, 1], mods_d, 0, 6, consts)
    if "l1ffn" in stages:
        stage_ffn(C, both, w1_d[1, 0], w3_d[1, 0], w2_d[1, 0], mods_d, 1, 0, consts)
    if "rwkv" in stages:
        hT_d = C.dram("hTr", [D, NK + 4], BF16)
        rT_d = C.dram("rT", [D, NK], BF16)
        nkkT_d = C.dram("nkkT", [D, NK], BF16)
        bT_d = C.dram("bT", [2, D, NK], BF16)
        kdT_d = C.dram("kdT", [2, D, NK], BF16)
        sig_d = C.dram("sigw", [2, NK, D], F32)
        vr_d = C.dram("vr", [NK, D], BF16)
        sbon_d = C.dram("sbon", [NK, 16], F32)
        sigG_d = C.dram("sigG", [128, NK], BF16)
        y_d = C.dram("yscan", [2, NK, D], F32)
        sub = [x for x in stages if x.startswith("rw_")] or ["rw_h", "rw_proj", "rw_scan", "rw_out"]
        if "rw_h" in sub:
            stage_rwkv_h(C, [(octx_d, CTX, 1, 1), (out_d, S, 0, CTX + 3)], mods_d, 1, consts, hT_d)
        if "rw_proj" in sub:
            stage_rwkv_proj(C, [(CTX, 1, 0), (S, CTX + 3, CTX)],
                            (rw["w_r"], rw["w_k"], rw["w_v"], rw["w1"], rw["w2"], rw["a1"], rw["a2"], rw["g1"],
                             rw["w0"], rwv_d, rwc_d), hT_d,
                            (rT_d, nkkT_d, bT_d, kdT_d, sig_d, vr_d, sbon_d, sigG_d))
        if "rw_scan" in sub:
            stage_rwkv_scan(C, NK, (rT_d, nkkT_d, bT_d, kdT_d, sig_d, vr_d, y_d), rwc_d, consts)
        if "rw_out" in sub:
            stage_rwkv_out(C, [(out_d, S, 0, CTX)], (rw["g2"], rw["w_o"], rw["ln_g"], rw["ln_b"]),
                           (y_d, vr_d, sbon_d, sigG_d), mods_d, 1, consts)
    if "l1ffn2" in stages:
        stage_ffn(C, [(out_d, S, 0)], w1_d[1, 1], w3_d[1, 1], w2_d[1, 1], mods_d, 1, 6, consts)
    if "dbg" in stages:
        C.push()
        b = C.sb([128, 2, D], F32)
        P.dma("sp", b[:], octx_d.rearrange("(s p) d -> p s d", p=128), w=[b])
        P.dma("sp", dbg["octx"].rearrange("(s p) d -> p s d", p=128), b[:], r=[b])
        C.pop()
    C.pop()
    P.finish()
    return nc


ALL_STAGES = ("ada", "ffn", "mla", "ffn2", "l1ffn", "rwkv", "l1ffn2")
S_FULL = 8192


def kernel(**inputs):
    inp = {k: np.asarray(v) for k, v in inputs.items()}
    B = inp["x"].shape[0]
    S = inp["x"].shape[1]
    nc = build(S, ALL_STAGES)
    hc = host_consts(S)
    in_maps = []
    for b in range(B):
        d = host_layout(inp, b)
        d.update(hc)
        in_maps.append({k: np.ascontiguousarray(v, dtype=np.float32) for k, v in d.items()})
    res = run_bass_kernel_spmd(nc, in_maps, core_ids=list(range(B)))
    return np.stack([np.asarray(r["out"], dtype=np.float32) for r in res.results], axis=0)
```

```python
import os
import numpy as np
import concourse.bass as bass
import concourse.mybir as mybir
from concourse.bass_utils import run_bass_kernel_spmd

F32 = mybir.dt.float32
BF16 = mybir.dt.bfloat16
AF = mybir.ActivationFunctionType
ALU = mybir.AluOpType
AX = mybir.AxisListType

D = 1024
DFF = 2816
NFF = DFF // 128
EPS = 1e-6
CTX = 256


class Tk:
    __slots__ = ("w", "r")

    def __init__(self):
        self.w = None
        self.r = {}


class Buf:
    def __init__(self, t, k=None, excl=False):
        self.t = t
        self.k = k if k is not None else Tk()
        self.excl = excl

    def __getitem__(self, key):
        return self.t[key]


class GBuf:
    def __init__(self, buf, ngroups):
        self.t = buf.t
        self.g = [Buf(buf.t) for _ in range(ngroups)]

    def __getitem__(self, key):
        return self.t[key]


class Prog:
    def __init__(self, nc):
        self.nc = nc
        self.engs = {}
        for name, obj in [("pe", nc.tensor), ("act", nc.scalar), ("dve", nc.vector),
                          ("pool", nc.gpsimd), ("sp", nc.sync)]:
            self.engs[name] = dict(name=name, obj=obj, sem=nc.alloc_semaphore("s_" + name), cnt=0,
                                   waited={}, dsems=None)
        for name, n in [("sp", 12), ("pool", 8), ("act", 4)]:
            e = self.engs[name]
            e["dsems"] = [nc.alloc_semaphore(f"d_{name}{i}") for i in range(n)]
            e["dvals"] = [0] * n
            e["rr"] = 0
        self.nops = 0

    def _wait(self, e, tok):
        sem, val = tok
        key = sem.num
        if e["waited"].get(key, 0) >= val:
            return
        if e["name"] == "pe" and sem is e["sem"]:
            return
        e["obj"].wait_ge(sem, val)
        e["waited"][key] = val
        self.nops += 1

    def _deps(self, e, r, w):
        for b in r:
            k = b.k
            if k.w is not None:
                self._wait(e, k.w)
            if b.excl:
                for tok in k.r.values():
                    self._wait(e, tok)
        for b in w:
            k = b.k
            if k.w is not None:
                self._wait(e, k.w)
            for tok in k.r.values():
                self._wait(e, tok)

    def _update(self, tok, r, w):
        sem, val = tok
        for b in r:
            b.k.r[sem.num] = tok
        for b in w:
            b.k.w = tok
            b.k.r = {}

    def op(self, eng, fn, r=(), w=()):
        e = self.engs[eng]
        self._deps(e, r, w)
        ins = fn(e["obj"])
        e["cnt"] += 1
        ins.then_inc(e["sem"], 1)
        tok = (e["sem"], e["cnt"])
        self._update(tok, r, w)
        self.nops += 1
        return tok

    def dma(self, eng, out, in_, r=(), w=(), **kw):
        e = self.engs[eng]
        self._deps(e, r, w)
        i = e["rr"]
        e["rr"] = (i + 1) % len(e["dsems"])
        sem = e["dsems"][i]
        if e["dvals"][i] > 0:
            self._wait(e, (sem, e["dvals"][i]))
        ins = e["obj"].dma_start(out=out, in_=in_, **kw)
        e["dvals"][i] += 16
        ins.then_inc(sem, 16)
        tok = (sem, e["dvals"][i])
        self._update(tok, r, w)
        self.nops += 1
        return tok

    def barrier(self):
        toks = []
        for e in self.engs.values():
            if e["cnt"] > 0:
                toks.append((e["sem"], e["cnt"]))
            if e["dsems"]:
                for s, v in zip(e["dsems"], e["dvals"]):
                    if v > 0:
                        toks.append((s, v))
        for e in self.engs.values():
            for t in toks:
                if t[0] is e["sem"]:
                    continue
                self._wait(e, t)

    def finish(self):
        self.barrier()


class Ctx:
    def __init__(self, nc):
        self.nc = nc
        self.P = Prog(nc)
        self.uid = 0
        self.stack = []

    def sb(self, shape, dt, name=None):
        self.uid += 1
        cm = self.nc.sbuf_tensor(f"{name or 'sb'}_{self.uid}", list(shape), dt)
        t = cm.__enter__()
        self.stack[-1].append(cm)
        return Buf(t)

    def ps(self, shape, dt, name=None):
        self.uid += 1
        cm = self.nc.psum_tensor(f"{name or 'ps'}_{self.uid}", list(shape), dt)
        t = cm.__enter__()
        self.stack[-1].append(cm)
        return Buf(t, excl=True)

    def push(self):
        self.stack.append([])

    def pop(self):
        self.P.barrier()
        for cm in reversed(self.stack.pop()):
            cm.__exit__(None, None, None)

    def dram(self, name, shape, dt):
        return self.nc.dram_tensor(name, list(shape), dt, kind="Internal").ap()


def stage_ada(C, cond_d, ada_w_d, ada_b_d, mods_d):
    P = C.P
    C.push()
    cond = C.sb([128, 8, 2], F32)
    scond = C.sb([128, 8, 2], F32)
    wb = [C.sb([128, 8, 512], F32) for _ in range(3)]
    bb = [C.sb([2, 512], F32) for _ in range(3)]
    mrow = [C.sb([2, 9 * D], F32) for _ in range(2)]
    pss = [C.ps([2, 512], F32) for _ in range(2)]
    P.dma("sp", cond[:], cond_d, w=[cond])
    P.op("act", lambda e: e.activation(out=scond[:], in_=cond[:], func=AF.Silu), r=[cond], w=[scond])
    it = 0
    for l in range(2):
        wv = ada_w_d[l].rearrange("(k p) n -> p k n", p=128)
        for n in range(18):
            w = wb[it % 3]
            b = bb[it % 3]
            ps = pss[it % 2]
            P.dma("sp", w[:], wv[:, :, n * 512:(n + 1) * 512], w=[w])
            P.dma("sp", b[:], ada_b_d[l, n * 512:(n + 1) * 512].partition_broadcast(2), w=[b])
            for kc in range(8):
                P.op("pe", lambda e, kc=kc, w=w, ps=ps: e.matmul(ps[:], lhsT=scond[:, kc, :], rhs=w[:, kc, :],
                                                              start=(kc == 0), stop=(kc == 7)),
                     r=[scond, w], w=[ps])
            P.op("dve", lambda e, ps=ps, b=b, l=l, n=n: e.tensor_tensor(out=mrow[l][:, n * 512:(n + 1) * 512],
                                                                      in0=ps[:], in1=b[:], op=ALU.add),
                 r=[ps, b], w=[mrow[l]])
            it += 1
        P.dma("sp", mods_d[l], mrow[l][:], r=[mrow[l]])
    C.pop()


def load_bc(C, dst, src_row, eng="sp"):
    C.P.dma(eng, dst[:], src_row.partition_broadcast(128), w=[dst])


def hT_elem(C, xs, sc1, sh, tmp, hb, stat, epsb):
    P = C.P
    junk, ssq, rt, rstd = stat
    P.op("act", lambda e: e.activation(out=junk[:], in_=xs[:], func=AF.Square, accum_out=ssq[:]),
         r=[xs], w=[junk, ssq])
    P.op("act", lambda e: e.activation(out=rt[:], in_=ssq[:], func=AF.Sqrt, bias=epsb[:], scale=1.0 / D),
         r=[ssq, epsb], w=[rt])
    P.op("dve", lambda e: e.reciprocal(out=rstd[:], in_=rt[:]), r=[rt], w=[rstd])
    P.op("dve", lambda e: e.scalar_tensor_tensor(out=tmp[:], in0=xs[:], scalar=rstd[:, 0:1],
                                                 in1=sc1[:], op0=ALU.mult, op1=ALU.mult),
         r=[xs, rstd, sc1], w=[tmp])
    P.op("pool", lambda e: e.tensor_tensor(out=hb[:], in0=tmp[:], in1=sh[:], op=ALU.add),
         r=[tmp, sh], w=[hb])


def hT_tr(C, sub, hT, hb, psT, ident):
    P = C.P
    for kc in range(8):
        P.op("pe", lambda e, kc=kc: e.transpose(out=psT[:, kc * 128:(kc + 1) * 128],
                                                in_=hb[:, kc * 128:(kc + 1) * 128], identity=ident[:]),
             r=[hb, ident], w=[psT])
    P.op("act", lambda e: e.copy(out=hT[:, :, sub * 128:(sub + 1) * 128],
                                 in_=psT[:].rearrange("p (k t) -> p k t", k=8)),
         r=[psT], w=[hT])


def make_hT_sub(C, xs, sub, sc1, sh, hT, tmp, hb, psT, ident, stat, epsb):
    hT_elem(C, xs, sc1, sh, tmp, hb, stat, epsb)
    hT_tr(C, sub, hT, hb, psT, ident)


def stage_ffn(C, segs, w1_d, w3_d, w2_d, mods_d, l, mbase, consts):
    P = C.P
    C.push()
    W1 = C.sb([128, 8, DFF], BF16, "W1")
    W3 = C.sb([128, 8, DFF], BF16, "W3")
    W2 = C.sb([128, NFF, D], BF16, "W2")
    P.dma("pool", W1[:], w1_d.rearrange("(k p) n -> p k n", p=128), w=[W1])
    P.dma("pool", W3[:], w3_d.rearrange("(k p) n -> p k n", p=128), w=[W3])
    P.dma("pool", W2[:], w2_d.rearrange("(k p) n -> p k n", p=128), w=[W2])
    ident, epsb = consts
    sc1 = C.sb([128, D], F32)
    sh = C.sb([128, D], F32)
    gt = C.sb([128, D], F32)
    NXB = 3
    xbs = [C.sb([128, D], F32, "xs") for _ in range(NXB)]
    hT = C.sb([128, 8, 512], BF16, "hT")
    gT = C.sb([128, NFF, 512], BF16, "gT")
    tmp = C.sb([128, D], F32)
    hb = C.sb([128, D], BF16)
    junk = C.sb([128, D], BF16)
    ssq = C.sb([128, 1], F32)
    rt = C.sb([128, 1], F32)
    rstd = C.sb([128, 1], F32)
    sil = [C.sb([128, 512], BF16, "sil") for _ in range(2)]
    psT = C.ps([128, D], BF16, "psT")
    ps1 = [C.ps([128, 512], F32, "ps1") for _ in range(2)]
    ps3 = [C.ps([128, 512], F32, "ps3") for _ in range(2)]
    pso = [C.ps([128, 512], F32, "pso") for _ in range(2)]
    stat = (junk, ssq, rt, rstd)
    cur_which = None
    tiles = []
    for (xd, ntok, which) in segs:
        t0 = 0
        while t0 < ntok:
            n = min(512, ntok - t0)
            tiles.append((xd, t0, n, which))
            t0 += n
    xk = dict(i=0)

    def get_x(xd, r0):
        xb = xbs[xk["i"] % NXB]
        xk["i"] += 1
        P.dma("sp", xb[:], xd[r0:r0 + 128, :], w=[xb])
        return xb

    hTs = [hT, C.sb([128, 8, 512], BF16, "hT2")]
    ELEM_AT = {2: 0, 7: 1, 12: 2, 17: 3}
    TR_AT = {5: 0, 10: 1, 15: 2, 20: 3}
    it = 0
    oi = 0
    prefetched = False
    for ti, (xd, t0, n, which) in enumerate(tiles):
        nsub = n // 128
        hTc = hTs[ti % 2]
        if which != cur_which:
            cur_which = which
            load_bc(C, sh, mods_d[l, which, (mbase + 0) * D:(mbase + 1) * D], "act")
            load_bc(C, sc1, mods_d[l, which, (mbase + 1) * D:(mbase + 2) * D], "act")
            load_bc(C, gt, mods_d[l, which, (mbase + 2) * D:(mbase + 3) * D], "act")
            P.op("dve", lambda e: e.tensor_scalar_add(out=sc1[:], in0=sc1[:], scalar1=1.0), r=[sc1], w=[sc1])
            P.op("dve", lambda e: e.tensor_scalar_mul(out=gt[:], in0=gt[:], scalar1=0.5), r=[gt], w=[gt])
        if not prefetched:
            for sub in range(nsub):
                xs = get_x(xd, t0 + sub * 128)
                make_hT_sub(C, xs, sub, sc1, sh, hTc, tmp, hb, psT, ident, stat, epsb)
        nxt_t = tiles[ti + 1] if ti + 1 < len(tiles) else None
        do_pf = nxt_t is not None and nxt_t[3] == which
        for f in range(NFF):
            p1 = ps1[it % 2]
            p3 = ps3[it % 2]
            sl = sil[it % 2]
            it += 1
            for kc in range(8):
                P.op("pe", lambda e, kc=kc, f=f, p1=p1: e.matmul(p1[:, 0:n], lhsT=W1[:, kc, f * 128:(f + 1) * 128],
                                                                rhs=hTc[:, kc, 0:n], start=(kc == 0), stop=(kc == 7)),
                     r=[W1, hTc], w=[p1])
            for kc in range(8):
                P.op("pe", lambda e, kc=kc, f=f, p3=p3: e.matmul(p3[:, 0:n], lhsT=W3[:, kc, f * 128:(f + 1) * 128],
                                                                rhs=hTc[:, kc, 0:n], start=(kc == 0), stop=(kc == 7)),
                     r=[W3, hTc], w=[p3])
            P.op("act", lambda e, p1=p1, sl=sl: e.activation(out=sl[:, 0:n], in_=p1[:, 0:n], func=AF.Silu),
                 r=[p1], w=[sl])
            P.op("dve", lambda e, p3=p3, sl=sl, f=f: e.tensor_tensor(out=gT[:, f, 0:n], in0=sl[:, 0:n], in1=p3[:, 0:n],
                                                                    op=ALU.mult), r=[sl, p3], w=[gT])
            if do_pf:
                nxd, nt0, nn, _ = nxt_t
                if f in ELEM_AT and ELEM_AT[f] < nn // 128:
                    xs = get_x(nxd, nt0 + ELEM_AT[f] * 128)
                    hT_elem(C, xs, sc1, sh, tmp, hb, stat, epsb)
                if f in TR_AT and TR_AT[f] < nn // 128:
                    hT_tr(C, TR_AT[f], hTs[(ti + 1) % 2], hb, psT, ident)
        prefetched = do_pf
        for sub in range(nsub):
            xs = get_x(xd, t0 + sub * 128)
            for dh in range(2):
                po = pso[oi % 2]
                oi += 1
                for f in range(NFF):
                    P.op("pe", lambda e, f=f, sub=sub, dh=dh, po=po: e.matmul(
                        po[:], lhsT=gT[:, f, sub * 128:(sub + 1) * 128], rhs=W2[:, f, dh * 512:(dh + 1) * 512],
                        start=(f == 0), stop=(f == NFF - 1)), r=[gT, W2], w=[po])
                P.op("dve", lambda e, po=po, dh=dh: e.tensor_tensor(out=tmp[:, dh * 512:(dh + 1) * 512], in0=po[:],
                                                                   in1=gt[:, dh * 512:(dh + 1) * 512],
                                                                   op=ALU.mult), r=[po, gt], w=[tmp])
                P.op("pool", lambda e, dh=dh, xs=xs: e.tensor_tensor(
                    out=xs[:, dh * 512:(dh + 1) * 512], in0=tmp[:, dh * 512:(dh + 1) * 512],
                    in1=xs[:, dh * 512:(dh + 1) * 512], op=ALU.add), r=[tmp, xs], w=[xs])
            P.dma("act", xd[t0 + sub * 128:t0 + (sub + 1) * 128, :], xs[:], r=[xs])
    C.pop()


ATT_SCALE = 96 ** -0.5
NH = 8


def rms_bc(C, ss_ps, n_feat, rows, n, rt, rstd, epsb):
    P = C.P
    P.op("act", lambda e: e.activation(out=rt[0:rows, 0:n], in_=ss_ps[0:rows, 0:n], func=AF.Ln,
                                       bias=epsb[0:rows, :], scale=1.0 / n_feat), r=[ss_ps, epsb], w=[rt])
    P.op("act", lambda e: e.activation(out=rstd[0:rows, 0:n], in_=rt[0:rows, 0:n], func=AF.Exp, scale=-0.5),
         r=[rt], w=[rstd])


def stage_mla_proj(C, segs, S, wd, mods_d, l, consts, scr):
    P = C.P
    C.push()
    ident, epsb = consts
    w_in_d, w_qb_d, w_kvb_d, mlac_d, cos_d, sin_d, prot_d, ones_d = wd
    qT_d, qcT_d, kT_d, v_d, glu_l_d, glu_c_d = scr
    Win = C.sb([128, 8, 1568], BF16, "Win")
    Wqb = C.sb([128, 3, 768], BF16, "Wqb")
    Wkvb = C.sb([128, 1024], BF16, "Wkvb")
    P.dma("pool", Win[:], w_in_d.rearrange("(k p) n -> p k n", p=128), w=[Win])
    P.dma("pool", Wqb[:], w_qb_d.rearrange("(k p) n -> p k n", p=128), w=[Wqb])
    P.dma("pool", Wkvb[:], w_kvb_d, w=[Wkvb])
    mlac = C.sb([128, 144], F32, "mlac")
    P.dma("sp", mlac[:], mlac_d, w=[mlac])
    onesf = C.sb([128, 128], F32, "onesf")
    onesb = C.sb([128, 128], BF16, "onesb")
    prot = C.sb([32, 32], F32, "prot")
    P.dma("sp", onesf[:], ones_d, w=[onesf])
    P.dma("sp", prot[:], prot_d, w=[prot])
    P.op("dve", lambda e: e.tensor_copy(out=onesb[:], in_=onesf[:]), r=[onesf], w=[onesb])
    zero = C.sb([128, 4, 16], F32, "zero")
    P.op("dve", lambda e: e.memset(zero[:], 0.0), w=[zero])
    if "stop1" in os.environ.get("KDBG", ""):
        C.pop()
        return
    KDBG = os.environ.get("KDBG", "")
    for (gd, ntok) in ((glu_l_d, S), (glu_c_d, CTX)):
        if "nopad" in KDBG:
            break
        gv = gd.rearrange("(c p) t -> p c t", p=128)
        P.dma("sp", gv[:, :, 0:15], zero[:, :, 0:15], r=[zero])
        P.dma("sp", gv[:, :, 15 + ntok:30 + ntok], zero[:, :, 0:15], r=[zero])
    sc1 = C.sb([128, D], F32)
    sh = C.sb([128, D], F32)
    NXB = 3
    xbs = [C.sb([128, D], F32, "xs") for _ in range(NXB)]
    hT = C.sb([128, 8, 512], BF16, "hT")
    tmp = C.sb([128, D], F32)
    hb = C.sb([128, D], BF16)
    junk = C.sb([128, D], BF16)
    ssq = C.sb([128, 1], F32)
    rt1 = C.sb([128, 1], F32)
    rstd1 = C.sb([128, 1], F32)
    stat = (junk, ssq, rt1, rstd1)
    zq = C.sb([128, 3, 512], F32, "zq")
    sqb = [C.sb([128, 512], BF16, "sqb") for _ in range(2)]
    cqn = C.sb([128, 3, 512], BF16, "cqn")
    zkv = C.sb([128, 512], F32, "zkv")
    ckvn = C.sb([128, 512], BF16, "ckvn")
    kr_raw = C.sb([32, 512], F32, "kr_raw")
    sq_kr = C.sb([32, 512], BF16, "sq_kr")
    rts = [C.sb([128, 512], F32, "rt") for _ in range(3)]
    rstds = [C.sb([128, 512], F32, "rstd") for _ in range(3)]
    rr = dict(i=0)

    def nrs():
        rr["i"] += 1
        return rts[rr["i"] % 3], rstds[rr["i"] % 3]
    sig = [C.sb([128, 512], F32, "sig") for _ in range(2)]
    glu = [C.sb([128, 512], F32, "glu") for _ in range(2)]
    cos = C.sb([32, 512], F32, "cos")
    sin = C.sb([32, 512], F32, "sin")
    hn_o = [C.sb([64, 512], BF16, "hn_o") for _ in range(2)]
    hr = [C.sb([32, 512], F32, "hr") for _ in range(2)]
    t1 = [C.sb([32, 512], F32, "t1") for _ in range(2)]
    t2 = [C.sb([32, 512], F32, "t2") for _ in range(2)]
    hr_o = [C.sb([32, 512], BF16, "hr_o") for _ in range(2)]
    sqn = [C.sb([64, 512], BF16, "sqn") for _ in range(2)]
    sqr = [C.sb([32, 512], BF16, "sqr") for _ in range(2)]
    vt = [C.sb([128, 512], BF16, "vt") for _ in range(2)]
    psT = C.ps([128, D], BF16, "psT")
    psA = [C.ps([128, 512], F32, "psA") for _ in range(int(os.environ.get("NPSA", "3")))]
    psB = [C.ps([128, 512], F32, "psB") for _ in range(2)]
    psR = [C.ps([32, 512], F32, "psR") for _ in range(2)]
    cnt = dict(a=0, b=0, r=0, g=0, h=0, v=0, s=0)

    def nxt(lst, key):
        i = cnt[key]
        cnt[key] = i + 1
        return lst[i % len(lst)]

    cur_which = None
    for (xd, ntok, which, koff, is_lat) in segs:
        for t0 in range(0, ntok, 512):
            n = min(512, ntok - t0)
            nsub = n // 128
            if which != cur_which:
                cur_which = which
                load_bc(C, sh, mods_d[l, which, 3 * D:4 * D], "act")
                load_bc(C, sc1, mods_d[l, which, 4 * D:5 * D], "act")
                P.op("dve", lambda e: e.tensor_scalar_add(out=sc1[:], in0=sc1[:], scalar1=1.0), r=[sc1], w=[sc1])
            if is_lat:
                P.dma("sp", cos[:, 0:n], cos_d[:, t0:t0 + n], w=[cos])
                P.dma("sp", sin[:, 0:n], sin_d[:, t0:t0 + n], w=[sin])
            for sub in range(nsub):
                xs = xbs[cnt["s"] % NXB]
                cnt["s"] += 1
                P.dma("sp", xs[:], xd[t0 + sub * 128:t0 + (sub + 1) * 128, :], w=[xs])
                make_hT_sub(C, xs, sub, sc1, sh, hT, tmp, hb, psT, ident, stat, epsb)

            def proj(ps, col0, ncol):
                for kc in range(8):
                    P.op("pe", lambda e, kc=kc: e.matmul(ps[0:ncol, 0:n], lhsT=Win[:, kc, col0:col0 + ncol],
                                                         rhs=hT[:, kc, 0:n], start=(kc == 0), stop=(kc == 7)),
                         r=[Win, hT], w=[ps])

            if "stop2" in KDBG:
                continue
            pss = nxt(psB, "b")
            for c in range(3):
                ps = nxt(psA, "a")
                proj(ps, c * 128, 128)
                sq = nxt(sqb, "g")
                if "nosq" not in KDBG:
                    P.op("act", lambda e, ps=ps, sq=sq: e.activation(out=sq[:, 0:n], in_=ps[:, 0:n], func=AF.Square),
                         r=[ps], w=[sq])
                if "nocp" not in KDBG:
                    P.op("dve", lambda e, ps=ps, c=c: e.tensor_copy(out=zq[:, c, 0:n], in_=ps[:, 0:n]), r=[ps], w=[zq])
                if "cq1" in KDBG:
                    continue
                P.op("pe", lambda e, sq=sq, c=c: e.matmul(pss[:, 0:n], lhsT=onesb[:], rhs=sq[:, 0:n],
                                                         start=(c == 0), stop=(c == 2)), r=[onesb, sq], w=[pss])
            if "cq1" in KDBG or "cq2" in KDBG:
                continue
            rt, rstd = nrs()
            rms_bc(C, pss, 384, 128, n, rt, rstd, epsb)
            if "cq3" in KDBG:
                continue
            for c in range(3):
                P.op("dve", lambda e, c=c: e.scalar_tensor_tensor(out=cqn[:, c, 0:n], in0=zq[:, c, 0:n],
                                                                  scalar=mlac[:, c:c + 1], in1=rstd[:, 0:n],
                                                                  op0=ALU.mult, op1=ALU.mult),
                     r=[zq, mlac, rstd], w=[cqn])
            if "stop3" in KDBG:
                continue
            ps = nxt(psA, "a")
            proj(ps, 384, 128)
            sq = nxt(sqb, "g")
            P.op("act", lambda e, ps=ps, sq=sq: e.activation(out=sq[:, 0:n], in_=ps[:, 0:n], func=AF.Square),
                 r=[ps], w=[sq])
            P.op("dve", lambda e, ps=ps: e.tensor_copy(out=zkv[:, 0:n], in_=ps[:, 0:n]), r=[ps], w=[zkv])
            pss = nxt(psB, "b")
            P.op("pe", lambda e, sq=sq: e.matmul(pss[:, 0:n], lhsT=onesb[:], rhs=sq[:, 0:n], start=True, stop=True),
                 r=[onesb, sq], w=[pss])
            rt, rstd = nrs()
            rms_bc(C, pss, 128, 128, n, rt, rstd, epsb)
            P.op("dve", lambda e: e.scalar_tensor_tensor(out=ckvn[:, 0:n], in0=zkv[:, 0:n], scalar=mlac[:, 3:4],
                                                         in1=rstd[:, 0:n], op0=ALU.mult, op1=ALU.mult),
                 r=[zkv, mlac, rstd], w=[ckvn])
            if "stop4" in KDBG:
                continue
            ps = nxt(psA, "a")
            proj(ps, 512, 32)
            P.op("act", lambda e, ps=ps: e.activation(out=sq_kr[:, 0:n], in_=ps[0:32, 0:n], func=AF.Square),
                 r=[ps], w=[sq_kr])
            P.op("dve", lambda e, ps=ps: e.tensor_copy(out=kr_raw[:, 0:n], in_=ps[0:32, 0:n]), r=[ps], w=[kr_raw])
            gd = glu_l_d if is_lat else glu_c_d
            for c in range(4):
                if "noglu" in KDBG:
                    break
                pa = nxt(psA, "a")
                proj(pa, 544 + c * 128, 128)
                pg = nxt(psA, "a")
                proj(pg, 1056 + c * 128, 128)
                sg = nxt(sig, "h")
                gl = glu[cnt["h"] % 2]
                P.op("act", lambda e, pg=pg, sg=sg: e.activation(out=sg[:, 0:n], in_=pg[:, 0:n], func=AF.Sigmoid),
                     r=[pg], w=[sg])
                P.op("dve", lambda e, pa=pa, sg=sg, gl=gl: e.tensor_tensor(out=gl[:, 0:n], in0=sg[:, 0:n],
                                                                          in1=pa[:, 0:n], op=ALU.mult),
                     r=[sg, pa], w=[gl])
                P.dma("act", gd[c * 128:(c + 1) * 128, 15 + t0:15 + t0 + n], gl[:, 0:n], r=[gl])

            def head_side(ps_n, ps_r_or_raw, raw_is_sbuf, sq_r_shared, gcol_n, gcol_r, dst, dcol0, rope):
                sn = nxt(sqn, "v")
                P.op("act", lambda e: e.activation(out=sn[:, 0:n], in_=ps_n[0:64, 0:n], func=AF.Square),
                     r=[ps_n], w=[sn])
                if sq_r_shared is None:
                    sr = sqr[cnt["v"] % 2]
                    P.op("act", lambda e: e.activation(out=sr[:, 0:n], in_=ps_r_or_raw[0:32, 0:n], func=AF.Square),
                         r=[ps_r_or_raw], w=[sr])
                else:
                    sr = sq_r_shared
                pss = nxt(psB, "b")
                P.op("pe", lambda e: e.matmul(pss[0:64, 0:n], lhsT=onesb[0:64, 0:64], rhs=sn[:, 0:n],
                                              start=True, stop=False), r=[onesb, sn], w=[pss])
                P.op("pe", lambda e: e.matmul(pss[0:64, 0:n], lhsT=onesb[0:32, 0:64], rhs=sr[:, 0:n],
                                              start=False, stop=True), r=[onesb, sr], w=[pss])
                rt, rstd = nrs()
                rms_bc(C, pss, 96, 64, n, rt, rstd, epsb)
                ho = nxt(hn_o, "r")
                P.op("dve", lambda e: e.scalar_tensor_tensor(out=ho[:, 0:n], in0=ps_n[0:64, 0:n],
                                                             scalar=mlac[0:64, gcol_n:gcol_n + 1],
                                                             in1=rstd[0:64, 0:n], op0=ALU.mult, op1=ALU.mult),
                     r=[ps_n, mlac, rstd], w=[ho])
                P.dma("act", dst[0:64, dcol0:dcol0 + n], ho[:, 0:n], r=[ho])
                i = cnt["r"]
                h_r = hr[i % 2]
                ro = hr_o[i % 2]
                if rope:
                    P.op("dve", lambda e: e.scalar_tensor_tensor(out=h_r[:, 0:n], in0=ps_r_or_raw[0:32, 0:n],
                                                                 scalar=mlac[0:32, gcol_r:gcol_r + 1],
                                                                 in1=rstd[0:32, 0:n], op0=ALU.mult, op1=ALU.mult),
                         r=[ps_r_or_raw, mlac, rstd], w=[h_r])
                    pr = nxt(psR, "s")
                    P.op("pe", lambda e: e.matmul(pr[:, 0:n], lhsT=prot[:], rhs=h_r[:, 0:n], start=True, stop=True),
                         r=[prot, h_r], w=[pr])
                    a1 = t1[i % 2]
                    a2 = t2[i % 2]
                    P.op("pool", lambda e: e.tensor_tensor(out=a1[:, 0:n], in0=h_r[:, 0:n], in1=cos[:, 0:n],
                                                           op=ALU.mult), r=[h_r, cos], w=[a1])
                    P.op("dve", lambda e: e.tensor_tensor(out=a2[:, 0:n], in0=pr[:, 0:n], in1=sin[:, 0:n],
                                                          op=ALU.mult), r=[pr, sin], w=[a2])
                    P.op("pool", lambda e: e.tensor_tensor(out=ro[:, 0:n], in0=a1[:, 0:n], in1=a2[:, 0:n],
                                                           op=ALU.add), r=[a1, a2], w=[ro])
                else:
                    P.op("dve", lambda e: e.scalar_tensor_tensor(out=ro[:, 0:n], in0=ps_r_or_raw[0:32, 0:n],
                                                                 scalar=mlac[0:32, gcol_r:gcol_r + 1],
                                                                 in1=rstd[0:32, 0:n], op0=ALU.mult, op1=ALU.mult),
                         r=[ps_r_or_raw, mlac, rstd], w=[ro])
                P.dma("act", dst[64:96, dcol0:dcol0 + n], ro[:, 0:n], r=[ro])

            for h in range(NH):
                if "noheads" in KDBG:
                    break
                pqn = nxt(psA, "a")
                for c in range(3):
                    P.op("pe", lambda e, c=c: e.matmul(pqn[0:64, 0:n], lhsT=Wqb[:, c, h * 96:h * 96 + 64],
                                                       rhs=cqn[:, c, 0:n], start=(c == 0), stop=(c == 2)),
                         r=[Wqb, cqn], w=[pqn])
                pqr = nxt(psA, "a")
                for c in range(3):
                    P.op("pe", lambda e, c=c: e.matmul(pqr[0:32, 0:n], lhsT=Wqb[:, c, h * 96 + 64:h * 96 + 96],
                                                       rhs=cqn[:, c, 0:n], start=(c == 0), stop=(c == 2)),
                         r=[Wqb, cqn], w=[pqr])
                if is_lat:
                    head_side(pqn, pqr, False, None, 4, 5, qT_d[h], t0, True)
                else:
                    head_side(pqn, pqr, False, None, 4, 5, qcT_d[h], t0, False)
                pkn = nxt(psA, "a")
                P.op("pe", lambda e: e.matmul(pkn[0:64, 0:n], lhsT=Wkvb[:, h * 128:h * 128 + 64], rhs=ckvn[:, 0:n],
                                              start=True, stop=True), r=[Wkvb, ckvn], w=[pkn])
                head_side(pkn, kr_raw, True, sq_kr, 6, 7, kT_d[h], koff + t0, is_lat)
            for sub in range(nsub):
                if "nov" in KDBG:
                    break
                pv = nxt(psA, "a")
                P.op("pe", lambda e, sub=sub: e.matmul(
                    pv[:].rearrange("p (h e) -> p h e", e=64), lhsT=ckvn[:, sub * 128:(sub + 1) * 128],
                    rhs=Wkvb[:].rearrange("p (h e) -> p h e", e=128)[:, :, 64:128], start=True, stop=True),
                     r=[ckvn, Wkvb], w=[pv])
                vb = nxt(vt, "v")
                P.op("act", lambda e: e.copy(out=vb[:], in_=pv[:]), r=[pv], w=[vb])
                r0 = koff + t0 + sub * 128
                P.dma("act", v_d[r0:r0 + 128, :], vb[:], r=[vb])
    C.pop()


def stage_conv(C, segs, mlac_d, ones_d, scr, consts):
    P = C.P
    C.push()
    ident, epsb = consts
    mixT_d = scr
    mlac = C.sb([128, 144], F32, "mlac")
    onesf = C.sb([128, 128], F32, "onesf")
    P.dma("sp", mlac[:], mlac_d, w=[mlac])
    P.dma("sp", onesf[:], ones_d, w=[onesf])
    G = [C.sb([128, 4, 542], F32, "G") for _ in range(2)]
    Gb = [C.sb([128, 4, 542], BF16, "Gb") for _ in range(2)]
    dg = C.sb([128, 124, 128], BF16, "dg")
    identb = C.sb([128, 128], BF16, "identb")
    P.op("dve", lambda e: e.tensor_copy(out=identb[:], in_=ident[:]), r=[ident], w=[identb])
    for cj in range(124):
        P.op("dve" if cj % 2 == 0 else "pool",
             lambda e: e.tensor_scalar_mul(out=dg[:, cj, :], in0=identb[:], scalar1=mlac[:, 8 + cj:9 + cj]),
             r=[identb, mlac], w=[dg])
    psc = [C.ps([128, 512], F32, "psc") for _ in range(4)]
    acc = [C.sb([128, 512], F32, "acc") for _ in range(4)]
    sq = [C.sb([128, 512], F32, "sq") for _ in range(2)]
    mean = C.sb([128, 512], F32, "mean")
    m2 = C.sb([128, 512], F32, "m2")
    var = C.sb([128, 512], F32, "var")
    rt = C.sb([128, 512], F32, "rt")
    rstd = C.sb([128, 512], F32, "rstd")
    tt = [C.sb([128, 512], F32, "tt") for _ in range(2)]
    ob = [C.sb([128, 512], BF16, "ob") for _ in range(2)]
    ps1 = C.ps([128, 512], F32, "ps1")
    ps2 = C.ps([128, 512], F32, "ps2")
    W0 = 8
    it = 0
    for (gd, ntok, koff) in segs:
        gv = gd.rearrange("(c p) t -> p c t", p=128)
        for t0 in range(0, ntok, 512):
            n = min(512, ntok - t0)
            g = G[it % 2]
            it += 1
            P.dma("sp", g[:, :, 0:n + 30], gv[:, :, t0:t0 + n + 30], w=[g])
            gb = Gb[it % 2]
            P.op("act", lambda e: e.copy(out=gb[:, :, 0:n + 30], in_=g[:, :, 0:n + 30]), r=[g], w=[gb])
            for c in range(4):
                a = acc[c]
                pc = psc[c]
                for j in range(31):
                    P.op("pe", lambda e: e.matmul(pc[:, 0:n], lhsT=dg[:, c * 31 + j, :], rhs=gb[:, c, j:j + n],
                                                  start=(j == 0), stop=(j == 30)), r=[dg, gb], w=[pc])
                P.op("act", lambda e: e.activation(out=a[:, 0:n], in_=pc[:, 0:n], func=AF.Identity,
                                                   bias=mlac[:, 132 + c:133 + c], scale=1.0), r=[pc, mlac], w=[a])
            for c in range(4):
                a = acc[c]
                s_ = sq[c % 2]
                P.op("act", lambda e, a=a, s_=s_: e.activation(out=s_[:, 0:n], in_=a[:, 0:n], func=AF.Square),
                     r=[a], w=[s_])
                P.op("pe", lambda e, a=a, c=c: e.matmul(ps1[:, 0:n], lhsT=onesf[:], rhs=a[:, 0:n],
                                                       start=(c == 0), stop=(c == 3)), r=[onesf, a], w=[ps1])
                P.op("pe", lambda e, s_=s_, c=c: e.matmul(ps2[:, 0:n], lhsT=onesf[:], rhs=s_[:, 0:n],
                                                         start=(c == 0), stop=(c == 3)), r=[onesf, s_], w=[ps2])
            P.op("act", lambda e: e.activation(out=mean[:, 0:n], in_=ps1[:, 0:n], func=AF.Copy, scale=1.0 / 512),
                 r=[ps1], w=[mean])
            P.op("dve", lambda e: e.tensor_tensor(out=m2[:, 0:n], in0=mean[:, 0:n], in1=mean[:, 0:n], op=ALU.mult),
                 r=[mean], w=[m2])
            P.op("dve", lambda e: e.scalar_tensor_tensor(out=var[:, 0:n], in0=ps2[:, 0:n], scalar=1.0 / 512,
                                                         in1=m2[:, 0:n], op0=ALU.mult, op1=ALU.subtract),
                 r=[ps2, m2], w=[var])
            P.op("act", lambda e: e.activation(out=rt[:, 0:n], in_=var[:, 0:n], func=AF.Ln, bias=epsb[:],
                                               scale=1.0), r=[var, epsb], w=[rt])
            P.op("act", lambda e: e.activation(out=rstd[:, 0:n], in_=rt[:, 0:n], func=AF.Exp, scale=-0.5),
                 r=[rt], w=[rstd])
            for c in range(4):
                a = acc[c]
                t_ = tt[c % 2]
                o_ = ob[c % 2]
                P.op("dve", lambda e, a=a, t_=t_: e.tensor_tensor(out=t_[:, 0:n], in0=a[:, 0:n], in1=mean[:, 0:n],
                                                                 op=ALU.subtract), r=[a, mean], w=[t_])
                P.op("pool", lambda e, t_=t_: e.tensor_tensor(out=t_[:, 0:n], in0=t_[:, 0:n], in1=rstd[:, 0:n],
                                                             op=ALU.mult), r=[t_, rstd], w=[t_])
                P.op("act", lambda e, t_=t_, o_=o_, c=c: e.activation(out=o_[:, 0:n], in_=t_[:, 0:n], func=AF.Silu,
                                                                     bias=mlac[:, 140 + c:141 + c],
                                                                     scale=mlac[:, 136 + c:137 + c]),
                     r=[t_, mlac], w=[o_])
                P.dma("act", mixT_d[512 + c * 128:512 + (c + 1) * 128, koff + t0:koff + t0 + n], o_[:, 0:n], r=[o_])
    C.pop()


def stage_attn(C, S, scr, ones_d, do_ctx=True):
    P = C.P
    C.push()
    qT_d, qcT_d, kT_d, v_d, mixT_d = scr
    NK = CTX + S
    NKC = NK // 128
    onesf = C.sb([128, 128], F32, "onesf")
    P.dma("sp", onesf[:], ones_d, w=[onesf])
    kTs = [C.sb([96, NK], BF16, "kT") for _ in range(2)]
    Vs = [C.sb([128, NKC, 65], BF16, "V") for _ in range(2)]
    for V in Vs:
        P.op("dve", lambda e, V=V: e.memset(V[:, :, 64:65], 1.0), w=[V])
    qs = [C.sb([96, 512], BF16, "q") for _ in range(2)]
    pTs = [C.sb([128, 512], BF16, "pT") for _ in range(3)]
    oT = [C.sb([65, 512], F32, "oT") for _ in range(2)]
    rden = [C.sb([65, 512], F32, "rden") for _ in range(2)]
    att = [C.sb([64, 512], BF16, "att") for _ in range(2)]
    psS = [C.ps([128, 512], F32, "psS") for _ in range(3)]
    psO = [C.ps([65, 512], F32, "psO") for _ in range(2)]
    psB = [C.ps([64, 512], F32, "psB") for _ in range(2)]
    vv = v_d.rearrange("(kc p) (h e) -> p kc h e", p=128, e=64)
    si = 0
    qi = 0
    for h in range(NH):
        kT = kTs[h % 2]
        V = Vs[h % 2]
        P.dma("sp", kT[:], kT_d[h], w=[kT])
        P.dma("pool", V[:, :, 0:64], vv[:, :, h, :], w=[V])
        qtiles = [(qT_d, t0, 512, 0, NKC, CTX + t0) for t0 in range(0, S, 512)]
        if do_ctx:
            qtiles.append((qcT_d, 0, CTX, 0, CTX // 128, 0))
        for (qd, t0, n, kc0, kc1, ocol) in qtiles:
            q = qs[qi % 2]
            po = psO[qi % 2]
            o_ = oT[qi % 2]
            rd = rden[qi % 2]
            pb = psB[qi % 2]
            at = att[qi % 2]
            qi += 1
            P.dma("sp", q[:, 0:n], qd[h, :, t0:t0 + n], w=[q])
            prev = None
            for kc in range(kc0, kc1):
                ps = psS[si % 3]
                pT = pTs[si % 3]
                si += 1
                P.op("pe", lambda e: e.matmul(ps[:, 0:n], lhsT=kT[:, kc * 128:(kc + 1) * 128],
                                              rhs=q[:, 0:n], start=True, stop=True), r=[kT, q], w=[ps])
                P.op("act", lambda e: e.activation(out=pT[:, 0:n], in_=ps[:, 0:n], func=AF.Exp,
                                                   scale=ATT_SCALE), r=[ps], w=[pT])
                if prev is not None:
                    pk, ppT = prev
                    P.op("pe", lambda e: e.matmul(po[:, 0:n], lhsT=V[:, pk, :], rhs=ppT[:, 0:n],
                                                  start=(pk == kc0), stop=False), r=[V, ppT], w=[po])
                prev = (kc, pT)
            pk, ppT = prev
            P.op("pe", lambda e: e.matmul(po[:, 0:n], lhsT=V[:, pk, :], rhs=ppT[:, 0:n],
                                          start=(pk == kc0), stop=True), r=[V, ppT], w=[po])
            P.op("dve", lambda e: e.tensor_copy(out=o_[:, 0:n], in_=po[:, 0:n]), r=[po], w=[o_])
            P.op("act", lambda e: e.activation(out=rd[64:65, 0:n], in_=o_[64:65, 0:n], func=AF.Ln), r=[o_], w=[rd])
            P.op("act", lambda e: e.activation(out=rd[64:65, 0:n], in_=rd[64:65, 0:n], func=AF.Exp, scale=-1.0),
                 r=[rd], w=[rd])
            P.op("pe", lambda e: e.matmul(pb[:, 0:n], lhsT=onesf[64:65, 0:64], rhs=rd[64:65, 0:n],
                                          start=True, stop=True), r=[onesf, rd], w=[pb])
            P.op("dve", lambda e: e.tensor_tensor(out=at[:, 0:n], in0=o_[0:64, 0:n], in1=pb[:, 0:n], op=ALU.mult),
                 r=[o_, pb], w=[at])
            P.dma("pool", mixT_d[h * 64:(h + 1) * 64, ocol:ocol + n], at[:, 0:n], r=[at])
    C.pop()


def stage_outproj(C, segs, w_out_d, mixT_d, mods_d, l, row0=0):
    P = C.P
    C.push()
    Wout = C.sb([128, 8, D], BF16, "Wout")
    P.dma("pool", Wout[:], w_out_d.rearrange("(k p) n -> p k n", p=128), w=[Wout])
    gt = C.sb([128, D], F32)
    mixs = [C.sb([128, 8, 512], BF16, "mix") for _ in range(2)]
    xbs = [C.sb([128, D], F32, "xs") for _ in range(3)]
    tmp = C.sb([128, D], F32)
    pso = [C.ps([128, 512], F32, "pso") for _ in range(2)]
    mv = mixT_d.rearrange("(c p) t -> p c t", p=128)
    cur_which = None
    it = 0
    xi = 0
    oi = 0
    for (xd, ntok, which, koff) in segs:
        for t0 in range(0, ntok, 512):
            n = min(512, ntok - t0)
            if which != cur_which:
                cur_which = which
                load_bc(C, gt, mods_d[l, which, 5 * D:6 * D], "act")
            mx = mixs[it % 2]
            it += 1
            P.dma("sp", mx[:, :, 0:n], mv[:, :, koff + t0:koff + t0 + n], w=[mx])
            for sub in range(n // 128):
                xs = xbs[xi % 3]
                xi += 1
                r0 = t0 + sub * 128
                P.dma("sp", xs[:], xd[r0:r0 + 128, :], w=[xs])
                for dh in range(2):
                    po = pso[oi % 2]
                    oi += 1
                    for c in range(8):
                        P.op("pe", lambda e, c=c, po=po: e.matmul(po[:], lhsT=mx[:, c, sub * 128:(sub + 1) * 128],
                                                                 rhs=Wout[:, c, dh * 512:(dh + 1) * 512],
                                                                 start=(c == 0), stop=(c == 7)), r=[mx, Wout], w=[po])
                    P.op("dve", lambda e, po=po: e.tensor_tensor(out=tmp[:, dh * 512:(dh + 1) * 512], in0=po[:],
                                                                in1=gt[:, dh * 512:(dh + 1) * 512], op=ALU.mult),
                         r=[po, gt], w=[tmp])
                    P.op("pool", lambda e, xs=xs: e.tensor_tensor(out=xs[:, dh * 512:(dh + 1) * 512],
                                                                 in0=tmp[:, dh * 512:(dh + 1) * 512],
                                                                 in1=xs[:, dh * 512:(dh + 1) * 512], op=ALU.add),
                         r=[tmp, xs], w=[xs])
                P.dma("act", xd[r0:r0 + 128, :], xs[:], r=[xs])
    C.pop()


def make_consts(C, ident_d):
    P = C.P
    ident = C.sb([128, 128], BF16, "ident")
    identf = C.sb([128, 128], F32, "identf")
    epsb = C.sb([128, 1], F32, "epsb")
    P.dma("sp", identf[:], ident_d, w=[identf])
    P.op("dve", lambda e: e.tensor_copy(out=ident[:], in_=identf[:]), r=[identf], w=[ident])
    P.op("dve", lambda e: e.memset(epsb[:], EPS), w=[epsb])
    return ident, epsb


NHR = 16
GN_EPS = 64 * 1e-5
DEC_C = -float(np.exp(-0.5))


def bc3(ap2, n):
    return ap2.unsqueeze(2).to_broadcast([ap2.shape[0], ap2.shape[1], n])


def stage_rwkv_h(C, segs, mods_d, l, consts, hT_d):
    P = C.P
    C.push()
    ident, epsb = consts
    sc1 = C.sb([128, D], F32)
    sh = C.sb([128, D], F32)
    xbs = [C.sb([128, D], F32, "xs") for _ in range(3)]
    hTs = [C.sb([128, 8, 512], BF16, "hT") for _ in range(2)]
    tmp = C.sb([128, D], F32)
    hb = C.sb([128, D], BF16)
    junk = C.sb([128, D], BF16)
    ssq = C.sb([128, 1], F32)
    rt1 = C.sb([128, 1], F32)
    rstd1 = C.sb([128, 1], F32)
    stat = (junk, ssq, rt1, rstd1)
    zero = C.sb([128, 8, 1], BF16, "zero")
    P.op("dve", lambda e: e.memset(zero[:], 0.0), w=[zero])
    psT = C.ps([128, D], BF16, "psT")
    hv = hT_d.rearrange("(c p) t -> p c t", p=128)
    xi = 0
    ti = 0
    cur_which = None
    with C.nc.allow_non_contiguous_dma(reason="tiny zero pad columns"):
        for (xd, ntok, which, col0) in segs:
            P.dma("sp", hv[:, :, col0 - 1:col0], zero[:], r=[zero])
            P.dma("sp", hv[:, :, col0 + ntok:col0 + ntok + 1], zero[:], r=[zero])
    for (xd, ntok, which, col0) in segs:
        for t0 in range(0, ntok, 512):
            n = min(512, ntok - t0)
            if which != cur_which:
                cur_which = which
                load_bc(C, sh, mods_d[l, which, 3 * D:4 * D], "act")
                load_bc(C, sc1, mods_d[l, which, 4 * D:5 * D], "act")
                P.op("dve", lambda e: e.tensor_scalar_add(out=sc1[:], in0=sc1[:], scalar1=1.0), r=[sc1], w=[sc1])
            hT = hTs[ti % 2]
            ti += 1
            for sub in range(n // 128):
                xs = xbs[xi % 3]
                xi += 1
                P.dma("sp", xs[:], xd[t0 + sub * 128:t0 + (sub + 1) * 128, :], w=[xs])
                make_hT_sub(C, xs, sub, sc1, sh, hT, tmp, hb, psT, ident, stat, epsb)
            P.dma("act", hv[:, :, col0 + t0:col0 + t0 + n], hT[:, :, 0:n], r=[hT])
    C.pop()


def stage_rwkv_proj(C, segs, wd, hT_d, scr):
    P = C.P
    C.push()
    (w_r_d, w_k_d, w_v_d, w1_d, w2_d, a1_d, a2_d, g1_d, w0_d, rwv_d, rwc_d) = wd
    (rT_d, nkkT_d, bT_d, kdT_d, sig_d, v_d, sbon_d, sigG_d) = scr
    Wr = C.sb([128, 8, D], BF16, "Wr")
    Wk = C.sb([128, 8, D], BF16, "Wk")
    Wv = C.sb([128, 8, D], BF16, "Wv")
    for W, wdram in ((Wr, w_r_d), (Wk, w_k_d), (Wv, w_v_d)):
        P.dma("pool", W[:], wdram.rearrange("(k p) n -> p k n", p=128), w=[W])
    W1 = [C.sb([128, 8, 64], BF16, "W1") for _ in range(2)]
    A1 = [C.sb([128, 8, 64], BF16, "A1") for _ in range(2)]
    W2 = [C.sb([64, D], BF16, "W2") for _ in range(2)]
    A2 = [C.sb([64, D], BF16, "A2") for _ in range(2)]
    G1 = C.sb([128, 8, 128], BF16, "G1")
    w0bc = [C.sb([128, D], F32, "w0bc") for _ in range(2)]
    for d in range(2):
        P.dma("pool", W1[d][:], w1_d[d].rearrange("(k p) n -> p k n", p=128), w=[W1[d]])
        P.dma("pool", A1[d][:], a1_d[d].rearrange("(k p) n -> p k n", p=128), w=[A1[d]])
        P.dma("pool", W2[d][:], w2_d[d], w=[W2[d]])
        P.dma("pool", A2[d][:], a2_d[d], w=[A2[d]])
        load_bc(C, w0bc[d], w0_d[d], "sp")
    P.dma("pool", G1[:], g1_d.rearrange("(k p) n -> p k n", p=128), w=[G1])
    rwv = C.sb([128, 88], F32, "rwv")
    P.dma("sp", rwv[:], rwv_d, w=[rwv])
    XM, KK, KA, RK, A0 = 0, 48, 56, 64, 72
    omk = C.sb([128, 8], F32, "omk")
    rkh = C.sb([128, 8], F32, "rkh")
    P.op("dve", lambda e: e.tensor_scalar(out=omk[:], in0=rwv[:, KA:KA + 8], scalar1=-1.0, scalar2=1.0,
                                          op0=ALU.mult, op1=ALU.add), r=[rwv], w=[omk])
    P.op("dve", lambda e: e.tensor_scalar_mul(out=rkh[:], in0=rwv[:, RK:RK + 8], scalar1=0.5), r=[rwv], w=[rkh])
    blk = C.sb([128, 128], BF16, "blk")
    hsel = C.sb([128, 2], BF16, "hsel")
    blkf = C.sb([128, 130], F32, "blkf")
    P.dma("sp", blkf[:], rwc_d[:, 0:130], w=[blkf])
    P.op("dve", lambda e: e.tensor_copy(out=blk[:], in_=blkf[:, 0:128]), r=[blkf], w=[blk])
    P.op("dve", lambda e: e.tensor_copy(out=hsel[:], in_=blkf[:, 128:130]), r=[blkf], w=[hsel])
    hTh = [C.sb([128, 8, 514], BF16, "hTh") for _ in range(2)]
    hcAs = [C.sb([128, 8, 512], BF16, "hcA") for _ in range(2)]
    tt = GBuf(C.sb([128, 8, 512], F32, "tt"), 8)
    xx = C.sb([128, 8, 512], F32, "xx")
    xj = [C.sb([128, 8, 512], BF16, "xj") for _ in range(2)]
    kT = GBuf(C.sb([128, 8, 512], F32, "kT"), 8)
    kkT = GBuf(C.sb([128, 8, 512], BF16, "kkT"), 8)
    rTs = GBuf(C.sb([128, 8, 512], BF16, "rTs"), 8)
    kdsum = tt
    ob = [C.sb([128, 512], BF16, "ob") for _ in range(3)]
    of = [C.sb([128, 512], F32, "of") for _ in range(3)]
    hid = [C.sb([128, 512], BF16, "hid") for _ in range(2)]
    sbo = [C.sb([128, 16], F32, "sbo") for _ in range(2)]
    psA = [C.ps([128, 512], F32, "psA") for _ in range(5)]
    psS = [C.ps([128, 512], F32, "psS") for _ in range(2)]
    cnt = dict(a=0, o=0, f=0, h=0, x=0, t=0, s=0)

    def nxt(lst, key):
        i = cnt[key]
        cnt[key] = i + 1
        return lst[i % len(lst)]

    hv = hT_d.rearrange("(c p) t -> p c t", p=128)
    for (ntok, col0, koff) in segs:
        for t0 in range(0, ntok, 512):
            n = min(512, ntok - t0)
            nsub = n // 128
            hh = nxt(hTh, "t")
            P.dma("sp", hh[:, :, 0:n + 2], hv[:, :, col0 + t0 - 1:col0 + t0 + n + 1], w=[hh])
            hc = hh[:, :, 1:n + 1]
            hcA = hcAs[cnt["t"] % 2]
            P.dma("sp", hcA[:, :, 0:n], hv[:, :, col0 + t0:col0 + t0 + n], w=[hcA])
            P.op("dve", lambda e: e.tensor_tensor(out=tt[:, :, 0:n], in0=hh[:, :, 0:n], in1=hh[:, :, 2:n + 2],
                                                  op=ALU.add), r=[hh], w=tt.g)
            P.op("dve", lambda e: e.scalar_tensor_tensor(out=xx[:, :, 0:n], in0=tt[:, :, 0:n], scalar=0.5, in1=hc,
                                                         op0=ALU.mult, op1=ALU.subtract), r=tt.g + [hh], w=[xx])

            def mix(j):
                x_ = nxt(xj, "x")
                P.op("dve", lambda e: e.tensor_tensor(out=x_[:, :, 0:n], in0=xx[:, :, 0:n],
                                                      in1=bc3(rwv[:, XM + j * 8:XM + j * 8 + 8], n), op=ALU.mult),
                     r=[xx, rwv], w=[x_])
                return x_

            def projT(W, x_, p, ncol=128, col0_=None):
                ps = nxt(psA, "a")
                c0 = p * 128 if col0_ is None else col0_
                for kc in range(8):
                    P.op("pe", lambda e, kc=kc: e.matmul(ps[0:ncol, 0:n], lhsT=W[:, kc, c0:c0 + ncol],
                                                         rhs=hcA[:, kc, 0:n], start=(kc == 0), stop=False),
                         r=[W, hcA], w=[ps])
                for kc in range(8):
                    P.op("pe", lambda e, kc=kc: e.matmul(ps[0:ncol, 0:n], lhsT=W[:, kc, c0:c0 + ncol],
                                                         rhs=x_[:, kc, 0:n], start=False, stop=(kc == 7)),
                         r=[W, x_], w=[ps])
                return ps

            tok0 = koff + t0
            x_ = mix(0)
            for p in range(8):
                ps = projT(Wr, x_, p)
                P.op("act", lambda e: e.copy(out=rTs[:, p, 0:n], in_=ps[:, 0:n]), r=[ps], w=[rTs.g[p]])
            P.dma("act", rT_d.rearrange("(c p) t -> p c t", p=128)[:, :, tok0:tok0 + n], rTs[:, :, 0:n], r=rTs.g)
            x_ = mix(2)
            for p in range(8):
                ps = projT(Wk, x_, p)
                P.op("act", lambda e: e.copy(out=kT[:, p, 0:n], in_=ps[:, 0:n]), r=[ps], w=[kT.g[p]])
                kr = nxt(of, "f")
                P.op("dve", lambda e: e.tensor_scalar_mul(out=kr[:, 0:n], in0=kT[:, p, 0:n],
                                                          scalar1=rwv[:, KK + p:KK + p + 1]), r=[kT.g[p], rwv], w=[kr])
                sq = nxt(ob, "o")
                P.op("act", lambda e: e.activation(out=sq[:, 0:n], in_=kr[:, 0:n], func=AF.Square), r=[kr], w=[sq])
                pss = nxt(psA, "a")
                P.op("pe", lambda e: e.matmul(pss[:, 0:n], lhsT=blk[:], rhs=sq[:, 0:n], start=True, stop=True),
                     r=[blk, sq], w=[pss])
                nr = nxt(of, "f")
                P.op("dve", lambda e: e.tensor_scalar_max(out=nr[:, 0:n], in0=pss[:, 0:n], scalar1=1e-24),
                     r=[pss], w=[nr])
                P.op("act", lambda e: e.activation(out=nr[:, 0:n], in_=nr[:, 0:n], func=AF.Ln), r=[nr], w=[nr])
                P.op("act", lambda e: e.activation(out=nr[:, 0:n], in_=nr[:, 0:n], func=AF.Exp, scale=-0.5),
                     r=[nr], w=[nr])
                P.op("dve", lambda e: e.tensor_tensor(out=kkT[:, p, 0:n], in0=kr[:, 0:n], in1=nr[:, 0:n], op=ALU.mult),
                     r=[kr, nr], w=[kkT.g[p]])
                nk = nxt(ob, "o")
                P.op("pool", lambda e: e.tensor_scalar_mul(out=nk[:, 0:n], in0=kkT[:, p, 0:n], scalar1=-1.0),
                     r=[kkT.g[p]], w=[nk])
                P.dma("act", nkkT_d[p * 128:(p + 1) * 128, tok0:tok0 + n], nk[:, 0:n], r=[nk])
            x_ = mix(3)
            for sub in range(nsub):
                for dh in range(2):
                    ps = nxt(psA, "a")
                    for kc in range(8):
                        P.op("pe", lambda e, kc=kc: e.matmul(ps[:], lhsT=hcA[:, kc, sub * 128:(sub + 1) * 128],
                                                             rhs=Wv[:, kc, dh * 512:(dh + 1) * 512],
                                                             start=(kc == 0), stop=False), r=[hcA, Wv], w=[ps])
                    for kc in range(8):
                        P.op("pe", lambda e, kc=kc: e.matmul(ps[:], lhsT=x_[:, kc, sub * 128:(sub + 1) * 128],
                                                             rhs=Wv[:, kc, dh * 512:(dh + 1) * 512],
                                                             start=False, stop=(kc == 7)), r=[x_, Wv], w=[ps])
                    vb = nxt(ob, "o")
                    P.op("act", lambda e: e.copy(out=vb[:], in_=ps[:]), r=[ps], w=[vb])
                    P.dma("act", v_d[tok0 + sub * 128:tok0 + (sub + 1) * 128, dh * 512:(dh + 1) * 512], vb[:], r=[vb])
            x_ = mix(1)
            for d in range(2):
                ps = projT(W1[d], x_, 0, 64, 0)
                hd = nxt(hid, "h")
                P.op("act", lambda e: e.activation(out=hd[0:64, 0:n], in_=ps[0:64, 0:n], func=AF.Tanh), r=[ps], w=[hd])
                for sub in range(nsub):
                    for dh in range(2):
                        ps2 = nxt(psA, "a")
                        P.op("pe", lambda e: e.matmul(ps2[:], lhsT=hd[0:64, sub * 128:(sub + 1) * 128],
                                                      rhs=W2[d][:, dh * 512:(dh + 1) * 512], start=True, stop=True),
                             r=[hd, W2[d]], w=[ps2])
                        o1 = nxt(of, "f")
                        P.op("dve", lambda e: e.tensor_tensor(out=o1[:], in0=ps2[:],
                                                              in1=w0bc[d][:, dh * 512:(dh + 1) * 512], op=ALU.add),
                             r=[ps2, w0bc[d]], w=[o1])
                        P.op("act", lambda e: e.activation(out=o1[:], in_=o1[:], func=AF.Sigmoid), r=[o1], w=[o1])
                        P.dma("act", sig_d[d, tok0 + sub * 128:tok0 + (sub + 1) * 128, dh * 512:(dh + 1) * 512],
                              o1[:], r=[o1])
            x_ = mix(5)
            ps = projT(G1, x_, 0, 128, 0)
            sg = nxt(ob, "o")
            P.op("act", lambda e: e.activation(out=sg[:, 0:n], in_=ps[:, 0:n], func=AF.Sigmoid), r=[ps], w=[sg])
            P.dma("act", sigG_d[:, tok0:tok0 + n], sg[:, 0:n], r=[sg])
            x_ = mix(4)
            for d in range(2):
                ps = projT(A1[d], x_, 0, 64, 0)
                hd = nxt(hid, "h")
                P.op("act", lambda e: e.copy(out=hd[0:64, 0:n], in_=ps[0:64, 0:n]), r=[ps], w=[hd])
                for p in range(8):
                    ps2 = nxt(psA, "a")
                    P.op("pe", lambda e: e.matmul(ps2[:, 0:n], lhsT=A2[d][:, p * 128:(p + 1) * 128], rhs=hd[0:64, 0:n],
                                                  start=True, stop=True), r=[A2[d], hd], w=[ps2])
                    av = nxt(of, "f")
                    P.op("act", lambda e: e.activation(out=av[:, 0:n], in_=ps2[:, 0:n], func=AF.Sigmoid,
                                                       bias=rwv[:, A0 + d * 8 + p:A0 + d * 8 + p + 1]),
                         r=[ps2, rwv], w=[av])
                    bb = nxt(ob, "o")
                    P.op("dve", lambda e: e.tensor_tensor(out=bb[:, 0:n], in0=kkT[:, p, 0:n], in1=av[:, 0:n],
                                                          op=ALU.mult), r=[kkT.g[p], av], w=[bb])
                    P.dma("act", bT_d[d, p * 128:(p + 1) * 128, tok0:tok0 + n], bb[:, 0:n], r=[bb])
                    P.op("dve", lambda e: e.tensor_scalar(out=av[:, 0:n], in0=av[:, 0:n],
                                                          scalar1=rwv[:, KA + p:KA + p + 1], scalar2=omk[:, p:p + 1],
                                                          op0=ALU.mult, op1=ALU.add), r=[av, rwv, omk], w=[av])
                    kd = nxt(ob, "o")
                    P.op("dve", lambda e: e.tensor_tensor(out=kd[:, 0:n], in0=kT[:, p, 0:n], in1=av[:, 0:n],
                                                          op=ALU.mult), r=[kT.g[p], av], w=[kd])
                    P.dma("act", kdT_d[d, p * 128:(p + 1) * 128, tok0:tok0 + n], kd[:, 0:n], r=[kd])
                    if d == 0:
                        P.op("pool", lambda e: e.tensor_tensor(out=kdsum[:, p, 0:n], in0=kT[:, p, 0:n], in1=av[:, 0:n],
                                                               op=ALU.mult), r=[kT.g[p], av], w=[kdsum.g[p]])
                    else:
                        P.op("dve", lambda e: e.tensor_tensor(out=av[:, 0:n], in0=kT[:, p, 0:n], in1=av[:, 0:n],
                                                              op=ALU.mult), r=[kT.g[p], av], w=[av])
                        P.op("dve", lambda e: e.tensor_tensor(out=kdsum[:, p, 0:n], in0=kdsum[:, p, 0:n],
                                                              in1=av[:, 0:n], op=ALU.add), r=[kdsum.g[p], av], w=[kdsum.g[p]])
            for p in range(8):
                P.op("dve", lambda e: e.scalar_tensor_tensor(out=kdsum[:, p, 0:n], in0=kdsum[:, p, 0:n],
                                                             scalar=rkh[:, p:p + 1], in1=rTs[:, p, 0:n],
                                                             op0=ALU.mult, op1=ALU.mult),
                     r=[kdsum.g[p], rkh, rTs.g[p]], w=[kdsum.g[p]])
            prod = nxt(xj, "x")
            P.op("act", lambda e: e.copy(out=prod[:, :, 0:n], in_=kdsum[:, :, 0:n]), r=kdsum.g, w=[prod])
            for sub in range(nsub):
                pb = nxt(psS, "s")
                for p in range(8):
                    P.op("pe", lambda e: e.matmul(pb[:, 2 * p:2 * p + 2], lhsT=prod[:, p, sub * 128:(sub + 1) * 128],
                                                  rhs=hsel[:], start=True, stop=True), r=[prod, hsel], w=[pb])
                so = sbo[sub % 2]
                P.op("dve", lambda e: e.tensor_copy(out=so[:], in_=pb[:, 0:16]), r=[pb], w=[so])
                P.dma("act", sbon_d[tok0 + sub * 128:tok0 + (sub + 1) * 128, :], so[:], r=[so])
    C.pop()


RWC_TRI = 130
RWC_LV = 256 + 1536
RWC_DIR = 256 + 1536 + 1024 + 12 * 512
RWC_IREP = 130 + 2 * RWC_DIR
RWC_N = RWC_IREP + 512


def stage_rwkv_scan(C, NK, scr, rwc_d, consts):
    P = C.P
    C.push()
    ident, epsb = consts
    (rT_d, nkkT_d, bT_d, kdT_d, sig_d, v_d, y_d) = scr
    NKC = NK // 128
    NCC = CTX // 128
    KD = os.environ.get("KDBG", "")
    irf = C.sb([128, 512], F32, "irf")
    irep = C.sb([128, 512], BF16, "irep")
    P.dma("sp", irf[:], rwc_d[:, RWC_IREP:RWC_IREP + 512], w=[irf])
    P.op("dve", lambda e: e.tensor_copy(out=irep[:], in_=irf[:]), r=[irf], w=[irep])
    triI = C.sb([128, 128], F32, "triI")
    triE = C.sb([128, 128], F32, "triE")
    mS = C.sb([128, 512], F32, "mS")
    mST = C.sb([128, 512], F32, "mST")
    mI = C.sb([128, 512], F32, "mI")
    mSb = C.sb([128, 512], BF16, "mSb")
    mIb = C.sb([128, 512], BF16, "mIb")
    lvm = C.sb([128, 14, 512], BF16, "lvm")
    lvf = C.sb([128, 2, 512], F32, "lvf")
    NLB = 2
    ld = [dict(r=C.sb([128, 8, 128], BF16, "l_r"), nk=C.sb([128, 8, 128], BF16, "l_nk"),
               b=C.sb([128, 8, 128], BF16, "l_b"), kd=C.sb([128, 8, 128], BF16, "l_kd"),
               sig=C.sb([128, D], F32, "l_sig"), v=C.sb([128, D], BF16, "l_v")) for _ in range(NLB)]
    eL = C.sb([128, D], F32, "eL")
    eLx = C.sb([128, D], F32, "eLx")
    enL = C.sb([128, D], F32, "enL")
    pre = []
    for i in range(2):
        d_ = dict(rt=C.sb([128, D], BF16, "rt"), at=C.sb([128, D], BF16, "at"), bt=C.sb([128, D], BF16, "bt"),
                  kt=C.sb([128, D], BF16, "kt"),
                  bpA=C.sb([128, 8, 128], BF16, "bpA"), bpB=C.sb([128, 8, 128], BF16, "bpB"),
                  kpA=C.sb([128, 8, 128], BF16, "kpA"), kpB=C.sb([128, 8, 128], BF16, "kpB"),
                  T=[GBuf(C.sb([128, 16, 128], BF16, "T"), 4) for _ in range(2)],
                  Tt=[GBuf(C.sb([128, 16, 128], BF16, "Tt"), 4) for _ in range(2)],
                  Mak=GBuf(C.sb([128, 16, 128], BF16, "Mak"), 4), Mbr=GBuf(C.sb([128, 16, 128], BF16, "Mbr"), 4),
                  Mkr=GBuf(C.sb([128, 16, 128], BF16, "Mkr"), 4), gam=C.sb([128, 8], F32, "gam"))
        for nm in ("bpA", "bpB", "kpA", "kpB"):
            P.op("pool", lambda e, b_=d_[nm]: e.memset(b_[:], 0.0), w=[d_[nm]])
        pre.append(d_)
    Pb = [GBuf(C.sb([128, 16, 128], BF16, "Pb"), 4) for _ in range(2)]
    Qb = [GBuf(C.sb([128, 16, 128], BF16, "Qb"), 4) for _ in range(2)]
    Sf = C.sb([128, 8, 64], F32, "Sf")
    Sb = C.sb([128, 8, 64], BF16, "Sb")
    Stmp = C.sb([128, 8, 64], F32, "Stmp")
    XT = GBuf(C.sb([128, 16, 64], BF16, "XT"), 2)
    UT = GBuf(C.sb([128, 16, 64], BF16, "UT"), 2)
    yt = [C.sb([128, D], F32, "yt") for _ in range(2)]
    pbF = [C.ps([128, 512], F32, "pbF") for _ in range(6)]
    pbT = [C.ps([128, D], BF16, "pbT") for _ in range(2)]
    cnt = dict(f=0, t=0, y=0)

    def nf():
        i = cnt["f"]
        cnt["f"] = i + 1
        return pbF[i % 6]

    def nt():
        i = cnt["t"]
        cnt["t"] = i + 1
        return pbT[i % 2]

    rv = rT_d.rearrange("(c p) t -> p c t", p=128)
    nkv = nkkT_d.rearrange("(c p) t -> p c t", p=128)

    for d in range(2):
        base = RWC_TRI + d * RWC_DIR
        P.dma("sp", triI[:], rwc_d[:, base:base + 128], w=[triI])
        P.dma("sp", triE[:], rwc_d[:, base + 128:base + 256], w=[triE])
        P.dma("sp", mS[:], rwc_d[:, base + 256:base + 768], w=[mS])
        P.dma("sp", mST[:], rwc_d[:, base + 768:base + 1280], w=[mST])
        P.dma("sp", mI[:], rwc_d[:, base + 1280:base + 1792], w=[mI])
        P.op("dve", lambda e: e.tensor_copy(out=mSb[:], in_=mS[:]), r=[mS], w=[mSb])
        P.op("dve", lambda e: e.tensor_copy(out=mIb[:], in_=mI[:]), r=[mI], w=[mIb])
        for q_ in range(7):
            o = base + RWC_LV + q_ * 1024
            P.dma("sp", lvf[:], rwc_d[:, o:o + 1024].rearrange("p (a b) -> p a b", a=2), w=[lvf])
            P.op("dve", lambda e: e.tensor_copy(out=lvm[:, 2 * q_:2 * q_ + 2, :], in_=lvf[:]), r=[lvf], w=[lvm])
        P.op("dve", lambda e: e.memset(Sf[:], 0.0), w=[Sf])
        P.op("dve", lambda e: e.memset(Sb[:], 0.0), w=[Sb])
        if d == 0:
            order = list(range(NKC))
        else:
            order = list(range(NCC - 1, -1, -1)) + list(range(NKC - 1, NCC - 1, -1))
        last = 127 if d == 0 else 0
        bv = bT_d[d].rearrange("(c p) t -> p c t", p=128)
        kv = kdT_d[d].rearrange("(c p) t -> p c t", p=128)

        def load(i):
            c = order[i]
            L = ld[i % NLB]
            t0 = c * 128
            P.dma("sp", L["r"][:], rv[:, :, t0:t0 + 128], w=[L["r"]])
            P.dma("sp", L["nk"][:], nkv[:, :, t0:t0 + 128], w=[L["nk"]])
            P.dma("sp", L["b"][:], bv[:, :, t0:t0 + 128], w=[L["b"]])
            P.dma("sp", L["kd"][:], kv[:, :, t0:t0 + 128], w=[L["kd"]])
            P.dma("sp", L["sig"][:], sig_d[d, t0:t0 + 128, :], w=[L["sig"]])
            P.dma("sp", L["v"][:], v_d[t0:t0 + 128, :], w=[L["v"]])

        def precompute(i):
            L = ld[i % NLB]
            R = pre[i % 2]
            bL = [nf(), nf()]
            for p in range(8):
                P.op("pe", lambda e: e.matmul(bL[p // 4][:, (p % 4) * 128:(p % 4 + 1) * 128],
                                              lhsT=L["sig"][:, p * 128:(p + 1) * 128], rhs=triI[:],
                                              start=True, stop=True), r=[L["sig"], triI], w=[bL[p // 4]])
            for hf in range(2):
                sl = slice(hf * 512, (hf + 1) * 512)
                P.op("act", lambda e: e.activation(out=eL[:, sl], in_=bL[hf][:], func=AF.Exp), r=[bL[hf]], w=[eL])
                P.op("act", lambda e: e.activation(out=enL[:, sl], in_=bL[hf][:], func=AF.Exp, scale=-1.0),
                     r=[bL[hf]], w=[enL])
            yield
            fl = lambda b_: b_[:].rearrange("p c t -> p (c t)")
            P.op("dve", lambda e: e.tensor_tensor(out=R["rt"][:], in0=fl(L["r"]), in1=eL[:], op=ALU.mult),
                 r=[L["r"], eL], w=[R["rt"]])
            at3 = R["at"][:].rearrange("p (c t) -> p c t", t=128)
            eL3 = eL[:].rearrange("p (c t) -> p c t", t=128)
            if d == 0:
                P.op("pool", lambda e: e.tensor_tensor(out=at3[:, :, 1:128], in0=L["nk"][:, :, 1:128],
                                                       in1=eL3[:, :, 0:127], op=ALU.mult), r=[L["nk"], eL], w=[R["at"]])
                P.op("pool", lambda e: e.tensor_copy(out=at3[:, :, 0:1], in_=L["nk"][:, :, 0:1]), r=[L["nk"]], w=[R["at"]])
            else:
                P.op("pool", lambda e: e.tensor_tensor(out=at3[:, :, 0:127], in0=L["nk"][:, :, 0:127],
                                                       in1=eL3[:, :, 1:128], op=ALU.mult), r=[L["nk"], eL], w=[R["at"]])
                P.op("pool", lambda e: e.tensor_copy(out=at3[:, :, 127:128], in_=L["nk"][:, :, 127:128]),
                     r=[L["nk"]], w=[R["at"]])
            P.op("dve", lambda e: e.tensor_tensor(out=R["bt"][:], in0=fl(L["b"]), in1=enL[:], op=ALU.mult),
                 r=[L["b"], enL], w=[R["bt"]])
            P.op("pool", lambda e: e.tensor_tensor(out=R["kt"][:], in0=fl(L["kd"]), in1=enL[:], op=ALU.mult),
                 r=[L["kd"], enL], w=[R["kt"]])
            P.op("dve", lambda e: e.tensor_copy(out=R["gam"][:],
                                                in_=eL[:].rearrange("p (c t) -> p c t", t=128)[:, :, last]),
                 r=[eL], w=[R["gam"]])
            yield
            for (src, dA, dB) in ((R["bt"], R["bpA"], R["bpB"]), (R["kt"], R["kpA"], R["kpB"])):
                pt = nt()
                for p in range(8):
                    P.op("pe", lambda e: e.transpose(out=pt[:, p * 128:(p + 1) * 128],
                                                     in_=src[:, p * 128:(p + 1) * 128], identity=ident[:]),
                         r=[src, ident], w=[pt])
                ptv = pt[:].rearrange("p (c k) -> p c k", k=128)
                P.op("act", lambda e: e.copy(out=dA[:, :, 0:64], in_=ptv[:, :, 0:64]), r=[pt], w=[dA])
                P.op("dve", lambda e: e.tensor_copy(out=dB[:, :, 64:128], in_=ptv[:, :, 64:128]), r=[pt], w=[dB])

            def hm(lhs, rhs, mask, dst, eng):
                for G in range(2):
                    pbs = (nf(), nf())
                    for j in range(8):
                        h = G * 8 + j
                        p, q = h // 2, h % 2
                        rows = slice(q * 64, q * 64 + 64)
                        pb = pbs[q]
                        P.op("pe", lambda e: e.matmul(pb[:, (j // 2) * 128:(j // 2 + 1) * 128],
                                                      lhsT=lhs[rows, p * 128:(p + 1) * 128],
                                                      rhs=rhs[rows, p * 128:(p + 1) * 128], start=True, stop=True),
                             r=[lhs, rhs], w=[pb])
                    for q in range(2):
                        dv = dst[:, G * 8 + q:G * 8 + 8:2, :]
                        m3 = mask[:].rearrange("p (h t) -> p h t", t=128)
                        dg = [dst.g[2 * G], dst.g[2 * G + 1]]
                        if eng == "dve":
                            P.op("dve", lambda e: e.tensor_tensor(
                                out=dv, in0=pbs[q][:].rearrange("p (h t) -> p h t", t=128), in1=m3, op=ALU.mult),
                                 r=[pbs[q], mask], w=dg)
                        else:
                            P.op("act", lambda e: e.copy(out=dv, in_=pbs[q][:].rearrange("p (h t) -> p h t", t=128)),
                                 r=[pbs[q]], w=dg)
                            P.op("pool", lambda e: e.tensor_tensor(out=dv, in0=dv, in1=m3, op=ALU.mult),
                                 r=dg + [mask], w=dg)

            yield
            hm(R["bt"], R["at"], mS, Pb[0], "dve")
            yield
            hm(R["at"], R["bt"], mST, Qb[0], "dve")
            yield
            hm(R["kt"], R["at"], mS, R["Mak"], "dve")
            yield
            hm(R["bt"], R["rt"], mIb, R["Mbr"], "actpool")
            yield
            hm(R["kt"], R["rt"], mIb, R["Mkr"], "actpool")
            yield
            T, Tt = R["T"][0], R["Tt"][0]
            g4 = lambda b_, g: b_[:, g * 4:(g + 1) * 4, :].rearrange("p h t -> p (h t)")
            for g in range(4):
                P.op("dve", lambda e: e.tensor_tensor(out=g4(T, g), in0=g4(Pb[0], g), in1=lvm[:, 0, :], op=ALU.mult),
                     r=[Pb[0].g[g], lvm], w=[T.g[g]])
                P.op("pool", lambda e: e.tensor_tensor(out=g4(T, g), in0=g4(T, g), in1=irep[:], op=ALU.add),
                     r=[T.g[g], irep], w=[T.g[g]])
                P.op("dve", lambda e: e.tensor_tensor(out=g4(Tt, g), in0=g4(Qb[0], g), in1=lvm[:, 1, :], op=ALU.mult),
                     r=[Qb[0].g[g], lvm], w=[Tt.g[g]])
                P.op("pool", lambda e: e.tensor_tensor(out=g4(Tt, g), in0=g4(Tt, g), in1=irep[:], op=ALU.add),
                     r=[Tt.g[g], irep], w=[Tt.g[g]])
            yield
            Wb = Pb[1]
            cur = 0
            for li in range(6):
                mk = lvm[:, 2 + 2 * li, :]
                T, Tt = R["T"][cur], R["Tt"][cur]
                Tn, Ttn = R["T"][1 - cur], R["Tt"][1 - cur]
                for g in range(4):
                    pw = nf()
                    for j in range(4):
                        h = g * 4 + j
                        P.op("pe", lambda e: e.matmul(pw[:, j * 128:(j + 1) * 128], lhsT=Qb[0][:, h, :], rhs=T[:, h, :],
                                                      start=True, stop=True), r=[Qb[0].g[g], T.g[g]], w=[pw])
                    if g < 3:
                        P.op("dve", lambda e: e.tensor_tensor(out=g4(Wb, g), in0=pw[:], in1=mk, op=ALU.mult),
                             r=[pw, lvm], w=[Wb.g[g]])
                    else:
                        P.op("act", lambda e: e.copy(out=g4(Wb, g), in_=pw[:]), r=[pw], w=[Wb.g[g]])
                        P.op("pool", lambda e: e.tensor_tensor(out=g4(Wb, g), in0=g4(Wb, g), in1=mk, op=ALU.mult),
                             r=[Wb.g[g], lvm], w=[Wb.g[g]])
                    if g % 2 == 1:
                        yield
                for g in range(4):
                    pm = nf()
                    for j in range(4):
                        h = g * 4 + j
                        P.op("pe", lambda e: e.matmul(pm[:, j * 128:(j + 1) * 128], lhsT=Tt[:, h, :], rhs=Wb[:, h, :],
                                                      start=True, stop=True), r=[Tt.g[g], Wb.g[g]], w=[pm])
                    P.op("dve", lambda e: e.tensor_tensor(out=g4(Tn, g), in0=pm[:], in1=g4(T, g), op=ALU.add),
                         r=[pm, T.g[g]], w=[Tn.g[g]])
                    if g % 2 == 1:
                        yield
                if li < 5:
                    for g in range(4):
                        ptt = nt()
                        for j in range(4):
                            h = g * 4 + j
                            P.op("pe", lambda e: e.transpose(out=ptt[:, j * 128:(j + 1) * 128], in_=Tn[:, h, :],
                                                             identity=ident[:]), r=[Tn.g[g], ident], w=[ptt])
                        P.op("act", lambda e: e.copy(out=g4(Ttn, g), in_=ptt[:, 0:512]), r=[ptt], w=[Ttn.g[g]])
                        if g % 2 == 1:
                            yield
                cur = 1 - cur
            R["Tf"] = R["T"][cur]

        def chain(i):
            c = order[i]
            L = ld[i % NLB]
            R = pre[i % 2]
            V = L["v"]
            T = R["Tf"]
            for g in range(2):
                pb = nf()
                for j in range(8):
                    h = g * 8 + j
                    p, q = h // 2, h % 2
                    rows = slice(q * 64, q * 64 + 64)
                    P.op("pe", lambda e: e.matmul(pb[:, j * 64:(j + 1) * 64], lhsT=R["at"][rows, p * 128:(p + 1) * 128],
                                                  rhs=Sb[rows, p, :], start=True, stop=False), r=[R["at"], Sb], w=[pb])
                    P.op("pe", lambda e: e.matmul(pb[:, j * 64:(j + 1) * 64], lhsT=R["Mak"][:, h, :],
                                                  rhs=V[:, h * 64:(h + 1) * 64], start=False, stop=True),
                         r=[R["Mak"].g[h // 4], V], w=[pb])
                P.op("act", lambda e: e.copy(out=XT[:, g * 8:(g + 1) * 8, :].rearrange("p h v -> p (h v)"), in_=pb[:]),
                     r=[pb], w=[XT.g[g]])
            yield
            for g in range(2):
                pb = nf()
                for j in range(8):
                    h = g * 8 + j
                    P.op("pe", lambda e: e.matmul(pb[:, j * 64:(j + 1) * 64], lhsT=T[:, h, :], rhs=XT[:, h, :],
                                                  start=True, stop=True), r=[T.g[h // 4], XT.g[g]], w=[pb])
                P.op("dve", lambda e: e.tensor_copy(out=UT[:, g * 8:(g + 1) * 8, :].rearrange("p h v -> p (h v)"),
                                                    in_=pb[:]), r=[pb], w=[UT.g[g]])
            yield
            y_ = yt[cnt["y"] % 2]
            cnt["y"] += 1
            for g in range(2):
                pb = nf()
                for j in range(8):
                    h = g * 8 + j
                    p, q = h // 2, h % 2
                    rows = slice(q * 64, q * 64 + 64)
                    P.op("pe", lambda e: e.matmul(pb[:, j * 64:(j + 1) * 64], lhsT=R["rt"][rows, p * 128:(p + 1) * 128],
                                                  rhs=Sb[rows, p, :], start=True, stop=False), r=[R["rt"], Sb], w=[pb])
                    P.op("pe", lambda e: e.matmul(pb[:, j * 64:(j + 1) * 64], lhsT=R["Mbr"][:, h, :], rhs=UT[:, h, :],
                                                  start=False, stop=False), r=[R["Mbr"].g[h // 4], UT.g[g]], w=[pb])
                    P.op("pe", lambda e: e.matmul(pb[:, j * 64:(j + 1) * 64], lhsT=R["Mkr"][:, h, :],
                                                  rhs=V[:, h * 64:(h + 1) * 64], start=False, stop=True),
                         r=[R["Mkr"].g[h // 4], V], w=[pb])
                P.op("act", lambda e: e.copy(out=y_[:, g * 512:(g + 1) * 512], in_=pb[:]), r=[pb], w=[y_])
            P.dma("act", y_d[d, c * 128:(c + 1) * 128, :], y_[:], r=[y_])
            yield
            pb = nf()
            for p in range(8):
                o_ = pb[:, p * 64:(p + 1) * 64]
                P.op("pe", lambda e: e.matmul(o_, lhsT=R["bpA"][:, p, :], rhs=UT[:, 2 * p, :], start=True, stop=False),
                     r=[R["bpA"]] + UT.g, w=[pb])
                P.op("pe", lambda e: e.matmul(o_, lhsT=R["bpB"][:, p, :], rhs=UT[:, 2 * p + 1, :], start=False,
                                              stop=False), r=[R["bpB"]] + UT.g, w=[pb])
                P.op("pe", lambda e: e.matmul(o_, lhsT=R["kpA"][:, p, :], rhs=V[:, (2 * p) * 64:(2 * p + 1) * 64],
                                              start=False, stop=False), r=[R["kpA"], V], w=[pb])
                P.op("pe", lambda e: e.matmul(o_, lhsT=R["kpB"][:, p, :], rhs=V[:, (2 * p + 1) * 64:(2 * p + 2) * 64],
                                              start=False, stop=True), r=[R["kpB"], V], w=[pb])
            P.op("dve", lambda e: e.tensor_tensor(out=Stmp[:].rearrange("p c v -> p (c v)"),
                                                  in0=pb[:], in1=Sf[:].rearrange("p c v -> p (c v)"), op=ALU.add),
                 r=[pb, Sf], w=[Stmp])
            P.op("dve", lambda e: e.tensor_tensor(out=Sf[:], in0=Stmp[:], in1=bc3(R["gam"][:], 64), op=ALU.mult),
                 r=[Stmp, R["gam"]], w=[Sf])
            P.op("act", lambda e: e.copy(out=Sb[:], in_=Sf[:]), r=[Sf], w=[Sb])
            yield

        def run2(gp, gc, ratio):
            done_p, done_c, k = gp is None, False, 0
            while not (done_p and done_c):
                if not done_p:
                    try:
                        next(gp)
                    except StopIteration:
                        done_p = True
                k += 1
                if not done_c and (done_p or k % ratio == 0):
                    try:
                        next(gc)
                    except StopIteration:
                        done_c = True

        n_it = len(order)
        load(0)
        if n_it > 1:
            load(1)
        for _ in precompute(0):
            pass
        for i in range(n_it):
            run2(precompute(i + 1) if i + 1 < n_it else None, chain(i), 3)
            if i + 2 < n_it:
                load(i + 2)
    C.pop()


def stage_rwkv_out(C, segs, wd, scr, mods_d, l, consts):
    P = C.P
    C.push()
    ident, epsb = consts
    (g2_d, w_o_d, ln_g_d, ln_b_d) = wd
    (y_d, v_d, sbon_d, sigG_d) = scr
    G2 = C.sb([128, D], BF16, "G2")
    Wo = C.sb([128, 8, D], BF16, "Wo")
    P.dma("pool", G2[:], g2_d, w=[G2])
    P.dma("pool", Wo[:], w_o_d.rearrange("(k p) n -> p k n", p=128), w=[Wo])
    lng = C.sb([128, D], F32, "lng")
    lnb = C.sb([128, D], F32, "lnb")
    gt = C.sb([128, D], F32, "gt")
    load_bc(C, lng, ln_g_d, "sp")
    load_bc(C, lnb, ln_b_d, "sp")
    gneps = C.sb([128, 1], F32, "gneps")
    P.op("dve", lambda e: e.memset(gneps[:], GN_EPS), w=[gneps])
    y0 = [C.sb([128, D], F32, "y0") for _ in range(2)]
    y1 = [C.sb([128, D], F32, "y1") for _ in range(2)]
    vb = [C.sb([128, D], BF16, "vb") for _ in range(2)]
    sb_ = [C.sb([128, 16], F32, "sb") for _ in range(2)]
    sg = [C.sb([128, 128], BF16, "sg") for _ in range(2)]
    xs_ = [C.sb([128, D], F32, "xs") for _ in range(2)]
    yc_l = [C.sb([128, D], F32, "yc") for _ in range(2)]
    sq_l = [C.sb([128, D], F32, "sq") for _ in range(2)]
    mean_l = [C.sb([128, 16], F32, "mean") for _ in range(2)]
    var_l = [C.sb([128, 16], F32, "var") for _ in range(2)]
    rstd_l = [C.sb([128, 16], F32, "rstd") for _ in range(2)]
    bon_l = [C.sb([128, D], F32, "bon") for _ in range(2)]
    zb_l = [C.sb([128, D], BF16, "zb") for _ in range(2)]
    zT_l = [C.sb([128, 8, 128], BF16, "zT") for _ in range(2)]
    tmp_l = [C.sb([128, D], F32, "tmp") for _ in range(2)]
    psG = [C.ps([128, 512], F32, "psG") for _ in range(2)]
    psO = [C.ps([128, 512], F32, "psO") for _ in range(2)]
    psT_l = [C.ps([128, D], BF16, "psT") for _ in range(2)]
    it = 0
    cur_which = None
    v3 = lambda b_: b_[:].rearrange("p (h e) -> p h e", e=64)
    for (xd, ntok, which, koff) in segs:
        if which != cur_which:
            cur_which = which
            load_bc(C, gt, mods_d[l, which, 5 * D:6 * D], "act")
        for r0 in range(0, ntok, 128):
            i = it % 2
            it += 1
            g0 = koff + r0
            P.dma("sp", y0[i][:], y_d[0, g0:g0 + 128, :], w=[y0[i]])
            P.dma("sp", y1[i][:], y_d[1, g0:g0 + 128, :], w=[y1[i]])
            P.dma("sp", vb[i][:], v_d[g0:g0 + 128, :], w=[vb[i]])
            P.dma("sp", sb_[i][:], sbon_d[g0:g0 + 128, :], w=[sb_[i]])
            P.dma("sp", sg[i][:], sigG_d[:, g0:g0 + 128], w=[sg[i]])
            P.dma("sp", xs_[i][:], xd[r0:r0 + 128, :], w=[xs_[i]])
            yc, sq, mean, var, rstd, bon, zb, zT, tmp, psT = (yc_l[i], sq_l[i], mean_l[i], var_l[i], rstd_l[i], bon_l[i],
                                                              zb_l[i], zT_l[i], tmp_l[i], psT_l[i])
            Y = y0[i]
            P.op("dve", lambda e: e.tensor_tensor(out=Y[:], in0=Y[:], in1=y1[i][:], op=ALU.add), r=[Y, y1[i]], w=[Y])
            P.op("dve", lambda e: e.tensor_reduce(out=mean[:], in_=v3(Y), axis=AX.X, op=ALU.add), r=[Y], w=[mean])
            P.op("dve", lambda e: e.tensor_scalar_mul(out=mean[:], in0=mean[:], scalar1=1.0 / 64), r=[mean], w=[mean])
            P.op("dve", lambda e: e.tensor_tensor(out=v3(yc), in0=v3(Y), in1=bc3(mean[:], 64), op=ALU.subtract),
                 r=[Y, mean], w=[yc])
            P.op("act", lambda e: e.activation(out=sq[:], in_=yc[:], func=AF.Square), r=[yc], w=[sq])
            P.op("dve", lambda e: e.tensor_reduce(out=var[:], in_=v3(sq), axis=AX.X, op=ALU.add), r=[sq], w=[var])
            P.op("act", lambda e: e.activation(out=var[:], in_=var[:], func=AF.Sqrt, bias=gneps[:], scale=1.0 / 64),
                 r=[var, gneps], w=[var])
            P.op("dve", lambda e: e.reciprocal(out=rstd[:], in_=var[:]), r=[var], w=[rstd])
            P.op("dve", lambda e: e.tensor_tensor(out=v3(yc), in0=v3(yc), in1=bc3(rstd[:], 64), op=ALU.mult),
                 r=[yc, rstd], w=[yc])
            P.op("pool", lambda e: e.tensor_tensor(out=yc[:], in0=yc[:], in1=lng[:], op=ALU.mult), r=[yc, lng], w=[yc])
            P.op("pool", lambda e: e.tensor_tensor(out=yc[:], in0=yc[:], in1=lnb[:], op=ALU.add), r=[yc, lnb], w=[yc])
            P.op("dve", lambda e: e.tensor_tensor(out=v3(bon), in0=v3(vb[i]), in1=bc3(sb_[i][:], 64), op=ALU.mult),
                 r=[vb[i], sb_[i]], w=[bon])
            P.op("pool", lambda e: e.tensor_tensor(out=yc[:], in0=yc[:], in1=bon[:], op=ALU.add), r=[yc, bon], w=[yc])
            for dh in range(2):
                pg = psG[dh]
                P.op("pe", lambda e: e.matmul(pg[:], lhsT=sg[i][:], rhs=G2[:, dh * 512:(dh + 1) * 512],
                                              start=True, stop=True), r=[sg[i], G2], w=[pg])
                P.op("dve", lambda e: e.tensor_tensor(out=zb[:, dh * 512:(dh + 1) * 512],
                                                      in0=pg[:], in1=yc[:, dh * 512:(dh + 1) * 512], op=ALU.mult),
                     r=[pg, yc], w=[zb])
            for kc in range(8):
                P.op("pe", lambda e: e.transpose(out=psT[:, kc * 128:(kc + 1) * 128], in_=zb[:, kc * 128:(kc + 1) * 128],
                                                 identity=ident[:]), r=[zb, ident], w=[psT])
            P.op("act", lambda e: e.copy(out=zT[:], in_=psT[:].rearrange("p (k t) -> p k t", k=8)), r=[psT], w=[zT])
            xs = xs_[i]
            for dh in range(2):
                po = psO[dh]
                for kc in range(8):
                    P.op("pe", lambda e: e.matmul(po[:], lhsT=zT[:, kc, :], rhs=Wo[:, kc, dh * 512:(dh + 1) * 512],
                                                  start=(kc == 0), stop=(kc == 7)), r=[zT, Wo], w=[po])
                P.op("dve", lambda e: e.tensor_tensor(out=tmp[:, dh * 512:(dh + 1) * 512], in0=po[:],
                                                      in1=gt[:, dh * 512:(dh + 1) * 512], op=ALU.mult),
                     r=[po, gt], w=[tmp])
                P.op("pool", lambda e: e.tensor_tensor(out=xs[:, dh * 512:(dh + 1) * 512],
                                                       in0=tmp[:, dh * 512:(dh + 1) * 512],
                                                       in1=xs[:, dh * 512:(dh + 1) * 512], op=ALU.add),
                     r=[tmp, xs], w=[xs])
            P.dma("act", xd[r0:r0 + 128, :], xs[:], r=[xs])
    C.pop()


def host_consts(S):
    pos = np.arange(S)
    row = (pos // 64).astype(np.float32)
    col = (pos % 64).astype(np.float32)
    inv = (10000.0 ** (-np.arange(8, dtype=np.float32) / 8)).astype(np.float32)
    ang_r = row[None, :] * inv[:, None]
    ang_c = col[None, :] * inv[:, None]
    cos = np.concatenate([np.cos(ang_r), np.cos(ang_r), np.cos(ang_c), np.cos(ang_c)], 0).astype(np.float32)
    sin = np.concatenate([-np.sin(ang_r), np.sin(ang_r), -np.sin(ang_c), np.sin(ang_c)], 0).astype(np.float32)
    prot = np.zeros((32, 32), np.float32)
    for i in range(32):
        j = i + 8 if (i % 16) < 8 else i - 8
        prot[j, i] = 1.0
    rwc = np.zeros((128, RWC_N), np.float32)
    rwc[0:64, 0:64] = 1.0
    rwc[64:128, 64:128] = 1.0
    rwc[0:64, 128] = 1.0
    rwc[64:128, 129] = 1.0
    ii = np.arange(128)
    for d in range(2):
        before = (ii[:, None] < ii[None, :]) if d == 0 else (ii[:, None] > ii[None, :])
        incl = before | np.eye(128, dtype=bool)
        base = RWC_TRI + d * RWC_DIR
        rwc[:, base:base + 128] = incl * DEC_C
        rwc[:, base + 128:base + 256] = before * DEC_C
        rwc[:, base + 256:base + 768] = np.tile(before.astype(np.float32), (1, 4))
        rwc[:, base + 768:base + 1280] = np.tile(before.T.astype(np.float32), (1, 4))
        rwc[:, base + 1280:base + 1792] = np.tile(incl.astype(np.float32), (1, 4))
        blkid = lambda m: (ii[:, None] // m) == (ii[None, :] // m)
        m1 = before & blkid(2)
        o = base + RWC_LV
        rwc[:, o:o + 512] = np.tile(m1.astype(np.float32), (1, 4))
        rwc[:, o + 512:o + 1024] = np.tile(m1.T.astype(np.float32), (1, 4))
        for li in range(6):
            m = 2 << li
            mm = before & blkid(2 * m) & ~blkid(m)
            o2 = o + 1024 + li * 1024
            rwc[:, o2:o2 + 512] = np.tile(mm.astype(np.float32), (1, 4))
            rwc[:, o2 + 512:o2 + 1024] = np.tile(mm.T.astype(np.float32), (1, 4))
    rwc[:, RWC_IREP:RWC_IREP + 512] = np.tile(np.eye(128, dtype=np.float32), (1, 4))
    return dict(ident=np.eye(128, dtype=np.float32), ones=np.ones((128, 128), np.float32),
                cos=np.ascontiguousarray(cos), sin=np.ascontiguousarray(sin), prot=prot, rwc=rwc)


def host_layout(inp, b):
    f = np.float32
    cond = np.stack([inp["c"][b].reshape(8, 128).T, inp["c_ctx"].reshape(8, 128).T], axis=-1).astype(f)
    mlac = np.zeros((128, 144), f)
    mlac[:, 0:3] = inp["mla_q_norm"][0].reshape(3, 128).T
    mlac[:, 3] = inp["mla_kv_norm"][0]
    mlac[0:64, 4] = inp["mla_qk_norm_q"][0][0:64]
    mlac[0:32, 5] = inp["mla_qk_norm_q"][0][64:96]
    mlac[0:64, 6] = inp["mla_qk_norm_k"][0][0:64]
    mlac[0:32, 7] = inp["mla_qk_norm_k"][0][64:96]
    dw = inp["conv_dw_w"][0][:, 0, :]
    mlac[:, 8:132] = dw.T.reshape(4, 128, 31).transpose(1, 0, 2).reshape(128, 124)
    mlac[:, 132:136] = inp["conv_dw_b"][0].reshape(4, 128).T
    mlac[:, 136:140] = inp["conv_norm_g"][0].reshape(4, 128).T
    mlac[:, 140:144] = inp["conv_norm_b"][0].reshape(4, 128).T
    d = dict(x=inp["x"][b], ctx=inp["ctx"][b], cond=np.ascontiguousarray(cond), mlac=mlac)
    for k in ("ada_w", "ada_b", "ffn_w1", "ffn_w3", "ffn_w2"):
        d[k] = inp[k]
    d["mla_w_in"] = inp["mla_w_in"][0]
    d["mla_w_qb"] = inp["mla_w_qb"][0]
    d["mla_w_kvb"] = inp["mla_w_kvb"][0]
    d["mix_w_out"] = inp["mix_w_out"][0]
    pp = lambda v: v.reshape(8, 128).T
    rwv = np.zeros((128, 88), f)
    for j in range(6):
        rwv[:, j * 8:(j + 1) * 8] = pp(inp["rwkv_x_mix"][0][j])
    rwv[:, 48:56] = pp(inp["rwkv_k_k"][0])
    rwv[:, 56:64] = pp(inp["rwkv_k_a"][0])
    rwv[:, 64:72] = pp(inp["rwkv_r_k"][0].reshape(-1))
    rwv[:, 72:80] = pp(inp["rwkv_a0"][0][0])
    rwv[:, 80:88] = pp(inp["rwkv_a0"][0][1])
    d["rwv"] = rwv
    for k in ("rwkv_w_r", "rwkv_w_k", "rwkv_w_v", "rwkv_w0", "rwkv_w1", "rwkv_w2", "rwkv_a1", "rwkv_a2",
              "rwkv_g1", "rwkv_g2", "rwkv_ln_g", "rwkv_ln_b", "rwkv_w_o"):
        d[k] = inp[k][0]
    return d


def build(S, stages=("ada", "ffn")):
    nc = bass.Bass("TRN2", target_bir_lowering=False)

    def din(name, shape, dt=F32):
        return nc.dram_tensor(name, list(shape), dt, kind="ExternalInput").ap()

    x_d = din("x", [S, D])
    ctx_d = din("ctx", [CTX, D])
    cond_d = din("cond", [128, 8, 2])
    ada_w_d = din("ada_w", [2, D, 9 * D])
    ada_b_d = din("ada_b", [2, 9 * D])
    w1_d = din("ffn_w1", [2, 2, D, DFF])
    w3_d = din("ffn_w3", [2, 2, D, DFF])
    w2_d = din("ffn_w2", [2, 2, DFF, D])
    ident_d = din("ident", [128, 128])
    ones_d = din("ones", [128, 128])
    cos_d = din("cos", [32, S])
    sin_d = din("sin", [32, S])
    prot_d = din("prot", [32, 32])
    mlac_d = din("mlac", [128, 144])
    w_in_d = din("mla_w_in", [D, 1568])
    w_qb_d = din("mla_w_qb", [384, 768])
    w_kvb_d = din("mla_w_kvb", [128, 1024])
    w_out_d = din("mix_w_out", [D, D])
    rwv_d = din("rwv", [128, 88])
    rwc_d = din("rwc", [128, RWC_N])
    rw = {k: din("rwkv_" + k, shp) for k, shp in (
        ("w_r", [D, D]), ("w_k", [D, D]), ("w_v", [D, D]), ("w0", [2, D]), ("w1", [2, D, 64]), ("w2", [2, 64, D]),
        ("a1", [2, D, 64]), ("a2", [2, 64, D]), ("g1", [D, 128]), ("g2", [128, D]), ("ln_g", [D]), ("ln_b", [D]),
        ("w_o", [D, D]))}
    out_d = nc.dram_tensor("out", [S, D], F32, kind="ExternalOutput").ap()
    C = Ctx(nc)
    P = C.P
    NK = CTX + S
    mods_d = C.dram("mods", [2, 2, 9 * D], F32)
    octx_d = C.dram("octx", [CTX, D], F32)
    qT_d = C.dram("qT", [NH, 96, S], BF16)
    qcT_d = C.dram("qcT", [NH, 96, CTX], BF16)
    kT_d = C.dram("kT", [NH, 96, NK], BF16)
    v_d = C.dram("v", [NK, 512], BF16)
    glu_l_d = C.dram("glu_l", [512, S + 30], F32)
    glu_c_d = C.dram("glu_c", [512, CTX + 30], F32)
    mixT_d = C.dram("mixT", [D, NK], BF16)
    dbg = {}
    if "dbg" in stages:
        dbg["octx"] = nc.dram_tensor("octx_o", [CTX, D], F32, kind="ExternalOutput").ap()
    C.push()
    consts = make_consts(C, ident_d)
    C.push()
    cpb = [C.sb([128, 4, D], F32) for _ in range(2)]
    i = 0
    for (src, dst, ntok) in ((x_d, out_d, S), (ctx_d, octx_d, CTX)):
        for t0 in range(0, ntok, 512):
            n = min(512, ntok - t0)
            b = cpb[i % 2]
            i += 1
            P.dma("sp", b[:, 0:n // 128, :], src[t0:t0 + n, :].rearrange("(s p) d -> p s d", p=128), w=[b])
            P.dma("sp", dst[t0:t0 + n, :].rearrange("(s p) d -> p s d", p=128), b[:, 0:n // 128, :], r=[b])
    C.pop()
    stage_ada(C, cond_d, ada_w_d, ada_b_d, mods_d)
    both = [(octx_d, CTX, 1), (out_d, S, 0)]
    if "ffn" in stages:
        stage_ffn(C, both, w1_d[0, 0], w3_d[0, 0], w2_d[0, 0], mods_d, 0, 0, consts)
    if "mla" in stages:
        wd = (w_in_d, w_qb_d, w_kvb_d, mlac_d, cos_d, sin_d, prot_d, ones_d)
        sub = [x for x in stages if x.startswith("mla_")] or ["mla_proj", "mla_conv", "mla_attn", "mla_out"]
        if "mla_proj" in sub:
            stage_mla_proj(C, [(octx_d, CTX, 1, 0, False), (out_d, S, 0, CTX, True)], S, wd, mods_d, 0, consts,
                           (qT_d, qcT_d, kT_d, v_d, glu_l_d, glu_c_d))
        if "mla_conv" in sub:
            stage_conv(C, [(glu_c_d, CTX, 0), (glu_l_d, S, CTX)], mlac_d, ones_d, mixT_d, consts)
        if "mla_attn" in sub:
            stage_attn(C, S, (qT_d, qcT_d, kT_d, v_d, mixT_d), ones_d)
        if "mla_out" in sub:
            stage_outproj(C, [(octx_d, CTX, 1, 0), (out_d, S, 0, CTX)], w_out_d, mixT_d, mods_d, 0)
    if "ffn2" in stages:
        stage_ffn(C, both, w1_d[0, 1], w3_d[0, 1], w2_d[0, 1], mods_d, 0, 6, consts)
    if "l1ffn" in stages:
        stage_ffn(C, both, w1_d[1, 0], w3_d[1, 0], w2_d[1, 0], mods_d, 1, 0, consts)
    if "rwkv" in stages:
        hT_d = C.dram("hTr", [D, NK + 4], BF16)
        rT_d = C.dram("rT", [D, NK], BF16)
        nkkT_d = C.dram("nkkT", [D, NK], BF16)
        bT_d = C.dram("bT", [2, D, NK], BF16)
        kdT_d = C.dram("kdT", [2, D, NK], BF16)
        sig_d = C.dram("sigw", [2, NK, D], F32)
        vr_d = C.dram("vr", [NK, D], BF16)
        sbon_d = C.dram("sbon", [NK, 16], F32)
        sigG_d = C.dram("sigG", [128, NK], BF16)
        y_d = C.dram("yscan", [2, NK, D], F32)
        sub = [x for x in stages if x.startswith("rw_")] or ["rw_h", "rw_proj", "rw_scan", "rw_out"]
        if "rw_h" in sub:
            stage_rwkv_h(C, [(octx_d, CTX, 1, 1), (out_d, S, 0, CTX + 3)], mods_d, 1, consts, hT_d)
        if "rw_proj" in sub:
            stage_rwkv_proj(C, [(CTX, 1, 0), (S, CTX + 3, CTX)],
                            (rw["w_r"], rw["w_k"], rw["w_v"], rw["w1"], rw["w2"], rw["a1"], rw["a2"], rw["g1"],
                             rw["w0"], rwv_d, rwc_d), hT_d,
                            (rT_d, nkkT_d, bT_d, kdT_d, sig_d, vr_d, sbon_d, sigG_d))
        if "rw_scan" in sub:
            stage_rwkv_scan(C, NK, (rT_d, nkkT_d, bT_d, kdT_d, sig_d, vr_d, y_d), rwc_d, consts)
        if "rw_out" in sub:
            stage_rwkv_out(C, [(out_d, S, 0, CTX)], (rw["g2"], rw["w_o"], rw["ln_g"], rw["ln_b"]),
                           (y_d, vr_d, sbon_d, sigG_d), mods_d, 1, consts)
    if "l1ffn2" in stages:
        stage_ffn(C, [(out_d, S, 0)], w1_d[1, 1], w3_d[1, 1], w2_d[1, 1], mods_d, 1, 6, consts)
    if "dbg" in stages:
        C.push()
        b = C.sb([128, 2, D], F32)
        P.dma("sp", b[:], octx_d.rearrange("(s p) d -> p s d", p=128), w=[b])
        P.dma("sp", dbg["octx"].rearrange("(s p) d -> p s d", p=128), b[:], r=[b])
        C.pop()
    C.pop()
    P.finish()
    return nc


ALL_STAGES = ("ada", "ffn", "mla", "ffn2", "l1ffn", "rwkv", "l1ffn2")
S_FULL = 8192


def kernel(**inputs):
    inp = {k: np.asarray(v) for k, v in inputs.items()}
    B = inp["x"].shape[0]
    S = inp["x"].shape[1]
    nc = build(S, ALL_STAGES)
    hc = host_consts(S)
    in_maps = []
    for b in range(B):
        d = host_layout(inp, b)
        d.update(hc)
        in_maps.append({k: np.ascontiguousarray(v, dtype=np.float32) for k, v in d.items()})
    res = run_bass_kernel_spmd(nc, in_maps, core_ids=list(range(B)))
    return np.stack([np.asarray(r["out"], dtype=np.float32) for r in res.results], axis=0)
```
